# Optimizing a Trainium2 kernel written in Bass

```python
import math
import jax, jax.numpy as jnp
from jax import lax
import numpy as np

D_MODEL = 1024
BATCH = 32
SEQ = 2048
DEPTH = 4

CTX_LEN = 256
GRID_W = 64
CHUNK = 128

ML_HEADS = 4
ML_DH = 64
ML_W = ML_HEADS * ML_DH
ML_CONV = 3

DA_HEADS = 4
DA_DQK = 64
DA_DV = 2 * DA_DQK
DA_W = DA_HEADS * DA_DV

SG_GROUPS = 4
SG_W = 256
SG_DG = SG_W // SG_GROUPS

D_MIX = ML_W + DA_W + SG_W
ROPE_BASE = 10000.0
LN_EPS = 1e-5
DEEPNORM_ALPHA = (2 * DEPTH) ** 0.25
DEEPNORM_BETA = (8 * DEPTH) ** -0.25

COL_SPLIT = (('ml_qk', 2 * ML_W), ('ml_v', ML_W), ('ml_o', ML_W), ('ml_z', ML_W),
             ('ml_gates', 4 * ML_HEADS), ('da_q', DA_W), ('da_k', DA_W), ('da_v', DA_W),
             ('da_z', DA_W), ('sg_u', SG_W), ('sg_v', SG_W), ('sg_z', SG_W))
D_IN = sum(w for _, w in COL_SPLIT)

kernel_name = 'hybrid_mlstm_diffattn_sgu_prefix_block'

F32 = jnp.float32


def layer_norm(x):
    xf = x.astype(F32)
    xc = xf - jnp.mean(xf, -1, keepdims=True)
    return xc * lax.rsqrt(jnp.mean(xc * xc, -1, keepdims=True) + LN_EPS)


def split_cols(p):
    out, off = {}, 0
    for name, w in COL_SPLIT:
        out[name] = p[..., off:off + w]
        off += w
    return out


def dwconv_centred(x, w, b):
    y = lax.conv_general_dilated(x, w[:, None, :].astype(x.dtype), window_strides=(1,),
                                 padding=[(ML_CONV // 2, ML_CONV // 2)],
                                 dimension_numbers=('NWC', 'WIO', 'NWC'),
                                 feature_group_count=x.shape[-1])
    return y + b


def mlstm_zero_state(batch):
    return (jnp.zeros((batch, ML_HEADS, ML_DH, ML_DH), F32),
            jnp.zeros((batch, ML_HEADS, ML_DH), F32),
            jnp.zeros((batch, ML_HEADS), F32))


def mlstm_chunkwise(q, k, v, log_i, log_f, state0, want_h):
    bsz, nh, length, dh = q.shape
    nc = length // CHUNK
    to_chunks = lambda t: t.astype(F32).reshape(bsz, nh, nc, CHUNK, *t.shape[3:])
    qc, kc, vc = to_chunks(q), to_chunks(k), to_chunks(v)
    ic, fc = to_chunks(log_i), to_chunks(log_f)
    b = jnp.cumsum(fc, axis=-1)
    b_end = b[..., -1]
    g = b_end[..., None] - b + ic
    m_loc = jnp.max(g, -1)
    w = jnp.exp(g - m_loc[..., None])
    c_loc = jnp.einsum('bhcs,bhcsv,bhcsk->bhcvk', w, vc, kc)
    n_loc = jnp.einsum('bhcs,bhcsk->bhck', w, kc)

    def step(carry, xs):
        c_st, n_st, m_st = carry
        cl, nl, ml, be = xs
        m_new = jnp.maximum(be + m_st, ml)
        a = jnp.exp(be + m_st - m_new)
        s = jnp.exp(ml - m_new)
        c_new = a[..., None, None] * c_st + s[..., None, None] * cl
        n_new = a[..., None] * n_st + s[..., None] * nl
        return (c_new, n_new, m_new), carry

    lead = lambda t: jnp.moveaxis(t, 2, 0)
    final, starts = lax.scan(step, state0, (lead(c_loc), lead(n_loc), lead(m_loc), lead(b_end)))
    if not want_h:
        return None, final
    c_prev, n_prev, m_prev = (jnp.moveaxis(t, 0, 2) for t in starts)
    a_log = b + m_prev[..., None]
    d_log = b[..., :, None] - b[..., None, :] + ic[..., None, :]
    mask = jnp.tril(jnp.ones((CHUNK, CHUNK), bool))
    d_log = jnp.where(mask, d_log, -jnp.inf)
    m_j = jnp.maximum(a_log, jnp.max(d_log, -1))
    w_inter = jnp.exp(a_log - m_j)
    s = jnp.einsum('bhcjd,bhcsd->bhcjs', qc, kc) * jnp.exp(d_log - m_j[..., None])
    num = (w_inter[..., None] * jnp.einsum('bhcvk,bhcjk->bhcjv', c_prev, qc)
           + jnp.einsum('bhcjs,bhcsv->bhcjv', s, vc))
    den = w_inter * jnp.einsum('bhck,bhcjk->bhcj', n_prev, qc) + jnp.sum(s, -1)
    h = num / jnp.maximum(jnp.abs(den), jnp.exp(-m_j))[..., None]
    return h.reshape(bsz, nh, length, dh), final


def mlstm_inputs(sp, conv_w, conv_b):
    bsz, length, _ = sp['ml_v'].shape
    qk = jax.nn.silu(dwconv_centred(sp['ml_qk'], conv_w, conv_b))
    heads = lambda t: t.reshape(bsz, length, ML_HEADS, ML_DH).transpose(0, 2, 1, 3)
    q = heads(qk[..., :ML_W])
    k = heads(qk[..., ML_W:]) * (ML_DH ** -0.5)
    v = heads(sp['ml_v'])
    gates = sp['ml_gates'].astype(F32).reshape(bsz, length, 4, ML_HEADS).transpose(2, 0, 3, 1)
    fwd = (gates[0], jax.nn.log_sigmoid(gates[1]))
    bwd = (gates[2], jax.nn.log_sigmoid(gates[3]))
    return q, k, v, fwd, bwd


def mlstm_bidir(q, k, v, fwd, bwd, state_f, state_b, want_h):
    flip = lambda t: jnp.flip(t, axis=2)
    h_f, fin_f = mlstm_chunkwise(q, k, v, fwd[0], fwd[1], state_f, want_h)
    h_b, fin_b = mlstm_chunkwise(flip(q), flip(k), flip(v), flip(bwd[0]), flip(bwd[1]), state_b, want_h)
    h = h_f + flip(h_b) if want_h else None
    return h, fin_f, fin_b


def mlstm_output(h, sp, norm_g):
    bsz, length, _ = sp['ml_o'].shape
    h = h.transpose(0, 2, 1, 3)
    o = jax.nn.sigmoid(sp['ml_o'].astype(F32)).reshape(bsz, length, ML_HEADS, ML_DH)
    y = layer_norm(o * h).reshape(bsz, length, ML_W) * norm_g
    return (y * jax.nn.silu(sp['ml_z'].astype(F32))).astype(sp['ml_z'].dtype)


def axial_rope_tables(n_tokens):
    n_rows = n_tokens // GRID_W
    row = jnp.repeat(jnp.arange(n_rows, dtype=F32), GRID_W)
    col = jnp.tile(jnp.arange(GRID_W, dtype=F32), n_rows)
    half = DA_DQK // 2
    inv = ROPE_BASE ** (-jnp.arange(0, half, 2, dtype=F32) / half)
    ang_r, ang_c = row[:, None] * inv, col[:, None] * inv
    ang = jnp.concatenate([ang_r, ang_r, ang_c, ang_c], -1)
    return jnp.cos(ang), jnp.sin(ang)


def rotate_half(t):
    t1, t2 = jnp.split(t, 2, -1)
    return jnp.concatenate([-t2, t1], -1)


def apply_axial_rope(t, cos, sin):
    tf = t.astype(F32)
    tr, tc = jnp.split(tf, 2, -1)
    rot = jnp.concatenate([rotate_half(tr), rotate_half(tc)], -1)
    return (tf * cos[None, :, None, None, :] + rot * sin[None, :, None, None, :]).astype(t.dtype)


def diff_attn_heads(sp):
    bsz, length, _ = sp['da_q'].shape
    q = sp['da_q'].reshape(bsz, length, DA_HEADS, 2, DA_DQK)
    k = sp['da_k'].reshape(bsz, length, DA_HEADS, 2, DA_DQK)
    v = sp['da_v'].reshape(bsz, length, DA_HEADS, DA_DV)
    return q, k, v


def diff_attn_core(q, k, v, lam):
    s = jnp.einsum('bqhcd,bkhcd->bhcqk', q, k, preferred_element_type=F32) * (DA_DQK ** -0.5)
    p = jax.nn.softmax(s, axis=-1)
    a = p[:, :, 0] - lam * p[:, :, 1]
    return jnp.einsum('bhqk,bkhv->bqhv', a.astype(v.dtype), v)


def diff_attn_blocked(q, k, v, lam):
    bsz, length = q.shape[:2]
    nb = length // CHUNK
    qb = jnp.moveaxis(q.reshape(bsz, nb, CHUNK, *q.shape[2:]), 1, 0)
    ob = lax.map(lambda qq: diff_attn_core(qq, k, v, lam), qb)
    return jnp.moveaxis(ob, 0, 1).reshape(bsz, length, DA_HEADS, DA_DV)


def diff_attn_output(o, sp, norm_g, lam_init):
    bsz, length = o.shape[:2]
    of = o.astype(F32)
    y = of * lax.rsqrt(jnp.mean(of * of, -1, keepdims=True) + LN_EPS) * norm_g * (1.0 - lam_init)
    return (y.reshape(bsz, length, DA_W) * jax.nn.silu(sp['da_z'].astype(F32))).astype(sp['da_z'].dtype)


def spatial_gating(sp, norm_g, norm_b, w_s, b_s):
    bsz, length, _ = sp['sg_u'].shape
    nc = length // CHUNK
    u = jax.nn.gelu(sp['sg_u'].astype(F32), approximate=False)
    v = layer_norm(jax.nn.gelu(sp['sg_v'].astype(F32), approximate=False)) * norm_g + norm_b
    vg = v.reshape(bsz, nc, CHUNK, SG_GROUPS, SG_DG)
    vs = jnp.einsum('gpq,bcqgd->bcpgd', w_s.astype(F32), vg) + b_s.T[:, :, None]
    y = u * vs.reshape(bsz, length, SG_W)
    return (y * jax.nn.silu(sp['sg_z'].astype(F32))).astype(sp['sg_z'].dtype)


def hybrid_layer(xl, xc, c, c_ctx, w_mod, b_mod, w_in, b_in, conv_w, conv_b, ml_g,
                 lq1, lk1, lq2, lk2, da_g, sg_g, sg_b, w_s, b_s, w_out, ln_g, ln_b,
                 lam_init, cos, sin, need_ctx_out):
    dt = xl.dtype
    shift, scale, gate = jnp.split(jax.nn.silu(c) @ w_mod + b_mod, 3, -1)
    shift_c, scale_c, gate_c = jnp.split(jax.nn.silu(c_ctx) @ w_mod + b_mod, 3, -1)
    hl = (layer_norm(xl) * (1.0 + scale[:, None]) + shift[:, None]).astype(dt)
    hc = (layer_norm(xc) * (1.0 + scale_c) + shift_c).astype(xc.dtype)
    pl = split_cols(hl @ w_in + b_in)
    pc = split_cols(hc @ w_in + b_in)

    q_l, k_l, v_l, fwd_l, bwd_l = mlstm_inputs(pl, conv_w, conv_b)
    q_c, k_c, v_c, fwd_c, bwd_c = mlstm_inputs(pc, conv_w, conv_b)
    zero = mlstm_zero_state(xc.shape[0])
    h_c, st_f, st_b = mlstm_bidir(q_c, k_c, v_c, fwd_c, bwd_c, zero, zero, need_ctx_out)
    h_l, _, _ = mlstm_bidir(q_l, k_l, v_l, fwd_l, bwd_l, st_f, st_b, True)
    ml_l = mlstm_output(h_l, pl, ml_g)

    lam = (jnp.exp(jnp.sum(lq1.astype(F32) * lk1.astype(F32)))
           - jnp.exp(jnp.sum(lq2.astype(F32) * lk2.astype(F32))) + lam_init)
    dq_l, dk_l, dv_l = diff_attn_heads(pl)
    dq_c, dk_c, dv_c = diff_attn_heads(pc)
    dq_l = apply_axial_rope(dq_l, cos, sin)
    dk_l = apply_axial_rope(dk_l, cos, sin)
    k_all = jnp.concatenate([dk_l, dk_c], axis=1)
    v_all = jnp.concatenate([dv_l, dv_c], axis=1)
    da_l = diff_attn_output(diff_attn_blocked(dq_l, k_all, v_all, lam), pl, da_g, lam_init)

    sg_l = spatial_gating(pl, sg_g, sg_b, w_s, b_s)

    y_l = jnp.concatenate([ml_l, da_l, sg_l], -1) @ w_out
    xl_new = (layer_norm(DEEPNORM_ALPHA * xl + gate[:, None] * y_l) * ln_g + ln_b).astype(dt)
    if not need_ctx_out:
        return xl_new, None
    ml_c = mlstm_output(h_c, pc, ml_g)
    da_c = diff_attn_output(diff_attn_core(dq_c, dk_c, dv_c, lam), pc, da_g, lam_init)
    sg_c = spatial_gating(pc, sg_g, sg_b, w_s, b_s)
    y_c = jnp.concatenate([ml_c, da_c, sg_c], -1) @ w_out
    xc_new = (layer_norm(DEEPNORM_ALPHA * xc + gate_c * y_c) * ln_g + ln_b).astype(xc.dtype)
    return xl_new, xc_new


def setup_inputs(seed: int = 0) -> dict:
    key = jax.random.key(seed)
    ks = jax.random.split(key, 24)
    nrm = lambda k, shape, s: s * jax.random.normal(k, shape, F32)
    off = 0
    for name, w in COL_SPLIT:
        if name == 'ml_gates':
            break
        off += w
    bias_off = np.zeros((D_IN,), np.float32)
    f_init = np.linspace(3.0, 6.0, ML_HEADS).astype(np.float32)
    bias_off[off + ML_HEADS:off + 2 * ML_HEADS] = f_init
    bias_off[off + 3 * ML_HEADS:off + 4 * ML_HEADS] = f_init
    return {
        'x': nrm(ks[0], (BATCH, SEQ, D_MODEL), 1.0),
        'c': nrm(ks[1], (BATCH, D_MODEL), 1.0),
        'ctx': nrm(ks[2], (BATCH, CTX_LEN, D_MODEL), 1.0),
        'c_ctx': nrm(ks[3], (D_MODEL,), 1.0),
        'w_mod': nrm(ks[4], (DEPTH, D_MODEL, 3 * D_MODEL), D_MODEL ** -0.5),
        'b_mod': nrm(ks[5], (DEPTH, 3 * D_MODEL), 0.02),
        'w_in': nrm(ks[6], (DEPTH, D_MODEL, D_IN), D_MODEL ** -0.5),
        'b_in': nrm(ks[7], (DEPTH, D_IN), 0.02) + jnp.asarray(bias_off),
        'ml_conv_w': nrm(ks[8], (DEPTH, ML_CONV, 2 * ML_W), ML_CONV ** -0.5),
        'ml_conv_b': nrm(ks[9], (DEPTH, 2 * ML_W), 0.02),
        'ml_norm_g': 1.0 + nrm(ks[10], (DEPTH, ML_W), 0.02),
        'da_lam_q1': nrm(ks[11], (DEPTH, DA_DQK), 0.1),
        'da_lam_k1': nrm(ks[12], (DEPTH, DA_DQK), 0.1),
        'da_lam_q2': nrm(ks[13], (DEPTH, DA_DQK), 0.1),
        'da_lam_k2': nrm(ks[14], (DEPTH, DA_DQK), 0.1),
        'da_norm_g': 1.0 + nrm(ks[15], (DEPTH, DA_DV), 0.02),
        'sg_norm_g': 1.0 + nrm(ks[16], (DEPTH, SG_W), 0.02),
        'sg_norm_b': nrm(ks[17], (DEPTH, SG_W), 0.02),
        'sg_w_s': nrm(ks[18], (DEPTH, SG_GROUPS, CHUNK, CHUNK), CHUNK ** -0.5),
        'sg_b_s': 1.0 + nrm(ks[19], (DEPTH, SG_GROUPS, CHUNK), 0.02),
        'w_out': nrm(ks[20], (DEPTH, D_MIX, D_MODEL), DEEPNORM_BETA * D_MIX ** -0.5),
        'ln_g': 1.0 + nrm(ks[21], (DEPTH, D_MODEL), 0.02),
        'ln_b': nrm(ks[22], (DEPTH, D_MODEL), 0.02),
    }


def reference(x, c, ctx, c_ctx, w_mod, b_mod, w_in, b_in, ml_conv_w, ml_conv_b, ml_norm_g,
              da_lam_q1, da_lam_k1, da_lam_q2, da_lam_k2, da_norm_g, sg_norm_g, sg_norm_b,
              sg_w_s, sg_b_s, w_out, ln_g, ln_b):
    cos, sin = axial_rope_tables(x.shape[1])
    xl, xc = x, ctx
    for l in range(DEPTH):
        lam_init = 0.8 - 0.6 * math.exp(-0.3 * l)
        xl, xc = hybrid_layer(xl, xc, c, c_ctx, w_mod[l], b_mod[l], w_in[l], b_in[l],
                              ml_conv_w[l], ml_conv_b[l], ml_norm_g[l],
                              da_lam_q1[l], da_lam_k1[l], da_lam_q2[l], da_lam_k2[l], da_norm_g[l],
                              sg_norm_g[l], sg_norm_b[l], sg_w_s[l], sg_b_s[l], w_out[l],
                              ln_g[l], ln_b[l], lam_init, cos, sin, l < DEPTH - 1)
    return xl
```

```python
import math
import numpy as np
from contextlib import ExitStack
import concourse.bass as bass
import concourse.mybir as mybir
from concourse.bass_utils import run_bass_kernel_spmd

F32 = mybir.dt.float32
BF16 = mybir.dt.bfloat16
AF = mybir.ActivationFunctionType
ALU = mybir.AluOpType
AX = mybir.AxisListType

D = 1024
CTXL = 256
LAT = 2048
T = CTXL + LAT
NT = T // 128
DEPTH = 4
EPS = 1e-5
ALPHA = (2 * DEPTH) ** 0.25
NWC = 5136
BLKS = [(0, 256), (256, 768), (768, 1280), (1280, 1792), (1792, 2304)]
NDMA = 24
PSUM_KEYS = frozenset(["g0", "g1", "g2", "g3", "g4", "tpa", "tpb", "tb"])

C_MLQK, C_MLV, C_DAK, C_DAV, C_SG, C_DAQ, C_MLOZ, C_DAZ = 0, 512, 784, 1808, 2320, 3088, 4112, 4624
B_MLV, B_DAV, B_SG, B_MLOZ, B_DAZ, NBIAS = 0, 272, 784, 1552, 2064, 2576


class Sched:
    def __init__(self):
        self.ins = []
        self.lastw = {}
        self.readers = {}
        self.region = {}
        self.rlast = {}
        self.rdma = {}

    def _add(self, eng, fn, reads, writes, is_dma, extra=()):
        reads = list(reads)
        writes = list(writes)
        for k in reads:
            if k in PSUM_KEYS:
                writes.append(("rdser", k))
        deps = set(extra)
        touched = set()
        for k in reads + writes:
            nm = k[0] if isinstance(k, tuple) else k
            r = self.region.get(nm)
            if r is not None:
                touched.add(r)
        for r in touched:
            w = self.lastw.get(("R", r))
            if w is not None:
                deps.add(w)
        for k in reads:
            w = self.lastw.get(k)
            if w is not None:
                deps.add(w)
        for k in writes:
            w = self.lastw.get(k)
            if w is not None:
                deps.add(w)
            rd = self.readers.get(k)
            if rd:
                deps.update(rd[0].values())
                deps.update(rd[1])
        idx = len(self.ins)
        self.ins.append([eng, fn, deps, is_dma])
        for k in writes:
            self.lastw[k] = idx
            self.readers[k] = [{}, []]
        ws = set(writes)
        for k in reads:
            if k not in ws:
                rd = self.readers.setdefault(k, [{}, []])
                if is_dma:
                    rd[1].append(idx)
                else:
                    rd[0][eng] = idx
        for r in touched:
            if is_dma:
                self.rdma.setdefault(r, []).append(idx)
            else:
                self.rlast.setdefault(r, {})[eng] = idx
        return idx

    def fence(self, region, eng, fn):
        deps = set(self.rlast.get(region, {}).values()) | set(self.rdma.get(region, []))
        idx = self._add(eng, fn, (), (), False, extra=deps)
        self.lastw[("R", region)] = idx
        self.rlast[region] = {}
        self.rdma[region] = []
        return idx

    def op(self, eng, fn, reads=(), writes=()):
        return self._add(eng, fn, reads, writes, False)

    def dma(self, eng, fn, reads=(), writes=()):
        return self._add(eng, fn, reads, writes, True)

    def emit(self, nc, stack):
        ins = self.ins
        n = len(ins)
        engs = ["pe", "act", "dve", "pool", "sp"]
        dma_list = [i for i in range(n) if ins[i][3]]
        dma_slot = {}
        for j, i in enumerate(dma_list):
            dma_slot[i] = j
            if j >= NDMA:
                ins[i][2].add(dma_list[j - NDMA])
        needed = [False] * n
        for i in range(n):
            e = ins[i][0]
            nd = set()
            for d in ins[i][2]:
                if ins[d][0] == e and e == "pe" and not ins[d][3]:
                    continue
                nd.add(d)
                needed[d] = True
            ins[i][2] = nd
        esem = {e: stack.enter_context(nc.semaphore("s_" + e)) for e in engs}
        dsem = [stack.enter_context(nc.semaphore("d_%d" % j)) for j in range(NDMA)]
        cnt = {e: 0 for e in engs}
        tok = [None] * n
        for i in range(n):
            e, fn, deps, is_dma = ins[i]
            if is_dma:
                j = dma_slot[i]
                tok[i] = (("d", j % NDMA), 16 * (j // NDMA + 1))
            elif needed[i]:
                cnt[e] += 1
                tok[i] = (("e", e), cnt[e])
        per = {e: [] for e in engs}
        for i in range(n):
            per[ins[i][0]].append(i)
        self.counts = {e: len(per[e]) for e in engs}

        def semof(key):
            return esem[key[1]] if key[0] == "e" else dsem[key[1]]

        def run(e, h):
            seen = {}
            for i in per[e]:
                _, fn, deps, is_dma = ins[i]
                want = {}
                for d in deps:
                    k, v = tok[d]
                    if v > want.get(k, 0):
                        want[k] = v
                for k, v in want.items():
                    if seen.get(k, 0) < v:
                        h.wait_ge(semof(k), v)
                        seen[k] = v
                r = fn(h)
                if tok[i] is not None:
                    k, v = tok[i]
                    r.then_inc(semof(k), 16 if is_dma else 1)
            for i in per[e]:
                if ins[i][3]:
                    k, v = tok[i]
                    if seen.get(k, 0) < v:
                        h.wait_ge(semof(k), v)
                        seen[k] = v

        with nc.Block() as block:
            @block.tensor
            def _(h):
                run("pe", h)

            @block.scalar
            def _(h):
                run("act", h)

            @block.vector
            def _(h):
                run("dve", h)

            @block.gpsimd
            def _(h):
                run("pool", h)

            @block.sync
            def _(h):
                run("sp", h)


class Arena:
    def __init__(self, ap):
        self.ap = ap
        self.off = 0
        self.size = ap.shape[1]

    def f32(self, n):
        a = self.ap[:, self.off:self.off + n]
        self.off += (n + 7) // 8 * 8
        assert self.off <= self.size, (self.off, self.size)
        return a

    def bf16(self, n):
        w = (n + 1) // 2
        a = self.ap[:, self.off:self.off + w].bitcast(BF16)
        self.off += (w + 7) // 8 * 8
        assert self.off <= self.size, (self.off, self.size)
        return a[:, 0:n]


def v3(ap, a):
    return ap.rearrange("p (a b) -> p a b", a=a)


def v4(ap, a, b):
    return ap.rearrange("p (a b c) -> p a b c", a=a, b=b)


def bc(ap, shape):
    return ap.to_broadcast(shape)


class _Stop(Exception):
    pass


def build(NB, NL, dbg=None, stop=None):
    nc = bass.Bass("TRN2", target_bir_lowering=False)
    S = Sched()

    def dram(name, shape, dtype=F32, kind="ExternalInput"):
        return nc.dram_tensor(name, shape, dtype, kind=kind).ap()

    x_d = dram("x", [NB, LAT, D])
    ctx_d = dram("ctx", [NB, CTXL, D])
    c_d = dram("c", [NB, D])
    cctx_d = dram("c_ctx", [1, D])
    wmod_d = dram("w_mod", [DEPTH, D, 3 * D])
    bmod_d = dram("b_mod", [DEPTH, 3 * D])
    win_d = dram("w_in", [DEPTH, D, 4112])
    bin_d = dram("b_in", [DEPTH, 4112])
    convw_d = dram("ml_conv_w", [DEPTH, 3, 512])
    convb_d = dram("ml_conv_b", [DEPTH, 512])
    mlg_d = dram("ml_norm_g", [DEPTH, 256])
    lq1_d = dram("da_lam_q1", [DEPTH, 64])
    lk1_d = dram("da_lam_k1", [DEPTH, 64])
    lq2_d = dram("da_lam_q2", [DEPTH, 64])
    lk2_d = dram("da_lam_k2", [DEPTH, 64])
    dag_d = dram("da_norm_g", [DEPTH, 128])
    sgg_d = dram("sg_norm_g", [DEPTH, 256])
    sgb_d = dram("sg_norm_b", [DEPTH, 256])
    sgw_d = dram("sg_w_s", [DEPTH, 4, 128, 128])
    sgbs_d = dram("sg_b_s", [DEPTH, 4, 128])
    wout_d = dram("w_out", [DEPTH, D, D])
    lng_d = dram("ln_g", [DEPTH, D])
    lnb_d = dram("ln_b", [DEPTH, D])
    cst_d = dram("cst", [128, 640])
    rope_d = dram("rope", [2, 128, LAT])
    y_d = dram("y", [NB, LAT, D], kind="ExternalOutput")
    wbf_d = dram("wbf", [DEPTH, D, NWC], BF16, kind="Internal")
    wobf_d = dram("wobf", [DEPTH, D, D], BF16, kind="Internal")
    mod_d = dram("modscr", [DEPTH, 8, 3 * D], F32, kind="Internal")
    xs_d = dram("xs", [NB, T, D], F32, kind="Internal")
    dbg_out = {}

    with ExitStack() as st:
        arena_t = st.enter_context(nc.sbuf_tensor("arena", [128, 51200], F32))
        AR = Arena(arena_t[:, :])
        ps_tp = st.enter_context(nc.psum_tensor("ps_tp", [128, 1024], F32))
        ps_tb = st.enter_context(nc.psum_tensor("ps_tb", [128, 1024], BF16))
        ps_g = [st.enter_context(nc.psum_tensor("ps_g%d" % i, [128, 512], F32)) for i in range(5)]

        cstF = AR.f32(640)
        identF, triF, triB, onesF, rperm = (cstF[:, i * 128:(i + 1) * 128] for i in range(5))
        identB = AR.bf16(128)
        maskF = AR.bf16(128)
        maskB = AR.bf16(128)
        neghalf = AR.f32(64)
        dummy = AR.f32(8)
        sgW = v4(AR.bf16(DEPTH * 4 * 128), DEPTH, 4)
        sgbs = v3(AR.f32(DEPTH * 4), DEPTH)
        fmb = v3(AR.f32(DEPTH * 20), DEPTH)
        convw = v4(AR.f32(DEPTH * 12), DEPTH, 3)
        convb = v3(AR.f32(DEPTH * 4), DEPTH)
        hconvb = v3(AR.f32(DEPTH * 4), DEPTH)
        lamv = AR.f32(DEPTH)
        neglam = AR.f32(DEPTH)
        biasb = AR.f32(NBIAS)
        lng = AR.f32(D)
        lnb = AR.f32(D)
        mlgc = AR.f32(256)
        dagc = AR.f32(128)
        sggt = AR.f32(256)
        sgbt = AR.f32(256)
        gate_l = AR.f32(D)
        gate_c = AR.f32(D)
        modp = v3(AR.f32(32), 4)
        WR_OFF = AR.off
        wring = [v3(AR.bf16(8 * 512), 8) for _ in range(2)]
        wq32 = v3(arena_t[:, WR_OFF:WR_OFF + 4096], 8)
        wg32 = v3(AR.f32(128), 8)
        Hml = v3(AR.f32(NT * 256), NT)
        small = AR.f32(256)
        XOFF = AR.off
        XSZ = 8192
        AR.off += XSZ
        BOFF = AR.off
        BSZ = 12900
        AR.off += BSZ
        AOFF = AR.off
        ASZ = 51200 - AOFF
        assert ASZ >= 9216, ASZ

        def sub(off, size):
            return Arena(arena_t[:, off:off + size])

        for nm in ["X", "B", "A"]:
            pass

        aX = sub(XOFF, XSZ)
        xblk = [v3(aX.f32(4096), 4), v3(aX.f32(4096), 4)]
        aX = sub(XOFF, XSZ)
        pst = aX.f32(T)
        cacc = aX.f32(T)
        aX = sub(XOFF, XSZ)
        m_pt = [v3(aX.f32(512), 4) for _ in range(2)]
        m_vsf = [aX.f32(288) for _ in range(2)]
        m_vs = [v3(m_vsf[i], 4)[:, :, 0:65] for i in range(2)]
        m_kt = [aX.f32(256) for _ in range(2)]
        m_hs = [v3(aX.f32(288), 4)[:, :, 0:65] for _ in range(2)]
        m_cst_flat = aX.f32(576)
        m_cst = v4(m_cst_flat, 2, 4)[:, :, :, 0:65]
        m_tmp = v3(aX.f32(288), 4)[:, :, 0:65]
        m_hn = v3(aX.f32(256), 4)
        aX = sub(XOFF, XSZ)
        ropeC = aX.f32(512)
        ropeS = aX.f32(512)
        rstA = aX.f32(512)
        rstB = aX.f32(512)
        sgp = aX.f32(768)
        sgu = aX.f32(256)
        sgjunk = aX.f32(256)
        sgvn = aX.bf16(256)
        sgy = aX.f32(256)
        sgzs = aX.f32(256)
        aB = sub(BOFF, BSZ)
        qkT = v3(aB.f32(4 * T), 4)
        vaug = v4(aB.bf16(NT * 4 * 72), NT, 4)[:, :, :, 0:65]
        Gt = v3(aB.f32(NT * 16), NT)
        LFp = v3(aB.f32(144), 2)
        Ej = v3(aB.f32(144), 2)
        Ws = v3(aB.f32(144), 2)
        Eend = v3(aB.f32(144), 2)
        gtmp = v3(aB.f32(144), 2)
        aB = sub(BOFF, BSZ)
        dkT = v3(aB.bf16(4 * T), 4)
        dvaug = v4(aB.bf16(NT * 4 * 136), NT, 4)[:, :, :, 0:129]
        ysg = v3(aB.bf16(NT * 256), NT)
        aA = sub(AOFF, ASZ)
        hTall = v3(aA.bf16(8 * T), 8)
        xn_a = aA.f32(0) if False else None
        aA = sub(AOFF, ASZ)
        hTblk = v3(aA.bf16(8 * 512), 8)
        qTblk = v3(aA.bf16(4 * 512), 4)
        gates = v3(aA.bf16(4 * 1024), 4)
        mixT = v3(aA.bf16(8 * 512), 8)
        ptb = [aA.bf16(512) for _ in range(4)]
        oev = v4(aA.f32(4 * 2 * 132), 4, 2)[:, :, :, 0:129]
        att = aA.f32(128)
        xn5 = aA.f32(1024)
        aX5 = sub(XOFF + 4096, 4096)
        p5ropeC = aX5.f32(512)
        p5ropeS = aX5.f32(512)
        p5stA = aX5.f32(512)
        p5stB = aX5.f32(512)
        p5tmp = aX5.f32(512)
        p5t2 = aX5.f32(512)
        p5ml = aX5.f32(256)
        p5x_ml = arena_t[:, XOFF + 4096:XOFF + 4096 + 1024]
        p5x_sq = arena_t[:, XOFF + 4096 + 2048:XOFF + 4096 + 3072]
        p5sq = aX5.f32(256)

        for nm in ["xblk", "pst", "cacc", "mtmp", "sgt", "rope4", "p5x"]:
            S.region[nm] = "X"
        for nm in ["qkT", "qkpre", "vaug", "ktok", "Gt", "gder", "dkT", "dvaug", "ysg"]:
            S.region[nm] = "B"
        for nm in ["hTall", "p5"]:
            S.region[nm] = "A"

        def fence(region):
            S.fence(region, "pool", lambda h: h.memset(dummy[:, 0:1], 0.0))

        def mm(out, lhsT, rhs, start, stop, reads, writes):
            S.op("pe", lambda h: h.matmul(out, lhsT=lhsT, rhs=rhs, start=start, stop=stop), reads, writes)

        def tr(out, in_, ident, reads, writes):
            S.op("pe", lambda h: h.transpose(out, in_, ident), reads, writes)

        def act(out, in_, func, reads, writes, bias=None, scale=None, accum=None):
            kw = {}
            if accum is not None:
                kw["accum_out"] = accum
            if bias is not None:
                kw["bias"] = bias
            if scale is not None:
                kw["scale"] = scale
            S.op("act", lambda h: h.activation(out=out, in_=in_, func=func, **kw), reads, writes)

        def tt(eng, out, in0, in1, op, reads, writes):
            S.op(eng, lambda h: h.tensor_tensor(out=out, in0=in0, in1=in1, op=op), reads, writes)

        def ts(eng, out, in0, s1, s2, op0, op1, reads, writes):
            if s2 is None:
                S.op(eng, lambda h: h.tensor_scalar(out=out, in0=in0, scalar1=s1, scalar2=None, op0=op0), reads, writes)
            else:
                S.op(eng, lambda h: h.tensor_scalar(out=out, in0=in0, scalar1=s1, scalar2=s2, op0=op0, op1=op1), reads, writes)

        def stt(eng, out, in0, scalar, in1, op0, op1, reads, writes):
            S.op(eng, lambda h: h.scalar_tensor_tensor(out=out, in0=in0, scalar=scalar, in1=in1, op0=op0, op1=op1), reads, writes)

        def cp(eng, out, in_, reads, writes):
            if eng == "act":
                S.op(eng, lambda h: h.activation(out=out, in_=in_, func=AF.Copy), reads, writes)
            else:
                S.op(eng, lambda h: h.tensor_copy(out=out, in_=in_), reads, writes)

        def red(out, in_, reads, writes):
            S.op("dve", lambda h: h.tensor_reduce(out=out, in_=in_, axis=AX.X, op=ALU.add), reads, writes)

        def dma(out, in_, reads, writes, slow=False):
            if slow:
                S.dma("sp", lambda h: h.dma_start(out=out, in_=in_, allow_slow_non_contiguous=True), reads, writes)
            else:
                S.dma("sp", lambda h: h.dma_start(out=out, in_=in_), reads, writes)

        def tap(name, ap, key, dtype=F32):
            if dbg is None or name not in dbg:
                return
            shp = list(ap.shape)
            d = nc.dram_tensor("dbg_" + name, shp, dtype, kind="ExternalOutput").ap()
            dbg_out[name] = shp
            dma(d, ap, list(key) if isinstance(key, list) else [key], ["dbg_" + name])

        def stats(s1, s2, k, n, tag):
            rk1 = tag[0] if isinstance(tag, tuple) else [tag + "s1"]
            rk2 = tag[1] if isinstance(tag, tuple) else [tag + "s2"]
            mean = small[:, 0:k]
            msq = small[:, 16:16 + k]
            var = small[:, 32:32 + k]
            rstd = small[:, 48:48 + k]
            ts("dve", mean, s1, 1.0 / n, None, ALU.mult, None, rk1, ["st_mean"])
            tt("dve", msq, mean, mean, ALU.mult, ["st_mean"], ["st_msq"])
            stt("dve", var, s2, 1.0 / n, msq, ALU.mult, ALU.subtract, rk2 + ["st_msq"], ["st_var"])
            ts("dve", var, var, EPS, None, ALU.add, None, ["st_var"], ["st_var"])
            tt("pool", rstd, var, neghalf[:, 0:k], ALU.pow, ["st_var", "cst"], ["st_rstd"])
            return mean, rstd

        def chk(name):
            if stop == name:
                raise _Stop()

        try:
            dma(cstF, cst_d[:, :], [], ["cst"])
            cp("dve", identB, identF, ["cst"], ["cstb"])
            cp("dve", maskF, triF, ["cst"], ["cstb"])
            cp("dve", maskB, triB, ["cst"], ["cstb"])
            S.op("pool", lambda h: h.memset(neghalf, -0.5), [], ["cst"])

            S.op("pool", lambda h: h.memset(fmb[:, :, :], 0.0), [], ["fmb"])
            for l in range(DEPTH):
                for j in range(4):
                    dma(fmb[:, l, j:j + 1], bin_d[l, j * 128:(j + 1) * 128].rearrange("(p o) -> p o", o=1), [], ["fmb"])
                for h4 in range(4):
                    dma(fmb[:, l, 4 + 2 * h4:5 + 2 * h4], bin_d[l, 1808 + h4 * 128:1808 + (h4 + 1) * 128].rearrange("(p o) -> p o", o=1), [], ["fmb"])
                    dma(fmb[:, l, 12 + 2 * h4:13 + 2 * h4], bin_d[l, 1296 + h4 * 128:1296 + (h4 + 1) * 128].rearrange("(p o) -> p o", o=1), [], ["fmb"])
                for j in range(3):
                    dma(convw[:, l, j, :], convw_d[l, j, :].rearrange("(c p) -> p c", p=128), [], ["convp"], slow=True)
                dma(convb[:, l, :], convb_d[l, :].rearrange("(c p) -> p c", p=128), [], ["convp"], slow=True)
                dma(sgbs[:, l, :], sgbs_d[l, :, :].rearrange("g p -> p g"), [], ["sgbs"], slow=True)
            ts("dve", hconvb[:, :, :], convb[:, :, :], 0.5, None, ALU.mult, None, ["convp"], ["hconvb"])
            for l in range(DEPTH):
                pso = ps_g[0][:, 0:16]
                mm(pso, rperm, fmb[:, l, 4:20], True, True, ["cst", "fmb"], ["g0"])
                src = v3(pso, 8)[:, :, 0:1]
                dst = v3(fmb[:, l, 4:20], 8)[:, :, 1:2]
                cp("dve", dst, src, ["g0"], ["fmb"])
            for l in range(DEPTH):
                for g in range(4):
                    stg = small[:, 64:192]
                    dma(stg, sgw_d[l, g, :, :], [], ["sgstg"])
                    tr(ps_g[1][:, 0:128], stg, identF, ["sgstg", "cst"], ["g1"])
                    cp("dve", sgW[:, l, g, :], ps_g[1][:, 0:128], ["g1"], ["sgW"])
            lt = rstA
            fence("X")
            for i, (a_d, b_d) in enumerate([(lq1_d, lk1_d), (lq2_d, lk2_d)]):
                dma(lt[:, 0:256], a_d.rearrange("l k -> (l k)").partition_broadcast(128), [], [("rope4", "a")])
                dma(lt[:, 256:512], b_d.rearrange("l k -> (l k)").partition_broadcast(128), [], [("rope4", "b")])
                tt("dve", lt[:, 0:256], lt[:, 0:256], lt[:, 256:512], ALU.mult, [("rope4", "a"), ("rope4", "b")], [("rope4", "a")])
                red(small[:, 200 + 4 * i:204 + 4 * i], v3(lt[:, 0:256], 4), [("rope4", "a")], ["lam%d" % i])
                act(small[:, 200 + 4 * i:204 + 4 * i], small[:, 200 + 4 * i:204 + 4 * i], AF.Exp, ["lam%d" % i], ["lam%d" % i])
            tt("dve", lamv, small[:, 200:204], small[:, 204:208], ALU.subtract, ["lam0", "lam1"], ["lamv"])
            for l in range(DEPTH):
                lam_init = 0.8 - 0.6 * math.exp(-0.3 * l)
                ts("dve", lamv[:, l:l + 1], lamv[:, l:l + 1], lam_init, None, ALU.add, None, ["lamv"], ["lamv"])
            ts("dve", neglam, lamv, -1.0, None, ALU.mult, None, ["lamv"], ["neglam"])

            chk('consts')
            groups = [
                (C_MLQK, [(0, 0, 512, False)]),
                (C_MLV, [(0, 512, 256, False), (256, 1280, 16, False)]),
                (C_DAK, [(0, 1808, 128, False), (128, 1808, 128, True), (256, 1936, 128, False), (384, 1936, 128, True)]),
                (C_DAK + 512, [(0, 2064, 128, False), (128, 2064, 128, True), (256, 2192, 128, False), (384, 2192, 128, True)]),
                (C_DAV, [(0, 2320, 512, False)]),
                (C_SG, [(0, 3344, 512, False)]),
                (C_SG + 512, [(0, 3856, 256, False)]),
                (C_DAQ, [(0, 1296, 128, False), (128, 1296, 128, True), (256, 1424, 128, False), (384, 1424, 128, True)]),
                (C_DAQ + 512, [(0, 1552, 128, False), (128, 1552, 128, True), (256, 1680, 128, False), (384, 1680, 128, True)]),
                (C_MLOZ, [(0, 768, 512, False)]),
                (C_DAZ, [(0, 2832, 512, False)]),
            ]
            fence("X")
            fence("A")
            stg32 = [v3(sub(XOFF, XSZ).f32(4096), 8), v3(sub(XOFF + 4096, 4096).f32(4096), 8)]
            aA = sub(AOFF, ASZ)
            stg16 = [v3(aA.bf16(4096), 8), v3(aA.bf16(4096), 8)]
            S.region["stg32"] = "X"
            S.region["stg16"] = "A"
            gi = 0
            prev_store = [[], []]
            ceng = ["dve", "pool", "act"]

            def castcp(i, out, in_, reads, writes):
                e = ceng[i % 3]
                if e == "act":
                    act(out, in_, AF.Copy, reads, writes)
                else:
                    cp(e, out, in_, reads, writes)

            for l in range(DEPTH):
                for (dst0, pieces) in groups:
                    sl = gi % 2
                    wtot = max(p[0] + p[2] for p in pieces)
                    for (doff, src0, w, sw) in pieces:
                        dma(stg32[sl][:, :, doff:doff + w], win_d[l, :, src0:src0 + w].rearrange("(kc p) w -> p kc w", p=128), prev_store[sl], [("stg32", sl, doff)])
                    for pi, (doff, src0, w, sw) in enumerate(pieces):
                        if not sw:
                            castcp(gi + pi, stg16[sl][:, :, doff:doff + w], stg32[sl][:, :, doff:doff + w], [("stg32", sl, doff)], [("stg16", sl, doff)])
                        else:
                            i5 = stg32[sl][:, :, doff:doff + w].rearrange("p k (b t s) -> p k b t s", b=4, t=2)
                            o5 = stg16[sl][:, :, doff:doff + w].rearrange("p k (b t s) -> p k b t s", b=4, t=2)
                            for kc in range(8):
                                cp("pool" if kc % 2 else "dve", o5[:, kc, :, 0, :], i5[:, kc, :, 1, :], [("stg32", sl, doff)], [("stg16", sl, doff, kc, 0)])
                                cp("dve" if kc % 2 else "pool", o5[:, kc, :, 1, :], i5[:, kc, :, 0, :], [("stg32", sl, doff)], [("stg16", sl, doff, kc, 1)])
                    rk = []
                    for (doff, src0, w, sw) in pieces:
                        if sw:
                            rk += [("stg16", sl, doff, kc, t) for kc in range(8) for t in range(2)]
                        else:
                            rk.append(("stg16", sl, doff))
                    dma(wbf_d[l, :, dst0:dst0 + wtot].rearrange("(kc p) w -> p kc w", p=128), stg16[sl][:, :, 0:wtot], rk, [("wbf", l, dst0)])
                    prev_store[sl] = [("wbf", l, dst0)]
                    gi += 1
                for hf in range(2):
                    sl = gi % 2
                    dma(stg32[sl][:, :, :], wout_d[l, :, hf * 512:(hf + 1) * 512].rearrange("(kc p) w -> p kc w", p=128), prev_store[sl], [("stg32", sl, 0)])
                    castcp(gi, stg16[sl][:, :, :], stg32[sl][:, :, :], [("stg32", sl, 0)], [("stg16", sl, 0)])
                    dma(wobf_d[l, :, hf * 512:(hf + 1) * 512].rearrange("(kc p) w -> p kc w", p=128), stg16[sl][:, :, :], [("stg16", sl, 0)], [("wobf", l, hf)])
                    prev_store[sl] = [("wobf", l, hf)]
                    gi += 1

            chk('conv')
            csT = v3(small[:, 64:64 + 64], 8)
            for r in range(NB + 1):
                src = c_d[r, :] if r < NB else cctx_d[0, :]
                dma(csT[:, :, r:r + 1], src.rearrange("(kc p o) -> p kc o", p=128, o=1), ["sgW"], [("csT", r)], slow=True)
            NR = NB + 1
            cs_r = [("csT", r) for r in range(NR)]
            tnh = v3(small[:, 128:192], 8)
            act(tnh[:, :, 0:NR], csT[:, :, 0:NR], AF.Tanh, cs_r, ["cs_t"], scale=0.5)
            ts("dve", tnh[:, :, 0:NR], tnh[:, :, 0:NR], 0.5, 0.5, ALU.mult, ALU.add, ["cs_t"], ["cs_t"])
            tt("dve", csT[:, :, 0:NR], csT[:, :, 0:NR], tnh[:, :, 0:NR], ALU.mult, cs_r + ["cs_t"], ["csS"])
            mrow = sub(AOFF, ASZ).f32(1024)
            S.region["mrow"] = "A"
            wi = 0
            for l in range(NL):
                for cb in range(6):
                    sl = wi % 2
                    dma(stg32[sl][:, :, :], wmod_d[l, :, cb * 512:(cb + 1) * 512].rearrange("(kc p) w -> p kc w", p=128), prev_store[sl], [("stg32", sl, 0)])
                    pg = ps_g[2 + (wi % 2)]
                    for kc in range(8):
                        mm(pg[0:NR, :], csT[:, kc, 0:NR], stg32[sl][:, kc, :], kc == 0, kc == 7, ["csS", ("stg32", sl, 0)], ["g%d" % (2 + wi % 2)])
                    bm = mrow[0:NR, 512:1024]
                    dma(bm, bmod_d[l, cb * 512:(cb + 1) * 512].partition_broadcast(NR), [], [("mrow", "b")])
                    tt("dve", mrow[0:NR, 0:512], pg[0:NR, :], bm, ALU.add, ["g%d" % (2 + wi % 2), ("mrow", "b")], [("mrow", "o")])
                    dma(mod_d[l, 0:NR, cb * 512:(cb + 1) * 512], mrow[0:NR, 0:512], [("mrow", "o")], [("mod", l)])
                    wi += 1

            chk('mod')
            def wload(l, col0, ncols, slot, src=None):
                srcd = wbf_d if src is None else src
                key = ("wbf", l, col0) if src is None else ("wobf", l, col0 // 512)
                rk = [("wbf", l, g[0]) for g in groups] if src is None else [key]
                dma(wring[slot][:, :, 0:ncols], srcd[l, :, col0:col0 + ncols].rearrange("(kc p) w -> p kc w", p=128), rk, [("wring", slot)])

            ring_ctr = [0]

            def next_slot():
                s_ = ring_ctr[0] % 2
                ring_ctr[0] += 1
                return s_

            g_ctr = [0]

            def next_g(lo=0, hi=5):
                i = lo + g_ctr[0] % (hi - lo)
                g_ctr[0] += 1
                return i

            def ln_block(xb, ntile, xkey, xnbuf, xnkey, hT_of, hkey_of, s1p, shp, modkey, hTf=None, hfkey=None, post=None):
                s1 = small[:, 224:224 + ntile]
                s2 = small[:, 232:232 + ntile]
                k1 = [("lnst", 1, ti) for ti in range(ntile)]
                k2 = [("lnst", 2, ti) for ti in range(ntile)]
                for ti in range(ntile):
                    act(xnbuf, xb[:, ti, :], AF.Square, [xkey], [xnkey, k2[ti]], accum=s2[:, ti:ti + 1])
                    act(xnbuf, xb[:, ti, :], AF.Identity, [xkey], [xnkey, k1[ti]], accum=s1[:, ti:ti + 1])
                mean, rstd = stats(s1, s2, ntile, 1024.0, (k1, k2))
                for ti in range(ntile):
                    ts("dve", xnbuf, xb[:, ti, :], mean[:, ti:ti + 1], rstd[:, ti:ti + 1], ALU.subtract, ALU.mult, [xkey, "st_mean", "st_rstd"], [xnkey])
                    for kc in range(8):
                        tr(ps_tp[:, kc * 128:(kc + 1) * 128], xnbuf[:, kc * 128:(kc + 1) * 128], identF, [xnkey, "cst"], ["tpa", "tpb"])
                    tmpm = v3(xnbuf, 8)
                    tt("dve", tmpm, v3(ps_tp[:, :], 8), bc(s1p.unsqueeze(2), [128, 8, 128]), ALU.mult, ["tpa", "tpb"] + modkey, [xnkey])
                    if hTf is None:
                        tt("pool", hT_of(ti), tmpm, bc(shp.unsqueeze(2), [128, 8, 128]), ALU.add, [xnkey] + modkey, [hkey_of(ti)])
                    else:
                        tt("pool", hTf(ti), tmpm, bc(shp.unsqueeze(2), [128, 8, 128]), ALU.add, [xnkey] + modkey, [hfkey(ti)])
                        cp("act", hT_of(ti), hTf(ti), [hfkey(ti)], [hkey_of(ti)])
                    if post is not None:
                        post(ti)

            for b in range(NB):
                for l in range(NL):
                    lam_init = 0.8 - 0.6 * math.exp(-0.3 * l)
                    pk = ("par", b, l)
                    for (off, s0, w) in [(B_MLV, 512, 256), (B_MLV + 256, 1280, 16), (B_DAV, 2320, 512), (B_SG, 3344, 768), (B_MLOZ, 768, 512), (B_DAZ, 2832, 512)]:
                        dma(biasb[:, off:off + w], bin_d[l, s0:s0 + w].partition_broadcast(128), [], [("biasb", off)])
                    dma(lng, lng_d[l, :].partition_broadcast(128), [], ["lng"])
                    dma(lnb, lnb_d[l, :].partition_broadcast(128), [], ["lnb"])
                    dma(mlgc, mlg_d[l, :].partition_broadcast(128), [], ["mlgc"])
                    dma(dagc, dag_d[l, :].partition_broadcast(128), [], ["dagc"])
                    dma(sggt, sgg_d[l, :].partition_broadcast(128), [], ["sggt"])
                    dma(sgbt, sgb_d[l, :].partition_broadcast(128), [], ["sgbt"])
                    ts("dve", mlgc, mlgc, 0.5, None, ALU.mult, None, ["mlgc"], ["mlgc"])
                    ts("dve", dagc, dagc, 0.5 * (1.0 - lam_init), None, ALU.mult, None, ["dagc"], ["dagc"])
                    dma(gate_l, mod_d[l, b, 2048:3072].partition_broadcast(128), [("mod", l)], ["gate_l"])
                    dma(gate_c, mod_d[l, NB, 2048:3072].partition_broadcast(128), [("mod", l)], ["gate_c"])
                    for i, (row, c0) in enumerate([(b, 0), (b, 1024), (NB, 0), (NB, 1024)]):
                        dma(modp[:, i, :], mod_d[l, row, c0:c0 + 1024].rearrange("(kc p) -> p kc", p=128), [("mod", l)], [("modp", i)], slow=True)
                    for i in (1, 3):
                        ts("dve", modp[:, i, :], modp[:, i, :], 1.0, None, ALU.add, None, [("modp", i)], [("modp", i)])
                    modk = [("modp", i) for i in range(4)]

                    fence("X")
                    fence("A")
                    fence("B")
                    dma(wq32, win_d[l, :, 0:512].rearrange("(kc p) w -> p kc w", p=128), [], [("wring", 0), ("wring", 1)])
                    dma(wg32, win_d[l, :, 1280:1296].rearrange("(kc p) w -> p kc w", p=128), [], ["wg32"])
                    hTf2 = xblk[1][:, 1:3, :].rearrange("p a (k t) -> p k (a t)", k=8) if False else None
                    hTfbuf = v3(arena_t[:, XOFF + 4096 + 1024:XOFF + 4096 + 3072], 8)
                    for bi, (t0, t1) in enumerate(BLKS):
                        ntile = (t1 - t0) // 128
                        xb = xblk[0]
                        if l == 0:
                            srcx = ctx_d[b, :, :] if bi == 0 else x_d[b, t0 - 256:t1 - 256, :]
                            rk = []
                        else:
                            srcx = xs_d[b, t0:t1, :]
                            rk = [("xs", b, bi)]
                        dma(xb[:, 0:ntile, :], srcx.rearrange("(n p) d -> p n d", p=128), rk, [("xblk", 0)])
                        s1p, shp = (modp[:, 3, :], modp[:, 2, :]) if bi == 0 else (modp[:, 1, :], modp[:, 0, :])

                        def post1(ti, t0=t0):
                            tk = t0 + ti * 128
                            tg = tk // 128
                            hTf_t = hTfbuf[:, :, (ti % 2) * 128:(ti % 2) * 128 + 128]
                            gg = 2 + tg % 2
                            for kc in range(8):
                                mm(ps_g[gg][:, 0:16], hTf_t[:, kc, :], wg32[:, kc, :], kc == 0, kc == 7, ["wg32", ("xblk", 2, ti % 2)], ["g%d" % gg])
                            tt("dve", Gt[:, tg, :], ps_g[gg][:, 0:16], biasb[:, B_MLV + 256:B_MLV + 272], ALU.add,
                               ["g%d" % gg, ("biasb", B_MLV + 256)], [("Gt", tg)])
                            if ti % 2 == 0:
                                return
                            tk0 = tk - 128
                            for c in range(4):
                                pq = ps_g[c // 2][:, (c % 2) * 256:(c % 2) * 256 + 256]
                                for kc in range(8):
                                    mm(pq, wq32[:, kc, c * 128:(c + 1) * 128], hTfbuf[:, kc, :], kc == 0, kc == 7,
                                       [("wring", 0), ("wring", 1), ("xblk", 2, 0), ("xblk", 2, 1)], ["g%d" % (c // 2)])
                            for hb in range(2):
                                tt("dve", qkT[:, 2 * hb:2 * hb + 2, tk0:tk0 + 256], v3(ps_g[hb][:, :], 2), bc(fmb[:, l, 2 * hb:2 * hb + 2].unsqueeze(2), [128, 2, 256]), ALU.add,
                                   ["g%d" % hb, "fmb"], [("qkpre", tg - 1, hb), ("qkpre", tg, hb)])

                        ln_block(xb, ntile, ("xblk", 0), xblk[1][:, 0, :], ("xblk", 1),
                                 lambda ti, t0=t0: hTall[:, :, t0 + ti * 128:t0 + (ti + 1) * 128], lambda ti, t0=t0: ("hTall", t0 // 128 + ti),
                                 s1p, shp, modk, hTf=lambda ti: hTfbuf[:, :, (ti % 2) * 128:(ti % 2) * 128 + 128], hfkey=lambda ti: ("xblk", 2, ti % 2), post=post1)
                    if dbg:
                        tap("hT", hTall[:, :, :], [("hTall", i) for i in range(NT)], BF16)

                    chk('ph1')
                    hT_keys = [("hTall", i) for i in range(NT)]
                    pre_all = [("qkpre", ti, hb) for ti in range(NT) for hb in range(2)]
                    for c in range(4):
                        pst = qkT[:, c, :]
                        w0, w1, w2 = (convw[:, l, j, c:c + 1] for j in range(3))
                        ts("dve", cacc, pst, w1, convb[:, l, c:c + 1], ALU.mult, ALU.add, pre_all + ["convp"], ["cacc"])
                        for (a, e) in [(0, CTXL), (CTXL, T)]:
                            stt("dve", cacc[:, a + 1:e], pst[:, a:e - 1], w0, cacc[:, a + 1:e], ALU.mult, ALU.add, pre_all + ["convp", "cacc"], ["cacc"])
                            stt("dve", cacc[:, a:e - 1], pst[:, a + 1:e], w2, cacc[:, a:e - 1], ALU.mult, ALU.add, pre_all + ["convp", "cacc"], ["cacc"])
                        act(pst, cacc, AF.Tanh, ["cacc"] + pre_all, [("qkT", c)], scale=0.5)
                        sc_ = 0.5 if c < 2 else 0.0625
                        ts("pool", pst, pst, sc_, sc_, ALU.mult, ALU.add, [("qkT", c)], [("qkT", c)])
                        tt("dve", pst, pst, cacc, ALU.mult, [("qkT", c), "cacc"], [("qkT", c)])
                    slot = next_slot()
                    wload(l, C_MLV, 272, slot)
                    S.op("pool", lambda h: h.memset(vaug[:, :, :, 64:65], 1.0), [], [("vaug", "ones")])
                    for ti in range(NT):
                        gix = next_g()
                        pg = ps_g[gix]
                        for kc in range(8):
                            mm(pg[:, 0:272], hTall[:, kc, ti * 128:(ti + 1) * 128], wring[slot][:, kc, 0:272], kc == 0, kc == 7,
                               [("wring", slot), hT_keys[ti]], ["g%d" % gix])
                        tt("dve", vaug[:, ti, :, 0:64], v3(pg[:, 0:256], 4), v3(biasb[:, B_MLV:B_MLV + 256], 4), ALU.add,
                           ["g%d" % gix, ("biasb", B_MLV)], [("vaug", ti)])
                    if dbg:
                        tap("qkT", qkT[:, :, :], [("qkT", i) for i in range(4)], F32)
                        tap("vaug", vaug[:, :, :, :], [("vaug", i) for i in range(NT)] + [("vaug", "ones")], BF16)
                        tap("Gt", Gt[:, :, :], [("Gt", i) for i in range(NT)])

                    chk('ph2')
                    fence("X")
                    Gk = [("Gt", ti) for ti in range(NT)]
                    for d_ in range(2):
                        fsl = Gt[:, :, 4 + 8 * d_:8 + 8 * d_]
                        act(v3(gtmp[:, d_, :], NT), fsl, AF.Exp, Gk, [("gder", "t", d_)], scale=-1.0)
                        act(LFp[:, d_, :], gtmp[:, d_, :], AF.Ln, [("gder", "t", d_)], [("gder", "L", d_)], bias=1.0)
                    chk('g1')
                    pc = ps_g[0]
                    mm(pc[:, 0:72], triF, LFp[:, 0, :], True, True, ["cst", ("gder", "L", 0)], ["g0"])
                    mm(pc[:, 72:144], triB, LFp[:, 1, :], True, True, ["cst", ("gder", "L", 1)], ["g0"])
                    mm(pc[:, 144:288], onesF, LFp[:, :, :].rearrange("p a b -> p (a b)"), True, True, ["cst", ("gder", "L", 0), ("gder", "L", 1)], ["g0"])
                    chk('g2')
                    act(Ej[:, :, :].rearrange("p a b -> p (a b)"), pc[:, 0:144], AF.Exp, ["g0"], [("gder", "Ej")], scale=-1.0)
                    act(Eend[:, :, :].rearrange("p a b -> p (a b)"), pc[:, 144:288], AF.Exp, ["g0"], [("gder", "Eend")], scale=-1.0)
                    if dbg and 'cum' in dbg:
                        cp('dve', sgp[:, 0:288], pc[:, 0:288], ['g0'], [('mtmp', 'dbgc')])
                        tap('cum', sgp[:, 0:288], [('mtmp', 'dbgc')])
                        tap('LFp', LFp[:, :, :], [('gder', 'L', 0), ('gder', 'L', 1)])
                    chk('g3')
                    for d_ in range(2):
                        tt("dve", v3(gtmp[:, d_, :], NT), v3(pc[:, 72 * d_:72 * d_ + 72], NT), Gt[:, :, 8 * d_:8 * d_ + 4], ALU.add,
                           ["g0"] + Gk, [("gder", "t2", d_)])
                        act(Ws[:, d_, :], gtmp[:, d_, :], AF.Exp, [("gder", "t2", d_)], [("gder", "Ws", d_)])
                    chk('ph3a')
                    orders = [list(range(NT)), [1, 0] + list(range(NT - 1, 1, -1))]
                    written = set()
                    S.op("pool", lambda h: h.memset(m_cst_flat, 0.0), [], [("mtmp", "cst", 0), ("mtmp", "cst", 1)])
                    for d_ in range(2):
                        S.op("pool", lambda h, d_=d_: h.memset(m_vsf[d_], 0.0), [], [("mtmp", "vs", d_)])
                    for step in range(NT):
                        for d_ in range(2):
                            ti = orders[d_][step]
                            first = step == 0
                            tks = slice(ti * 128, (ti + 1) * 128)
                            for h4 in range(4):
                                pr = slice((h4 % 2) * 64, (h4 % 2) * 64 + 64)
                                gS = 1 + (h4 % 2)
                                mm(ps_g[gS][:, (h4 // 2) * 128:(h4 // 2 + 1) * 128], qkT[pr, 2 + h4 // 2, tks], qkT[pr, h4 // 2, tks], True, True,
                                   [("qkT", 2 + h4 // 2), ("qkT", h4 // 2)], ["g%d" % gS])
                            if step == 0 and d_ == 0: chk('sa')
                            mk = triF if d_ == 0 else triB
                            for j in range(2):
                                tr(ps_tp[:, j * 128:(j + 1) * 128], qkT[:, 2 + j, tks], identF, [("qkT", 2 + j), "cst"], ["tpa"])
                            cp("act", m_kt[d_], ps_tp[:, 0:256], ["tpa"], [("mtmp", "kt", d_)])
                            for par in range(2):
                                tt("dve", m_pt[d_][:, par * 2:par * 2 + 2, :], v3(ps_g[1 + par][:, 0:256], 2), bc(mk.unsqueeze(1), [128, 2, 128]), ALU.mult,
                                   ["g%d" % (1 + par), "cst"], [("mtmp", "pt", d_, par)])
                            wsl = v3(Ws[:, d_, :], NT)[:, ti, :]
                            tt("pool", m_vs[d_], vaug[:, ti, :, :], bc(wsl.unsqueeze(2), [128, 4, 65]), ALU.mult,
                               [("vaug", ti), ("vaug", "ones"), ("gder", "Ws", d_)], [("mtmp", "vs", d_)])
                            if step == 0 and d_ == 0: chk('sb')
                            gH = 3
                            pH = ps_g[gH]
                            pD = ps_g[4]
                            for h4 in range(4):
                                pr = slice((h4 % 2) * 64, (h4 % 2) * 64 + 64)
                                mm(pH[:, h4 * 72:h4 * 72 + 65], m_pt[d_][:, (h4 % 2) * 2 + h4 // 2, :], m_vs[d_][:, h4, :], True, first,
                                   [("mtmp", "pt", d_, h4 % 2), ("mtmp", "vs", d_)], ["g3"])
                                if not first:
                                    mm(pH[:, h4 * 72:h4 * 72 + 65], qkT[pr, h4 // 2, tks], m_cst[pr, d_, h4, :], False, True,
                                       [("qkT", h4 // 2), ("mtmp", "cst", d_)], ["g3"])
                            for pp in range(2):
                                mm(pD[:, pp * 144:pp * 144 + 144], m_kt[d_][:, pp * 128:pp * 128 + 128], m_vsf[d_][:, pp * 144:pp * 144 + 144], True, True,
                                   [("mtmp", "kt", d_), ("mtmp", "vs", d_)], ["g4"])
                            if step == 0 and d_ == 0: chk('sc')
                            ejs = v3(Ej[:, d_, :], NT)[:, ti, :]
                            tt("dve", m_hs[d_], v3(pH[:, 0:288], 4)[:, :, 0:65], bc(ejs.unsqueeze(2), [128, 4, 65]), ALU.mult, ["g3", ("gder", "Ej")], [("mtmp", "hs", d_)])
                            if step == 0 and d_ == 0: chk('sd')
                            den = m_hs[d_][:, :, 64]
                            d2 = small[:, 208:212]
                            tt("dve", d2, den, den, ALU.mult, [("mtmp", "hs", d_)], ["md2"])
                            ts("dve", d2, d2, 1.0, None, ALU.max, None, ["md2"], ["md2"])
                            tt("pool", d2, d2, neghalf[:, 0:4], ALU.pow, ["md2", "cst"], ["md2"])
                            if ti not in written:
                                tt("dve", v3(Hml[:, ti, :], 4), m_hs[d_][:, :, 0:64], bc(d2.unsqueeze(2), [128, 4, 64]), ALU.mult,
                                   [("mtmp", "hs", d_), "md2"], [("Hml", ti)])
                                written.add(ti)
                            else:
                                tt("dve", m_hn, m_hs[d_][:, :, 0:64], bc(d2.unsqueeze(2), [128, 4, 64]), ALU.mult,
                                   [("mtmp", "hs", d_), "md2"], [("mtmp", "hn")])
                                tt("pool", v3(Hml[:, ti, :], 4), v3(Hml[:, ti, :], 4), m_hn, ALU.add, [("mtmp", "hn"), ("Hml", ti)], [("Hml", ti)])
                            if step == 0 and d_ == 0: chk('se')
                            tt("dve", m_tmp, v3(pD[:, 0:288], 4)[:, :, 0:65], m_cst[:, d_, :, :], ALU.add, ["g4", ("mtmp", "cst", d_)], [("mtmp", "tmp")])
                            ees = v3(Eend[:, d_, :], NT)[:, ti, :]
                            tt("dve", m_cst[:, d_, :, :], m_tmp, bc(ees.unsqueeze(2), [128, 4, 65]), ALU.mult, [("mtmp", "tmp"), ("gder", "Eend")], [("mtmp", "cst", d_)])
                    chk('sf') if False else None
                    if dbg:
                        tap("Hml", Hml[:, :, :], [("Hml", i) for i in range(NT)])

                    chk('ph3')
                    fence("X")
                    fence("B")
                    S.op("pool", lambda h: h.memset(dvaug[:, :, :, 128:129], 1.0), [], [("dvaug", "ones")])
                    for hh in range(2):
                        slot = next_slot()
                        wload(l, C_DAK + hh * 512, 512, slot)
                        for h2 in range(2):
                            h4 = hh * 2 + h2
                            for bi, (t0, t1) in enumerate(BLKS):
                                n = t1 - t0
                                ga, gb = [(1, 2), (3, 4)][bi % 2]
                                for kc in range(8):
                                    mm(ps_g[ga][:, 0:n], wring[slot][:, kc, h2 * 256:h2 * 256 + 128], hTall[:, kc, t0:t1], kc == 0, kc == 7,
                                       [("wring", slot)] + hT_keys[t0 // 128:t1 // 128], ["g%d" % ga])
                                if bi == 0:
                                    act(dkT[:, h4, t0:t1], ps_g[ga][:, 0:n], AF.Identity, ["g%d" % ga, "fmb"], [("dkT", h4, bi)], bias=fmb[:, l, 4 + 2 * h4:5 + 2 * h4])
                                    continue
                                for kc in range(8):
                                    mm(ps_g[gb][:, 0:n], wring[slot][:, kc, h2 * 256 + 128:h2 * 256 + 256], hTall[:, kc, t0:t1], kc == 0, kc == 7,
                                       [("wring", slot)] + hT_keys[t0 // 128:t1 // 128], ["g%d" % gb])
                                p0 = t0 - 256
                                dma(ropeC[:, 0:n], rope_d[0, :, p0:p0 + n], [], [("rope4", "c")])
                                dma(ropeS[:, 0:n], rope_d[1, :, p0:p0 + n], [], [("rope4", "s")])
                                stt("dve", rstA[:, 0:n], ps_g[ga][:, 0:n], fmb[:, l, 4 + 2 * h4:5 + 2 * h4], ropeC[:, 0:n], ALU.add, ALU.mult,
                                    ["g%d" % ga, "fmb", ("rope4", "c")], [("rope4", "a")])
                                stt("dve", rstB[:, 0:n], ps_g[gb][:, 0:n], fmb[:, l, 5 + 2 * h4:6 + 2 * h4], ropeS[:, 0:n], ALU.add, ALU.mult,
                                    ["g%d" % gb, "fmb", ("rope4", "s")], [("rope4", "b")])
                                tt("pool", dkT[:, h4, t0:t1], rstA[:, 0:n], rstB[:, 0:n], ALU.add, [("rope4", "a"), ("rope4", "b")], [("dkT", h4, bi)])
                    slot = next_slot()
                    wload(l, C_DAV, 512, slot)
                    for ti in range(NT):
                        gix = next_g(3, 5)
                        pg = ps_g[gix]
                        for kc in range(8):
                            mm(pg[:, :], hTall[:, kc, ti * 128:(ti + 1) * 128], wring[slot][:, kc, :], kc == 0, kc == 7,
                               [("wring", slot), hT_keys[ti]], ["g%d" % gix])
                        tt("dve", dvaug[:, ti, :, 0:128], v3(pg[:, :], 4), v3(biasb[:, B_DAV:B_DAV + 512], 4), ALU.add,
                           ["g%d" % gix, ("biasb", B_DAV)], [("dvaug", ti)])
                    s_a = next_slot()
                    wload(l, C_SG, 512, s_a)
                    s_b = next_slot()
                    wload(l, C_SG + 512, 256, s_b)
                    for ti in range(NT):
                        ga, gb = 1, 2
                        for kc in range(8):
                            mm(ps_g[ga][:, :], hTall[:, kc, ti * 128:(ti + 1) * 128], wring[s_a][:, kc, :], kc == 0, kc == 7,
                               [("wring", s_a), hT_keys[ti]], ["g%d" % ga])
                        for kc in range(8):
                            mm(ps_g[gb][:, 0:256], hTall[:, kc, ti * 128:(ti + 1) * 128], wring[s_b][:, kc, 0:256], kc == 0, kc == 7,
                               [("wring", s_b), hT_keys[ti]], ["g%d" % gb])
                        tt("dve", sgp[:, 0:512], ps_g[ga][:, :], biasb[:, B_SG:B_SG + 512], ALU.add, ["g%d" % ga, ("biasb", B_SG)], [("sgt", "p")])
                        tt("dve", sgp[:, 512:768], ps_g[gb][:, 0:256], biasb[:, B_SG + 512:B_SG + 768], ALU.add, ["g%d" % gb, ("biasb", B_SG)], [("sgt", "pz")])
                        act(sgu, sgp[:, 0:256], AF.Gelu, [("sgt", "p")], [("sgt", "u")])
                        act(sgp[:, 256:512], sgp[:, 256:512], AF.Gelu, [("sgt", "p")], [("sgt", "gv")])
                        act(sgzs, sgp[:, 512:768], AF.Tanh, [("sgt", "pz")], [("sgt", "zs")], scale=0.5)
                        red(small[:, 240:241], sgp[:, 256:512], [("sgt", "gv")], ["sgs1"])
                        act(sgjunk, sgp[:, 256:512], AF.Square, [("sgt", "gv")], [("sgt", "junk")])
                        red(small[:, 241:242], sgjunk, [("sgt", "junk")], ["sgs2"])
                        mean, rstd = stats(small[:, 240:241], small[:, 241:242], 1, 256.0, "sg")
                        ts("dve", sgjunk, sgp[:, 256:512], mean, rstd, ALU.subtract, ALU.mult, [("sgt", "gv"), "st_mean", "st_rstd"], [("sgt", "junk")])
                        tt("dve", sgjunk, sgjunk, sggt, ALU.mult, [("sgt", "junk"), "sggt"], [("sgt", "junk")])
                        tt("dve", sgvn, sgjunk, sgbt, ALU.add, [("sgt", "junk"), "sgbt"], [("sgt", "vn")])
                        gv_ = 3
                        for g in range(4):
                            mm(ps_g[gv_][:, g * 64:(g + 1) * 64], sgW[:, l, g, :], sgvn[:, g * 64:(g + 1) * 64], True, True, ["sgW", ("sgt", "vn")], ["g%d" % gv_])
                        tt("dve", v3(sgy, 4), v3(ps_g[gv_][:, 0:256], 4), bc(sgbs[:, l, :].unsqueeze(2), [128, 4, 64]), ALU.add, ["g%d" % gv_, "sgbs"], [("sgt", "y")])
                        tt("dve", sgy, sgy, sgu, ALU.mult, [("sgt", "y"), ("sgt", "u")], [("sgt", "y")])
                        stt("dve", sgzs, sgzs, 1.0, sgp[:, 512:768], ALU.add, ALU.mult, [("sgt", "zs"), ("sgt", "pz")], [("sgt", "zs")])
                        stt("dve", ysg[:, ti, :], sgy, 0.5, sgzs, ALU.mult, ALU.mult, [("sgt", "y"), ("sgt", "zs")], [("ysg", ti)])
                    if dbg:
                        tap("dkT", dkT[:, :, :], [("dkT", h_, b_) for h_ in range(4) for b_ in range(5)], BF16)
                        tap("ysg", ysg[:, :, :], [("ysg", i) for i in range(NT)], BF16)

                    chk('ph4')
                    fence("X")
                    fence("A")
                    last = (l == NL - 1)
                    for bi, (t0, t1) in enumerate(BLKS):
                        n = t1 - t0
                        ntile = n // 128
                        isctx = bi == 0
                        if last and isctx:
                            continue
                        xb = xblk[0]
                        if l == 0:
                            srcx = ctx_d[b, :, :] if isctx else x_d[b, t0 - 256:t1 - 256, :]
                            rk = []
                        else:
                            srcx = xs_d[b, t0:t1, :]
                            rk = [("xs", b, bi)]
                        dma(xb[:, 0:ntile, :], srcx.rearrange("(n p) d -> p n d", p=128), rk, [("xblk", 0)])
                        s1p, shp = (modp[:, 3, :], modp[:, 2, :]) if isctx else (modp[:, 1, :], modp[:, 0, :])
                        ln_block(xb, ntile, ("xblk", 0), xn5, ("p5", "xn"), lambda ti: hTblk[:, :, ti * 128:(ti + 1) * 128], lambda ti: ("p5", "hT", ti), s1p, shp, modk)
                        hk = [("p5", "hT", ti) for ti in range(ntile)]
                        if not isctx:
                            p0 = t0 - 256
                            dma(p5ropeC[:, 0:n], rope_d[0, :, p0:p0 + n], [], [("p5x", "c")])
                            dma(p5ropeS[:, 0:n], rope_d[1, :, p0:p0 + n], [], [("p5x", "s")])
                        for hh in range(2):
                            slot = next_slot()
                            wload(l, C_DAQ + hh * 512, 512, slot)
                            for h2 in range(2):
                                h4 = hh * 2 + h2
                                ga, gb = 0, 1
                                for kc in range(8):
                                    mm(ps_g[ga][:, 0:n], wring[slot][:, kc, h2 * 256:h2 * 256 + 128], hTblk[:, kc, 0:n], kc == 0, kc == 7, [("wring", slot)] + hk, ["g%d" % ga])
                                if isctx:
                                    act(qTblk[:, h4, 0:n], ps_g[ga][:, 0:n], AF.Identity, ["g%d" % ga, "fmb"], [("p5", "qT", h4)], bias=fmb[:, l, 12 + 2 * h4:13 + 2 * h4])
                                    continue
                                for kc in range(8):
                                    mm(ps_g[gb][:, 0:n], wring[slot][:, kc, h2 * 256 + 128:h2 * 256 + 256], hTblk[:, kc, 0:n], kc == 0, kc == 7, [("wring", slot)] + hk, ["g%d" % gb])
                                stt("dve", p5stA[:, 0:n], ps_g[ga][:, 0:n], fmb[:, l, 12 + 2 * h4:13 + 2 * h4], p5ropeC[:, 0:n], ALU.add, ALU.mult,
                                    ["g%d" % ga, "fmb", ("p5x", "c")], [("p5x", "a")])
                                stt("dve", p5stB[:, 0:n], ps_g[gb][:, 0:n], fmb[:, l, 13 + 2 * h4:14 + 2 * h4], p5ropeS[:, 0:n], ALU.add, ALU.mult,
                                    ["g%d" % gb, "fmb", ("p5x", "s")], [("p5x", "b")])
                                tt("pool", qTblk[:, h4, 0:n], p5stA[:, 0:n], p5stB[:, 0:n], ALU.add, [("p5x", "a"), ("p5x", "b")], [("p5", "qT", h4)])
                        for gi_, (c0, boff) in enumerate([(C_MLOZ, B_MLOZ), (C_DAZ, B_DAZ)]):
                            slot = next_slot()
                            wload(l, c0, 512, slot)
                            for ti in range(ntile):
                                gix = 2 + ti % 2
                                pg = ps_g[gix]
                                for kc in range(8):
                                    mm(pg[:, :], hTblk[:, kc, ti * 128:(ti + 1) * 128], wring[slot][:, kc, :], kc == 0, kc == 7, [("wring", slot), hk[ti]], ["g%d" % gix])
                                tt("dve", p5tmp, pg[:, :], biasb[:, boff:boff + 512], ALU.add, ["g%d" % gix, ("biasb", boff)], [("p5x", "tmp")])
                                act(p5t2, p5tmp, AF.Tanh, [("p5x", "tmp")], [("p5x", "t2")], scale=0.5)
                                gdst = gates[:, ti, gi_ * 512:(gi_ + 1) * 512]
                                if gi_ == 0:
                                    ts("dve", gdst[:, 0:256], p5t2[:, 0:256], 0.5, 0.5, ALU.mult, ALU.add, [("p5x", "t2")], [("p5", "g", ti, 0)])
                                    stt("dve", gdst[:, 256:512], p5t2[:, 256:512], 1.0, p5tmp[:, 256:512], ALU.add, ALU.mult, [("p5x", "t2"), ("p5x", "tmp")], [("p5", "g", ti, 1)])
                                else:
                                    stt("dve", gdst, p5t2, 1.0, p5tmp, ALU.add, ALU.mult, [("p5x", "t2"), ("p5x", "tmp")], [("p5", "g", ti, 2)])
                        tg0 = t0 // 128
                        mlv = v3(p5x_ml, 4)[:, 0:ntile, :]
                        sqv = v3(p5x_sq, 4)[:, 0:ntile, :]
                        kml = [("p5x", "c"), ("p5x", "s")]
                        ksq = [("p5x", "tmp"), ("p5x", "t2")]
                        k4 = ntile * 4
                        tt("dve", mlv, gates[:, 0:ntile, 0:256], Hml[:, tg0:tg0 + ntile, :], ALU.mult,
                           [("p5", "g", ti, 0) for ti in range(ntile)] + [("Hml", tg0 + ti) for ti in range(ntile)], kml)
                        red(small[:, 64:64 + k4], mlv.rearrange("p a (h d) -> p (a h) d", h=4), kml, ["mls1"])
                        tt("dve", sqv, mlv, mlv, ALU.mult, kml, ksq)
                        red(small[:, 80:80 + k4], sqv.rearrange("p a (h d) -> p (a h) d", h=4), ksq, ["mls2"])
                        mean, rstd = stats(small[:, 64:64 + k4], small[:, 80:80 + k4], k4, 64.0, "ml")
                        ml3 = mlv.rearrange("p a (h d) -> p (a h) d", h=4)
                        tt("dve", ml3, ml3, bc(mean.unsqueeze(2), [128, k4, 64]), ALU.subtract, kml + ["st_mean"], kml)
                        tt("dve", ml3, ml3, bc(rstd.unsqueeze(2), [128, k4, 64]), ALU.mult, kml + ["st_rstd"], kml)
                        tt("dve", mlv, mlv, bc(mlgc.unsqueeze(1), [128, ntile, 256]), ALU.mult, kml + ["mlgc"], kml)
                        gml = gates[:, 0:ntile, 0:256]
                        tt("dve", gml, mlv, gates[:, 0:ntile, 256:512], ALU.mult, kml + [("p5", "g", ti, 1) for ti in range(ntile)] + [("p5", "g", ti, 0) for ti in range(ntile)], [("p5", "mixml")])
                        ktiles = list(range(0, 2)) if isctx else list(range(NT))
                        STB = [(ps_g[4], "g4"), (ps_tp[:, 0:512], "tpa"), (ps_tp[:, 512:1024], "tpb")]
                        nk = len(ktiles)
                        LOOK = 2

                        def combine(h4):
                            nt_ = ntile
                            ok_ = [("p5", "oev", qi, c) for qi in range(nt_) for c in range(2)]
                            rr = v3(small[:, 192:200], 4)[:, 0:nt_, :]
                            attv = v3(p5stA, 4)[:, 0:nt_, :]
                            tmpv = v3(p5stB, 4)[:, 0:nt_, :]
                            ssv = small[:, 200:200 + nt_]
                            S.op("dve", lambda h, rr=rr, nt_=nt_: h.reciprocal(out=rr, in_=oev[:, 0:nt_, :, 128]), ok_, ["p5rr"])
                            ts("dve", rr[:, :, 1], rr[:, :, 1], neglam[:, l:l + 1], None, ALU.mult, None, ["p5rr", "neglam"], ["p5rr"])
                            tt("dve", attv, oev[:, 0:nt_, 0, 0:128], bc(rr[:, :, 0:1], [128, nt_, 128]), ALU.mult, ok_ + ["p5rr"], [("p5x", "a")])
                            tt("dve", tmpv, oev[:, 0:nt_, 1, 0:128], bc(rr[:, :, 1:2], [128, nt_, 128]), ALU.mult, ok_ + ["p5rr"], [("p5x", "b")])
                            tt("pool", attv, attv, tmpv, ALU.add, [("p5x", "a"), ("p5x", "b")], [("p5x", "a")])
                            tt("dve", tmpv, attv, attv, ALU.mult, [("p5x", "a")], [("p5x", "b")])
                            red(ssv, tmpv, [("p5x", "b")], ["p5ss"])
                            ts("dve", ssv, ssv, 1.0 / 128.0, EPS, ALU.mult, ALU.add, ["p5ss"], ["p5ss"])
                            tt("pool", ssv, ssv, neghalf[:, 0:nt_], ALU.pow, ["p5ss", "cst"], ["p5ss"])
                            tt("dve", attv, attv, bc(ssv.unsqueeze(2), [128, nt_, 128]), ALU.mult, [("p5x", "a"), "p5ss"], [("p5x", "a")])
                            tt("dve", attv, attv, bc(dagc.unsqueeze(1), [128, nt_, 128]), ALU.mult, [("p5x", "a"), "dagc"], [("p5x", "a")])
                            gsl = gates[:, 0:nt_, 512 + h4 * 128:512 + (h4 + 1) * 128]
                            tt("dve", gsl, attv, gsl, ALU.mult, [("p5x", "a")] + [("p5", "g", qi, 2) for qi in range(nt_)], [("p5", "damix", h4)])
                        tg0 = t0 // 128

                        items = [(h4, c, kti) for h4 in range(4) for c in range(2) for kti in range(nk)]
                        for g in range(len(items) + LOOK):
                            if g < len(items):
                                h4, c, kti = items[g]
                                kt = ktiles[kti]
                                pr = slice(c * 64, c * 64 + 64)
                                pS, gk = STB[g % 3]
                                mm(pS[:, 0:n], dkT[pr, h4, kt * 128:(kt + 1) * 128], qTblk[pr, h4, 0:n], True, True,
                                   [("dkT", h4, 0 if kt < 2 else 1 + (kt - 2) // 4), ("p5", "qT", h4)], [gk])
                                act(ptb[g % 4][:, 0:n], pS[:, 0:n], AF.Exp, [gk], [("p5", "pt", g % 4)], scale=0.125)
                            j = g - LOOK
                            if j >= 0:
                                h4, c, kti = items[j]
                                kt = ktiles[kti]
                                for qi in range(ntile):
                                    mm(ps_g[qi][:, 0:129], ptb[j % 4][:, qi * 128:(qi + 1) * 128], dvaug[:, kt, h4, :], kti == 0, kti == nk - 1,
                                       [("p5", "pt", j % 4), ("dvaug", kt), ("dvaug", "ones")], ["g%d" % qi])
                                if kti == nk - 1:
                                    for qi in range(ntile):
                                        cp("act" if qi % 2 else "dve", oev[:, qi, c, :], ps_g[qi][:, 0:129], ["g%d" % qi], [("p5", "oev", qi, c)])
                                    if c == 1:
                                        combine(h4)
                        for ti in range(ntile):
                            tg = tg0 + ti
                            for kc in range(8):
                                if kc < 2:
                                    src_, rk_ = gates[:, ti, kc * 128:(kc + 1) * 128], [("p5", "mixml")]
                                elif kc < 6:
                                    src_, rk_ = gates[:, ti, 512 + (kc - 2) * 128:512 + (kc - 1) * 128], [("p5", "damix", kc - 2)]
                                else:
                                    src_, rk_ = ysg[:, tg, (kc - 6) * 128:(kc - 5) * 128], [("ysg", tg)]
                                tr(ps_tb[:, kc * 128:(kc + 1) * 128], src_, identB, rk_ + ["cstb"], ["tb"])
                            cp("act", mixT[:, :, ti * 128:(ti + 1) * 128], v3(ps_tb[:, :], 8), ["tb"], [("p5", "mixT", ti)])
                        if dbg:
                            tap("mix%d" % bi, mixT[:, :, 0:n], [("p5", "mixT", ti) for ti in range(ntile)], BF16)
                        gateb = gate_c if isctx else gate_l
                        gkey = "gate_c" if isctx else "gate_l"
                        for hf in range(2):
                            slot = next_slot()
                            wload(l, hf * 512, 512, slot, src=wobf_d)
                            for ti in range(ntile):
                                gix = ti % 2
                                pg = ps_g[gix]
                                for kc in range(8):
                                    mm(pg[:, :], mixT[:, kc, ti * 128:(ti + 1) * 128], wring[slot][:, kc, :], kc == 0, kc == 7, [("wring", slot), ("p5", "mixT", ti)], ["g%d" % gix])
                                tt("dve", p5tmp, pg[:, :], gateb[:, hf * 512:(hf + 1) * 512], ALU.mult, ["g%d" % gix, gkey], [("p5x", "tmp")])
                                stt("dve", xb[:, ti, hf * 512:(hf + 1) * 512], xb[:, ti, hf * 512:(hf + 1) * 512], ALPHA, p5tmp, ALU.mult, ALU.add,
                                    [("xblk", 0), ("p5x", "tmp")], [("xblk", 0)])
                        s1 = small[:, 224:224 + ntile]
                        s2 = small[:, 232:232 + ntile]
                        k1 = [("lnst", 1, ti) for ti in range(ntile)]
                        k2 = [("lnst", 2, ti) for ti in range(ntile)]
                        for ti in range(ntile):
                            act(xn5, xb[:, ti, :], AF.Square, [("xblk", 0)], [("p5", "xn"), k2[ti]], accum=s2[:, ti:ti + 1])
                            act(xn5, xb[:, ti, :], AF.Identity, [("xblk", 0)], [("p5", "xn"), k1[ti]], accum=s1[:, ti:ti + 1])
                        mean, rstd = stats(s1, s2, ntile, 1024.0, (k1, k2))
                        for ti in range(ntile):
                            ts("dve", xb[:, ti, :], xb[:, ti, :], mean[:, ti:ti + 1], rstd[:, ti:ti + 1], ALU.subtract, ALU.mult, [("xblk", 0), "st_mean", "st_rstd"], [("xblk", 0)])
                        xall = xb[:, 0:ntile, :]
                        tt("dve", xall, xall, bc(lng.unsqueeze(1), [128, ntile, D]), ALU.mult, [("xblk", 0), "lng"], [("xblk", 0)])
                        tt("pool", xall, xall, bc(lnb.unsqueeze(1), [128, ntile, D]), ALU.add, [("xblk", 0), "lnb"], [("xblk", 0)])
                        if last:
                            dma(y_d[b, t0 - 256:t1 - 256, :].rearrange("(n p) d -> p n d", p=128), xb[:, 0:ntile, :], [("xblk", 0)], [("y", b, bi)])
                        else:
                            dma(xs_d[b, t0:t1, :].rearrange("(n p) d -> p n d", p=128), xb[:, 0:ntile, :], [("xblk", 0)], [("xs", b, bi)])

        except _Stop:
            pass
        S.emit(nc, st)
    return nc, S, dbg_out


def make_consts():
    i = np.arange(128)
    ident = np.eye(128, dtype=np.float32)
    triF = (i[:, None] <= i[None, :]).astype(np.float32)
    triB = (i[:, None] >= i[None, :]).astype(np.float32)
    ones = np.ones((128, 128), np.float32)
    partner = np.where((i % 32) < 16, i + 16, i - 16)
    rperm = np.zeros((128, 128), np.float32)
    rperm[partner, i] = 1.0
    cst = np.concatenate([ident, triF, triB, ones, rperm], 1)
    t = np.arange(LAT)
    row = (t // 64).astype(np.float32)
    col = (t % 64).astype(np.float32)
    half = 32
    inv = (10000.0 ** (-(np.arange(0, half, 2, dtype=np.float32)) / half)).astype(np.float32)
    ang_r = row[:, None] * inv
    ang_c = col[:, None] * inv
    ang = np.concatenate([ang_r, ang_r, ang_c, ang_c], -1).astype(np.float32)
    cos = np.cos(ang).astype(np.float32)
    sin = np.sin(ang).astype(np.float32)
    d = np.arange(64)
    sign = np.where((d % 32) < 16, -1.0, 1.0).astype(np.float32)
    sinS = sin * sign[None, :]
    cosT = np.concatenate([cos.T, cos.T], 0)
    sinT = np.concatenate([sinS.T, sinS.T], 0)
    rope = np.ascontiguousarray(np.stack([cosT, sinT], 0)).astype(np.float32)
    return np.ascontiguousarray(cst), rope


_CACHE = {}


def kernel(**inputs):
    NCORES = 8
    NB = 32 // NCORES
    key = (NB, DEPTH)
    if key not in _CACHE:
        _CACHE[key] = build(NB, DEPTH)
    nc = _CACHE[key][0]
    cst, rope = make_consts()
    f = lambda a: np.ascontiguousarray(np.asarray(a, dtype=np.float32))
    shared = {k: f(inputs[k]) for k in ["w_mod", "b_mod", "w_in", "b_in", "ml_conv_w", "ml_conv_b", "ml_norm_g",
                                        "da_lam_q1", "da_lam_k1", "da_lam_q2", "da_lam_k2", "da_norm_g", "sg_norm_g",
                                        "sg_norm_b", "sg_w_s", "sg_b_s", "w_out", "ln_g", "ln_b"]}
    shared["c_ctx"] = f(inputs["c_ctx"]).reshape(1, D)
    shared["cst"] = cst
    shared["rope"] = rope
    x = f(inputs["x"])
    ctx = f(inputs["ctx"])
    c = f(inputs["c"])
    in_maps = []
    for i in range(NCORES):
        m = dict(shared)
        m["x"] = x[i * NB:(i + 1) * NB]
        m["ctx"] = ctx[i * NB:(i + 1) * NB]
        m["c"] = c[i * NB:(i + 1) * NB]
        in_maps.append(m)
    res = run_bass_kernel_spmd(nc, in_maps, core_ids=list(range(NCORES)))
    return np.concatenate([r["y"] for r in res.results], axis=0).astype(np.float32)
```

```python
import math
import numpy as np
from contextlib import ExitStack
import concourse.bass as bass
import concourse.mybir as mybir
from concourse.bass_utils import run_bass_kernel_spmd

F32 = mybir.dt.float32
BF16 = mybir.dt.bfloat16
AF = mybir.ActivationFunctionType
ALU = mybir.AluOpType
AX = mybir.AxisListType

D = 1024
CTXL = 256
LAT = 2048
T = CTXL + LAT
NT = T // 128
DEPTH = 4
EPS = 1e-5
ALPHA = (2 * DEPTH) ** 0.25
NWC = 5136
BLKS = [(0, 256), (256, 768), (768, 1280), (1280, 1792), (1792, 2304)]
NDMA = 24
PSUM_KEYS = frozenset(["g0", "g1", "g2", "g3", "g4", "tpa", "tpb", "tb"])

C_MLQK, C_MLV, C_DAK, C_DAV, C_SG, C_DAQ, C_MLOZ, C_DAZ = 0, 512, 784, 1808, 2320, 3088, 4112, 4624
B_MLV, B_DAV, B_SG, B_MLOZ, B_DAZ, NBIAS = 0, 272, 784, 1552, 2064, 2576


class Sched:
    def __init__(self):
        self.ins = []
        self.lastw = {}
        self.readers = {}
        self.region = {}
        self.rlast = {}
        self.rdma = {}

    def _add(self, eng, fn, reads, writes, is_dma, extra=()):
        reads = list(reads)
        writes = list(writes)
        for k in reads:
            if k in PSUM_KEYS:
                writes.append(("rdser", k))
        deps = set(extra)
        touched = set()
        for k in reads + writes:
            nm = k[0] if isinstance(k, tuple) else k
            r = self.region.get(nm)
            if r is not None:
                touched.add(r)
        for r in touched:
            w = self.lastw.get(("R", r))
            if w is not None:
                deps.add(w)
        for k in reads:
            w = self.lastw.get(k)
            if w is not None:
                deps.add(w)
        for k in writes:
            w = self.lastw.get(k)
            if w is not None:
                deps.add(w)
            rd = self.readers.get(k)
            if rd:
                deps.update(rd[0].values())
                deps.update(rd[1])
        idx = len(self.ins)
        self.ins.append([eng, fn, deps, is_dma])
        for k in writes:
            self.lastw[k] = idx
            self.readers[k] = [{}, []]
        ws = set(writes)
        for k in reads:
            if k not in ws:
                rd = self.readers.setdefault(k, [{}, []])
                if is_dma:
                    rd[1].append(idx)
                else:
                    rd[0][eng] = idx
        for r in touched:
            if is_dma:
                self.rdma.setdefault(r, []).append(idx)
            else:
                self.rlast.setdefault(r, {})[eng] = idx
        return idx

    def fence(self, region, eng, fn):
        deps = set(self.rlast.get(region, {}).values()) | set(self.rdma.get(region, []))
        idx = self._add(eng, fn, (), (), False, extra=deps)
        self.lastw[("R", region)] = idx
        self.rlast[region] = {}
        self.rdma[region] = []
        return idx

    def op(self, eng, fn, reads=(), writes=()):
        return self._add(eng, fn, reads, writes, False)

    def dma(self, eng, fn, reads=(), writes=()):
        return self._add(eng, fn, reads, writes, True)

    def emit(self, nc, stack):
        ins = self.ins
        n = len(ins)
        engs = ["pe", "act", "dve", "pool", "sp"]
        dma_list = [i for i in range(n) if ins[i][3]]
        dma_slot = {}
        for j, i in enumerate(dma_list):
            dma_slot[i] = j
            if j >= NDMA:
                ins[i][2].add(dma_list[j - NDMA])
        needed = [False] * n
        for i in range(n):
            e = ins[i][0]
            nd = set()
            for d in ins[i][2]:
                if ins[d][0] == e and e == "pe" and not ins[d][3]:
                    continue
                nd.add(d)
                needed[d] = True
            ins[i][2] = nd
        esem = {e: stack.enter_context(nc.semaphore("s_" + e)) for e in engs}
        dsem = [stack.enter_context(nc.semaphore("d_%d" % j)) for j in range(NDMA)]
        cnt = {e: 0 for e in engs}
        tok = [None] * n
        for i in range(n):
            e, fn, deps, is_dma = ins[i]
            if is_dma:
                j = dma_slot[i]
                tok[i] = (("d", j % NDMA), 16 * (j // NDMA + 1))
            elif needed[i]:
                cnt[e] += 1
                tok[i] = (("e", e), cnt[e])
        per = {e: [] for e in engs}
        for i in range(n):
            per[ins[i][0]].append(i)
        self.counts = {e: len(per[e]) for e in engs}

        def semof(key):
            return esem[key[1]] if key[0] == "e" else dsem[key[1]]

        def run(e, h):
            seen = {}
            for i in per[e]:
                _, fn, deps, is_dma = ins[i]
                want = {}
                for d in deps:
                    k, v = tok[d]
                    if v > want.get(k, 0):
                        want[k] = v
                for k, v in want.items():
                    if seen.get(k, 0) < v:
                        h.wait_ge(semof(k), v)
                        seen[k] = v
                r = fn(h)
                if tok[i] is not None:
                    k, v = tok[i]
                    r.then_inc(semof(k), 16 if is_dma else 1)
            for i in per[e]:
                if ins[i][3]:
                    k, v = tok[i]
                    if seen.get(k, 0) < v:
                        h.wait_ge(semof(k), v)
                        seen[k] = v

        with nc.Block() as block:
            @block.tensor
            def _(h):
                run("pe", h)

            @block.scalar
            def _(h):
                run("act", h)

            @block.vector
            def _(h):
                run("dve", h)

            @block.gpsimd
            def _(h):
                run("pool", h)

            @block.sync
            def _(h):
                run("sp", h)


class Arena:
    def __init__(self, ap):
        self.ap = ap
        self.off = 0
        self.size = ap.shape[1]

    def f32(self, n):
        a = self.ap[:, self.off:self.off + n]
        self.off += (n + 7) // 8 * 8
        assert self.off <= self.size, (self.off, self.size)
        return a

    def bf16(self, n):
        w = (n + 1) // 2
        a = self.ap[:, self.off:self.off + w].bitcast(BF16)
        self.off += (w + 7) // 8 * 8
        assert self.off <= self.size, (self.off, self.size)
        return a[:, 0:n]


def v3(ap, a):
    return ap.rearrange("p (a b) -> p a b", a=a)


def v4(ap, a, b):
    return ap.rearrange("p (a b c) -> p a b c", a=a, b=b)


def bc(ap, shape):
    return ap.to_broadcast(shape)


class _Stop(Exception):
    pass


def build(NB, NL, dbg=None, stop=None):
    nc = bass.Bass("TRN2", target_bir_lowering=False)
    S = Sched()

    def dram(name, shape, dtype=F32, kind="ExternalInput"):
        return nc.dram_tensor(name, shape, dtype, kind=kind).ap()

    x_d = dram("x", [NB, LAT, D])
    ctx_d = dram("ctx", [NB, CTXL, D])
    c_d = dram("c", [NB, D])
    cctx_d = dram("c_ctx", [1, D])
    wmod_d = dram("w_mod", [DEPTH, D, 3 * D])
    bmod_d = dram("b_mod", [DEPTH, 3 * D])
    win_d = dram("w_in", [DEPTH, D, 4112])
    bin_d = dram("b_in", [DEPTH, 4112])
    convw_d = dram("ml_conv_w", [DEPTH, 3, 512])
    convb_d = dram("ml_conv_b", [DEPTH, 512])
    mlg_d = dram("ml_norm_g", [DEPTH, 256])
    lq1_d = dram("da_lam_q1", [DEPTH, 64])
    lk1_d = dram("da_lam_k1", [DEPTH, 64])
    lq2_d = dram("da_lam_q2", [DEPTH, 64])
    lk2_d = dram("da_lam_k2", [DEPTH, 64])
    dag_d = dram("da_norm_g", [DEPTH, 128])
    sgg_d = dram("sg_norm_g", [DEPTH, 256])
    sgb_d = dram("sg_norm_b", [DEPTH, 256])
    sgw_d = dram("sg_w_s", [DEPTH, 4, 128, 128])
    sgbs_d = dram("sg_b_s", [DEPTH, 4, 128])
    wout_d = dram("w_out", [DEPTH, D, D])
    lng_d = dram("ln_g", [DEPTH, D])
    lnb_d = dram("ln_b", [DEPTH, D])
    cst_d = dram("cst", [128, 640])
    rope_d = dram("rope", [2, 128, LAT])
    y_d = dram("y", [NB, LAT, D], kind="ExternalOutput")
    wbf_d = dram("wbf", [DEPTH, D, NWC], BF16, kind="Internal")
    wobf_d = dram("wobf", [DEPTH, D, D], BF16, kind="Internal")
    mod_d = dram("modscr", [DEPTH, 8, 3 * D], F32, kind="Internal")
    xs_d = dram("xs", [NB, T, D], F32, kind="Internal")
    dbg_out = {}

    with ExitStack() as st:
        arena_t = st.enter_context(nc.sbuf_tensor("arena", [128, 51200], F32))
        AR = Arena(arena_t[:, :])
        ps_tp = st.enter_context(nc.psum_tensor("ps_tp", [128, 1024], F32))
        ps_tb = st.enter_context(nc.psum_tensor("ps_tb", [128, 1024], BF16))
        ps_g = [st.enter_context(nc.psum_tensor("ps_g%d" % i, [128, 512], F32)) for i in range(5)]

        cstF = AR.f32(640)
        identF, triF, triB, onesF, rperm = (cstF[:, i * 128:(i + 1) * 128] for i in range(5))
        identB = AR.bf16(128)
        maskF = AR.bf16(128)
        maskB = AR.bf16(128)
        neghalf = AR.f32(64)
        dummy = AR.f32(8)
        sgW = v4(AR.bf16(DEPTH * 4 * 128), DEPTH, 4)
        sgbs = v3(AR.f32(DEPTH * 4), DEPTH)
        fmb = v3(AR.f32(DEPTH * 20), DEPTH)
        convw = v4(AR.f32(DEPTH * 12), DEPTH, 3)
        convb = v3(AR.f32(DEPTH * 4), DEPTH)
        hconvb = v3(AR.f32(DEPTH * 4), DEPTH)
        lamv = AR.f32(DEPTH)
        neglam = AR.f32(DEPTH)
        biasb = AR.f32(NBIAS)
        lng = AR.f32(D)
        lnb = AR.f32(D)
        mlgc = AR.f32(256)
        dagc = AR.f32(128)
        sggt = AR.f32(256)
        sgbt = AR.f32(256)
        gate_l = AR.f32(D)
        gate_c = AR.f32(D)
        modp = v3(AR.f32(32), 4)
        WR_OFF = AR.off
        wring = [v3(AR.bf16(8 * 512), 8) for _ in range(2)]
        wq32 = v3(arena_t[:, WR_OFF:WR_OFF + 4096], 8)
        wg32 = v3(AR.f32(128), 8)
        Hml = v3(AR.f32(NT * 256), NT)
        small = AR.f32(256)
        XOFF = AR.off
        XSZ = 8192
        AR.off += XSZ
        BOFF = AR.off
        BSZ = 12900
        AR.off += BSZ
        AOFF = AR.off
        ASZ = 51200 - AOFF
        assert ASZ >= 9216, ASZ

        def sub(off, size):
            return Arena(arena_t[:, off:off + size])

        for nm in ["X", "B", "A"]:
            pass

        aX = sub(XOFF, XSZ)
        xblk = [v3(aX.f32(4096), 4), v3(aX.f32(4096), 4)]
        aX = sub(XOFF, XSZ)
        pst = aX.f32(T)
        cacc = aX.f32(T)
        aX = sub(XOFF, XSZ)
        m_pt = [v3(aX.f32(512), 4) for _ in range(2)]
        m_vsf = [aX.f32(288) for _ in range(2)]
        m_vs = [v3(m_vsf[i], 4)[:, :, 0:65] for i in range(2)]
        m_kt = [aX.f32(256) for _ in range(2)]
        m_hs = [v3(aX.f32(288), 4)[:, :, 0:65] for _ in range(2)]
        m_cst_flat = aX.f32(576)
        m_cst = v4(m_cst_flat, 2, 4)[:, :, :, 0:65]
        m_tmp = v3(aX.f32(288), 4)[:, :, 0:65]
        m_hn = v3(aX.f32(256), 4)
        aX = sub(XOFF, XSZ)
        ropeC = aX.f32(512)
        ropeS = aX.f32(512)
        rstA = aX.f32(512)
        rstB = aX.f32(512)
        sgp = aX.f32(768)
        sgu = aX.f32(256)
        sgjunk = aX.f32(256)
        sgvn = aX.bf16(256)
        sgy = aX.f32(256)
        sgzs = aX.f32(256)
        aB = sub(BOFF, BSZ)
        qkT = v3(aB.f32(4 * T), 4)
        vaug = v4(aB.bf16(NT * 4 * 72), NT, 4)[:, :, :, 0:65]
        Gt = v3(aB.f32(NT * 16), NT)
        LFp = v3(aB.f32(144), 2)
        Ej = v3(aB.f32(144), 2)
        Ws = v3(aB.f32(144), 2)
        Eend = v3(aB.f32(144), 2)
        gtmp = v3(aB.f32(144), 2)
        aB = sub(BOFF, BSZ)
        dkT = v3(aB.bf16(4 * T), 4)
        dvaug = v4(aB.bf16(NT * 4 * 136), NT, 4)[:, :, :, 0:129]
        ysg = v3(aB.bf16(NT * 256), NT)
        aA = sub(AOFF, ASZ)
        hTall = v3(aA.bf16(8 * T), 8)
        xn_a = aA.f32(0) if False else None
        aA = sub(AOFF, ASZ)
        hTblk = v3(aA.bf16(8 * 512), 8)
        qTblk = v3(aA.bf16(4 * 512), 4)
        gates = v3(aA.bf16(4 * 1024), 4)
        mixT = v3(aA.bf16(8 * 512), 8)
        ptb = [aA.bf16(512) for _ in range(4)]
        oev = v4(aA.f32(4 * 2 * 132), 4, 2)[:, :, :, 0:129]
        att = aA.f32(128)
        xn5 = aA.f32(1024)
        aX5 = sub(XOFF + 4096, 4096)
        p5ropeC = aX5.f32(512)
        p5ropeS = aX5.f32(512)
        p5stA = aX5.f32(512)
        p5stB = aX5.f32(512)
        p5tmp = aX5.f32(512)
        p5t2 = aX5.f32(512)
        p5ml = aX5.f32(256)
        p5x_ml = arena_t[:, XOFF + 4096:XOFF + 4096 + 1024]
        p5x_sq = arena_t[:, XOFF + 4096 + 2048:XOFF + 4096 + 3072]
        p5sq = aX5.f32(256)

        for nm in ["xblk", "pst", "cacc", "mtmp", "sgt", "rope4", "p5x"]:
            S.region[nm] = "X"
        for nm in ["qkT", "qkpre", "vaug", "ktok", "Gt", "gder", "dkT", "dvaug", "ysg"]:
            S.region[nm] = "B"
        for nm in ["hTall", "p5"]:
            S.region[nm] = "A"

        def fence(region):
            S.fence(region, "pool", lambda h: h.memset(dummy[:, 0:1], 0.0))

        def mm(out, lhsT, rhs, start, stop, reads, writes):
            S.op("pe", lambda h: h.matmul(out, lhsT=lhsT, rhs=rhs, start=start, stop=stop), reads, writes)

        def tr(out, in_, ident, reads, writes):
            S.op("pe", lambda h: h.transpose(out, in_, ident), reads, writes)

        def act(out, in_, func, reads, writes, bias=None, scale=None, accum=None):
            kw = {}
            if accum is not None:
                kw["accum_out"] = accum
            if bias is not None:
                kw["bias"] = bias
            if scale is not None:
                kw["scale"] = scale
            S.op("act", lambda h: h.activation(out=out, in_=in_, func=func, **kw), reads, writes)

        def tt(eng, out, in0, in1, op, reads, writes):
            S.op(eng, lambda h: h.tensor_tensor(out=out, in0=in0, in1=in1, op=op), reads, writes)

        def ts(eng, out, in0, s1, s2, op0, op1, reads, writes):
            if s2 is None:
                S.op(eng, lambda h: h.tensor_scalar(out=out, in0=in0, scalar1=s1, scalar2=None, op0=op0), reads, writes)
            else:
                S.op(eng, lambda h: h.tensor_scalar(out=out, in0=in0, scalar1=s1, scalar2=s2, op0=op0, op1=op1), reads, writes)

        def stt(eng, out, in0, scalar, in1, op0, op1, reads, writes):
            S.op(eng, lambda h: h.scalar_tensor_tensor(out=out, in0=in0, scalar=scalar, in1=in1, op0=op0, op1=op1), reads, writes)

        def cp(eng, out, in_, reads, writes):
            if eng == "act":
                S.op(eng, lambda h: h.activation(out=out, in_=in_, func=AF.Copy), reads, writes)
            else:
                S.op(eng, lambda h: h.tensor_copy(out=out, in_=in_), reads, writes)

        def red(out, in_, reads, writes):
            S.op("dve", lambda h: h.tensor_reduce(out=out, in_=in_, axis=AX.X, op=ALU.add), reads, writes)

        def dma(out, in_, reads, writes, slow=False):
            if slow:
                S.dma("sp", lambda h: h.dma_start(out=out, in_=in_, allow_slow_non_contiguous=True), reads, writes)
            else:
                S.dma("sp", lambda h: h.dma_start(out=out, in_=in_), reads, writes)

        def tap(name, ap, key, dtype=F32):
            if dbg is None or name not in dbg:
                return
            shp = list(ap.shape)
            d = nc.dram_tensor("dbg_" + name, shp, dtype, kind="ExternalOutput").ap()
            dbg_out[name] = shp
            dma(d, ap, list(key) if isinstance(key, list) else [key], ["dbg_" + name])

        def stats(s1, s2, k, n, tag):
            rk1 = tag[0] if isinstance(tag, tuple) else [tag + "s1"]
            rk2 = tag[1] if isinstance(tag, tuple) else [tag + "s2"]
            mean = small[:, 0:k]
            msq = small[:, 16:16 + k]
            var = small[:, 32:32 + k]
            rstd = small[:, 48:48 + k]
            ts("dve", mean, s1, 1.0 / n, None, ALU.mult, None, rk1, ["st_mean"])
            tt("dve", msq, mean, mean, ALU.mult, ["st_mean"], ["st_msq"])
            stt("dve", var, s2, 1.0 / n, msq, ALU.mult, ALU.subtract, rk2 + ["st_msq"], ["st_var"])
            ts("dve", var, var, EPS, None, ALU.add, None, ["st_var"], ["st_var"])
            tt("pool", rstd, var, neghalf[:, 0:k], ALU.pow, ["st_var", "cst"], ["st_rstd"])
            return mean, rstd

        def chk(name):
            if stop == name:
                raise _Stop()

        try:
            dma(cstF, cst_d[:, :], [], ["cst"])
            cp("dve", identB, identF, ["cst"], ["cstb"])
            cp("dve", maskF, triF, ["cst"], ["cstb"])
            cp("dve", maskB, triB, ["cst"], ["cstb"])
            S.op("pool", lambda h: h.memset(neghalf, -0.5), [], ["cst"])

            S.op("pool", lambda h: h.memset(fmb[:, :, :], 0.0), [], ["fmb"])
            for l in range(DEPTH):
                for j in range(4):
                    dma(fmb[:, l, j:j + 1], bin_d[l, j * 128:(j + 1) * 128].rearrange("(p o) -> p o", o=1), [], ["fmb"])
                for h4 in range(4):
                    dma(fmb[:, l, 4 + 2 * h4:5 + 2 * h4], bin_d[l, 1808 + h4 * 128:1808 + (h4 + 1) * 128].rearrange("(p o) -> p o", o=1), [], ["fmb"])
                    dma(fmb[:, l, 12 + 2 * h4:13 + 2 * h4], bin_d[l, 1296 + h4 * 128:1296 + (h4 + 1) * 128].rearrange("(p o) -> p o", o=1), [], ["fmb"])
                for j in range(3):
                    dma(convw[:, l, j, :], convw_d[l, j, :].rearrange("(c p) -> p c", p=128), [], ["convp"], slow=True)
                dma(convb[:, l, :], convb_d[l, :].rearrange("(c p) -> p c", p=128), [], ["convp"], slow=True)
                dma(sgbs[:, l, :], sgbs_d[l, :, :].rearrange("g p -> p g"), [], ["sgbs"], slow=True)
            ts("dve", hconvb[:, :, :], convb[:, :, :], 0.5, None, ALU.mult, None, ["convp"], ["hconvb"])
            for l in range(DEPTH):
                pso = ps_g[0][:, 0:16]
                mm(pso, rperm, fmb[:, l, 4:20], True, True, ["cst", "fmb"], ["g0"])
                src = v3(pso, 8)[:, :, 0:1]
                dst = v3(fmb[:, l, 4:20], 8)[:, :, 1:2]
                cp("dve", dst, src, ["g0"], ["fmb"])
            for l in range(DEPTH):
                for g in range(4):
                    stg = small[:, 64:192]
                    dma(stg, sgw_d[l, g, :, :], [], ["sgstg"])
                    tr(ps_g[1][:, 0:128], stg, identF, ["sgstg", "cst"], ["g1"])
                    cp("dve", sgW[:, l, g, :], ps_g[1][:, 0:128], ["g1"], ["sgW"])
            lt = rstA
            fence("X")
            for i, (a_d, b_d) in enumerate([(lq1_d, lk1_d), (lq2_d, lk2_d)]):
                dma(lt[:, 0:256], a_d.rearrange("l k -> (l k)").partition_broadcast(128), [], [("rope4", "a")])
                dma(lt[:, 256:512], b_d.rearrange("l k -> (l k)").partition_broadcast(128), [], [("rope4", "b")])
                tt("dve", lt[:, 0:256], lt[:, 0:256], lt[:, 256:512], ALU.mult, [("rope4", "a"), ("rope4", "b")], [("rope4", "a")])
                red(small[:, 200 + 4 * i:204 + 4 * i], v3(lt[:, 0:256], 4), [("rope4", "a")], ["lam%d" % i])
                act(small[:, 200 + 4 * i:204 + 4 * i], small[:, 200 + 4 * i:204 + 4 * i], AF.Exp, ["lam%d" % i], ["lam%d" % i])
            tt("dve", lamv, small[:, 200:204], small[:, 204:208], ALU.subtract, ["lam0", "lam1"], ["lamv"])
            for l in range(DEPTH):
                lam_init = 0.8 - 0.6 * math.exp(-0.3 * l)
                ts("dve", lamv[:, l:l + 1], lamv[:, l:l + 1], lam_init, None, ALU.add, None, ["lamv"], ["lamv"])
            ts("dve", neglam, lamv, -1.0, None, ALU.mult, None, ["lamv"], ["neglam"])

            chk('consts')
            groups = [
                (C_MLQK, [(0, 0, 512, False)]),
                (C_MLV, [(0, 512, 256, False), (256, 1280, 16, False)]),
                (C_DAK, [(0, 1808, 128, False), (128, 1808, 128, True), (256, 1936, 128, False), (384, 1936, 128, True)]),
                (C_DAK + 512, [(0, 2064, 128, False), (128, 2064, 128, True), (256, 2192, 128, False), (384, 2192, 128, True)]),
                (C_DAV, [(0, 2320, 512, False)]),
                (C_SG, [(0, 3344, 512, False)]),
                (C_SG + 512, [(0, 3856, 256, False)]),
                (C_DAQ, [(0, 1296, 128, False), (128, 1296, 128, True), (256, 1424, 128, False), (384, 1424, 128, True)]),
                (C_DAQ + 512, [(0, 1552, 128, False), (128, 1552, 128, True), (256, 1680, 128, False), (384, 1680, 128, True)]),
                (C_MLOZ, [(0, 768, 512, False)]),
                (C_DAZ, [(0, 2832, 512, False)]),
            ]
            fence("X")
            fence("A")
            stg32 = [v3(sub(XOFF, XSZ).f32(4096), 8), v3(sub(XOFF + 4096, 4096).f32(4096), 8)]
            aA = sub(AOFF, ASZ)
            stg16 = [v3(aA.bf16(4096), 8), v3(aA.bf16(4096), 8)]
            S.region["stg32"] = "X"
            S.region["stg16"] = "A"
            gi = 0
            prev_store = [[], []]
            ceng = ["dve", "pool", "act"]

            def castcp(i, out, in_, reads, writes):
                e = ceng[i % 3]
                if e == "act":
                    act(out, in_, AF.Copy, reads, writes)
                else:
                    cp(e, out, in_, reads, writes)

            for l in range(DEPTH):
                for (dst0, pieces) in groups:
                    sl = gi % 2
                    wtot = max(p[0] + p[2] for p in pieces)
                    for (doff, src0, w, sw) in pieces:
                        dma(stg32[sl][:, :, doff:doff + w], win_d[l, :, src0:src0 + w].rearrange("(kc p) w -> p kc w", p=128), prev_store[sl], [("stg32", sl, doff)])
                    for pi, (doff, src0, w, sw) in enumerate(pieces):
                        if not sw:
                            castcp(gi + pi, stg16[sl][:, :, doff:doff + w], stg32[sl][:, :, doff:doff + w], [("stg32", sl, doff)], [("stg16", sl, doff)])
                        else:
                            i5 = stg32[sl][:, :, doff:doff + w].rearrange("p k (b t s) -> p k b t s", b=4, t=2)
                            o5 = stg16[sl][:, :, doff:doff + w].rearrange("p k (b t s) -> p k b t s", b=4, t=2)
                            for kc in range(8):
                                cp("pool" if kc % 2 else "dve", o5[:, kc, :, 0, :], i5[:, kc, :, 1, :], [("stg32", sl, doff)], [("stg16", sl, doff, kc, 0)])
                                cp("dve" if kc % 2 else "pool", o5[:, kc, :, 1, :], i5[:, kc, :, 0, :], [("stg32", sl, doff)], [("stg16", sl, doff, kc, 1)])
                    rk = []
                    for (doff, src0, w, sw) in pieces:
                        if sw:
                            rk += [("stg16", sl, doff, kc, t) for kc in range(8) for t in range(2)]
                        else:
                            rk.append(("stg16", sl, doff))
                    dma(wbf_d[l, :, dst0:dst0 + wtot].rearrange("(kc p) w -> p kc w", p=128), stg16[sl][:, :, 0:wtot], rk, [("wbf", l, dst0)])
                    prev_store[sl] = [("wbf", l, dst0)]
                    gi += 1
                for hf in range(2):
                    sl = gi % 2
                    dma(stg32[sl][:, :, :], wout_d[l, :, hf * 512:(hf + 1) * 512].rearrange("(kc p) w -> p kc w", p=128), prev_store[sl], [("stg32", sl, 0)])
                    castcp(gi, stg16[sl][:, :, :], stg32[sl][:, :, :], [("stg32", sl, 0)], [("stg16", sl, 0)])
                    dma(wobf_d[l, :, hf * 512:(hf + 1) * 512].rearrange("(kc p) w -> p kc w", p=128), stg16[sl][:, :, :], [("stg16", sl, 0)], [("wobf", l, hf)])
                    prev_store[sl] = [("wobf", l, hf)]
                    gi += 1

            chk('conv')
            csT = v3(small[:, 64:64 + 64], 8)
            for r in range(NB + 1):
                src = c_d[r, :] if r < NB else cctx_d[0, :]
                dma(csT[:, :, r:r + 1], src.rearrange("(kc p o) -> p kc o", p=128, o=1), ["sgW"], [("csT", r)], slow=True)
            NR = NB + 1
            cs_r = [("csT", r) for r in range(NR)]
            tnh = v3(small[:, 128:192], 8)
            act(tnh[:, :, 0:NR], csT[:, :, 0:NR], AF.Tanh, cs_r, ["cs_t"], scale=0.5)
            ts("dve", tnh[:, :, 0:NR], tnh[:, :, 0:NR], 0.5, 0.5, ALU.mult, ALU.add, ["cs_t"], ["cs_t"])
            tt("dve", csT[:, :, 0:NR], csT[:, :, 0:NR], tnh[:, :, 0:NR], ALU.mult, cs_r + ["cs_t"], ["csS"])
            mrow = sub(AOFF, ASZ).f32(1024)
            S.region["mrow"] = "A"
            wi = 0
            for l in range(NL):
                for cb in range(6):
                    sl = wi % 2
                    dma(stg32[sl][:, :, :], wmod_d[l, :, cb * 512:(cb + 1) * 512].rearrange("(kc p) w -> p kc w", p=128), prev_store[sl], [("stg32", sl, 0)])
                    pg = ps_g[2 + (wi % 2)]
                    for kc in range(8):
                        mm(pg[0:NR, :], csT[:, kc, 0:NR], stg32[sl][:, kc, :], kc == 0, kc == 7, ["csS", ("stg32", sl, 0)], ["g%d" % (2 + wi % 2)])
                    bm = mrow[0:NR, 512:1024]
                    dma(bm, bmod_d[l, cb * 512:(cb + 1) * 512].partition_broadcast(NR), [], [("mrow", "b")])
                    tt("dve", mrow[0:NR, 0:512], pg[0:NR, :], bm, ALU.add, ["g%d" % (2 + wi % 2), ("mrow", "b")], [("mrow", "o")])
                    dma(mod_d[l, 0:NR, cb * 512:(cb + 1) * 512], mrow[0:NR, 0:512], [("mrow", "o")], [("mod", l)])
                    wi += 1

            chk('mod')
            def wload(l, col0, ncols, slot, src=None):
                srcd = wbf_d if src is None else src
                key = ("wbf", l, col0) if src is None else ("wobf", l, col0 // 512)
                rk = [("wbf", l, g[0]) for g in groups] if src is None else [key]
                dma(wring[slot][:, :, 0:ncols], srcd[l, :, col0:col0 + ncols].rearrange("(kc p) w -> p kc w", p=128), rk, [("wring", slot)])

            ring_ctr = [0]

            def next_slot():
                s_ = ring_ctr[0] % 2
                ring_ctr[0] += 1
                return s_

            g_ctr = [0]

            def next_g(lo=0, hi=5):
                i = lo + g_ctr[0] % (hi - lo)
                g_ctr[0] += 1
                return i

            def ln_block(xb, ntile, xkey, xnbuf, xnkey, hT_of, hkey_of, s1p, shp, modkey, hTf=None, hfkey=None, post=None):
                s1 = small[:, 224:224 + ntile]
                s2 = small[:, 232:232 + ntile]
                k1 = [("lnst", 1, ti) for ti in range(ntile)]
                k2 = [("lnst", 2, ti) for ti in range(ntile)]
                xk = xkey if callable(xkey) else (lambda ti: xkey)
                for ti in range(ntile):
                    act(xnbuf, xb[:, ti, :], AF.Square, [xk(ti)], [xnkey, k2[ti]], accum=s2[:, ti:ti + 1])
                    act(xnbuf, xb[:, ti, :], AF.Identity, [xk(ti)], [xnkey, k1[ti]], accum=s1[:, ti:ti + 1])
                mean, rstd = stats(s1, s2, ntile, 1024.0, (k1, k2))
                for ti in range(ntile):
                    ts("dve", xnbuf, xb[:, ti, :], mean[:, ti:ti + 1], rstd[:, ti:ti + 1], ALU.subtract, ALU.mult, [xk(ti), "st_mean", "st_rstd"], [xnkey])
                    for kc in range(8):
                        tr(ps_tp[:, kc * 128:(kc + 1) * 128], xnbuf[:, kc * 128:(kc + 1) * 128], identF, [xnkey, "cst"], ["tpa", "tpb"])
                    tmpm = v3(xnbuf, 8)
                    tt("dve", tmpm, v3(ps_tp[:, :], 8), bc(s1p.unsqueeze(2), [128, 8, 128]), ALU.mult, ["tpa", "tpb"] + modkey, [xnkey])
                    if hTf is None:
                        tt("pool", hT_of(ti), tmpm, bc(shp.unsqueeze(2), [128, 8, 128]), ALU.add, [xnkey] + modkey, [hkey_of(ti)])
                    else:
                        tt("pool", hTf(ti), tmpm, bc(shp.unsqueeze(2), [128, 8, 128]), ALU.add, [xnkey] + modkey, [hfkey(ti)])
                        cp("act", hT_of(ti), hTf(ti), [hfkey(ti)], [hkey_of(ti)])
                    if post is not None:
                        post(ti)

            for b in range(NB):
                for l in range(NL):
                    lam_init = 0.8 - 0.6 * math.exp(-0.3 * l)
                    pk = ("par", b, l)
                    for (off, s0, w) in [(B_MLV, 512, 256), (B_MLV + 256, 1280, 16), (B_DAV, 2320, 512), (B_SG, 3344, 768), (B_MLOZ, 768, 512), (B_DAZ, 2832, 512)]:
                        dma(biasb[:, off:off + w], bin_d[l, s0:s0 + w].partition_broadcast(128), [], [("biasb", off)])
                    dma(lng, lng_d[l, :].partition_broadcast(128), [], ["lng"])
                    dma(lnb, lnb_d[l, :].partition_broadcast(128), [], ["lnb"])
                    dma(mlgc, mlg_d[l, :].partition_broadcast(128), [], ["mlgc"])
                    dma(dagc, dag_d[l, :].partition_broadcast(128), [], ["dagc"])
                    dma(sggt, sgg_d[l, :].partition_broadcast(128), [], ["sggt"])
                    dma(sgbt, sgb_d[l, :].partition_broadcast(128), [], ["sgbt"])
                    ts("dve", mlgc, mlgc, 0.5, None, ALU.mult, None, ["mlgc"], ["mlgc"])
                    ts("dve", dagc, dagc, 0.5 * (1.0 - lam_init), None, ALU.mult, None, ["dagc"], ["dagc"])
                    dma(gate_l, mod_d[l, b, 2048:3072].partition_broadcast(128), [("mod", l)], ["gate_l"])
                    dma(gate_c, mod_d[l, NB, 2048:3072].partition_broadcast(128), [("mod", l)], ["gate_c"])
                    for i, (row, c0) in enumerate([(b, 0), (b, 1024), (NB, 0), (NB, 1024)]):
                        dma(modp[:, i, :], mod_d[l, row, c0:c0 + 1024].rearrange("(kc p) -> p kc", p=128), [("mod", l)], [("modp", i)], slow=True)
                    for i in (1, 3):
                        ts("dve", modp[:, i, :], modp[:, i, :], 1.0, None, ALU.add, None, [("modp", i)], [("modp", i)])
                    modk = [("modp", i) for i in range(4)]

                    fence("X")
                    fence("A")
                    fence("B")
                    dma(wq32, win_d[l, :, 0:512].rearrange("(kc p) w -> p kc w", p=128), [], [("wring", 0), ("wring", 1)])
                    dma(wg32, win_d[l, :, 1280:1296].rearrange("(kc p) w -> p kc w", p=128), [], ["wg32"])
                    hTf2 = xblk[1][:, 1:3, :].rearrange("p a (k t) -> p k (a t)", k=8) if False else None
                    hTfbuf = v3(arena_t[:, XOFF + 4096 + 1024:XOFF + 4096 + 3072], 8)
                    for bi, (t0, t1) in enumerate(BLKS):
                        ntile = (t1 - t0) // 128
                        xb = xblk[0]
                        if l == 0:
                            srcx = ctx_d[b, :, :] if bi == 0 else x_d[b, t0 - 256:t1 - 256, :]
                            rk = []
                        else:
                            srcx = xs_d[b, t0:t1, :]
                            rk = [("xs", b, bi, ti_) for ti_ in range((t1 - t0) // 128)]
                        dma(xb[:, 0:ntile, :], srcx.rearrange("(n p) d -> p n d", p=128), rk, [("xblk", 0)])
                        s1p, shp = (modp[:, 3, :], modp[:, 2, :]) if bi == 0 else (modp[:, 1, :], modp[:, 0, :])

                        def post1(ti, t0=t0):
                            tk = t0 + ti * 128
                            tg = tk // 128
                            hTf_t = hTfbuf[:, :, (ti % 2) * 128:(ti % 2) * 128 + 128]
                            gg = 2 + tg % 2
                            for kc in range(8):
                                mm(ps_g[gg][:, 0:16], hTf_t[:, kc, :], wg32[:, kc, :], kc == 0, kc == 7, ["wg32", ("xblk", 2, ti % 2)], ["g%d" % gg])
                            tt("dve", Gt[:, tg, :], ps_g[gg][:, 0:16], biasb[:, B_MLV + 256:B_MLV + 272], ALU.add,
                               ["g%d" % gg, ("biasb", B_MLV + 256)], [("Gt", tg)])
                            if ti % 2 == 0:
                                return
                            tk0 = tk - 128
                            for c in range(4):
                                pq = ps_g[c // 2][:, (c % 2) * 256:(c % 2) * 256 + 256]
                                for kc in range(8):
                                    mm(pq, wq32[:, kc, c * 128:(c + 1) * 128], hTfbuf[:, kc, :], kc == 0, kc == 7,
                                       [("wring", 0), ("wring", 1), ("xblk", 2, 0), ("xblk", 2, 1)], ["g%d" % (c // 2)])
                            for hb in range(2):
                                tt("dve", qkT[:, 2 * hb:2 * hb + 2, tk0:tk0 + 256], v3(ps_g[hb][:, :], 2), bc(fmb[:, l, 2 * hb:2 * hb + 2].unsqueeze(2), [128, 2, 256]), ALU.add,
                                   ["g%d" % hb, "fmb"], [("qkpre", tg - 1, hb), ("qkpre", tg, hb)])

                        ln_block(xb, ntile, ("xblk", 0), xblk[1][:, 0, :], ("xblk", 1),
                                 lambda ti, t0=t0: hTall[:, :, t0 + ti * 128:t0 + (ti + 1) * 128], lambda ti, t0=t0: ("hTall", t0 // 128 + ti),
                                 s1p, shp, modk, hTf=lambda ti: hTfbuf[:, :, (ti % 2) * 128:(ti % 2) * 128 + 128], hfkey=lambda ti: ("xblk", 2, ti % 2), post=post1)
                    if dbg:
                        tap("hT", hTall[:, :, :], [("hTall", i) for i in range(NT)], BF16)

                    chk('ph1')
                    hT_keys = [("hTall", i) for i in range(NT)]
                    pre_all = [("qkpre", ti, hb) for ti in range(NT) for hb in range(2)]
                    for c in range(4):
                        pst = qkT[:, c, :]
                        w0, w1, w2 = (convw[:, l, j, c:c + 1] for j in range(3))
                        ts("dve", cacc, pst, w1, convb[:, l, c:c + 1], ALU.mult, ALU.add, pre_all + ["convp"], ["cacc"])
                        for (a, e) in [(0, CTXL), (CTXL, T)]:
                            stt("dve", cacc[:, a + 1:e], pst[:, a:e - 1], w0, cacc[:, a + 1:e], ALU.mult, ALU.add, pre_all + ["convp", "cacc"], ["cacc"])
                            stt("dve", cacc[:, a:e - 1], pst[:, a + 1:e], w2, cacc[:, a:e - 1], ALU.mult, ALU.add, pre_all + ["convp", "cacc"], ["cacc"])
                        act(pst, cacc, AF.Tanh, ["cacc"] + pre_all, [("qkT", c)], scale=0.5)
                        sc_ = 0.5 if c < 2 else 0.0625
                        ts("pool", pst, pst, sc_, sc_, ALU.mult, ALU.add, [("qkT", c)], [("qkT", c)])
                        tt("dve", pst, pst, cacc, ALU.mult, [("qkT", c), "cacc"], [("qkT", c)])
                    slot = next_slot()
                    wload(l, C_MLV, 272, slot)
                    S.op("pool", lambda h: h.memset(vaug[:, :, :, 64:65], 1.0), [], [("vaug", "ones")])
                    for ti in range(NT):
                        gix = next_g()
                        pg = ps_g[gix]
                        for kc in range(8):
                            mm(pg[:, 0:272], hTall[:, kc, ti * 128:(ti + 1) * 128], wring[slot][:, kc, 0:272], kc == 0, kc == 7,
                               [("wring", slot), hT_keys[ti]], ["g%d" % gix])
                        tt("dve", vaug[:, ti, :, 0:64], v3(pg[:, 0:256], 4), v3(biasb[:, B_MLV:B_MLV + 256], 4), ALU.add,
                           ["g%d" % gix, ("biasb", B_MLV)], [("vaug", ti)])
                    if dbg:
                        tap("qkT", qkT[:, :, :], [("qkT", i) for i in range(4)], F32)
                        tap("vaug", vaug[:, :, :, :], [("vaug", i) for i in range(NT)] + [("vaug", "ones")], BF16)
                        tap("Gt", Gt[:, :, :], [("Gt", i) for i in range(NT)])

                    chk('ph2')
                    fence("X")
                    Gk = [("Gt", ti) for ti in range(NT)]
                    for d_ in range(2):
                        fsl = Gt[:, :, 4 + 8 * d_:8 + 8 * d_]
                        act(v3(gtmp[:, d_, :], NT), fsl, AF.Exp, Gk, [("gder", "t", d_)], scale=-1.0)
                        act(LFp[:, d_, :], gtmp[:, d_, :], AF.Ln, [("gder", "t", d_)], [("gder", "L", d_)], bias=1.0)
                    chk('g1')
                    pc = ps_g[0]
                    mm(pc[:, 0:72], triF, LFp[:, 0, :], True, True, ["cst", ("gder", "L", 0)], ["g0"])
                    mm(pc[:, 72:144], triB, LFp[:, 1, :], True, True, ["cst", ("gder", "L", 1)], ["g0"])
                    mm(pc[:, 144:288], onesF, LFp[:, :, :].rearrange("p a b -> p (a b)"), True, True, ["cst", ("gder", "L", 0), ("gder", "L", 1)], ["g0"])
                    chk('g2')
                    act(Ej[:, :, :].rearrange("p a b -> p (a b)"), pc[:, 0:144], AF.Exp, ["g0"], [("gder", "Ej")], scale=-1.0)
                    act(Eend[:, :, :].rearrange("p a b -> p (a b)"), pc[:, 144:288], AF.Exp, ["g0"], [("gder", "Eend")], scale=-1.0)
                    if dbg and 'cum' in dbg:
                        cp('dve', sgp[:, 0:288], pc[:, 0:288], ['g0'], [('mtmp', 'dbgc')])
                        tap('cum', sgp[:, 0:288], [('mtmp', 'dbgc')])
                        tap('LFp', LFp[:, :, :], [('gder', 'L', 0), ('gder', 'L', 1)])
                    chk('g3')
                    for d_ in range(2):
                        tt("dve", v3(gtmp[:, d_, :], NT), v3(pc[:, 72 * d_:72 * d_ + 72], NT), Gt[:, :, 8 * d_:8 * d_ + 4], ALU.add,
                           ["g0"] + Gk, [("gder", "t2", d_)])
                        act(Ws[:, d_, :], gtmp[:, d_, :], AF.Exp, [("gder", "t2", d_)], [("gder", "Ws", d_)])
                    chk('ph3a')
                    orders = [list(range(NT)), [1, 0] + list(range(NT - 1, 1, -1))]
                    written = set()
                    S.op("pool", lambda h: h.memset(m_cst_flat, 0.0), [], [("mtmp", "cst", 0), ("mtmp", "cst", 1)])
                    for d_ in range(2):
                        S.op("pool", lambda h, d_=d_: h.memset(m_vsf[d_], 0.0), [], [("mtmp", "vs", d_)])
                    for step in range(NT):
                        for d_ in range(2):
                            ti = orders[d_][step]
                            first = step == 0
                            tks = slice(ti * 128, (ti + 1) * 128)
                            for h4 in range(4):
                                pr = slice((h4 % 2) * 64, (h4 % 2) * 64 + 64)
                                gS = 1 + (h4 % 2)
                                mm(ps_g[gS][:, (h4 // 2) * 128:(h4 // 2 + 1) * 128], qkT[pr, 2 + h4 // 2, tks], qkT[pr, h4 // 2, tks], True, True,
                                   [("qkT", 2 + h4 // 2), ("qkT", h4 // 2)], ["g%d" % gS])
                            if step == 0 and d_ == 0: chk('sa')
                            mk = triF if d_ == 0 else triB
                            for j in range(2):
                                tr(ps_tp[:, j * 128:(j + 1) * 128], qkT[:, 2 + j, tks], identF, [("qkT", 2 + j), "cst"], ["tpa"])
                            cp("act", m_kt[d_], ps_tp[:, 0:256], ["tpa"], [("mtmp", "kt", d_)])
                            for par in range(2):
                                tt("dve", m_pt[d_][:, par * 2:par * 2 + 2, :], v3(ps_g[1 + par][:, 0:256], 2), bc(mk.unsqueeze(1), [128, 2, 128]), ALU.mult,
                                   ["g%d" % (1 + par), "cst"], [("mtmp", "pt", d_, par)])
                            wsl = v3(Ws[:, d_, :], NT)[:, ti, :]
                            tt("pool", m_vs[d_], vaug[:, ti, :, :], bc(wsl.unsqueeze(2), [128, 4, 65]), ALU.mult,
                               [("vaug", ti), ("vaug", "ones"), ("gder", "Ws", d_)], [("mtmp", "vs", d_)])
                            if step == 0 and d_ == 0: chk('sb')
                            gH = 3
                            pH = ps_g[gH]
                            pD = ps_g[4]
                            for h4 in range(4):
                                pr = slice((h4 % 2) * 64, (h4 % 2) * 64 + 64)
                                mm(pH[:, h4 * 72:h4 * 72 + 65], m_pt[d_][:, (h4 % 2) * 2 + h4 // 2, :], m_vs[d_][:, h4, :], True, first,
                                   [("mtmp", "pt", d_, h4 % 2), ("mtmp", "vs", d_)], ["g3"])
                                if not first:
                                    mm(pH[:, h4 * 72:h4 * 72 + 65], qkT[pr, h4 // 2, tks], m_cst[pr, d_, h4, :], False, True,
                                       [("qkT", h4 // 2), ("mtmp", "cst", d_)], ["g3"])
                            for pp in range(2):
                                mm(pD[:, pp * 144:pp * 144 + 144], m_kt[d_][:, pp * 128:pp * 128 + 128], m_vsf[d_][:, pp * 144:pp * 144 + 144], True, True,
                                   [("mtmp", "kt", d_), ("mtmp", "vs", d_)], ["g4"])
                            if step == 0 and d_ == 0: chk('sc')
                            ejs = v3(Ej[:, d_, :], NT)[:, ti, :]
                            tt("dve", m_hs[d_], v3(pH[:, 0:288], 4)[:, :, 0:65], bc(ejs.unsqueeze(2), [128, 4, 65]), ALU.mult, ["g3", ("gder", "Ej")], [("mtmp", "hs", d_)])
                            if step == 0 and d_ == 0: chk('sd')
                            den = m_hs[d_][:, :, 64]
                            d2 = small[:, 208:212]
                            tt("dve", d2, den, den, ALU.mult, [("mtmp", "hs", d_)], ["md2"])
                            ts("dve", d2, d2, 1.0, None, ALU.max, None, ["md2"], ["md2"])
                            tt("pool", d2, d2, neghalf[:, 0:4], ALU.pow, ["md2", "cst"], ["md2"])
                            if ti not in written:
                                tt("dve", v3(Hml[:, ti, :], 4), m_hs[d_][:, :, 0:64], bc(d2.unsqueeze(2), [128, 4, 64]), ALU.mult,
                                   [("mtmp", "hs", d_), "md2"], [("Hml", ti)])
                                written.add(ti)
                            else:
                                tt("dve", m_hn, m_hs[d_][:, :, 0:64], bc(d2.unsqueeze(2), [128, 4, 64]), ALU.mult,
                                   [("mtmp", "hs", d_), "md2"], [("mtmp", "hn")])
                                tt("pool", v3(Hml[:, ti, :], 4), v3(Hml[:, ti, :], 4), m_hn, ALU.add, [("mtmp", "hn"), ("Hml", ti)], [("Hml", ti)])
                            if step == 0 and d_ == 0: chk('se')
                            tt("dve", m_tmp, v3(pD[:, 0:288], 4)[:, :, 0:65], m_cst[:, d_, :, :], ALU.add, ["g4", ("mtmp", "cst", d_)], [("mtmp", "tmp")])
                            ees = v3(Eend[:, d_, :], NT)[:, ti, :]
                            tt("dve", m_cst[:, d_, :, :], m_tmp, bc(ees.unsqueeze(2), [128, 4, 65]), ALU.mult, [("mtmp", "tmp"), ("gder", "Eend")], [("mtmp", "cst", d_)])
                    chk('sf') if False else None
                    if dbg:
                        tap("Hml", Hml[:, :, :], [("Hml", i) for i in range(NT)])

                    chk('ph3')
                    fence("X")
                    fence("B")
                    S.op("pool", lambda h: h.memset(dvaug[:, :, :, 128:129], 1.0), [], [("dvaug", "ones")])
                    for hh in range(2):
                        slot = next_slot()
                        wload(l, C_DAK + hh * 512, 512, slot)
                        for h2 in range(2):
                            h4 = hh * 2 + h2
                            for bi, (t0, t1) in enumerate(BLKS):
                                n = t1 - t0
                                ga, gb = [(1, 2), (3, 4)][bi % 2]
                                for kc in range(8):
                                    mm(ps_g[ga][:, 0:n], wring[slot][:, kc, h2 * 256:h2 * 256 + 128], hTall[:, kc, t0:t1], kc == 0, kc == 7,
                                       [("wring", slot)] + hT_keys[t0 // 128:t1 // 128], ["g%d" % ga])
                                if bi == 0:
                                    act(dkT[:, h4, t0:t1], ps_g[ga][:, 0:n], AF.Identity, ["g%d" % ga, "fmb"], [("dkT", h4, bi)], bias=fmb[:, l, 4 + 2 * h4:5 + 2 * h4])
                                    continue
                                for kc in range(8):
                                    mm(ps_g[gb][:, 0:n], wring[slot][:, kc, h2 * 256 + 128:h2 * 256 + 256], hTall[:, kc, t0:t1], kc == 0, kc == 7,
                                       [("wring", slot)] + hT_keys[t0 // 128:t1 // 128], ["g%d" % gb])
                                p0 = t0 - 256
                                dma(ropeC[:, 0:n], rope_d[0, :, p0:p0 + n], [], [("rope4", "c")])
                                dma(ropeS[:, 0:n], rope_d[1, :, p0:p0 + n], [], [("rope4", "s")])
                                stt("dve", rstA[:, 0:n], ps_g[ga][:, 0:n], fmb[:, l, 4 + 2 * h4:5 + 2 * h4], ropeC[:, 0:n], ALU.add, ALU.mult,
                                    ["g%d" % ga, "fmb", ("rope4", "c")], [("rope4", "a")])
                                stt("dve", rstB[:, 0:n], ps_g[gb][:, 0:n], fmb[:, l, 5 + 2 * h4:6 + 2 * h4], ropeS[:, 0:n], ALU.add, ALU.mult,
                                    ["g%d" % gb, "fmb", ("rope4", "s")], [("rope4", "b")])
                                tt("pool", dkT[:, h4, t0:t1], rstA[:, 0:n], rstB[:, 0:n], ALU.add, [("rope4", "a"), ("rope4", "b")], [("dkT", h4, bi)])
                    slot = next_slot()
                    wload(l, C_DAV, 512, slot)
                    for ti in range(NT):
                        gix = next_g(3, 5)
                        pg = ps_g[gix]
                        for kc in range(8):
                            mm(pg[:, :], hTall[:, kc, ti * 128:(ti + 1) * 128], wring[slot][:, kc, :], kc == 0, kc == 7,
                               [("wring", slot), hT_keys[ti]], ["g%d" % gix])
                        tt("dve", dvaug[:, ti, :, 0:128], v3(pg[:, :], 4), v3(biasb[:, B_DAV:B_DAV + 512], 4), ALU.add,
                           ["g%d" % gix, ("biasb", B_DAV)], [("dvaug", ti)])
                    s_a = next_slot()
                    wload(l, C_SG, 512, s_a)
                    s_b = next_slot()
                    wload(l, C_SG + 512, 256, s_b)
                    for ti in range(NT):
                        ga, gb = 1, 2
                        for kc in range(8):
                            mm(ps_g[ga][:, :], hTall[:, kc, ti * 128:(ti + 1) * 128], wring[s_a][:, kc, :], kc == 0, kc == 7,
                               [("wring", s_a), hT_keys[ti]], ["g%d" % ga])
                        for kc in range(8):
                            mm(ps_g[gb][:, 0:256], hTall[:, kc, ti * 128:(ti + 1) * 128], wring[s_b][:, kc, 0:256], kc == 0, kc == 7,
                               [("wring", s_b), hT_keys[ti]], ["g%d" % gb])
                        tt("dve", sgp[:, 0:512], ps_g[ga][:, :], biasb[:, B_SG:B_SG + 512], ALU.add, ["g%d" % ga, ("biasb", B_SG)], [("sgt", "p")])
                        tt("dve", sgp[:, 512:768], ps_g[gb][:, 0:256], biasb[:, B_SG + 512:B_SG + 768], ALU.add, ["g%d" % gb, ("biasb", B_SG)], [("sgt", "pz")])
                        act(sgu, sgp[:, 0:256], AF.Gelu, [("sgt", "p")], [("sgt", "u")])
                        act(sgp[:, 256:512], sgp[:, 256:512], AF.Gelu, [("sgt", "p")], [("sgt", "gv")])
                        act(sgzs, sgp[:, 512:768], AF.Tanh, [("sgt", "pz")], [("sgt", "zs")], scale=0.5)
                        red(small[:, 240:241], sgp[:, 256:512], [("sgt", "gv")], ["sgs1"])
                        act(sgjunk, sgp[:, 256:512], AF.Square, [("sgt", "gv")], [("sgt", "junk")])
                        red(small[:, 241:242], sgjunk, [("sgt", "junk")], ["sgs2"])
                        mean, rstd = stats(small[:, 240:241], small[:, 241:242], 1, 256.0, "sg")
                        ts("dve", sgjunk, sgp[:, 256:512], mean, rstd, ALU.subtract, ALU.mult, [("sgt", "gv"), "st_mean", "st_rstd"], [("sgt", "junk")])
                        tt("dve", sgjunk, sgjunk, sggt, ALU.mult, [("sgt", "junk"), "sggt"], [("sgt", "junk")])
                        tt("dve", sgvn, sgjunk, sgbt, ALU.add, [("sgt", "junk"), "sgbt"], [("sgt", "vn")])
                        gv_ = 3
                        for g in range(4):
                            mm(ps_g[gv_][:, g * 64:(g + 1) * 64], sgW[:, l, g, :], sgvn[:, g * 64:(g + 1) * 64], True, True, ["sgW", ("sgt", "vn")], ["g%d" % gv_])
                        tt("dve", v3(sgy, 4), v3(ps_g[gv_][:, 0:256], 4), bc(sgbs[:, l, :].unsqueeze(2), [128, 4, 64]), ALU.add, ["g%d" % gv_, "sgbs"], [("sgt", "y")])
                        tt("dve", sgy, sgy, sgu, ALU.mult, [("sgt", "y"), ("sgt", "u")], [("sgt", "y")])
                        stt("dve", sgzs, sgzs, 1.0, sgp[:, 512:768], ALU.add, ALU.mult, [("sgt", "zs"), ("sgt", "pz")], [("sgt", "zs")])
                        stt("dve", ysg[:, ti, :], sgy, 0.5, sgzs, ALU.mult, ALU.mult, [("sgt", "y"), ("sgt", "zs")], [("ysg", ti)])
                    if dbg:
                        tap("dkT", dkT[:, :, :], [("dkT", h_, b_) for h_ in range(4) for b_ in range(5)], BF16)
                        tap("ysg", ysg[:, :, :], [("ysg", i) for i in range(NT)], BF16)

                    chk('ph4')
                    fence("X")
                    fence("A")
                    last = (l == NL - 1)
                    for bi, (t0, t1) in enumerate(BLKS):
                        n = t1 - t0
                        ntile = n // 128
                        isctx = bi == 0
                        if last and isctx:
                            continue
                        xb = xblk[0]
                        if l == 0:
                            srcx = ctx_d[b, :, :] if isctx else x_d[b, t0 - 256:t1 - 256, :]
                            rk = []
                        else:
                            srcx = xs_d[b, t0:t1, :]
                            rk = [("xs", b, bi, ti_) for ti_ in range((t1 - t0) // 128)]
                        srcx3 = srcx.rearrange("(n p) d -> p n d", p=128)
                        for ti in range(ntile):
                            dma(xb[:, ti, :], srcx3[:, ti, :], rk, [("xblk", 0, ti)])
                        s1p, shp = (modp[:, 3, :], modp[:, 2, :]) if isctx else (modp[:, 1, :], modp[:, 0, :])
                        ln_block(xb, ntile, lambda ti: ("xblk", 0, ti), xn5, ("p5", "xn"), lambda ti: hTblk[:, :, ti * 128:(ti + 1) * 128], lambda ti: ("p5", "hT", ti), s1p, shp, modk)
                        hk = [("p5", "hT", ti) for ti in range(ntile)]
                        if not isctx:
                            p0 = t0 - 256
                            dma(p5ropeC[:, 0:n], rope_d[0, :, p0:p0 + n], [], [("p5x", "c")])
                            dma(p5ropeS[:, 0:n], rope_d[1, :, p0:p0 + n], [], [("p5x", "s")])
                        for hh in range(2):
                            slot = next_slot()
                            wload(l, C_DAQ + hh * 512, 512, slot)
                            for h2 in range(2):
                                h4 = hh * 2 + h2
                                ga, gb = 0, 1
                                for kc in range(8):
                                    mm(ps_g[ga][:, 0:n], wring[slot][:, kc, h2 * 256:h2 * 256 + 128], hTblk[:, kc, 0:n], kc == 0, kc == 7, [("wring", slot)] + hk, ["g%d" % ga])
                                if isctx:
                                    act(qTblk[:, h4, 0:n], ps_g[ga][:, 0:n], AF.Identity, ["g%d" % ga, "fmb"], [("p5", "qT", h4)], bias=fmb[:, l, 12 + 2 * h4:13 + 2 * h4])
                                    continue
                                for kc in range(8):
                                    mm(ps_g[gb][:, 0:n], wring[slot][:, kc, h2 * 256 + 128:h2 * 256 + 256], hTblk[:, kc, 0:n], kc == 0, kc == 7, [("wring", slot)] + hk, ["g%d" % gb])
                                stt("dve", p5stA[:, 0:n], ps_g[ga][:, 0:n], fmb[:, l, 12 + 2 * h4:13 + 2 * h4], p5ropeC[:, 0:n], ALU.add, ALU.mult,
                                    ["g%d" % ga, "fmb", ("p5x", "c")], [("p5x", "a")])
                                stt("dve", p5stB[:, 0:n], ps_g[gb][:, 0:n], fmb[:, l, 13 + 2 * h4:14 + 2 * h4], p5ropeS[:, 0:n], ALU.add, ALU.mult,
                                    ["g%d" % gb, "fmb", ("p5x", "s")], [("p5x", "b")])
                                tt("pool", qTblk[:, h4, 0:n], p5stA[:, 0:n], p5stB[:, 0:n], ALU.add, [("p5x", "a"), ("p5x", "b")], [("p5", "qT", h4)])
                        for gi_, (c0, boff) in enumerate([(C_MLOZ, B_MLOZ), (C_DAZ, B_DAZ)]):
                            slot = next_slot()
                            wload(l, c0, 512, slot)
                            for ti in range(ntile):
                                gix = 2 + ti % 2
                                pg = ps_g[gix]
                                for kc in range(8):
                                    mm(pg[:, :], hTblk[:, kc, ti * 128:(ti + 1) * 128], wring[slot][:, kc, :], kc == 0, kc == 7, [("wring", slot), hk[ti]], ["g%d" % gix])
                                tt("dve", p5tmp, pg[:, :], biasb[:, boff:boff + 512], ALU.add, ["g%d" % gix, ("biasb", boff)], [("p5x", "tmp")])
                                act(p5t2, p5tmp, AF.Tanh, [("p5x", "tmp")], [("p5x", "t2")], scale=0.5)
                                gdst = gates[:, ti, gi_ * 512:(gi_ + 1) * 512]
                                if gi_ == 0:
                                    ts("dve", gdst[:, 0:256], p5t2[:, 0:256], 0.5, 0.5, ALU.mult, ALU.add, [("p5x", "t2")], [("p5", "g", ti, 0)])
                                    stt("dve", gdst[:, 256:512], p5t2[:, 256:512], 1.0, p5tmp[:, 256:512], ALU.add, ALU.mult, [("p5x", "t2"), ("p5x", "tmp")], [("p5", "g", ti, 1)])
                                else:
                                    stt("dve", gdst, p5t2, 1.0, p5tmp, ALU.add, ALU.mult, [("p5x", "t2"), ("p5x", "tmp")], [("p5", "g", ti, 2)])
                        tg0 = t0 // 128
                        mlv = v3(p5x_ml, 4)[:, 0:ntile, :]
                        sqv = v3(p5x_sq, 4)[:, 0:ntile, :]
                        kml = [("p5x", "c"), ("p5x", "s")]
                        ksq = [("p5x", "tmp"), ("p5x", "t2")]
                        k4 = ntile * 4
                        tt("dve", mlv, gates[:, 0:ntile, 0:256], Hml[:, tg0:tg0 + ntile, :], ALU.mult,
                           [("p5", "g", ti, 0) for ti in range(ntile)] + [("Hml", tg0 + ti) for ti in range(ntile)], kml)
                        red(small[:, 64:64 + k4], mlv.rearrange("p a (h d) -> p (a h) d", h=4), kml, ["mls1"])
                        tt("dve", sqv, mlv, mlv, ALU.mult, kml, ksq)
                        red(small[:, 80:80 + k4], sqv.rearrange("p a (h d) -> p (a h) d", h=4), ksq, ["mls2"])
                        mean, rstd = stats(small[:, 64:64 + k4], small[:, 80:80 + k4], k4, 64.0, "ml")
                        ml3 = mlv.rearrange("p a (h d) -> p (a h) d", h=4)
                        tt("dve", ml3, ml3, bc(mean.unsqueeze(2), [128, k4, 64]), ALU.subtract, kml + ["st_mean"], kml)
                        tt("dve", ml3, ml3, bc(rstd.unsqueeze(2), [128, k4, 64]), ALU.mult, kml + ["st_rstd"], kml)
                        tt("dve", mlv, mlv, bc(mlgc.unsqueeze(1), [128, ntile, 256]), ALU.mult, kml + ["mlgc"], kml)
                        gml = gates[:, 0:ntile, 0:256]
                        tt("dve", gml, mlv, gates[:, 0:ntile, 256:512], ALU.mult, kml + [("p5", "g", ti, 1) for ti in range(ntile)] + [("p5", "g", ti, 0) for ti in range(ntile)], [("p5", "mixml")])
                        ktiles = list(range(0, 2)) if isctx else list(range(NT))
                        STB = [(ps_g[4], "g4"), (ps_tp[:, 0:512], "tpa"), (ps_tp[:, 512:1024], "tpb")]
                        nk = len(ktiles)
                        LOOK = 2

                        def combine(h4):
                            nt_ = ntile
                            ok_ = [("p5", "oev", qi, c) for qi in range(nt_) for c in range(2)]
                            rr = v3(small[:, 192:200], 4)[:, 0:nt_, :]
                            attv = v3(p5stA, 4)[:, 0:nt_, :]
                            tmpv = v3(p5stB, 4)[:, 0:nt_, :]
                            ssv = small[:, 200:200 + nt_]
                            S.op("dve", lambda h, rr=rr, nt_=nt_: h.reciprocal(out=rr, in_=oev[:, 0:nt_, :, 128]), ok_, ["p5rr"])
                            ts("dve", rr[:, :, 1], rr[:, :, 1], neglam[:, l:l + 1], None, ALU.mult, None, ["p5rr", "neglam"], ["p5rr"])
                            tt("dve", attv, oev[:, 0:nt_, 0, 0:128], bc(rr[:, :, 0:1], [128, nt_, 128]), ALU.mult, ok_ + ["p5rr"], [("p5x", "a")])
                            tt("dve", tmpv, oev[:, 0:nt_, 1, 0:128], bc(rr[:, :, 1:2], [128, nt_, 128]), ALU.mult, ok_ + ["p5rr"], [("p5x", "b")])
                            tt("pool", attv, attv, tmpv, ALU.add, [("p5x", "a"), ("p5x", "b")], [("p5x", "a")])
                            tt("dve", tmpv, attv, attv, ALU.mult, [("p5x", "a")], [("p5x", "b")])
                            red(ssv, tmpv, [("p5x", "b")], ["p5ss"])
                            ts("dve", ssv, ssv, 1.0 / 128.0, EPS, ALU.mult, ALU.add, ["p5ss"], ["p5ss"])
                            tt("pool", ssv, ssv, neghalf[:, 0:nt_], ALU.pow, ["p5ss", "cst"], ["p5ss"])
                            tt("dve", attv, attv, bc(ssv.unsqueeze(2), [128, nt_, 128]), ALU.mult, [("p5x", "a"), "p5ss"], [("p5x", "a")])
                            tt("dve", attv, attv, bc(dagc.unsqueeze(1), [128, nt_, 128]), ALU.mult, [("p5x", "a"), "dagc"], [("p5x", "a")])
                            gsl = gates[:, 0:nt_, 512 + h4 * 128:512 + (h4 + 1) * 128]
                            tt("dve", gsl, attv, gsl, ALU.mult, [("p5x", "a")] + [("p5", "g", qi, 2) for qi in range(nt_)], [("p5", "damix", h4)])
                        tg0 = t0 // 128

                        items = [(h4, c, kti) for h4 in range(4) for c in range(2) for kti in range(nk)]
                        for g in range(len(items) + LOOK):
                            if g < len(items):
                                h4, c, kti = items[g]
                                kt = ktiles[kti]
                                pr = slice(c * 64, c * 64 + 64)
                                pS, gk = STB[g % 3]
                                mm(pS[:, 0:n], dkT[pr, h4, kt * 128:(kt + 1) * 128], qTblk[pr, h4, 0:n], True, True,
                                   [("dkT", h4, 0 if kt < 2 else 1 + (kt - 2) // 4), ("p5", "qT", h4)], [gk])
                                act(ptb[g % 4][:, 0:n], pS[:, 0:n], AF.Exp, [gk], [("p5", "pt", g % 4)], scale=0.125)
                            j = g - LOOK
                            if j >= 0:
                                h4, c, kti = items[j]
                                kt = ktiles[kti]
                                for qi in range(ntile):
                                    mm(ps_g[qi][:, 0:129], ptb[j % 4][:, qi * 128:(qi + 1) * 128], dvaug[:, kt, h4, :], kti == 0, kti == nk - 1,
                                       [("p5", "pt", j % 4), ("dvaug", kt), ("dvaug", "ones")], ["g%d" % qi])
                                if kti == nk - 1:
                                    for qi in range(ntile):
                                        cp("act" if qi % 2 else "dve", oev[:, qi, c, :], ps_g[qi][:, 0:129], ["g%d" % qi], [("p5", "oev", qi, c)])
                                    if c == 1:
                                        combine(h4)
                        for ti in range(ntile):
                            tg = tg0 + ti
                            for kc in range(8):
                                if kc < 2:
                                    src_, rk_ = gates[:, ti, kc * 128:(kc + 1) * 128], [("p5", "mixml")]
                                elif kc < 6:
                                    src_, rk_ = gates[:, ti, 512 + (kc - 2) * 128:512 + (kc - 1) * 128], [("p5", "damix", kc - 2)]
                                else:
                                    src_, rk_ = ysg[:, tg, (kc - 6) * 128:(kc - 5) * 128], [("ysg", tg)]
                                tr(ps_tb[:, kc * 128:(kc + 1) * 128], src_, identB, rk_ + ["cstb"], ["tb"])
                            cp("act", mixT[:, :, ti * 128:(ti + 1) * 128], v3(ps_tb[:, :], 8), ["tb"], [("p5", "mixT", ti)])
                        if dbg:
                            tap("mix%d" % bi, mixT[:, :, 0:n], [("p5", "mixT", ti) for ti in range(ntile)], BF16)
                        gateb = gate_c if isctx else gate_l
                        gkey = "gate_c" if isctx else "gate_l"
                        for hf in range(2):
                            slot = next_slot()
                            wload(l, hf * 512, 512, slot, src=wobf_d)
                            for ti in range(ntile):
                                gix = ti % 2
                                pg = ps_g[gix]
                                for kc in range(8):
                                    mm(pg[:, :], mixT[:, kc, ti * 128:(ti + 1) * 128], wring[slot][:, kc, :], kc == 0, kc == 7, [("wring", slot), ("p5", "mixT", ti)], ["g%d" % gix])
                                tt("dve", p5tmp, pg[:, :], gateb[:, hf * 512:(hf + 1) * 512], ALU.mult, ["g%d" % gix, gkey], [("p5x", "tmp")])
                                stt("dve", xb[:, ti, hf * 512:(hf + 1) * 512], xb[:, ti, hf * 512:(hf + 1) * 512], ALPHA, p5tmp, ALU.mult, ALU.add,
                                    [("xblk", 0, ti), ("p5x", "tmp")], [("xblk", 0, ti)])
                        s1 = small[:, 224:224 + ntile]
                        s2 = small[:, 232:232 + ntile]
                        k1 = [("lnst", 1, ti) for ti in range(ntile)]
                        k2 = [("lnst", 2, ti) for ti in range(ntile)]
                        for ti in range(ntile):
                            act(xn5, xb[:, ti, :], AF.Square, [("xblk", 0, ti)], [("p5", "xn"), k2[ti]], accum=s2[:, ti:ti + 1])
                            act(xn5, xb[:, ti, :], AF.Identity, [("xblk", 0, ti)], [("p5", "xn"), k1[ti]], accum=s1[:, ti:ti + 1])
                        mean, rstd = stats(s1, s2, ntile, 1024.0, (k1, k2))
                        if last:
                            dst3 = y_d[b, t0 - 256:t1 - 256, :].rearrange("(n p) d -> p n d", p=128)
                        else:
                            dst3 = xs_d[b, t0:t1, :].rearrange("(n p) d -> p n d", p=128)
                        for ti in range(ntile):
                            xk_ = ("xblk", 0, ti)
                            ts("dve", xb[:, ti, :], xb[:, ti, :], mean[:, ti:ti + 1], rstd[:, ti:ti + 1], ALU.subtract, ALU.mult, [xk_, "st_mean", "st_rstd"], [xk_])
                            tt("dve", xb[:, ti, :], xb[:, ti, :], lng, ALU.mult, [xk_, "lng"], [xk_])
                            tt("pool" if ti % 2 else "dve", xb[:, ti, :], xb[:, ti, :], lnb, ALU.add, [xk_, "lnb"], [xk_])
                            dma(dst3[:, ti, :], xb[:, ti, :], [xk_], [("y", b, bi, ti) if last else ("xs", b, bi, ti)])

        except _Stop:
            pass
        S.emit(nc, st)
    return nc, S, dbg_out


def make_consts():
    i = np.arange(128)
    ident = np.eye(128, dtype=np.float32)
    triF = (i[:, None] <= i[None, :]).astype(np.float32)
    triB = (i[:, None] >= i[None, :]).astype(np.float32)
    ones = np.ones((128, 128), np.float32)
    partner = np.where((i % 32) < 16, i + 16, i - 16)
    rperm = np.zeros((128, 128), np.float32)
    rperm[partner, i] = 1.0
    cst = np.concatenate([ident, triF, triB, ones, rperm], 1)
    t = np.arange(LAT)
    row = (t // 64).astype(np.float32)
    col = (t % 64).astype(np.float32)
    half = 32
    inv = (10000.0 ** (-(np.arange(0, half, 2, dtype=np.float32)) / half)).astype(np.float32)
    ang_r = row[:, None] * inv
    ang_c = col[:, None] * inv
    ang = np.concatenate([ang_r, ang_r, ang_c, ang_c], -1).astype(np.float32)
    cos = np.cos(ang).astype(np.float32)
    sin = np.sin(ang).astype(np.float32)
    d = np.arange(64)
    sign = np.where((d % 32) < 16, -1.0, 1.0).astype(np.float32)
    sinS = sin * sign[None, :]
    cosT = np.concatenate([cos.T, cos.T], 0)
    sinT = np.concatenate([sinS.T, sinS.T], 0)
    rope = np.ascontiguousarray(np.stack([cosT, sinT], 0)).astype(np.float32)
    return np.ascontiguousarray(cst), rope


_CACHE = {}


def kernel(**inputs):
    NCORES = 8
    NB = 32 // NCORES
    key = (NB, DEPTH)
    if key not in _CACHE:
        _CACHE[key] = build(NB, DEPTH)
    nc = _CACHE[key][0]
    cst, rope = make_consts()
    f = lambda a: np.ascontiguousarray(np.asarray(a, dtype=np.float32))
    shared = {k: f(inputs[k]) for k in ["w_mod", "b_mod", "w_in", "b_in", "ml_conv_w", "ml_conv_b", "ml_norm_g",
                                        "da_lam_q1", "da_lam_k1", "da_lam_q2", "da_lam_k2", "da_norm_g", "sg_norm_g",
                                        "sg_norm_b", "sg_w_s", "sg_b_s", "w_out", "ln_g", "ln_b"]}
    shared["c_ctx"] = f(inputs["c_ctx"]).reshape(1, D)
    shared["cst"] = cst
    shared["rope"] = rope
    x = f(inputs["x"])
    ctx = f(inputs["ctx"])
    c = f(inputs["c"])
    in_maps = []
    for i in range(NCORES):
        m = dict(shared)
        m["x"] = x[i * NB:(i + 1) * NB]
        m["ctx"] = ctx[i * NB:(i + 1) * NB]
        m["c"] = c[i * NB:(i + 1) * NB]
        in_maps.append(m)
    res = run_bass_kernel_spmd(nc, in_maps, core_ids=list(range(NCORES)))
    return np.concatenate([r["y"] for r in res.results], axis=0).astype(np.float32)
```

```python
import math
import numpy as np
from contextlib import ExitStack
import concourse.bass as bass
import concourse.mybir as mybir
from concourse.bass_utils import run_bass_kernel_spmd

F32 = mybir.dt.float32
BF16 = mybir.dt.bfloat16
AF = mybir.ActivationFunctionType
ALU = mybir.AluOpType
AX = mybir.AxisListType

D = 1024
CTXL = 256
LAT = 2048
T = CTXL + LAT
NT = T // 128
DEPTH = 4
EPS = 1e-5
ALPHA = (2 * DEPTH) ** 0.25
NWC = 5136
BLKS = [(0, 256), (256, 768), (768, 1280), (1280, 1792), (1792, 2304)]
NDMA = 24
PSUM_KEYS = frozenset(["g0", "g1", "g2", "g3", "g4", "tpa", "tpb", "tb"])

C_MLQK, C_MLV, C_DAK, C_DAV, C_SG, C_DAQ, C_MLOZ, C_DAZ = 0, 512, 784, 1808, 2320, 3088, 4112, 4624
B_MLV, B_DAV, B_SG, B_MLOZ, B_DAZ, NBIAS = 0, 272, 784, 1552, 2064, 2576


class Sched:
    def __init__(self):
        self.ins = []
        self.lastw = {}
        self.readers = {}
        self.region = {}
        self.rlast = {}
        self.rdma = {}

    def _add(self, eng, fn, reads, writes, is_dma, extra=()):
        reads = list(reads)
        writes = list(writes)
        for k in reads:
            if k in PSUM_KEYS:
                writes.append(("rdser", k))
        deps = set(extra)
        touched = set()
        for k in reads + writes:
            nm = k[0] if isinstance(k, tuple) else k
            r = self.region.get(nm)
            if r is not None:
                touched.add(r)
        for r in touched:
            w = self.lastw.get(("R", r))
            if w is not None:
                deps.add(w)
        for k in reads:
            w = self.lastw.get(k)
            if w is not None:
                deps.add(w)
        for k in writes:
            w = self.lastw.get(k)
            if w is not None:
                deps.add(w)
            rd = self.readers.get(k)
            if rd:
                deps.update(rd[0].values())
                deps.update(rd[1])
        idx = len(self.ins)
        self.ins.append([eng, fn, deps, is_dma])
        for k in writes:
            self.lastw[k] = idx
            self.readers[k] = [{}, []]
        ws = set(writes)
        for k in reads:
            if k not in ws:
                rd = self.readers.setdefault(k, [{}, []])
                if is_dma:
                    rd[1].append(idx)
                else:
                    rd[0][eng] = idx
        for r in touched:
            if is_dma:
                self.rdma.setdefault(r, []).append(idx)
            else:
                self.rlast.setdefault(r, {})[eng] = idx
        return idx

    def fence(self, region, eng, fn):
        deps = set(self.rlast.get(region, {}).values()) | set(self.rdma.get(region, []))
        idx = self._add(eng, fn, (), (), False, extra=deps)
        self.lastw[("R", region)] = idx
        self.rlast[region] = {}
        self.rdma[region] = []
        return idx

    def op(self, eng, fn, reads=(), writes=()):
        return self._add(eng, fn, reads, writes, False)

    def dma(self, eng, fn, reads=(), writes=()):
        return self._add(eng, fn, reads, writes, True)

    def emit(self, nc, stack):
        ins = self.ins
        n = len(ins)
        engs = ["pe", "act", "dve", "pool", "sp"]
        dma_list = [i for i in range(n) if ins[i][3]]
        dma_slot = {}
        for j, i in enumerate(dma_list):
            dma_slot[i] = j
            if j >= NDMA:
                ins[i][2].add(dma_list[j - NDMA])
        needed = [False] * n
        for i in range(n):
            e = ins[i][0]
            nd = set()
            for d in ins[i][2]:
                if ins[d][0] == e and e == "pe" and not ins[d][3]:
                    continue
                nd.add(d)
                needed[d] = True
            ins[i][2] = nd
        esem = {e: stack.enter_context(nc.semaphore("s_" + e)) for e in engs}
        dsem = [stack.enter_context(nc.semaphore("d_%d" % j)) for j in range(NDMA)]
        cnt = {e: 0 for e in engs}
        tok = [None] * n
        for i in range(n):
            e, fn, deps, is_dma = ins[i]
            if is_dma:
                j = dma_slot[i]
                tok[i] = (("d", j % NDMA), 16 * (j // NDMA + 1))
            elif needed[i]:
                cnt[e] += 1
                tok[i] = (("e", e), cnt[e])
        per = {e: [] for e in engs}
        for i in range(n):
            per[ins[i][0]].append(i)
        self.counts = {e: len(per[e]) for e in engs}

        def semof(key):
            return esem[key[1]] if key[0] == "e" else dsem[key[1]]

        def run(e, h):
            seen = {}
            for i in per[e]:
                _, fn, deps, is_dma = ins[i]
                want = {}
                for d in deps:
                    k, v = tok[d]
                    if v > want.get(k, 0):
                        want[k] = v
                for k, v in want.items():
                    if seen.get(k, 0) < v:
                        h.wait_ge(semof(k), v)
                        seen[k] = v
                r = fn(h)
                if tok[i] is not None:
                    k, v = tok[i]
                    r.then_inc(semof(k), 16 if is_dma else 1)
            for i in per[e]:
                if ins[i][3]:
                    k, v = tok[i]
                    if seen.get(k, 0) < v:
                        h.wait_ge(semof(k), v)
                        seen[k] = v

        with nc.Block() as block:
            @block.tensor
            def _(h):
                run("pe", h)

            @block.scalar
            def _(h):
                run("act", h)

            @block.vector
            def _(h):
                run("dve", h)

            @block.gpsimd
            def _(h):
                run("pool", h)

            @block.sync
            def _(h):
                run("sp", h)


class Arena:
    def __init__(self, ap):
        self.ap = ap
        self.off = 0
        self.size = ap.shape[1]

    def f32(self, n):
        a = self.ap[:, self.off:self.off + n]
        self.off += (n + 7) // 8 * 8
        assert self.off <= self.size, (self.off, self.size)
        return a

    def bf16(self, n):
        w = (n + 1) // 2
        a = self.ap[:, self.off:self.off + w].bitcast(BF16)
        self.off += (w + 7) // 8 * 8
        assert self.off <= self.size, (self.off, self.size)
        return a[:, 0:n]


def v3(ap, a):
    return ap.rearrange("p (a b) -> p a b", a=a)


def v4(ap, a, b):
    return ap.rearrange("p (a b c) -> p a b c", a=a, b=b)


def bc(ap, shape):
    return ap.to_broadcast(shape)


class _Stop(Exception):
    pass


def build(NB, NL, dbg=None, stop=None):
    nc = bass.Bass("TRN2", target_bir_lowering=False)
    S = Sched()

    def dram(name, shape, dtype=F32, kind="ExternalInput"):
        return nc.dram_tensor(name, shape, dtype, kind=kind).ap()

    x_d = dram("x", [NB, LAT, D])
    ctx_d = dram("ctx", [NB, CTXL, D])
    c_d = dram("c", [NB, D])
    cctx_d = dram("c_ctx", [1, D])
    wmod_d = dram("w_mod", [DEPTH, D, 3 * D])
    bmod_d = dram("b_mod", [DEPTH, 3 * D])
    win_d = dram("w_in", [DEPTH, D, 4112])
    bin_d = dram("b_in", [DEPTH, 4112])
    convw_d = dram("ml_conv_w", [DEPTH, 3, 512])
    convb_d = dram("ml_conv_b", [DEPTH, 512])
    mlg_d = dram("ml_norm_g", [DEPTH, 256])
    lq1_d = dram("da_lam_q1", [DEPTH, 64])
    lk1_d = dram("da_lam_k1", [DEPTH, 64])
    lq2_d = dram("da_lam_q2", [DEPTH, 64])
    lk2_d = dram("da_lam_k2", [DEPTH, 64])
    dag_d = dram("da_norm_g", [DEPTH, 128])
    sgg_d = dram("sg_norm_g", [DEPTH, 256])
    sgb_d = dram("sg_norm_b", [DEPTH, 256])
    sgw_d = dram("sg_w_s", [DEPTH, 4, 128, 128])
    sgbs_d = dram("sg_b_s", [DEPTH, 4, 128])
    wout_d = dram("w_out", [DEPTH, D, D])
    lng_d = dram("ln_g", [DEPTH, D])
    lnb_d = dram("ln_b", [DEPTH, D])
    cst_d = dram("cst", [128, 640])
    rope_d = dram("rope", [2, 128, LAT])
    y_d = dram("y", [NB, LAT, D], kind="ExternalOutput")
    wbf_d = dram("wbf", [DEPTH, D, NWC], BF16, kind="Internal")
    wobf_d = dram("wobf", [DEPTH, D, D], BF16, kind="Internal")
    mod_d = dram("modscr", [DEPTH, 8, 3 * D], F32, kind="Internal")
    xs_d = dram("xs", [NB, T, D], F32, kind="Internal")
    dbg_out = {}

    with ExitStack() as st:
        arena_t = st.enter_context(nc.sbuf_tensor("arena", [128, 51200], F32))
        AR = Arena(arena_t[:, :])
        ps_tp = st.enter_context(nc.psum_tensor("ps_tp", [128, 1024], F32))
        ps_tb = st.enter_context(nc.psum_tensor("ps_tb", [128, 1024], BF16))
        ps_g = [st.enter_context(nc.psum_tensor("ps_g%d" % i, [128, 512], F32)) for i in range(5)]

        cstF = AR.f32(640)
        identF, triF, triB, onesF, rperm = (cstF[:, i * 128:(i + 1) * 128] for i in range(5))
        identB = AR.bf16(128)
        maskF = AR.bf16(128)
        maskB = AR.bf16(128)
        neghalf = AR.f32(64)
        dummy = AR.f32(8)
        sgW = v4(AR.bf16(DEPTH * 4 * 128), DEPTH, 4)
        sgbs = v3(AR.f32(DEPTH * 4), DEPTH)
        fmb = v3(AR.f32(DEPTH * 20), DEPTH)
        convw = v4(AR.f32(DEPTH * 12), DEPTH, 3)
        convb = v3(AR.f32(DEPTH * 4), DEPTH)
        hconvb = v3(AR.f32(DEPTH * 4), DEPTH)
        lamv = AR.f32(DEPTH)
        neglam = AR.f32(DEPTH)
        biasb = AR.f32(NBIAS)
        lng = AR.f32(D)
        lnb = AR.f32(D)
        mlgc = AR.f32(256)
        dagc = AR.f32(128)
        sggt = AR.f32(256)
        sgbt = AR.f32(256)
        gate_l = AR.f32(D)
        gate_c = AR.f32(D)
        modp = v3(AR.f32(32), 4)
        WR_OFF = AR.off
        wring = [v3(AR.bf16(8 * 512), 8) for _ in range(2)]
        wq32 = v3(arena_t[:, WR_OFF:WR_OFF + 4096], 8)
        wg32 = v3(AR.f32(128), 8)
        Hml = v3(AR.f32(NT * 256), NT)
        small = AR.f32(256)
        XOFF = AR.off
        XSZ = 8192
        AR.off += XSZ
        BOFF = AR.off
        BSZ = 12900
        AR.off += BSZ
        AOFF = AR.off
        ASZ = 51200 - AOFF
        assert ASZ >= 9216, ASZ

        def sub(off, size):
            return Arena(arena_t[:, off:off + size])

        for nm in ["X", "B", "A"]:
            pass

        aX = sub(XOFF, XSZ)
        xblk = [v3(aX.f32(4096), 4), v3(aX.f32(4096), 4)]
        aX = sub(XOFF, XSZ)
        pst = aX.f32(T)
        cacc = aX.f32(T)
        aX = sub(XOFF, XSZ)
        m_pt = [v3(aX.f32(512), 4) for _ in range(2)]
        m_vsf = [aX.f32(288) for _ in range(2)]
        m_vs = [v3(m_vsf[i], 4)[:, :, 0:65] for i in range(2)]
        m_kt = [aX.f32(256) for _ in range(2)]
        m_hs = [v3(aX.f32(288), 4)[:, :, 0:65] for _ in range(2)]
        m_cst_flat = aX.f32(576)
        m_cst = v4(m_cst_flat, 2, 4)[:, :, :, 0:65]
        m_tmp = v3(aX.f32(288), 4)[:, :, 0:65]
        m_hn = v3(aX.f32(256), 4)
        aX = sub(XOFF, XSZ)
        ropeC = aX.f32(512)
        ropeS = aX.f32(512)
        rstA = aX.f32(512)
        rstB = aX.f32(512)
        sgp = aX.f32(768)
        sgp2 = v3(aX.f32(2 * 768), 2)
        sgu2 = v3(aX.f32(512), 2)
        sgjunk2 = v3(aX.f32(512), 2)
        sgvn2 = v3(aX.bf16(512), 2)
        sgy2 = v3(aX.f32(512), 2)
        sgzs2 = v3(aX.f32(512), 2)
        aB = sub(BOFF, BSZ)
        qkT = v3(aB.f32(4 * T), 4)
        vaug = v4(aB.bf16(NT * 4 * 72), NT, 4)[:, :, :, 0:65]
        Gt = v3(aB.f32(NT * 16), NT)
        LFp = v3(aB.f32(144), 2)
        Ej = v3(aB.f32(144), 2)
        Ws = v3(aB.f32(144), 2)
        Eend = v3(aB.f32(144), 2)
        gtmp = v3(aB.f32(144), 2)
        aB = sub(BOFF, BSZ)
        dkT = v3(aB.bf16(4 * T), 4)
        dvaug = v4(aB.bf16(NT * 4 * 136), NT, 4)[:, :, :, 0:129]
        ysg = v3(aB.bf16(NT * 256), NT)
        aA = sub(AOFF, ASZ)
        hTall = v3(aA.bf16(8 * T), 8)
        xn_a = aA.f32(0) if False else None
        aA = sub(AOFF, ASZ)
        hTblk = v3(aA.bf16(8 * 512), 8)
        qTblk = v3(aA.bf16(4 * 512), 4)
        gates = v3(aA.bf16(4 * 1024), 4)
        mixT = v3(aA.bf16(8 * 512), 8)
        ptb = [aA.bf16(512) for _ in range(4)]
        oev = v4(aA.f32(4 * 2 * 132), 4, 2)[:, :, :, 0:129]
        att = aA.f32(128)
        xn5 = aA.f32(1024)
        aX5 = sub(XOFF + 4096, 4096)
        p5ropeC = aX5.f32(512)
        p5ropeS = aX5.f32(512)
        p5stA = aX5.f32(512)
        p5stB = aX5.f32(512)
        p5tmp = aX5.f32(512)
        p5t2 = aX5.f32(512)
        p5ml = aX5.f32(256)
        p5x_ml = arena_t[:, XOFF + 4096:XOFF + 4096 + 1024]
        p5x_sq = arena_t[:, XOFF + 4096 + 2048:XOFF + 4096 + 3072]
        p5sq = aX5.f32(256)

        for nm in ["xblk", "pst", "cacc", "mtmp", "sgt", "rope4", "p5x"]:
            S.region[nm] = "X"
        for nm in ["qkT", "qkpre", "vaug", "ktok", "Gt", "gder", "dkT", "dvaug", "ysg"]:
            S.region[nm] = "B"
        for nm in ["hTall", "p5"]:
            S.region[nm] = "A"

        def fence(region):
            S.fence(region, "pool", lambda h: h.memset(dummy[:, 0:1], 0.0))

        def mm(out, lhsT, rhs, start, stop, reads, writes):
            S.op("pe", lambda h: h.matmul(out, lhsT=lhsT, rhs=rhs, start=start, stop=stop), reads, writes)

        def tr(out, in_, ident, reads, writes):
            S.op("pe", lambda h: h.transpose(out, in_, ident), reads, writes)

        def act(out, in_, func, reads, writes, bias=None, scale=None, accum=None):
            kw = {}
            if accum is not None:
                kw["accum_out"] = accum
            if bias is not None:
                kw["bias"] = bias
            if scale is not None:
                kw["scale"] = scale
            S.op("act", lambda h: h.activation(out=out, in_=in_, func=func, **kw), reads, writes)

        def tt(eng, out, in0, in1, op, reads, writes):
            S.op(eng, lambda h: h.tensor_tensor(out=out, in0=in0, in1=in1, op=op), reads, writes)

        def ts(eng, out, in0, s1, s2, op0, op1, reads, writes):
            if s2 is None:
                S.op(eng, lambda h: h.tensor_scalar(out=out, in0=in0, scalar1=s1, scalar2=None, op0=op0), reads, writes)
            else:
                S.op(eng, lambda h: h.tensor_scalar(out=out, in0=in0, scalar1=s1, scalar2=s2, op0=op0, op1=op1), reads, writes)

        def stt(eng, out, in0, scalar, in1, op0, op1, reads, writes):
            S.op(eng, lambda h: h.scalar_tensor_tensor(out=out, in0=in0, scalar=scalar, in1=in1, op0=op0, op1=op1), reads, writes)

        def cp(eng, out, in_, reads, writes):
            if eng == "act":
                S.op(eng, lambda h: h.activation(out=out, in_=in_, func=AF.Copy), reads, writes)
            else:
                S.op(eng, lambda h: h.tensor_copy(out=out, in_=in_), reads, writes)

        def red(out, in_, reads, writes):
            S.op("dve", lambda h: h.tensor_reduce(out=out, in_=in_, axis=AX.X, op=ALU.add), reads, writes)

        def dma(out, in_, reads, writes, slow=False):
            if slow:
                S.dma("sp", lambda h: h.dma_start(out=out, in_=in_, allow_slow_non_contiguous=True), reads, writes)
            else:
                S.dma("sp", lambda h: h.dma_start(out=out, in_=in_), reads, writes)

        def tap(name, ap, key, dtype=F32):
            if dbg is None or name not in dbg:
                return
            shp = list(ap.shape)
            d = nc.dram_tensor("dbg_" + name, shp, dtype, kind="ExternalOutput").ap()
            dbg_out[name] = shp
            dma(d, ap, list(key) if isinstance(key, list) else [key], ["dbg_" + name])

        def stats(s1, s2, k, n, tag):
            rk1 = tag[0] if isinstance(tag, tuple) else [tag + "s1"]
            rk2 = tag[1] if isinstance(tag, tuple) else [tag + "s2"]
            mean = small[:, 0:k]
            msq = small[:, 16:16 + k]
            var = small[:, 32:32 + k]
            rstd = small[:, 48:48 + k]
            ts("dve", mean, s1, 1.0 / n, None, ALU.mult, None, rk1, ["st_mean"])
            tt("dve", msq, mean, mean, ALU.mult, ["st_mean"], ["st_msq"])
            stt("dve", var, s2, 1.0 / n, msq, ALU.mult, ALU.subtract, rk2 + ["st_msq"], ["st_var"])
            ts("dve", var, var, EPS, None, ALU.add, None, ["st_var"], ["st_var"])
            tt("pool", rstd, var, neghalf[:, 0:k], ALU.pow, ["st_var", "cst"], ["st_rstd"])
            return mean, rstd

        def chk(name):
            if stop == name:
                raise _Stop()

        try:
            dma(cstF, cst_d[:, :], [], ["cst"])
            cp("dve", identB, identF, ["cst"], ["cstb"])
            cp("dve", maskF, triF, ["cst"], ["cstb"])
            cp("dve", maskB, triB, ["cst"], ["cstb"])
            S.op("pool", lambda h: h.memset(neghalf, -0.5), [], ["cst"])

            S.op("pool", lambda h: h.memset(fmb[:, :, :], 0.0), [], ["fmb"])
            for l in range(DEPTH):
                for j in range(4):
                    dma(fmb[:, l, j:j + 1], bin_d[l, j * 128:(j + 1) * 128].rearrange("(p o) -> p o", o=1), [], ["fmb"])
                for h4 in range(4):
                    dma(fmb[:, l, 4 + 2 * h4:5 + 2 * h4], bin_d[l, 1808 + h4 * 128:1808 + (h4 + 1) * 128].rearrange("(p o) -> p o", o=1), [], ["fmb"])
                    dma(fmb[:, l, 12 + 2 * h4:13 + 2 * h4], bin_d[l, 1296 + h4 * 128:1296 + (h4 + 1) * 128].rearrange("(p o) -> p o", o=1), [], ["fmb"])
                for j in range(3):
                    dma(convw[:, l, j, :], convw_d[l, j, :].rearrange("(c p) -> p c", p=128), [], ["convp"], slow=True)
                dma(convb[:, l, :], convb_d[l, :].rearrange("(c p) -> p c", p=128), [], ["convp"], slow=True)
                dma(sgbs[:, l, :], sgbs_d[l, :, :].rearrange("g p -> p g"), [], ["sgbs"], slow=True)
            ts("dve", hconvb[:, :, :], convb[:, :, :], 0.5, None, ALU.mult, None, ["convp"], ["hconvb"])
            for l in range(DEPTH):
                pso = ps_g[0][:, 0:16]
                mm(pso, rperm, fmb[:, l, 4:20], True, True, ["cst", "fmb"], ["g0"])
                src = v3(pso, 8)[:, :, 0:1]
                dst = v3(fmb[:, l, 4:20], 8)[:, :, 1:2]
                cp("dve", dst, src, ["g0"], ["fmb"])
            for l in range(DEPTH):
                for g in range(4):
                    stg = small[:, 64:192]
                    dma(stg, sgw_d[l, g, :, :], [], ["sgstg"])
                    tr(ps_g[1][:, 0:128], stg, identF, ["sgstg", "cst"], ["g1"])
                    cp("dve", sgW[:, l, g, :], ps_g[1][:, 0:128], ["g1"], ["sgW"])
            lt = rstA
            fence("X")
            for i, (a_d, b_d) in enumerate([(lq1_d, lk1_d), (lq2_d, lk2_d)]):
                dma(lt[:, 0:256], a_d.rearrange("l k -> (l k)").partition_broadcast(128), [], [("rope4", "a")])
                dma(lt[:, 256:512], b_d.rearrange("l k -> (l k)").partition_broadcast(128), [], [("rope4", "b")])
                tt("dve", lt[:, 0:256], lt[:, 0:256], lt[:, 256:512], ALU.mult, [("rope4", "a"), ("rope4", "b")], [("rope4", "a")])
                red(small[:, 200 + 4 * i:204 + 4 * i], v3(lt[:, 0:256], 4), [("rope4", "a")], ["lam%d" % i])
                act(small[:, 200 + 4 * i:204 + 4 * i], small[:, 200 + 4 * i:204 + 4 * i], AF.Exp, ["lam%d" % i], ["lam%d" % i])
            tt("dve", lamv, small[:, 200:204], small[:, 204:208], ALU.subtract, ["lam0", "lam1"], ["lamv"])
            for l in range(DEPTH):
                lam_init = 0.8 - 0.6 * math.exp(-0.3 * l)
                ts("dve", lamv[:, l:l + 1], lamv[:, l:l + 1], lam_init, None, ALU.add, None, ["lamv"], ["lamv"])
            ts("dve", neglam, lamv, -1.0, None, ALU.mult, None, ["lamv"], ["neglam"])

            chk('consts')
            groups = [
                (C_MLQK, [(0, 0, 512, False)]),
                (C_MLV, [(0, 512, 256, False), (256, 1280, 16, False)]),
                (C_DAK, [(0, 1808, 128, False), (128, 1808, 128, True), (256, 1936, 128, False), (384, 1936, 128, True)]),
                (C_DAK + 512, [(0, 2064, 128, False), (128, 2064, 128, True), (256, 2192, 128, False), (384, 2192, 128, True)]),
                (C_DAV, [(0, 2320, 512, False)]),
                (C_SG, [(0, 3344, 512, False)]),
                (C_SG + 512, [(0, 3856, 256, False)]),
                (C_DAQ, [(0, 1296, 128, False), (128, 1296, 128, True), (256, 1424, 128, False), (384, 1424, 128, True)]),
                (C_DAQ + 512, [(0, 1552, 128, False), (128, 1552, 128, True), (256, 1680, 128, False), (384, 1680, 128, True)]),
                (C_MLOZ, [(0, 768, 512, False)]),
                (C_DAZ, [(0, 2832, 512, False)]),
            ]
            fence("X")
            fence("A")
            stg32 = [v3(sub(XOFF, XSZ).f32(4096), 8), v3(sub(XOFF + 4096, 4096).f32(4096), 8)]
            aA = sub(AOFF, ASZ)
            stg16 = [v3(aA.bf16(4096), 8), v3(aA.bf16(4096), 8)]
            S.region["stg32"] = "X"
            S.region["stg16"] = "A"
            gi = 0
            prev_store = [[], []]
            ceng = ["dve", "pool", "act"]

            def castcp(i, out, in_, reads, writes):
                e = ceng[i % 3]
                if e == "act":
                    act(out, in_, AF.Copy, reads, writes)
                else:
                    cp(e, out, in_, reads, writes)

            for l in range(DEPTH):
                for (dst0, pieces) in groups:
                    sl = gi % 2
                    wtot = max(p[0] + p[2] for p in pieces)
                    for (doff, src0, w, sw) in pieces:
                        dma(stg32[sl][:, :, doff:doff + w], win_d[l, :, src0:src0 + w].rearrange("(kc p) w -> p kc w", p=128), prev_store[sl], [("stg32", sl, doff)])
                    for pi, (doff, src0, w, sw) in enumerate(pieces):
                        if not sw:
                            castcp(gi + pi, stg16[sl][:, :, doff:doff + w], stg32[sl][:, :, doff:doff + w], [("stg32", sl, doff)], [("stg16", sl, doff)])
                        else:
                            i5 = stg32[sl][:, :, doff:doff + w].rearrange("p k (b t s) -> p k b t s", b=4, t=2)
                            o5 = stg16[sl][:, :, doff:doff + w].rearrange("p k (b t s) -> p k b t s", b=4, t=2)
                            for kc in range(8):
                                cp("pool" if kc % 2 else "dve", o5[:, kc, :, 0, :], i5[:, kc, :, 1, :], [("stg32", sl, doff)], [("stg16", sl, doff, kc, 0)])
                                cp("dve" if kc % 2 else "pool", o5[:, kc, :, 1, :], i5[:, kc, :, 0, :], [("stg32", sl, doff)], [("stg16", sl, doff, kc, 1)])
                    rk = []
                    for (doff, src0, w, sw) in pieces:
                        if sw:
                            rk += [("stg16", sl, doff, kc, t) for kc in range(8) for t in range(2)]
                        else:
                            rk.append(("stg16", sl, doff))
                    dma(wbf_d[l, :, dst0:dst0 + wtot].rearrange("(kc p) w -> p kc w", p=128), stg16[sl][:, :, 0:wtot], rk, [("wbf", l, dst0)])
                    prev_store[sl] = [("wbf", l, dst0)]
                    gi += 1
                for hf in range(2):
                    sl = gi % 2
                    dma(stg32[sl][:, :, :], wout_d[l, :, hf * 512:(hf + 1) * 512].rearrange("(kc p) w -> p kc w", p=128), prev_store[sl], [("stg32", sl, 0)])
                    castcp(gi, stg16[sl][:, :, :], stg32[sl][:, :, :], [("stg32", sl, 0)], [("stg16", sl, 0)])
                    dma(wobf_d[l, :, hf * 512:(hf + 1) * 512].rearrange("(kc p) w -> p kc w", p=128), stg16[sl][:, :, :], [("stg16", sl, 0)], [("wobf", l, hf)])
                    prev_store[sl] = [("wobf", l, hf)]
                    gi += 1

            chk('conv')
            csT = v3(small[:, 64:64 + 64], 8)
            for r in range(NB + 1):
                src = c_d[r, :] if r < NB else cctx_d[0, :]
                dma(csT[:, :, r:r + 1], src.rearrange("(kc p o) -> p kc o", p=128, o=1), ["sgW"], [("csT", r)], slow=True)
            NR = NB + 1
            cs_r = [("csT", r) for r in range(NR)]
            tnh = v3(small[:, 128:192], 8)
            act(tnh[:, :, 0:NR], csT[:, :, 0:NR], AF.Tanh, cs_r, ["cs_t"], scale=0.5)
            ts("dve", tnh[:, :, 0:NR], tnh[:, :, 0:NR], 0.5, 0.5, ALU.mult, ALU.add, ["cs_t"], ["cs_t"])
            tt("dve", csT[:, :, 0:NR], csT[:, :, 0:NR], tnh[:, :, 0:NR], ALU.mult, cs_r + ["cs_t"], ["csS"])
            mrow = sub(AOFF, ASZ).f32(1024)
            S.region["mrow"] = "A"
            wi = 0
            for l in range(NL):
                for cb in range(6):
                    sl = wi % 2
                    dma(stg32[sl][:, :, :], wmod_d[l, :, cb * 512:(cb + 1) * 512].rearrange("(kc p) w -> p kc w", p=128), prev_store[sl], [("stg32", sl, 0)])
                    pg = ps_g[2 + (wi % 2)]
                    for kc in range(8):
                        mm(pg[0:NR, :], csT[:, kc, 0:NR], stg32[sl][:, kc, :], kc == 0, kc == 7, ["csS", ("stg32", sl, 0)], ["g%d" % (2 + wi % 2)])
                    bm = mrow[0:NR, 512:1024]
                    dma(bm, bmod_d[l, cb * 512:(cb + 1) * 512].partition_broadcast(NR), [], [("mrow", "b")])
                    tt("dve", mrow[0:NR, 0:512], pg[0:NR, :], bm, ALU.add, ["g%d" % (2 + wi % 2), ("mrow", "b")], [("mrow", "o")])
                    dma(mod_d[l, 0:NR, cb * 512:(cb + 1) * 512], mrow[0:NR, 0:512], [("mrow", "o")], [("mod", l)])
                    wi += 1

            chk('mod')
            def wload(l, col0, ncols, slot, src=None):
                srcd = wbf_d if src is None else src
                key = ("wbf", l, col0) if src is None else ("wobf", l, col0 // 512)
                rk = [("wbf", l, g[0]) for g in groups] if src is None else [key]
                dma(wring[slot][:, :, 0:ncols], srcd[l, :, col0:col0 + ncols].rearrange("(kc p) w -> p kc w", p=128), rk, [("wring", slot)])

            ring_ctr = [0]

            def next_slot():
                s_ = ring_ctr[0] % 2
                ring_ctr[0] += 1
                return s_

            g_ctr = [0]

            def next_g(lo=0, hi=5):
                i = lo + g_ctr[0] % (hi - lo)
                g_ctr[0] += 1
                return i

            def ln_block(xb, ntile, xkey, xnbuf, xnkey, hT_of, hkey_of, s1p, shp, modkey, hTf=None, hfkey=None, post=None):
                s1 = small[:, 224:224 + ntile]
                s2 = small[:, 232:232 + ntile]
                k1 = [("lnst", 1, ti) for ti in range(ntile)]
                k2 = [("lnst", 2, ti) for ti in range(ntile)]
                xk = xkey if callable(xkey) else (lambda ti: xkey)
                for ti in range(ntile):
                    act(xnbuf, xb[:, ti, :], AF.Square, [xk(ti)], [xnkey, k2[ti]], accum=s2[:, ti:ti + 1])
                    act(xnbuf, xb[:, ti, :], AF.Identity, [xk(ti)], [xnkey, k1[ti]], accum=s1[:, ti:ti + 1])
                mean, rstd = stats(s1, s2, ntile, 1024.0, (k1, k2))
                for ti in range(ntile):
                    ts("dve", xnbuf, xb[:, ti, :], mean[:, ti:ti + 1], rstd[:, ti:ti + 1], ALU.subtract, ALU.mult, [xk(ti), "st_mean", "st_rstd"], [xnkey])
                    for kc in range(8):
                        tr(ps_tp[:, kc * 128:(kc + 1) * 128], xnbuf[:, kc * 128:(kc + 1) * 128], identF, [xnkey, "cst"], ["tpa", "tpb"])
                    tmpm = v3(xnbuf, 8)
                    tt("dve", tmpm, v3(ps_tp[:, :], 8), bc(s1p.unsqueeze(2), [128, 8, 128]), ALU.mult, ["tpa", "tpb"] + modkey, [xnkey])
                    if hTf is None:
                        tt("pool", hT_of(ti), tmpm, bc(shp.unsqueeze(2), [128, 8, 128]), ALU.add, [xnkey] + modkey, [hkey_of(ti)])
                    else:
                        tt("pool", hTf(ti), tmpm, bc(shp.unsqueeze(2), [128, 8, 128]), ALU.add, [xnkey] + modkey, [hfkey(ti)])
                        cp("act", hT_of(ti), hTf(ti), [hfkey(ti)], [hkey_of(ti)])
                    if post is not None:
                        post(ti)

            for b in range(NB):
                for l in range(NL):
                    lam_init = 0.8 - 0.6 * math.exp(-0.3 * l)
                    pk = ("par", b, l)
                    for (off, s0, w) in [(B_MLV, 512, 256), (B_MLV + 256, 1280, 16), (B_DAV, 2320, 512), (B_SG, 3344, 768), (B_MLOZ, 768, 512), (B_DAZ, 2832, 512)]:
                        dma(biasb[:, off:off + w], bin_d[l, s0:s0 + w].partition_broadcast(128), [], [("biasb", off)])
                    dma(lng, lng_d[l, :].partition_broadcast(128), [], ["lng"])
                    dma(lnb, lnb_d[l, :].partition_broadcast(128), [], ["lnb"])
                    dma(mlgc, mlg_d[l, :].partition_broadcast(128), [], ["mlgc"])
                    dma(dagc, dag_d[l, :].partition_broadcast(128), [], ["dagc"])
                    dma(sggt, sgg_d[l, :].partition_broadcast(128), [], ["sggt"])
                    dma(sgbt, sgb_d[l, :].partition_broadcast(128), [], ["sgbt"])
                    ts("dve", mlgc, mlgc, 0.5, None, ALU.mult, None, ["mlgc"], ["mlgc"])
                    ts("dve", dagc, dagc, 0.5 * (1.0 - lam_init), None, ALU.mult, None, ["dagc"], ["dagc"])
                    dma(gate_l, mod_d[l, b, 2048:3072].partition_broadcast(128), [("mod", l)], ["gate_l"])
                    dma(gate_c, mod_d[l, NB, 2048:3072].partition_broadcast(128), [("mod", l)], ["gate_c"])
                    for i, (row, c0) in enumerate([(b, 0), (b, 1024), (NB, 0), (NB, 1024)]):
                        dma(modp[:, i, :], mod_d[l, row, c0:c0 + 1024].rearrange("(kc p) -> p kc", p=128), [("mod", l)], [("modp", i)], slow=True)
                    for i in (1, 3):
                        ts("dve", modp[:, i, :], modp[:, i, :], 1.0, None, ALU.add, None, [("modp", i)], [("modp", i)])
                    modk = [("modp", i) for i in range(4)]

                    fence("X")
                    fence("A")
                    fence("B")
                    dma(wq32, win_d[l, :, 0:512].rearrange("(kc p) w -> p kc w", p=128), [], [("wring", 0), ("wring", 1)])
                    dma(wg32, win_d[l, :, 1280:1296].rearrange("(kc p) w -> p kc w", p=128), [], ["wg32"])
                    hTf2 = xblk[1][:, 1:3, :].rearrange("p a (k t) -> p k (a t)", k=8) if False else None
                    hTfbuf = v3(arena_t[:, XOFF + 4096 + 1024:XOFF + 4096 + 3072], 8)
                    for bi, (t0, t1) in enumerate(BLKS):
                        ntile = (t1 - t0) // 128
                        xb = xblk[0]
                        if l == 0:
                            srcx = ctx_d[b, :, :] if bi == 0 else x_d[b, t0 - 256:t1 - 256, :]
                            rk = []
                        else:
                            srcx = xs_d[b, t0:t1, :]
                            rk = [("xs", b, bi, ti_) for ti_ in range((t1 - t0) // 128)]
                        dma(xb[:, 0:ntile, :], srcx.rearrange("(n p) d -> p n d", p=128), rk, [("xblk", 0)])
                        s1p, shp = (modp[:, 3, :], modp[:, 2, :]) if bi == 0 else (modp[:, 1, :], modp[:, 0, :])

                        def post1(ti, t0=t0):
                            tk = t0 + ti * 128
                            tg = tk // 128
                            hTf_t = hTfbuf[:, :, (ti % 2) * 128:(ti % 2) * 128 + 128]
                            gg = 2 + tg % 2
                            for kc in range(8):
                                mm(ps_g[gg][:, 0:16], hTf_t[:, kc, :], wg32[:, kc, :], kc == 0, kc == 7, ["wg32", ("xblk", 2, ti % 2)], ["g%d" % gg])
                            tt("dve", Gt[:, tg, :], ps_g[gg][:, 0:16], biasb[:, B_MLV + 256:B_MLV + 272], ALU.add,
                               ["g%d" % gg, ("biasb", B_MLV + 256)], [("Gt", tg)])
                            if ti % 2 == 0:
                                return
                            tk0 = tk - 128
                            for c in range(4):
                                pq = ps_g[c // 2][:, (c % 2) * 256:(c % 2) * 256 + 256]
                                for kc in range(8):
                                    mm(pq, wq32[:, kc, c * 128:(c + 1) * 128], hTfbuf[:, kc, :], kc == 0, kc == 7,
                                       [("wring", 0), ("wring", 1), ("xblk", 2, 0), ("xblk", 2, 1)], ["g%d" % (c // 2)])
                            for hb in range(2):
                                tt("dve", qkT[:, 2 * hb:2 * hb + 2, tk0:tk0 + 256], v3(ps_g[hb][:, :], 2), bc(fmb[:, l, 2 * hb:2 * hb + 2].unsqueeze(2), [128, 2, 256]), ALU.add,
                                   ["g%d" % hb, "fmb"], [("qkpre", tg - 1, hb), ("qkpre", tg, hb)])

                        ln_block(xb, ntile, ("xblk", 0), xblk[1][:, 0, :], ("xblk", 1),
                                 lambda ti, t0=t0: hTall[:, :, t0 + ti * 128:t0 + (ti + 1) * 128], lambda ti, t0=t0: ("hTall", t0 // 128 + ti),
                                 s1p, shp, modk, hTf=lambda ti: hTfbuf[:, :, (ti % 2) * 128:(ti % 2) * 128 + 128], hfkey=lambda ti: ("xblk", 2, ti % 2), post=post1)
                    if dbg:
                        tap("hT", hTall[:, :, :], [("hTall", i) for i in range(NT)], BF16)

                    chk('ph1')
                    hT_keys = [("hTall", i) for i in range(NT)]
                    pre_all = [("qkpre", ti, hb) for ti in range(NT) for hb in range(2)]
                    for c in range(4):
                        pst = qkT[:, c, :]
                        w0, w1, w2 = (convw[:, l, j, c:c + 1] for j in range(3))
                        ts("dve", cacc, pst, w1, convb[:, l, c:c + 1], ALU.mult, ALU.add, pre_all + ["convp"], ["cacc"])
                        for (a, e) in [(0, CTXL), (CTXL, T)]:
                            stt("dve", cacc[:, a + 1:e], pst[:, a:e - 1], w0, cacc[:, a + 1:e], ALU.mult, ALU.add, pre_all + ["convp", "cacc"], ["cacc"])
                            stt("dve", cacc[:, a:e - 1], pst[:, a + 1:e], w2, cacc[:, a:e - 1], ALU.mult, ALU.add, pre_all + ["convp", "cacc"], ["cacc"])
                        act(pst, cacc, AF.Tanh, ["cacc"] + pre_all, [("qkT", c)], scale=0.5)
                        sc_ = 0.5 if c < 2 else 0.0625
                        ts("pool", pst, pst, sc_, sc_, ALU.mult, ALU.add, [("qkT", c)], [("qkT", c)])
                        tt("dve", pst, pst, cacc, ALU.mult, [("qkT", c), "cacc"], [("qkT", c)])
                    slot = next_slot()
                    wload(l, C_MLV, 272, slot)
                    S.op("pool", lambda h: h.memset(vaug[:, :, :, 64:65], 1.0), [], [("vaug", "ones")])
                    for ti in range(NT):
                        gix = next_g()
                        pg = ps_g[gix]
                        for kc in range(8):
                            mm(pg[:, 0:272], hTall[:, kc, ti * 128:(ti + 1) * 128], wring[slot][:, kc, 0:272], kc == 0, kc == 7,
                               [("wring", slot), hT_keys[ti]], ["g%d" % gix])
                        tt("dve", vaug[:, ti, :, 0:64], v3(pg[:, 0:256], 4), v3(biasb[:, B_MLV:B_MLV + 256], 4), ALU.add,
                           ["g%d" % gix, ("biasb", B_MLV)], [("vaug", ti)])
                    if dbg:
                        tap("qkT", qkT[:, :, :], [("qkT", i) for i in range(4)], F32)
                        tap("vaug", vaug[:, :, :, :], [("vaug", i) for i in range(NT)] + [("vaug", "ones")], BF16)
                        tap("Gt", Gt[:, :, :], [("Gt", i) for i in range(NT)])

                    chk('ph2')
                    fence("X")
                    Gk = [("Gt", ti) for ti in range(NT)]
                    for d_ in range(2):
                        fsl = Gt[:, :, 4 + 8 * d_:8 + 8 * d_]
                        act(v3(gtmp[:, d_, :], NT), fsl, AF.Exp, Gk, [("gder", "t", d_)], scale=-1.0)
                        act(LFp[:, d_, :], gtmp[:, d_, :], AF.Ln, [("gder", "t", d_)], [("gder", "L", d_)], bias=1.0)
                    chk('g1')
                    pc = ps_g[0]
                    mm(pc[:, 0:72], triF, LFp[:, 0, :], True, True, ["cst", ("gder", "L", 0)], ["g0"])
                    mm(pc[:, 72:144], triB, LFp[:, 1, :], True, True, ["cst", ("gder", "L", 1)], ["g0"])
                    mm(pc[:, 144:288], onesF, LFp[:, :, :].rearrange("p a b -> p (a b)"), True, True, ["cst", ("gder", "L", 0), ("gder", "L", 1)], ["g0"])
                    chk('g2')
                    act(Ej[:, :, :].rearrange("p a b -> p (a b)"), pc[:, 0:144], AF.Exp, ["g0"], [("gder", "Ej")], scale=-1.0)
                    act(Eend[:, :, :].rearrange("p a b -> p (a b)"), pc[:, 144:288], AF.Exp, ["g0"], [("gder", "Eend")], scale=-1.0)
                    if dbg and 'cum' in dbg:
                        cp('dve', sgp[:, 0:288], pc[:, 0:288], ['g0'], [('mtmp', 'dbgc')])
                        tap('cum', sgp[:, 0:288], [('mtmp', 'dbgc')])
                        tap('LFp', LFp[:, :, :], [('gder', 'L', 0), ('gder', 'L', 1)])
                    chk('g3')
                    for d_ in range(2):
                        tt("dve", v3(gtmp[:, d_, :], NT), v3(pc[:, 72 * d_:72 * d_ + 72], NT), Gt[:, :, 8 * d_:8 * d_ + 4], ALU.add,
                           ["g0"] + Gk, [("gder", "t2", d_)])
                        act(Ws[:, d_, :], gtmp[:, d_, :], AF.Exp, [("gder", "t2", d_)], [("gder", "Ws", d_)])
                    chk('ph3a')
                    orders = [list(range(NT)), [1, 0] + list(range(NT - 1, 1, -1))]
                    written = set()
                    S.op("pool", lambda h: h.memset(m_cst_flat, 0.0), [], [("mtmp", "cst", 0), ("mtmp", "cst", 1)])
                    for d_ in range(2):
                        S.op("pool", lambda h, d_=d_: h.memset(m_vsf[d_], 0.0), [], [("mtmp", "vs", d_)])
                    for step in range(NT):
                        for d_ in range(2):
                            ti = orders[d_][step]
                            first = step == 0
                            tks = slice(ti * 128, (ti + 1) * 128)
                            for h4 in range(4):
                                pr = slice((h4 % 2) * 64, (h4 % 2) * 64 + 64)
                                gS = 1 + (h4 % 2)
                                mm(ps_g[gS][:, (h4 // 2) * 128:(h4 // 2 + 1) * 128], qkT[pr, 2 + h4 // 2, tks], qkT[pr, h4 // 2, tks], True, True,
                                   [("qkT", 2 + h4 // 2), ("qkT", h4 // 2)], ["g%d" % gS])
                            if step == 0 and d_ == 0: chk('sa')
                            mk = triF if d_ == 0 else triB
                            for j in range(2):
                                tr(ps_tp[:, j * 128:(j + 1) * 128], qkT[:, 2 + j, tks], identF, [("qkT", 2 + j), "cst"], ["tpa"])
                            cp("act", m_kt[d_], ps_tp[:, 0:256], ["tpa"], [("mtmp", "kt", d_)])
                            for par in range(2):
                                tt("dve", m_pt[d_][:, par * 2:par * 2 + 2, :], v3(ps_g[1 + par][:, 0:256], 2), bc(mk.unsqueeze(1), [128, 2, 128]), ALU.mult,
                                   ["g%d" % (1 + par), "cst"], [("mtmp", "pt", d_, par)])
                            wsl = v3(Ws[:, d_, :], NT)[:, ti, :]
                            tt("pool", m_vs[d_], vaug[:, ti, :, :], bc(wsl.unsqueeze(2), [128, 4, 65]), ALU.mult,
                               [("vaug", ti), ("vaug", "ones"), ("gder", "Ws", d_)], [("mtmp", "vs", d_)])
                            if step == 0 and d_ == 0: chk('sb')
                            gH = 3
                            pH = ps_g[gH]
                            pD = ps_g[4]
                            for h4 in range(4):
                                pr = slice((h4 % 2) * 64, (h4 % 2) * 64 + 64)
                                mm(pH[:, h4 * 72:h4 * 72 + 65], m_pt[d_][:, (h4 % 2) * 2 + h4 // 2, :], m_vs[d_][:, h4, :], True, first,
                                   [("mtmp", "pt", d_, h4 % 2), ("mtmp", "vs", d_)], ["g3"])
                                if not first:
                                    mm(pH[:, h4 * 72:h4 * 72 + 65], qkT[pr, h4 // 2, tks], m_cst[pr, d_, h4, :], False, True,
                                       [("qkT", h4 // 2), ("mtmp", "cst", d_)], ["g3"])
                            for pp in range(2):
                                mm(pD[:, pp * 144:pp * 144 + 144], m_kt[d_][:, pp * 128:pp * 128 + 128], m_vsf[d_][:, pp * 144:pp * 144 + 144], True, True,
                                   [("mtmp", "kt", d_), ("mtmp", "vs", d_)], ["g4"])
                            if step == 0 and d_ == 0: chk('sc')
                            ejs = v3(Ej[:, d_, :], NT)[:, ti, :]
                            tt("dve", m_hs[d_], v3(pH[:, 0:288], 4)[:, :, 0:65], bc(ejs.unsqueeze(2), [128, 4, 65]), ALU.mult, ["g3", ("gder", "Ej")], [("mtmp", "hs", d_)])
                            if step == 0 and d_ == 0: chk('sd')
                            den = m_hs[d_][:, :, 64]
                            d2 = small[:, 208:212]
                            tt("dve", d2, den, den, ALU.mult, [("mtmp", "hs", d_)], ["md2"])
                            ts("dve", d2, d2, 1.0, None, ALU.max, None, ["md2"], ["md2"])
                            tt("pool", d2, d2, neghalf[:, 0:4], ALU.pow, ["md2", "cst"], ["md2"])
                            if ti not in written:
                                tt("dve", v3(Hml[:, ti, :], 4), m_hs[d_][:, :, 0:64], bc(d2.unsqueeze(2), [128, 4, 64]), ALU.mult,
                                   [("mtmp", "hs", d_), "md2"], [("Hml", ti)])
                                written.add(ti)
                            else:
                                tt("dve", m_hn, m_hs[d_][:, :, 0:64], bc(d2.unsqueeze(2), [128, 4, 64]), ALU.mult,
                                   [("mtmp", "hs", d_), "md2"], [("mtmp", "hn")])
                                tt("pool", v3(Hml[:, ti, :], 4), v3(Hml[:, ti, :], 4), m_hn, ALU.add, [("mtmp", "hn"), ("Hml", ti)], [("Hml", ti)])
                            if step == 0 and d_ == 0: chk('se')
                            tt("dve", m_tmp, v3(pD[:, 0:288], 4)[:, :, 0:65], m_cst[:, d_, :, :], ALU.add, ["g4", ("mtmp", "cst", d_)], [("mtmp", "tmp")])
                            ees = v3(Eend[:, d_, :], NT)[:, ti, :]
                            tt("dve", m_cst[:, d_, :, :], m_tmp, bc(ees.unsqueeze(2), [128, 4, 65]), ALU.mult, [("mtmp", "tmp"), ("gder", "Eend")], [("mtmp", "cst", d_)])
                    chk('sf') if False else None
                    if dbg:
                        tap("Hml", Hml[:, :, :], [("Hml", i) for i in range(NT)])

                    chk('ph3')
                    fence("X")
                    fence("B")
                    S.op("pool", lambda h: h.memset(dvaug[:, :, :, 128:129], 1.0), [], [("dvaug", "ones")])
                    for hh in range(2):
                        slot = next_slot()
                        wload(l, C_DAK + hh * 512, 512, slot)
                        for h2 in range(2):
                            h4 = hh * 2 + h2
                            for bi, (t0, t1) in enumerate(BLKS):
                                n = t1 - t0
                                ga, gb = [(1, 2), (3, 4)][bi % 2]
                                for kc in range(8):
                                    mm(ps_g[ga][:, 0:n], wring[slot][:, kc, h2 * 256:h2 * 256 + 128], hTall[:, kc, t0:t1], kc == 0, kc == 7,
                                       [("wring", slot)] + hT_keys[t0 // 128:t1 // 128], ["g%d" % ga])
                                if bi == 0:
                                    act(dkT[:, h4, t0:t1], ps_g[ga][:, 0:n], AF.Identity, ["g%d" % ga, "fmb"], [("dkT", h4, bi)], bias=fmb[:, l, 4 + 2 * h4:5 + 2 * h4])
                                    continue
                                for kc in range(8):
                                    mm(ps_g[gb][:, 0:n], wring[slot][:, kc, h2 * 256 + 128:h2 * 256 + 256], hTall[:, kc, t0:t1], kc == 0, kc == 7,
                                       [("wring", slot)] + hT_keys[t0 // 128:t1 // 128], ["g%d" % gb])
                                p0 = t0 - 256
                                dma(ropeC[:, 0:n], rope_d[0, :, p0:p0 + n], [], [("rope4", "c")])
                                dma(ropeS[:, 0:n], rope_d[1, :, p0:p0 + n], [], [("rope4", "s")])
                                stt("dve", rstA[:, 0:n], ps_g[ga][:, 0:n], fmb[:, l, 4 + 2 * h4:5 + 2 * h4], ropeC[:, 0:n], ALU.add, ALU.mult,
                                    ["g%d" % ga, "fmb", ("rope4", "c")], [("rope4", "a")])
                                stt("dve", rstB[:, 0:n], ps_g[gb][:, 0:n], fmb[:, l, 5 + 2 * h4:6 + 2 * h4], ropeS[:, 0:n], ALU.add, ALU.mult,
                                    ["g%d" % gb, "fmb", ("rope4", "s")], [("rope4", "b")])
                                tt("pool", dkT[:, h4, t0:t1], rstA[:, 0:n], rstB[:, 0:n], ALU.add, [("rope4", "a"), ("rope4", "b")], [("dkT", h4, bi)])
                    slot = next_slot()
                    wload(l, C_DAV, 512, slot)
                    for ti in range(NT):
                        gix = next_g(3, 5)
                        pg = ps_g[gix]
                        for kc in range(8):
                            mm(pg[:, :], hTall[:, kc, ti * 128:(ti + 1) * 128], wring[slot][:, kc, :], kc == 0, kc == 7,
                               [("wring", slot), hT_keys[ti]], ["g%d" % gix])
                        tt("dve", dvaug[:, ti, :, 0:128], v3(pg[:, :], 4), v3(biasb[:, B_DAV:B_DAV + 512], 4), ALU.add,
                           ["g%d" % gix, ("biasb", B_DAV)], [("dvaug", ti)])
                    s_a = next_slot()
                    wload(l, C_SG, 512, s_a)
                    s_b = next_slot()
                    wload(l, C_SG + 512, 256, s_b)
                    for tp_ in range(NT // 2):
                        for j in range(2):
                            ti = 2 * tp_ + j
                            ga = 1 if j == 0 else 3
                            for kc in range(8):
                                mm(ps_g[ga][:, :], hTall[:, kc, ti * 128:(ti + 1) * 128], wring[s_a][:, kc, :], kc == 0, kc == 7,
                                   [("wring", s_a), hT_keys[ti]], ["g%d" % ga])
                            for kc in range(8):
                                mm(ps_g[2][:, j * 256:(j + 1) * 256], hTall[:, kc, ti * 128:(ti + 1) * 128], wring[s_b][:, kc, 0:256], kc == 0, kc == 7,
                                   [("wring", s_b), hT_keys[ti]], ["g2"])
                            tt("dve", sgp2[:, j, 0:512], ps_g[ga][:, :], biasb[:, B_SG:B_SG + 512], ALU.add, ["g%d" % ga, ("biasb", B_SG)], [("sgt", "p", j)])
                        tt("dve", sgp2[:, :, 512:768], v3(ps_g[2][:, :], 2), bc(biasb[:, B_SG + 512:B_SG + 768].unsqueeze(1), [128, 2, 256]), ALU.add,
                           ["g2", ("biasb", B_SG)], [("sgt", "pz")])
                        kp = [("sgt", "p", 0), ("sgt", "p", 1)]
                        act(sgu2, sgp2[:, :, 0:256], AF.Gelu, kp, [("sgt", "u")])
                        for j in range(2):
                            act(sgp2[:, j, 256:512], sgp2[:, j, 256:512], AF.Gelu, [("sgt", "p", j)], [("sgt", "gv", j)], accum=small[:, 240 + j:241 + j])
                            act(sgjunk2[:, j, :], sgp2[:, j, 256:512], AF.Square, [("sgt", "gv", j)], [("sgt", "junk", j)], accum=small[:, 244 + j:245 + j])
                        act(sgzs2, sgp2[:, :, 512:768], AF.Tanh, [("sgt", "pz")], [("sgt", "zs")], scale=0.5)
                        kgv = [("sgt", "gv", 0), ("sgt", "gv", 1)]
                        kjk = [("sgt", "junk", 0), ("sgt", "junk", 1)]
                        mean, rstd = stats(small[:, 240:242], small[:, 244:246], 2, 256.0, (kgv, kjk))
                        for j in range(2):
                            ts("dve", sgjunk2[:, j, :], sgp2[:, j, 256:512], mean[:, j:j + 1], rstd[:, j:j + 1], ALU.subtract, ALU.mult,
                               [("sgt", "gv", j), "st_mean", "st_rstd"], [("sgt", "junk", j)])
                        tt("dve", sgjunk2, sgjunk2, bc(sggt.unsqueeze(1), [128, 2, 256]), ALU.mult, kjk + ["sggt"], kjk)
                        tt("dve", sgvn2, sgjunk2, bc(sgbt.unsqueeze(1), [128, 2, 256]), ALU.add, kjk + ["sgbt"], [("sgt", "vn")])
                        for j in range(2):
                            for g in range(4):
                                mm(ps_g[4][:, j * 256 + g * 64:j * 256 + (g + 1) * 64], sgW[:, l, g, :], sgvn2[:, j, g * 64:(g + 1) * 64], True, True,
                                   ["sgW", ("sgt", "vn")], ["g4"])
                        for j in range(2):
                            tt("dve", v3(sgy2[:, j, :], 4), v3(ps_g[4][:, j * 256:(j + 1) * 256], 4), bc(sgbs[:, l, :].unsqueeze(2), [128, 4, 64]), ALU.add,
                               ["g4", "sgbs"], [("sgt", "y", j)])
                        ky = [("sgt", "y", 0), ("sgt", "y", 1)]
                        tt("dve", sgy2, sgy2, sgu2, ALU.mult, ky + [("sgt", "u")], ky)
                        stt("dve", sgzs2, sgzs2, 1.0, sgp2[:, :, 512:768], ALU.add, ALU.mult, [("sgt", "zs"), ("sgt", "pz")], [("sgt", "zs")])
                        stt("dve", ysg[:, 2 * tp_:2 * tp_ + 2, :], sgy2, 0.5, sgzs2, ALU.mult, ALU.mult, ky + [("sgt", "zs")], [("ysg", 2 * tp_), ("ysg", 2 * tp_ + 1)])
                    if dbg:
                        tap("dkT", dkT[:, :, :], [("dkT", h_, b_) for h_ in range(4) for b_ in range(5)], BF16)
                        tap("ysg", ysg[:, :, :], [("ysg", i) for i in range(NT)], BF16)

                    chk('ph4')
                    fence("X")
                    fence("A")
                    last = (l == NL - 1)
                    for bi, (t0, t1) in enumerate(BLKS):
                        n = t1 - t0
                        ntile = n // 128
                        isctx = bi == 0
                        if last and isctx:
                            continue
                        xb = xblk[0]
                        if l == 0:
                            srcx = ctx_d[b, :, :] if isctx else x_d[b, t0 - 256:t1 - 256, :]
                            rk = []
                        else:
                            srcx = xs_d[b, t0:t1, :]
                            rk = [("xs", b, bi, ti_) for ti_ in range((t1 - t0) // 128)]
                        srcx3 = srcx.rearrange("(n p) d -> p n d", p=128)
                        for ti in range(ntile):
                            dma(xb[:, ti, :], srcx3[:, ti, :], rk, [("xblk", 0, ti)])
                        s1p, shp = (modp[:, 3, :], modp[:, 2, :]) if isctx else (modp[:, 1, :], modp[:, 0, :])
                        ln_block(xb, ntile, lambda ti: ("xblk", 0, ti), xn5, ("p5", "xn"), lambda ti: hTblk[:, :, ti * 128:(ti + 1) * 128], lambda ti: ("p5", "hT", ti), s1p, shp, modk)
                        hk = [("p5", "hT", ti) for ti in range(ntile)]
                        if not isctx:
                            p0 = t0 - 256
                            dma(p5ropeC[:, 0:n], rope_d[0, :, p0:p0 + n], [], [("p5x", "c")])
                            dma(p5ropeS[:, 0:n], rope_d[1, :, p0:p0 + n], [], [("p5x", "s")])
                        for hh in range(2):
                            slot = next_slot()
                            wload(l, C_DAQ + hh * 512, 512, slot)
                            for h2 in range(2):
                                h4 = hh * 2 + h2
                                ga, gb = 0, 1
                                for kc in range(8):
                                    mm(ps_g[ga][:, 0:n], wring[slot][:, kc, h2 * 256:h2 * 256 + 128], hTblk[:, kc, 0:n], kc == 0, kc == 7, [("wring", slot)] + hk, ["g%d" % ga])
                                if isctx:
                                    act(qTblk[:, h4, 0:n], ps_g[ga][:, 0:n], AF.Identity, ["g%d" % ga, "fmb"], [("p5", "qT", h4)], bias=fmb[:, l, 12 + 2 * h4:13 + 2 * h4])
                                    continue
                                for kc in range(8):
                                    mm(ps_g[gb][:, 0:n], wring[slot][:, kc, h2 * 256 + 128:h2 * 256 + 256], hTblk[:, kc, 0:n], kc == 0, kc == 7, [("wring", slot)] + hk, ["g%d" % gb])
                                stt("dve", p5stA[:, 0:n], ps_g[ga][:, 0:n], fmb[:, l, 12 + 2 * h4:13 + 2 * h4], p5ropeC[:, 0:n], ALU.add, ALU.mult,
                                    ["g%d" % ga, "fmb", ("p5x", "c")], [("p5x", "a")])
                                stt("dve", p5stB[:, 0:n], ps_g[gb][:, 0:n], fmb[:, l, 13 + 2 * h4:14 + 2 * h4], p5ropeS[:, 0:n], ALU.add, ALU.mult,
                                    ["g%d" % gb, "fmb", ("p5x", "s")], [("p5x", "b")])
                                tt("pool", qTblk[:, h4, 0:n], p5stA[:, 0:n], p5stB[:, 0:n], ALU.add, [("p5x", "a"), ("p5x", "b")], [("p5", "qT", h4)])
                        for gi_, (c0, boff) in enumerate([(C_MLOZ, B_MLOZ), (C_DAZ, B_DAZ)]):
                            slot = next_slot()
                            wload(l, c0, 512, slot)
                            for ti in range(ntile):
                                gix = 2 + ti % 2
                                pg = ps_g[gix]
                                for kc in range(8):
                                    mm(pg[:, :], hTblk[:, kc, ti * 128:(ti + 1) * 128], wring[slot][:, kc, :], kc == 0, kc == 7, [("wring", slot), hk[ti]], ["g%d" % gix])
                                tt("dve", p5tmp, pg[:, :], biasb[:, boff:boff + 512], ALU.add, ["g%d" % gix, ("biasb", boff)], [("p5x", "tmp")])
                                act(p5t2, p5tmp, AF.Tanh, [("p5x", "tmp")], [("p5x", "t2")], scale=0.5)
                                gdst = gates[:, ti, gi_ * 512:(gi_ + 1) * 512]
                                if gi_ == 0:
                                    ts("dve", gdst[:, 0:256], p5t2[:, 0:256], 0.5, 0.5, ALU.mult, ALU.add, [("p5x", "t2")], [("p5", "g", ti, 0)])
                                    stt("dve", gdst[:, 256:512], p5t2[:, 256:512], 1.0, p5tmp[:, 256:512], ALU.add, ALU.mult, [("p5x", "t2"), ("p5x", "tmp")], [("p5", "g", ti, 1)])
                                else:
                                    stt("dve", gdst, p5t2, 1.0, p5tmp, ALU.add, ALU.mult, [("p5x", "t2"), ("p5x", "tmp")], [("p5", "g", ti, 2)])
                        tg0 = t0 // 128
                        mlv = v3(p5x_ml, 4)[:, 0:ntile, :]
                        sqv = v3(p5x_sq, 4)[:, 0:ntile, :]
                        kml = [("p5x", "c"), ("p5x", "s")]
                        ksq = [("p5x", "tmp"), ("p5x", "t2")]
                        k4 = ntile * 4
                        tt("dve", mlv, gates[:, 0:ntile, 0:256], Hml[:, tg0:tg0 + ntile, :], ALU.mult,
                           [("p5", "g", ti, 0) for ti in range(ntile)] + [("Hml", tg0 + ti) for ti in range(ntile)], kml)
                        red(small[:, 64:64 + k4], mlv.rearrange("p a (h d) -> p (a h) d", h=4), kml, ["mls1"])
                        tt("dve", sqv, mlv, mlv, ALU.mult, kml, ksq)
                        red(small[:, 80:80 + k4], sqv.rearrange("p a (h d) -> p (a h) d", h=4), ksq, ["mls2"])
                        mean, rstd = stats(small[:, 64:64 + k4], small[:, 80:80 + k4], k4, 64.0, "ml")
                        ml3 = mlv.rearrange("p a (h d) -> p (a h) d", h=4)
                        tt("dve", ml3, ml3, bc(mean.unsqueeze(2), [128, k4, 64]), ALU.subtract, kml + ["st_mean"], kml)
                        tt("dve", ml3, ml3, bc(rstd.unsqueeze(2), [128, k4, 64]), ALU.mult, kml + ["st_rstd"], kml)
                        tt("dve", mlv, mlv, bc(mlgc.unsqueeze(1), [128, ntile, 256]), ALU.mult, kml + ["mlgc"], kml)
                        gml = gates[:, 0:ntile, 0:256]
                        tt("dve", gml, mlv, gates[:, 0:ntile, 256:512], ALU.mult, kml + [("p5", "g", ti, 1) for ti in range(ntile)] + [("p5", "g", ti, 0) for ti in range(ntile)], [("p5", "mixml")])
                        ktiles = list(range(0, 2)) if isctx else list(range(NT))
                        STB = [(ps_g[4], "g4"), (ps_tp[:, 0:512], "tpa"), (ps_tp[:, 512:1024], "tpb")]
                        nk = len(ktiles)
                        LOOK = 2

                        def combine(h4):
                            nt_ = ntile
                            ok_ = [("p5", "oev", qi, c) for qi in range(nt_) for c in range(2)]
                            rr = v3(small[:, 192:200], 4)[:, 0:nt_, :]
                            attv = v3(p5stA, 4)[:, 0:nt_, :]
                            tmpv = v3(p5stB, 4)[:, 0:nt_, :]
                            ssv = small[:, 200:200 + nt_]
                            S.op("dve", lambda h, rr=rr, nt_=nt_: h.reciprocal(out=rr, in_=oev[:, 0:nt_, :, 128]), ok_, ["p5rr"])
                            ts("dve", rr[:, :, 1], rr[:, :, 1], neglam[:, l:l + 1], None, ALU.mult, None, ["p5rr", "neglam"], ["p5rr"])
                            tt("dve", attv, oev[:, 0:nt_, 0, 0:128], bc(rr[:, :, 0:1], [128, nt_, 128]), ALU.mult, ok_ + ["p5rr"], [("p5x", "a")])
                            tt("dve", tmpv, oev[:, 0:nt_, 1, 0:128], bc(rr[:, :, 1:2], [128, nt_, 128]), ALU.mult, ok_ + ["p5rr"], [("p5x", "b")])
                            tt("pool", attv, attv, tmpv, ALU.add, [("p5x", "a"), ("p5x", "b")], [("p5x", "a")])
                            tt("dve", tmpv, attv, attv, ALU.mult, [("p5x", "a")], [("p5x", "b")])
                            red(ssv, tmpv, [("p5x", "b")], ["p5ss"])
                            ts("dve", ssv, ssv, 1.0 / 128.0, EPS, ALU.mult, ALU.add, ["p5ss"], ["p5ss"])
                            tt("pool", ssv, ssv, neghalf[:, 0:nt_], ALU.pow, ["p5ss", "cst"], ["p5ss"])
                            tt("dve", attv, attv, bc(ssv.unsqueeze(2), [128, nt_, 128]), ALU.mult, [("p5x", "a"), "p5ss"], [("p5x", "a")])
                            tt("dve", attv, attv, bc(dagc.unsqueeze(1), [128, nt_, 128]), ALU.mult, [("p5x", "a"), "dagc"], [("p5x", "a")])
                            gsl = gates[:, 0:nt_, 512 + h4 * 128:512 + (h4 + 1) * 128]
                            tt("dve", gsl, attv, gsl, ALU.mult, [("p5x", "a")] + [("p5", "g", qi, 2) for qi in range(nt_)], [("p5", "damix", h4)])
                        tg0 = t0 // 128

                        items = [(h4, c, kti) for h4 in range(4) for c in range(2) for kti in range(nk)]
                        for g in range(len(items) + LOOK):
                            if g < len(items):
                                h4, c, kti = items[g]
                                kt = ktiles[kti]
                                pr = slice(c * 64, c * 64 + 64)
                                pS, gk = STB[g % 3]
                                mm(pS[:, 0:n], dkT[pr, h4, kt * 128:(kt + 1) * 128], qTblk[pr, h4, 0:n], True, True,
                                   [("dkT", h4, 0 if kt < 2 else 1 + (kt - 2) // 4), ("p5", "qT", h4)], [gk])
                                act(ptb[g % 4][:, 0:n], pS[:, 0:n], AF.Exp, [gk], [("p5", "pt", g % 4)], scale=0.125)
                            j = g - LOOK
                            if j >= 0:
                                h4, c, kti = items[j]
                                kt = ktiles[kti]
                                for qi in range(ntile):
                                    mm(ps_g[qi][:, 0:129], ptb[j % 4][:, qi * 128:(qi + 1) * 128], dvaug[:, kt, h4, :], kti == 0, kti == nk - 1,
                                       [("p5", "pt", j % 4), ("dvaug", kt), ("dvaug", "ones")], ["g%d" % qi])
                                if kti == nk - 1:
                                    for qi in range(ntile):
                                        cp("act" if qi % 2 else "dve", oev[:, qi, c, :], ps_g[qi][:, 0:129], ["g%d" % qi], [("p5", "oev", qi, c)])
                                    if c == 1:
                                        combine(h4)
                        for ti in range(ntile):
                            tg = tg0 + ti
                            for kc in range(8):
                                if kc < 2:
                                    src_, rk_ = gates[:, ti, kc * 128:(kc + 1) * 128], [("p5", "mixml")]
                                elif kc < 6:
                                    src_, rk_ = gates[:, ti, 512 + (kc - 2) * 128:512 + (kc - 1) * 128], [("p5", "damix", kc - 2)]
                                else:
                                    src_, rk_ = ysg[:, tg, (kc - 6) * 128:(kc - 5) * 128], [("ysg", tg)]
                                tr(ps_tb[:, kc * 128:(kc + 1) * 128], src_, identB, rk_ + ["cstb"], ["tb"])
                            cp("act", mixT[:, :, ti * 128:(ti + 1) * 128], v3(ps_tb[:, :], 8), ["tb"], [("p5", "mixT", ti)])
                        if dbg:
                            tap("mix%d" % bi, mixT[:, :, 0:n], [("p5", "mixT", ti) for ti in range(ntile)], BF16)
                        gateb = gate_c if isctx else gate_l
                        gkey = "gate_c" if isctx else "gate_l"
                        for hf in range(2):
                            slot = next_slot()
                            wload(l, hf * 512, 512, slot, src=wobf_d)
                            for ti in range(ntile):
                                gix = ti % 2
                                pg = ps_g[gix]
                                for kc in range(8):
                                    mm(pg[:, :], mixT[:, kc, ti * 128:(ti + 1) * 128], wring[slot][:, kc, :], kc == 0, kc == 7, [("wring", slot), ("p5", "mixT", ti)], ["g%d" % gix])
                                tt("dve", p5tmp, pg[:, :], gateb[:, hf * 512:(hf + 1) * 512], ALU.mult, ["g%d" % gix, gkey], [("p5x", "tmp")])
                                stt("dve", xb[:, ti, hf * 512:(hf + 1) * 512], xb[:, ti, hf * 512:(hf + 1) * 512], ALPHA, p5tmp, ALU.mult, ALU.add,
                                    [("xblk", 0, ti), ("p5x", "tmp")], [("xblk", 0, ti)])
                        s1 = small[:, 224:224 + ntile]
                        s2 = small[:, 232:232 + ntile]
                        k1 = [("lnst", 1, ti) for ti in range(ntile)]
                        k2 = [("lnst", 2, ti) for ti in range(ntile)]
                        for ti in range(ntile):
                            act(xn5, xb[:, ti, :], AF.Square, [("xblk", 0, ti)], [("p5", "xn"), k2[ti]], accum=s2[:, ti:ti + 1])
                            act(xn5, xb[:, ti, :], AF.Identity, [("xblk", 0, ti)], [("p5", "xn"), k1[ti]], accum=s1[:, ti:ti + 1])
                        mean, rstd = stats(s1, s2, ntile, 1024.0, (k1, k2))
                        if last:
                            dst3 = y_d[b, t0 - 256:t1 - 256, :].rearrange("(n p) d -> p n d", p=128)
                        else:
                            dst3 = xs_d[b, t0:t1, :].rearrange("(n p) d -> p n d", p=128)
                        for ti in range(ntile):
                            xk_ = ("xblk", 0, ti)
                            ts("dve", xb[:, ti, :], xb[:, ti, :], mean[:, ti:ti + 1], rstd[:, ti:ti + 1], ALU.subtract, ALU.mult, [xk_, "st_mean", "st_rstd"], [xk_])
                            tt("dve", xb[:, ti, :], xb[:, ti, :], lng, ALU.mult, [xk_, "lng"], [xk_])
                            tt("pool" if ti % 2 else "dve", xb[:, ti, :], xb[:, ti, :], lnb, ALU.add, [xk_, "lnb"], [xk_])
                            dma(dst3[:, ti, :], xb[:, ti, :], [xk_], [("y", b, bi, ti) if last else ("xs", b, bi, ti)])

        except _Stop:
            pass
        S.emit(nc, st)
    return nc, S, dbg_out


def make_consts():
    i = np.arange(128)
    ident = np.eye(128, dtype=np.float32)
    triF = (i[:, None] <= i[None, :]).astype(np.float32)
    triB = (i[:, None] >= i[None, :]).astype(np.float32)
    ones = np.ones((128, 128), np.float32)
    partner = np.where((i % 32) < 16, i + 16, i - 16)
    rperm = np.zeros((128, 128), np.float32)
    rperm[partner, i] = 1.0
    cst = np.concatenate([ident, triF, triB, ones, rperm], 1)
    t = np.arange(LAT)
    row = (t // 64).astype(np.float32)
    col = (t % 64).astype(np.float32)
    half = 32
    inv = (10000.0 ** (-(np.arange(0, half, 2, dtype=np.float32)) / half)).astype(np.float32)
    ang_r = row[:, None] * inv
    ang_c = col[:, None] * inv
    ang = np.concatenate([ang_r, ang_r, ang_c, ang_c], -1).astype(np.float32)
    cos = np.cos(ang).astype(np.float32)
    sin = np.sin(ang).astype(np.float32)
    d = np.arange(64)
    sign = np.where((d % 32) < 16, -1.0, 1.0).astype(np.float32)
    sinS = sin * sign[None, :]
    cosT = np.concatenate([cos.T, cos.T], 0)
    sinT = np.concatenate([sinS.T, sinS.T], 0)
    rope = np.ascontiguousarray(np.stack([cosT, sinT], 0)).astype(np.float32)
    return np.ascontiguousarray(cst), rope


_CACHE = {}


def kernel(**inputs):
    NCORES = 8
    NB = 32 // NCORES
    key = (NB, DEPTH)
    if key not in _CACHE:
        _CACHE[key] = build(NB, DEPTH)
    nc = _CACHE[key][0]
    cst, rope = make_consts()
    f = lambda a: np.ascontiguousarray(np.asarray(a, dtype=np.float32))
    shared = {k: f(inputs[k]) for k in ["w_mod", "b_mod", "w_in", "b_in", "ml_conv_w", "ml_conv_b", "ml_norm_g",
                                        "da_lam_q1", "da_lam_k1", "da_lam_q2", "da_lam_k2", "da_norm_g", "sg_norm_g",
                                        "sg_norm_b", "sg_w_s", "sg_b_s", "w_out", "ln_g", "ln_b"]}
    shared["c_ctx"] = f(inputs["c_ctx"]).reshape(1, D)
    shared["cst"] = cst
    shared["rope"] = rope
    x = f(inputs["x"])
    ctx = f(inputs["ctx"])
    c = f(inputs["c"])
    in_maps = []
    for i in range(NCORES):
        m = dict(shared)
        m["x"] = x[i * NB:(i + 1) * NB]
        m["ctx"] = ctx[i * NB:(i + 1) * NB]
        m["c"] = c[i * NB:(i + 1) * NB]
        in_maps.append(m)
    res = run_bass_kernel_spmd(nc, in_maps, core_ids=list(range(NCORES)))
    return np.concatenate([r["y"] for r in res.results], axis=0).astype(np.float32)
```

```python
import math
import numpy as np
from contextlib import ExitStack
import concourse.bass as bass
import concourse.mybir as mybir
from concourse.bass_utils import run_bass_kernel_spmd

F32 = mybir.dt.float32
BF16 = mybir.dt.bfloat16
AF = mybir.ActivationFunctionType
ALU = mybir.AluOpType
AX = mybir.AxisListType

D = 1024
CTXL = 256
LAT = 2048
T = CTXL + LAT
NT = T // 128
DEPTH = 4
EPS = 1e-5
ALPHA = (2 * DEPTH) ** 0.25
NWC = 5136
BLKS = [(0, 256), (256, 768), (768, 1280), (1280, 1792), (1792, 2304)]
NDMA = 24
PSUM_KEYS = frozenset(["g0", "g1", "g2", "g3", "g4", "tpa", "tpb", "tb"])

C_MLQK, C_MLV, C_DAK, C_DAV, C_SG, C_DAQ, C_MLOZ, C_DAZ = 0, 512, 784, 1808, 2320, 3088, 4112, 4624
B_MLV, B_DAV, B_SG, B_MLOZ, B_DAZ, NBIAS = 0, 272, 784, 1552, 2064, 2576


class Sched:
    def __init__(self):
        self.ins = []
        self.lastw = {}
        self.readers = {}
        self.region = {}
        self.rlast = {}
        self.rdma = {}

    def _add(self, eng, fn, reads, writes, is_dma, extra=()):
        reads = list(reads)
        writes = list(writes)
        for k in reads:
            if k in PSUM_KEYS:
                writes.append(("rdser", k))
        deps = set(extra)
        touched = set()
        for k in reads + writes:
            nm = k[0] if isinstance(k, tuple) else k
            r = self.region.get(nm)
            if r is not None:
                touched.add(r)
        for r in touched:
            w = self.lastw.get(("R", r))
            if w is not None:
                deps.add(w)
        for k in reads:
            w = self.lastw.get(k)
            if w is not None:
                deps.add(w)
        for k in writes:
            w = self.lastw.get(k)
            if w is not None:
                deps.add(w)
            rd = self.readers.get(k)
            if rd:
                deps.update(rd[0].values())
                deps.update(rd[1])
        idx = len(self.ins)
        self.ins.append([eng, fn, deps, is_dma])
        for k in writes:
            self.lastw[k] = idx
            self.readers[k] = [{}, []]
        ws = set(writes)
        for k in reads:
            if k not in ws:
                rd = self.readers.setdefault(k, [{}, []])
                if is_dma:
                    rd[1].append(idx)
                else:
                    rd[0][eng] = idx
        for r in touched:
            if is_dma:
                self.rdma.setdefault(r, []).append(idx)
            else:
                self.rlast.setdefault(r, {})[eng] = idx
        return idx

    def fence(self, region, eng, fn):
        deps = set(self.rlast.get(region, {}).values()) | set(self.rdma.get(region, []))
        idx = self._add(eng, fn, (), (), False, extra=deps)
        self.lastw[("R", region)] = idx
        self.rlast[region] = {}
        self.rdma[region] = []
        return idx

    def op(self, eng, fn, reads=(), writes=()):
        return self._add(eng, fn, reads, writes, False)

    def dma(self, eng, fn, reads=(), writes=()):
        return self._add(eng, fn, reads, writes, True)

    def emit(self, nc, stack):
        ins = self.ins
        n = len(ins)
        engs = ["pe", "act", "dve", "pool", "sp"]
        dma_list = [i for i in range(n) if ins[i][3]]
        dma_slot = {}
        for j, i in enumerate(dma_list):
            dma_slot[i] = j
            if j >= NDMA:
                ins[i][2].add(dma_list[j - NDMA])
        needed = [False] * n
        for i in range(n):
            e = ins[i][0]
            nd = set()
            for d in ins[i][2]:
                if ins[d][0] == e and e == "pe" and not ins[d][3]:
                    continue
                nd.add(d)
                needed[d] = True
            ins[i][2] = nd
        esem = {e: stack.enter_context(nc.semaphore("s_" + e)) for e in engs}
        dsem = [stack.enter_context(nc.semaphore("d_%d" % j)) for j in range(NDMA)]
        cnt = {e: 0 for e in engs}
        tok = [None] * n
        for i in range(n):
            e, fn, deps, is_dma = ins[i]
            if is_dma:
                j = dma_slot[i]
                tok[i] = (("d", j % NDMA), 16 * (j // NDMA + 1))
            elif needed[i]:
                cnt[e] += 1
                tok[i] = (("e", e), cnt[e])
        per = {e: [] for e in engs}
        for i in range(n):
            per[ins[i][0]].append(i)
        self.counts = {e: len(per[e]) for e in engs}

        def semof(key):
            return esem[key[1]] if key[0] == "e" else dsem[key[1]]

        def run(e, h):
            seen = {}
            for i in per[e]:
                _, fn, deps, is_dma = ins[i]
                want = {}
                for d in deps:
                    k, v = tok[d]
                    if v > want.get(k, 0):
                        want[k] = v
                for k, v in want.items():
                    if seen.get(k, 0) < v:
                        h.wait_ge(semof(k), v)
                        seen[k] = v
                r = fn(h)
                if tok[i] is not None:
                    k, v = tok[i]
                    r.then_inc(semof(k), 16 if is_dma else 1)
            for i in per[e]:
                if ins[i][3]:
                    k, v = tok[i]
                    if seen.get(k, 0) < v:
                        h.wait_ge(semof(k), v)
                        seen[k] = v

        with nc.Block() as block:
            @block.tensor
            def _(h):
                run("pe", h)

            @block.scalar
            def _(h):
                run("act", h)

            @block.vector
            def _(h):
                run("dve", h)

            @block.gpsimd
            def _(h):
                run("pool", h)

            @block.sync
            def _(h):
                run("sp", h)


class Arena:
    def __init__(self, ap):
        self.ap = ap
        self.off = 0
        self.size = ap.shape[1]

    def f32(self, n):
        a = self.ap[:, self.off:self.off + n]
        self.off += (n + 7) // 8 * 8
        assert self.off <= self.size, (self.off, self.size)
        return a

    def bf16(self, n):
        w = (n + 1) // 2
        a = self.ap[:, self.off:self.off + w].bitcast(BF16)
        self.off += (w + 7) // 8 * 8
        assert self.off <= self.size, (self.off, self.size)
        return a[:, 0:n]


def v3(ap, a):
    return ap.rearrange("p (a b) -> p a b", a=a)


def v4(ap, a, b):
    return ap.rearrange("p (a b c) -> p a b c", a=a, b=b)


def bc(ap, shape):
    return ap.to_broadcast(shape)


class _Stop(Exception):
    pass


def build(NB, NL, dbg=None, stop=None):
    nc = bass.Bass("TRN2", target_bir_lowering=False)
    S = Sched()

    def dram(name, shape, dtype=F32, kind="ExternalInput"):
        return nc.dram_tensor(name, shape, dtype, kind=kind).ap()

    x_d = dram("x", [NB, LAT, D])
    ctx_d = dram("ctx", [NB, CTXL, D])
    c_d = dram("c", [NB, D])
    cctx_d = dram("c_ctx", [1, D])
    wmod_d = dram("w_mod", [DEPTH, D, 3 * D])
    bmod_d = dram("b_mod", [DEPTH, 3 * D])
    win_d = dram("w_in", [DEPTH, D, 4112])
    bin_d = dram("b_in", [DEPTH, 4112])
    convw_d = dram("ml_conv_w", [DEPTH, 3, 512])
    convb_d = dram("ml_conv_b", [DEPTH, 512])
    mlg_d = dram("ml_norm_g", [DEPTH, 256])
    lq1_d = dram("da_lam_q1", [DEPTH, 64])
    lk1_d = dram("da_lam_k1", [DEPTH, 64])
    lq2_d = dram("da_lam_q2", [DEPTH, 64])
    lk2_d = dram("da_lam_k2", [DEPTH, 64])
    dag_d = dram("da_norm_g", [DEPTH, 128])
    sgg_d = dram("sg_norm_g", [DEPTH, 256])
    sgb_d = dram("sg_norm_b", [DEPTH, 256])
    sgw_d = dram("sg_w_s", [DEPTH, 4, 128, 128])
    sgbs_d = dram("sg_b_s", [DEPTH, 4, 128])
    wout_d = dram("w_out", [DEPTH, D, D])
    lng_d = dram("ln_g", [DEPTH, D])
    lnb_d = dram("ln_b", [DEPTH, D])
    cst_d = dram("cst", [128, 640])
    rope_d = dram("rope", [2, 128, LAT])
    y_d = dram("y", [NB, LAT, D], kind="ExternalOutput")
    wbf_d = dram("wbf", [DEPTH, D, NWC], BF16, kind="Internal")
    wobf_d = dram("wobf", [DEPTH, D, D], BF16, kind="Internal")
    mod_d = dram("modscr", [DEPTH, 8, 3 * D], F32, kind="Internal")
    xs_d = dram("xs", [NB, T, D], F32, kind="Internal")
    dbg_out = {}

    with ExitStack() as st:
        arena_t = st.enter_context(nc.sbuf_tensor("arena", [128, 51200], F32))
        AR = Arena(arena_t[:, :])
        ps_tp = st.enter_context(nc.psum_tensor("ps_tp", [128, 1024], F32))
        ps_tb = st.enter_context(nc.psum_tensor("ps_tb", [128, 1024], BF16))
        ps_g = [st.enter_context(nc.psum_tensor("ps_g%d" % i, [128, 512], F32)) for i in range(5)]

        cstF = AR.f32(640)
        identF, triF, triB, onesF, rperm = (cstF[:, i * 128:(i + 1) * 128] for i in range(5))
        identB = AR.bf16(128)
        maskF = AR.bf16(128)
        maskB = AR.bf16(128)
        neghalf = AR.f32(64)
        dummy = AR.f32(8)
        sgW = v4(AR.bf16(DEPTH * 4 * 128), DEPTH, 4)
        sgbs = v3(AR.f32(DEPTH * 4), DEPTH)
        fmb = v3(AR.f32(DEPTH * 20), DEPTH)
        convw = v4(AR.f32(DEPTH * 12), DEPTH, 3)
        convb = v3(AR.f32(DEPTH * 4), DEPTH)
        hconvb = v3(AR.f32(DEPTH * 4), DEPTH)
        lamv = AR.f32(DEPTH)
        neglam = AR.f32(DEPTH)
        biasb = AR.f32(NBIAS)
        lng = AR.f32(D)
        lnb = AR.f32(D)
        mlgc = AR.f32(256)
        dagc = AR.f32(128)
        sggt = AR.f32(256)
        sgbt = AR.f32(256)
        gate_l = AR.f32(D)
        gate_c = AR.f32(D)
        modp = v3(AR.f32(32), 4)
        WR_OFF = AR.off
        wring = [v3(AR.bf16(8 * 512), 8) for _ in range(2)]
        wq32 = v3(arena_t[:, WR_OFF:WR_OFF + 4096], 8)
        wg32 = v3(AR.f32(128), 8)
        Hml = v3(AR.f32(NT * 256), NT)
        small = AR.f32(256)
        XOFF = AR.off
        XSZ = 8192
        AR.off += XSZ
        BOFF = AR.off
        BSZ = 12900
        AR.off += BSZ
        AOFF = AR.off
        ASZ = 51200 - AOFF
        assert ASZ >= 9216, ASZ

        def sub(off, size):
            return Arena(arena_t[:, off:off + size])

        for nm in ["X", "B", "A"]:
            pass

        aX = sub(XOFF, XSZ)
        xblk = [v3(aX.f32(4096), 4), v3(aX.f32(4096), 4)]
        aX = sub(XOFF, XSZ)
        pst = aX.f32(T)
        cacc = aX.f32(T)
        aX = sub(XOFF, XSZ)
        m_pt = [v3(aX.f32(512), 4) for _ in range(2)]
        m_vsf = [aX.f32(288) for _ in range(2)]
        m_vs = [v3(m_vsf[i], 4)[:, :, 0:65] for i in range(2)]
        m_kt = [aX.f32(256) for _ in range(2)]
        m_hs = [v3(aX.f32(288), 4)[:, :, 0:65] for _ in range(2)]
        m_cst_flat = aX.f32(576)
        m_cst = v4(m_cst_flat, 2, 4)[:, :, :, 0:65]
        m_tmp = v3(aX.f32(288), 4)[:, :, 0:65]
        m_hn = v3(aX.f32(256), 4)
        aX = sub(XOFF, XSZ)
        ropeC = aX.f32(512)
        ropeS = aX.f32(512)
        rstA = aX.f32(512)
        rstB = aX.f32(512)
        sgp = aX.f32(768)
        sgp2 = v3(aX.f32(2 * 768), 2)
        sgu2 = v3(aX.f32(512), 2)
        sgjunk2 = v3(aX.f32(512), 2)
        sgvn2 = v3(aX.bf16(512), 2)
        sgy2 = v3(aX.f32(512), 2)
        sgzs2 = v3(aX.f32(512), 2)
        aB = sub(BOFF, BSZ)
        qkT = v3(aB.f32(4 * T), 4)
        vaug = v4(aB.bf16(NT * 4 * 72), NT, 4)[:, :, :, 0:65]
        Gt = v3(aB.f32(NT * 16), NT)
        LFp = v3(aB.f32(144), 2)
        Ej = v3(aB.f32(144), 2)
        Ws = v3(aB.f32(144), 2)
        Eend = v3(aB.f32(144), 2)
        gtmp = v3(aB.f32(144), 2)
        aB = sub(BOFF, BSZ)
        dkT = v3(aB.bf16(4 * T), 4)
        dvaug = v4(aB.bf16(NT * 4 * 136), NT, 4)[:, :, :, 0:129]
        ysg = v3(aB.bf16(NT * 256), NT)
        aA = sub(AOFF, ASZ)
        hTall = v3(aA.bf16(8 * T), 8)
        xn_a = aA.f32(0) if False else None
        aA = sub(AOFF, ASZ)
        hTblk = v3(aA.bf16(8 * 512), 8)
        qTblk = v3(aA.bf16(4 * 512), 4)
        gates = v3(aA.bf16(4 * 1024), 4)
        mixT = v3(aA.bf16(8 * 512), 8)
        ptb = [aA.bf16(512) for _ in range(4)]
        oev = v4(aA.f32(4 * 2 * 132), 4, 2)[:, :, :, 0:129]
        att = aA.f32(128)
        xn5 = aA.f32(1024)
        aX5 = sub(XOFF + 4096, 4096)
        p5ropeC = aX5.f32(512)
        p5ropeS = aX5.f32(512)
        p5stA = aX5.f32(512)
        p5stB = aX5.f32(512)
        p5tmp = aX5.f32(512)
        p5t2 = aX5.f32(512)
        p5ml = aX5.f32(256)
        p5x_ml = arena_t[:, XOFF + 4096:XOFF + 4096 + 1024]
        p5x_sq = arena_t[:, XOFF + 4096 + 2048:XOFF + 4096 + 3072]
        p5sq = aX5.f32(256)

        for nm in ["xblk", "pst", "cacc", "mtmp", "sgt", "rope4", "p5x"]:
            S.region[nm] = "X"
        for nm in ["qkT", "qkpre", "vaug", "ktok", "Gt", "gder", "dkT", "dvaug", "ysg"]:
            S.region[nm] = "B"
        for nm in ["hTall", "p5"]:
            S.region[nm] = "A"

        def fence(region):
            S.fence(region, "pool", lambda h: h.memset(dummy[:, 0:1], 0.0))

        def mm(out, lhsT, rhs, start, stop, reads, writes):
            S.op("pe", lambda h: h.matmul(out, lhsT=lhsT, rhs=rhs, start=start, stop=stop), reads, writes)

        def tr(out, in_, ident, reads, writes):
            S.op("pe", lambda h: h.transpose(out, in_, ident), reads, writes)

        def act(out, in_, func, reads, writes, bias=None, scale=None, accum=None):
            kw = {}
            if accum is not None:
                kw["accum_out"] = accum
            if bias is not None:
                kw["bias"] = bias
            if scale is not None:
                kw["scale"] = scale
            S.op("act", lambda h: h.activation(out=out, in_=in_, func=func, **kw), reads, writes)

        def tt(eng, out, in0, in1, op, reads, writes):
            S.op(eng, lambda h: h.tensor_tensor(out=out, in0=in0, in1=in1, op=op), reads, writes)

        def ts(eng, out, in0, s1, s2, op0, op1, reads, writes):
            if s2 is None:
                S.op(eng, lambda h: h.tensor_scalar(out=out, in0=in0, scalar1=s1, scalar2=None, op0=op0), reads, writes)
            else:
                S.op(eng, lambda h: h.tensor_scalar(out=out, in0=in0, scalar1=s1, scalar2=s2, op0=op0, op1=op1), reads, writes)

        def stt(eng, out, in0, scalar, in1, op0, op1, reads, writes):
            S.op(eng, lambda h: h.scalar_tensor_tensor(out=out, in0=in0, scalar=scalar, in1=in1, op0=op0, op1=op1), reads, writes)

        def cp(eng, out, in_, reads, writes):
            if eng == "act":
                S.op(eng, lambda h: h.activation(out=out, in_=in_, func=AF.Copy), reads, writes)
            else:
                S.op(eng, lambda h: h.tensor_copy(out=out, in_=in_), reads, writes)

        def red(out, in_, reads, writes):
            S.op("dve", lambda h: h.tensor_reduce(out=out, in_=in_, axis=AX.X, op=ALU.add), reads, writes)

        def dma(out, in_, reads, writes, slow=False):
            if slow:
                S.dma("sp", lambda h: h.dma_start(out=out, in_=in_, allow_slow_non_contiguous=True), reads, writes)
            else:
                S.dma("sp", lambda h: h.dma_start(out=out, in_=in_), reads, writes)

        def tap(name, ap, key, dtype=F32):
            if dbg is None or name not in dbg:
                return
            shp = list(ap.shape)
            d = nc.dram_tensor("dbg_" + name, shp, dtype, kind="ExternalOutput").ap()
            dbg_out[name] = shp
            dma(d, ap, list(key) if isinstance(key, list) else [key], ["dbg_" + name])

        def stats(s1, s2, k, n, tag):
            rk1 = tag[0] if isinstance(tag, tuple) else [tag + "s1"]
            rk2 = tag[1] if isinstance(tag, tuple) else [tag + "s2"]
            mean = small[:, 0:k]
            msq = small[:, 16:16 + k]
            var = small[:, 32:32 + k]
            rstd = small[:, 48:48 + k]
            ts("dve", mean, s1, 1.0 / n, None, ALU.mult, None, rk1, ["st_mean"])
            tt("dve", msq, mean, mean, ALU.mult, ["st_mean"], ["st_msq"])
            stt("dve", var, s2, 1.0 / n, msq, ALU.mult, ALU.subtract, rk2 + ["st_msq"], ["st_var"])
            ts("dve", var, var, EPS, None, ALU.add, None, ["st_var"], ["st_var"])
            tt("pool", rstd, var, neghalf[:, 0:k], ALU.pow, ["st_var", "cst"], ["st_rstd"])
            return mean, rstd

        def chk(name):
            if stop == name:
                raise _Stop()

        try:
            dma(cstF, cst_d[:, :], [], ["cst"])
            cp("dve", identB, identF, ["cst"], ["cstb"])
            cp("dve", maskF, triF, ["cst"], ["cstb"])
            cp("dve", maskB, triB, ["cst"], ["cstb"])
            S.op("pool", lambda h: h.memset(neghalf, -0.5), [], ["cst"])

            S.op("pool", lambda h: h.memset(fmb[:, :, :], 0.0), [], ["fmb"])
            for l in range(DEPTH):
                for j in range(4):
                    dma(fmb[:, l, j:j + 1], bin_d[l, j * 128:(j + 1) * 128].rearrange("(p o) -> p o", o=1), [], ["fmb"])
                for h4 in range(4):
                    dma(fmb[:, l, 4 + 2 * h4:5 + 2 * h4], bin_d[l, 1808 + h4 * 128:1808 + (h4 + 1) * 128].rearrange("(p o) -> p o", o=1), [], ["fmb"])
                    dma(fmb[:, l, 12 + 2 * h4:13 + 2 * h4], bin_d[l, 1296 + h4 * 128:1296 + (h4 + 1) * 128].rearrange("(p o) -> p o", o=1), [], ["fmb"])
                for j in range(3):
                    dma(convw[:, l, j, :], convw_d[l, j, :].rearrange("(c p) -> p c", p=128), [], ["convp"], slow=True)
                dma(convb[:, l, :], convb_d[l, :].rearrange("(c p) -> p c", p=128), [], ["convp"], slow=True)
                dma(sgbs[:, l, :], sgbs_d[l, :, :].rearrange("g p -> p g"), [], ["sgbs"], slow=True)
            ts("dve", hconvb[:, :, :], convb[:, :, :], 0.5, None, ALU.mult, None, ["convp"], ["hconvb"])
            for l in range(DEPTH):
                pso = ps_g[0][:, 0:16]
                mm(pso, rperm, fmb[:, l, 4:20], True, True, ["cst", "fmb"], ["g0"])
                src = v3(pso, 8)[:, :, 0:1]
                dst = v3(fmb[:, l, 4:20], 8)[:, :, 1:2]
                cp("dve", dst, src, ["g0"], ["fmb"])
            for l in range(DEPTH):
                for g in range(4):
                    stg = small[:, 64:192]
                    dma(stg, sgw_d[l, g, :, :], [], ["sgstg"])
                    tr(ps_g[1][:, 0:128], stg, identF, ["sgstg", "cst"], ["g1"])
                    cp("dve", sgW[:, l, g, :], ps_g[1][:, 0:128], ["g1"], ["sgW"])
            lt = rstA
            fence("X")
            for i, (a_d, b_d) in enumerate([(lq1_d, lk1_d), (lq2_d, lk2_d)]):
                dma(lt[:, 0:256], a_d.rearrange("l k -> (l k)").partition_broadcast(128), [], [("rope4", "a")])
                dma(lt[:, 256:512], b_d.rearrange("l k -> (l k)").partition_broadcast(128), [], [("rope4", "b")])
                tt("dve", lt[:, 0:256], lt[:, 0:256], lt[:, 256:512], ALU.mult, [("rope4", "a"), ("rope4", "b")], [("rope4", "a")])
                red(small[:, 200 + 4 * i:204 + 4 * i], v3(lt[:, 0:256], 4), [("rope4", "a")], ["lam%d" % i])
                act(small[:, 200 + 4 * i:204 + 4 * i], small[:, 200 + 4 * i:204 + 4 * i], AF.Exp, ["lam%d" % i], ["lam%d" % i])
            tt("dve", lamv, small[:, 200:204], small[:, 204:208], ALU.subtract, ["lam0", "lam1"], ["lamv"])
            for l in range(DEPTH):
                lam_init = 0.8 - 0.6 * math.exp(-0.3 * l)
                ts("dve", lamv[:, l:l + 1], lamv[:, l:l + 1], lam_init, None, ALU.add, None, ["lamv"], ["lamv"])
            ts("dve", neglam, lamv, -1.0, None, ALU.mult, None, ["lamv"], ["neglam"])

            chk('consts')
            groups = [
                (C_MLQK, [(0, 0, 512, False)]),
                (C_MLV, [(0, 512, 256, False), (256, 1280, 16, False)]),
                (C_DAK, [(0, 1808, 128, False), (128, 1808, 128, True), (256, 1936, 128, False), (384, 1936, 128, True)]),
                (C_DAK + 512, [(0, 2064, 128, False), (128, 2064, 128, True), (256, 2192, 128, False), (384, 2192, 128, True)]),
                (C_DAV, [(0, 2320, 512, False)]),
                (C_SG, [(0, 3344, 512, False)]),
                (C_SG + 512, [(0, 3856, 256, False)]),
                (C_DAQ, [(0, 1296, 128, False), (128, 1296, 128, True), (256, 1424, 128, False), (384, 1424, 128, True)]),
                (C_DAQ + 512, [(0, 1552, 128, False), (128, 1552, 128, True), (256, 1680, 128, False), (384, 1680, 128, True)]),
                (C_MLOZ, [(0, 768, 512, False)]),
                (C_DAZ, [(0, 2832, 512, False)]),
            ]
            fence("X")
            fence("A")
            stg32 = [v3(sub(XOFF, XSZ).f32(4096), 8), v3(sub(XOFF + 4096, 4096).f32(4096), 8)]
            aA = sub(AOFF, ASZ)
            stg16 = [v3(aA.bf16(4096), 8), v3(aA.bf16(4096), 8)]
            S.region["stg32"] = "X"
            S.region["stg16"] = "A"
            gi = 0
            prev_store = [[], []]
            ceng = ["dve", "pool", "act"]

            def castcp(i, out, in_, reads, writes):
                e = ceng[i % 3]
                if e == "act":
                    act(out, in_, AF.Copy, reads, writes)
                else:
                    cp(e, out, in_, reads, writes)

            for l in range(DEPTH):
                for (dst0, pieces) in groups:
                    sl = gi % 2
                    wtot = max(p[0] + p[2] for p in pieces)
                    for (doff, src0, w, sw) in pieces:
                        dma(stg32[sl][:, :, doff:doff + w], win_d[l, :, src0:src0 + w].rearrange("(kc p) w -> p kc w", p=128), prev_store[sl], [("stg32", sl, doff)])
                    for pi, (doff, src0, w, sw) in enumerate(pieces):
                        if not sw:
                            castcp(gi + pi, stg16[sl][:, :, doff:doff + w], stg32[sl][:, :, doff:doff + w], [("stg32", sl, doff)], [("stg16", sl, doff)])
                        else:
                            i5 = stg32[sl][:, :, doff:doff + w].rearrange("p k (b t s) -> p k b t s", b=4, t=2)
                            o5 = stg16[sl][:, :, doff:doff + w].rearrange("p k (b t s) -> p k b t s", b=4, t=2)
                            for kc in range(8):
                                cp("pool" if kc % 2 else "dve", o5[:, kc, :, 0, :], i5[:, kc, :, 1, :], [("stg32", sl, doff)], [("stg16", sl, doff, kc, 0)])
                                cp("dve" if kc % 2 else "pool", o5[:, kc, :, 1, :], i5[:, kc, :, 0, :], [("stg32", sl, doff)], [("stg16", sl, doff, kc, 1)])
                    rk = []
                    for (doff, src0, w, sw) in pieces:
                        if sw:
                            rk += [("stg16", sl, doff, kc, t) for kc in range(8) for t in range(2)]
                        else:
                            rk.append(("stg16", sl, doff))
                    dma(wbf_d[l, :, dst0:dst0 + wtot].rearrange("(kc p) w -> p kc w", p=128), stg16[sl][:, :, 0:wtot], rk, [("wbf", l, dst0)])
                    prev_store[sl] = [("wbf", l, dst0)]
                    gi += 1
                for hf in range(2):
                    sl = gi % 2
                    dma(stg32[sl][:, :, :], wout_d[l, :, hf * 512:(hf + 1) * 512].rearrange("(kc p) w -> p kc w", p=128), prev_store[sl], [("stg32", sl, 0)])
                    castcp(gi, stg16[sl][:, :, :], stg32[sl][:, :, :], [("stg32", sl, 0)], [("stg16", sl, 0)])
                    dma(wobf_d[l, :, hf * 512:(hf + 1) * 512].rearrange("(kc p) w -> p kc w", p=128), stg16[sl][:, :, :], [("stg16", sl, 0)], [("wobf", l, hf)])
                    prev_store[sl] = [("wobf", l, hf)]
                    gi += 1

            chk('conv')
            csT = v3(small[:, 64:64 + 64], 8)
            for r in range(NB + 1):
                src = c_d[r, :] if r < NB else cctx_d[0, :]
                dma(csT[:, :, r:r + 1], src.rearrange("(kc p o) -> p kc o", p=128, o=1), ["sgW"], [("csT", r)], slow=True)
            NR = NB + 1
            cs_r = [("csT", r) for r in range(NR)]
            tnh = v3(small[:, 128:192], 8)
            act(tnh[:, :, 0:NR], csT[:, :, 0:NR], AF.Tanh, cs_r, ["cs_t"], scale=0.5)
            ts("dve", tnh[:, :, 0:NR], tnh[:, :, 0:NR], 0.5, 0.5, ALU.mult, ALU.add, ["cs_t"], ["cs_t"])
            tt("dve", csT[:, :, 0:NR], csT[:, :, 0:NR], tnh[:, :, 0:NR], ALU.mult, cs_r + ["cs_t"], ["csS"])
            mrow = sub(AOFF, ASZ).f32(1024)
            S.region["mrow"] = "A"
            wi = 0
            for l in range(NL):
                for cb in range(6):
                    sl = wi % 2
                    dma(stg32[sl][:, :, :], wmod_d[l, :, cb * 512:(cb + 1) * 512].rearrange("(kc p) w -> p kc w", p=128), prev_store[sl], [("stg32", sl, 0)])
                    pg = ps_g[2 + (wi % 2)]
                    for kc in range(8):
                        mm(pg[0:NR, :], csT[:, kc, 0:NR], stg32[sl][:, kc, :], kc == 0, kc == 7, ["csS", ("stg32", sl, 0)], ["g%d" % (2 + wi % 2)])
                    bm = mrow[0:NR, 512:1024]
                    dma(bm, bmod_d[l, cb * 512:(cb + 1) * 512].partition_broadcast(NR), [], [("mrow", "b")])
                    tt("dve", mrow[0:NR, 0:512], pg[0:NR, :], bm, ALU.add, ["g%d" % (2 + wi % 2), ("mrow", "b")], [("mrow", "o")])
                    dma(mod_d[l, 0:NR, cb * 512:(cb + 1) * 512], mrow[0:NR, 0:512], [("mrow", "o")], [("mod", l)])
                    wi += 1

            chk('mod')
            def wload(l, col0, ncols, slot, src=None):
                srcd = wbf_d if src is None else src
                key = ("wbf", l, col0) if src is None else ("wobf", l, col0 // 512)
                rk = [("wbf", l, g[0]) for g in groups] if src is None else [key]
                dma(wring[slot][:, :, 0:ncols], srcd[l, :, col0:col0 + ncols].rearrange("(kc p) w -> p kc w", p=128), rk, [("wring", slot)])

            ring_ctr = [0]

            def next_slot():
                s_ = ring_ctr[0] % 2
                ring_ctr[0] += 1
                return s_

            g_ctr = [0]

            def next_g(lo=0, hi=5):
                i = lo + g_ctr[0] % (hi - lo)
                g_ctr[0] += 1
                return i

            def ln_block(xb, ntile, xkey, xnbuf, xnkey, hT_of, hkey_of, s1p, shp, modkey, hTf=None, hfkey=None, post=None):
                s1 = small[:, 224:224 + ntile]
                s2 = small[:, 232:232 + ntile]
                k1 = [("lnst", 1, ti) for ti in range(ntile)]
                k2 = [("lnst", 2, ti) for ti in range(ntile)]
                xk = xkey if callable(xkey) else (lambda ti: xkey)
                for ti in range(ntile):
                    act(xnbuf, xb[:, ti, :], AF.Square, [xk(ti)], [xnkey, k2[ti]], accum=s2[:, ti:ti + 1])
                    act(xnbuf, xb[:, ti, :], AF.Identity, [xk(ti)], [xnkey, k1[ti]], accum=s1[:, ti:ti + 1])
                mean, rstd = stats(s1, s2, ntile, 1024.0, (k1, k2))
                for ti in range(ntile):
                    ts("dve", xnbuf, xb[:, ti, :], mean[:, ti:ti + 1], rstd[:, ti:ti + 1], ALU.subtract, ALU.mult, [xk(ti), "st_mean", "st_rstd"], [xnkey])
                    for kc in range(8):
                        tr(ps_tp[:, kc * 128:(kc + 1) * 128], xnbuf[:, kc * 128:(kc + 1) * 128], identF, [xnkey, "cst"], ["tpa", "tpb"])
                    tmpm = v3(xnbuf, 8)
                    tt("dve", tmpm, v3(ps_tp[:, :], 8), bc(s1p.unsqueeze(2), [128, 8, 128]), ALU.mult, ["tpa", "tpb"] + modkey, [xnkey])
                    if hTf is None:
                        tt("pool", hT_of(ti), tmpm, bc(shp.unsqueeze(2), [128, 8, 128]), ALU.add, [xnkey] + modkey, [hkey_of(ti)])
                    else:
                        tt("pool", hTf(ti), tmpm, bc(shp.unsqueeze(2), [128, 8, 128]), ALU.add, [xnkey] + modkey, [hfkey(ti)])
                        cp("act", hT_of(ti), hTf(ti), [hfkey(ti)], [hkey_of(ti)])
                    if post is not None:
                        post(ti)

            for b in range(NB):
                for l in range(NL):
                    lam_init = 0.8 - 0.6 * math.exp(-0.3 * l)
                    pk = ("par", b, l)
                    for (off, s0, w) in [(B_MLV, 512, 256), (B_MLV + 256, 1280, 16), (B_DAV, 2320, 512), (B_SG, 3344, 768), (B_MLOZ, 768, 512), (B_DAZ, 2832, 512)]:
                        dma(biasb[:, off:off + w], bin_d[l, s0:s0 + w].partition_broadcast(128), [], [("biasb", off)])
                    dma(lng, lng_d[l, :].partition_broadcast(128), [], ["lng"])
                    dma(lnb, lnb_d[l, :].partition_broadcast(128), [], ["lnb"])
                    dma(mlgc, mlg_d[l, :].partition_broadcast(128), [], ["mlgc"])
                    dma(dagc, dag_d[l, :].partition_broadcast(128), [], ["dagc"])
                    dma(sggt, sgg_d[l, :].partition_broadcast(128), [], ["sggt"])
                    dma(sgbt, sgb_d[l, :].partition_broadcast(128), [], ["sgbt"])
                    ts("dve", mlgc, mlgc, 0.5, None, ALU.mult, None, ["mlgc"], ["mlgc"])
                    ts("dve", dagc, dagc, 0.5 * (1.0 - lam_init), None, ALU.mult, None, ["dagc"], ["dagc"])
                    dma(gate_l, mod_d[l, b, 2048:3072].partition_broadcast(128), [("mod", l)], ["gate_l"])
                    dma(gate_c, mod_d[l, NB, 2048:3072].partition_broadcast(128), [("mod", l)], ["gate_c"])
                    for i, (row, c0) in enumerate([(b, 0), (b, 1024), (NB, 0), (NB, 1024)]):
                        dma(modp[:, i, :], mod_d[l, row, c0:c0 + 1024].rearrange("(kc p) -> p kc", p=128), [("mod", l)], [("modp", i)], slow=True)
                    for i in (1, 3):
                        ts("dve", modp[:, i, :], modp[:, i, :], 1.0, None, ALU.add, None, [("modp", i)], [("modp", i)])
                    modk = [("modp", i) for i in range(4)]

                    fence("X")
                    fence("A")
                    fence("B")
                    dma(wq32, win_d[l, :, 0:512].rearrange("(kc p) w -> p kc w", p=128), [], [("wring", 0), ("wring", 1)])
                    dma(wg32, win_d[l, :, 1280:1296].rearrange("(kc p) w -> p kc w", p=128), [], ["wg32"])
                    hTf2 = xblk[1][:, 1:3, :].rearrange("p a (k t) -> p k (a t)", k=8) if False else None
                    hTfbuf = v3(arena_t[:, XOFF + 4096 + 1024:XOFF + 4096 + 3072], 8)
                    for bi, (t0, t1) in enumerate(BLKS):
                        ntile = (t1 - t0) // 128
                        xb = xblk[0]
                        if l == 0:
                            srcx = ctx_d[b, :, :] if bi == 0 else x_d[b, t0 - 256:t1 - 256, :]
                            rk = []
                        else:
                            srcx = xs_d[b, t0:t1, :]
                            rk = [("xs", b, bi, ti_) for ti_ in range((t1 - t0) // 128)]
                        srcx3 = srcx.rearrange("(n p) d -> p n d", p=128)
                        for ti_ in range(ntile):
                            dma(xb[:, ti_, :], srcx3[:, ti_, :], rk, [("xblk", 0, ti_)])
                        s1p, shp = (modp[:, 3, :], modp[:, 2, :]) if bi == 0 else (modp[:, 1, :], modp[:, 0, :])

                        def post1(ti, t0=t0):
                            tk = t0 + ti * 128
                            tg = tk // 128
                            hTf_t = hTfbuf[:, :, (ti % 2) * 128:(ti % 2) * 128 + 128]
                            gg = 2 + tg % 2
                            for kc in range(8):
                                mm(ps_g[gg][:, 0:16], hTf_t[:, kc, :], wg32[:, kc, :], kc == 0, kc == 7, ["wg32", ("xblk", 2, ti % 2)], ["g%d" % gg])
                            tt("dve", Gt[:, tg, :], ps_g[gg][:, 0:16], biasb[:, B_MLV + 256:B_MLV + 272], ALU.add,
                               ["g%d" % gg, ("biasb", B_MLV + 256)], [("Gt", tg)])
                            if ti % 2 == 0:
                                return
                            tk0 = tk - 128
                            for c in range(4):
                                pq = ps_g[c // 2][:, (c % 2) * 256:(c % 2) * 256 + 256]
                                for kc in range(8):
                                    mm(pq, wq32[:, kc, c * 128:(c + 1) * 128], hTfbuf[:, kc, :], kc == 0, kc == 7,
                                       [("wring", 0), ("wring", 1), ("xblk", 2, 0), ("xblk", 2, 1)], ["g%d" % (c // 2)])
                            for hb in range(2):
                                tt("dve", qkT[:, 2 * hb:2 * hb + 2, tk0:tk0 + 256], v3(ps_g[hb][:, :], 2), bc(fmb[:, l, 2 * hb:2 * hb + 2].unsqueeze(2), [128, 2, 256]), ALU.add,
                                   ["g%d" % hb, "fmb"], [("qkpre", tg - 1, hb), ("qkpre", tg, hb)])

                        ln_block(xb, ntile, lambda ti: ("xblk", 0, ti), xblk[1][:, 0, :], ("xblk", 1),
                                 lambda ti, t0=t0: hTall[:, :, t0 + ti * 128:t0 + (ti + 1) * 128], lambda ti, t0=t0: ("hTall", t0 // 128 + ti),
                                 s1p, shp, modk, hTf=lambda ti: hTfbuf[:, :, (ti % 2) * 128:(ti % 2) * 128 + 128], hfkey=lambda ti: ("xblk", 2, ti % 2), post=post1)
                    if dbg:
                        tap("hT", hTall[:, :, :], [("hTall", i) for i in range(NT)], BF16)

                    chk('ph1')
                    hT_keys = [("hTall", i) for i in range(NT)]
                    slot = next_slot()
                    wload(l, C_MLV, 272, slot)
                    S.op("pool", lambda h: h.memset(vaug[:, :, :, 64:65], 1.0), [], [("vaug", "ones")])
                    for ti in range(NT):
                        gix = next_g()
                        pg = ps_g[gix]
                        for kc in range(8):
                            mm(pg[:, 0:272], hTall[:, kc, ti * 128:(ti + 1) * 128], wring[slot][:, kc, 0:272], kc == 0, kc == 7,
                               [("wring", slot), hT_keys[ti]], ["g%d" % gix])
                        tt("dve", vaug[:, ti, :, 0:64], v3(pg[:, 0:256], 4), v3(biasb[:, B_MLV:B_MLV + 256], 4), ALU.add,
                           ["g%d" % gix, ("biasb", B_MLV)], [("vaug", ti)])
                    Gk = [("Gt", ti) for ti in range(NT)]
                    for d_ in range(2):
                        fsl = Gt[:, :, 4 + 8 * d_:8 + 8 * d_]
                        act(v3(gtmp[:, d_, :], NT), fsl, AF.Exp, Gk, [("gder", "t", d_)], scale=-1.0)
                        act(LFp[:, d_, :], gtmp[:, d_, :], AF.Ln, [("gder", "t", d_)], [("gder", "L", d_)], bias=1.0)
                    chk('g1')
                    pc = ps_g[0]
                    mm(pc[:, 0:72], triF, LFp[:, 0, :], True, True, ["cst", ("gder", "L", 0)], ["g0"])
                    mm(pc[:, 72:144], triB, LFp[:, 1, :], True, True, ["cst", ("gder", "L", 1)], ["g0"])
                    mm(pc[:, 144:288], onesF, LFp[:, :, :].rearrange("p a b -> p (a b)"), True, True, ["cst", ("gder", "L", 0), ("gder", "L", 1)], ["g0"])
                    chk('g2')
                    act(Ej[:, :, :].rearrange("p a b -> p (a b)"), pc[:, 0:144], AF.Exp, ["g0"], [("gder", "Ej")], scale=-1.0)
                    act(Eend[:, :, :].rearrange("p a b -> p (a b)"), pc[:, 144:288], AF.Exp, ["g0"], [("gder", "Eend")], scale=-1.0)
                    if dbg and 'cum' in dbg:
                        cp('dve', sgp[:, 0:288], pc[:, 0:288], ['g0'], [('mtmp', 'dbgc')])
                        tap('cum', sgp[:, 0:288], [('mtmp', 'dbgc')])
                        tap('LFp', LFp[:, :, :], [('gder', 'L', 0), ('gder', 'L', 1)])
                    chk('g3')
                    for d_ in range(2):
                        tt("dve", v3(gtmp[:, d_, :], NT), v3(pc[:, 72 * d_:72 * d_ + 72], NT), Gt[:, :, 8 * d_:8 * d_ + 4], ALU.add,
                           ["g0"] + Gk, [("gder", "t2", d_)])
                        act(Ws[:, d_, :], gtmp[:, d_, :], AF.Exp, [("gder", "t2", d_)], [("gder", "Ws", d_)])
                    pre_all = [("qkpre", ti, hb) for ti in range(NT) for hb in range(2)]
                    for c in range(4):
                        pst = qkT[:, c, :]
                        w0, w1, w2 = (convw[:, l, j, c:c + 1] for j in range(3))
                        ts("dve", cacc, pst, w1, convb[:, l, c:c + 1], ALU.mult, ALU.add, pre_all + ["convp"], ["cacc"])
                        for (a, e) in [(0, CTXL), (CTXL, T)]:
                            stt("dve", cacc[:, a + 1:e], pst[:, a:e - 1], w0, cacc[:, a + 1:e], ALU.mult, ALU.add, pre_all + ["convp", "cacc"], ["cacc"])
                            stt("dve", cacc[:, a:e - 1], pst[:, a + 1:e], w2, cacc[:, a:e - 1], ALU.mult, ALU.add, pre_all + ["convp", "cacc"], ["cacc"])
                        act(pst, cacc, AF.Tanh, ["cacc"] + pre_all, [("qkT", c)], scale=0.5)
                        sc_ = 0.5 if c < 2 else 0.0625
                        ts("pool", pst, pst, sc_, sc_, ALU.mult, ALU.add, [("qkT", c)], [("qkT", c)])
                        tt("dve", pst, pst, cacc, ALU.mult, [("qkT", c), "cacc"], [("qkT", c)])
                    if dbg:
                        tap("qkT", qkT[:, :, :], [("qkT", i) for i in range(4)], F32)
                        tap("vaug", vaug[:, :, :, :], [("vaug", i) for i in range(NT)] + [("vaug", "ones")], BF16)
                        tap("Gt", Gt[:, :, :], [("Gt", i) for i in range(NT)])

                    chk('ph2')
                    fence("X")
                    chk('ph3a')
                    orders = [list(range(NT)), [1, 0] + list(range(NT - 1, 1, -1))]
                    written = set()
                    S.op("pool", lambda h: h.memset(m_cst_flat, 0.0), [], [("mtmp", "cst", 0), ("mtmp", "cst", 1)])
                    for d_ in range(2):
                        S.op("pool", lambda h, d_=d_: h.memset(m_vsf[d_], 0.0), [], [("mtmp", "vs", d_)])
                    for step in range(NT):
                        for d_ in range(2):
                            ti = orders[d_][step]
                            first = step == 0
                            tks = slice(ti * 128, (ti + 1) * 128)
                            for h4 in range(4):
                                pr = slice((h4 % 2) * 64, (h4 % 2) * 64 + 64)
                                gS = 1 + (h4 % 2)
                                mm(ps_g[gS][:, (h4 // 2) * 128:(h4 // 2 + 1) * 128], qkT[pr, 2 + h4 // 2, tks], qkT[pr, h4 // 2, tks], True, True,
                                   [("qkT", 2 + h4 // 2), ("qkT", h4 // 2)], ["g%d" % gS])
                            if step == 0 and d_ == 0: chk('sa')
                            mk = triF if d_ == 0 else triB
                            for j in range(2):
                                tr(ps_tp[:, j * 128:(j + 1) * 128], qkT[:, 2 + j, tks], identF, [("qkT", 2 + j), "cst"], ["tpa"])
                            cp("act", m_kt[d_], ps_tp[:, 0:256], ["tpa"], [("mtmp", "kt", d_)])
                            for par in range(2):
                                tt("dve", m_pt[d_][:, par * 2:par * 2 + 2, :], v3(ps_g[1 + par][:, 0:256], 2), bc(mk.unsqueeze(1), [128, 2, 128]), ALU.mult,
                                   ["g%d" % (1 + par), "cst"], [("mtmp", "pt", d_, par)])
                            wsl = v3(Ws[:, d_, :], NT)[:, ti, :]
                            tt("pool", m_vs[d_], vaug[:, ti, :, :], bc(wsl.unsqueeze(2), [128, 4, 65]), ALU.mult,
                               [("vaug", ti), ("vaug", "ones"), ("gder", "Ws", d_)], [("mtmp", "vs", d_)])
                            if step == 0 and d_ == 0: chk('sb')
                            gH = 3
                            pH = ps_g[gH]
                            pD = ps_g[4]
                            for h4 in range(4):
                                pr = slice((h4 % 2) * 64, (h4 % 2) * 64 + 64)
                                mm(pH[:, h4 * 72:h4 * 72 + 65], m_pt[d_][:, (h4 % 2) * 2 + h4 // 2, :], m_vs[d_][:, h4, :], True, first,
                                   [("mtmp", "pt", d_, h4 % 2), ("mtmp", "vs", d_)], ["g3"])
                                if not first:
                                    mm(pH[:, h4 * 72:h4 * 72 + 65], qkT[pr, h4 // 2, tks], m_cst[pr, d_, h4, :], False, True,
                                       [("qkT", h4 // 2), ("mtmp", "cst", d_)], ["g3"])
                            for pp in range(2):
                                mm(pD[:, pp * 144:pp * 144 + 144], m_kt[d_][:, pp * 128:pp * 128 + 128], m_vsf[d_][:, pp * 144:pp * 144 + 144], True, True,
                                   [("mtmp", "kt", d_), ("mtmp", "vs", d_)], ["g4"])
                            if step == 0 and d_ == 0: chk('sc')
                            ejs = v3(Ej[:, d_, :], NT)[:, ti, :]
                            tt("dve", m_hs[d_], v3(pH[:, 0:288], 4)[:, :, 0:65], bc(ejs.unsqueeze(2), [128, 4, 65]), ALU.mult, ["g3", ("gder", "Ej")], [("mtmp", "hs", d_)])
                            if step == 0 and d_ == 0: chk('sd')
                            den = m_hs[d_][:, :, 64]
                            d2 = small[:, 208:212]
                            tt("dve", d2, den, den, ALU.mult, [("mtmp", "hs", d_)], ["md2"])
                            ts("dve", d2, d2, 1.0, None, ALU.max, None, ["md2"], ["md2"])
                            tt("pool", d2, d2, neghalf[:, 0:4], ALU.pow, ["md2", "cst"], ["md2"])
                            if ti not in written:
                                tt("dve", v3(Hml[:, ti, :], 4), m_hs[d_][:, :, 0:64], bc(d2.unsqueeze(2), [128, 4, 64]), ALU.mult,
                                   [("mtmp", "hs", d_), "md2"], [("Hml", ti)])
                                written.add(ti)
                            else:
                                tt("dve", m_hn, m_hs[d_][:, :, 0:64], bc(d2.unsqueeze(2), [128, 4, 64]), ALU.mult,
                                   [("mtmp", "hs", d_), "md2"], [("mtmp", "hn")])
                                tt("pool", v3(Hml[:, ti, :], 4), v3(Hml[:, ti, :], 4), m_hn, ALU.add, [("mtmp", "hn"), ("Hml", ti)], [("Hml", ti)])
                            if step == 0 and d_ == 0: chk('se')
                            tt("dve", m_tmp, v3(pD[:, 0:288], 4)[:, :, 0:65], m_cst[:, d_, :, :], ALU.add, ["g4", ("mtmp", "cst", d_)], [("mtmp", "tmp")])
                            ees = v3(Eend[:, d_, :], NT)[:, ti, :]
                            tt("dve", m_cst[:, d_, :, :], m_tmp, bc(ees.unsqueeze(2), [128, 4, 65]), ALU.mult, [("mtmp", "tmp"), ("gder", "Eend")], [("mtmp", "cst", d_)])
                    chk('sf') if False else None
                    if dbg:
                        tap("Hml", Hml[:, :, :], [("Hml", i) for i in range(NT)])

                    chk('ph3')
                    fence("X")
                    fence("B")
                    S.op("pool", lambda h: h.memset(dvaug[:, :, :, 128:129], 1.0), [], [("dvaug", "ones")])
                    for hh in range(2):
                        slot = next_slot()
                        wload(l, C_DAK + hh * 512, 512, slot)
                        for h2 in range(2):
                            h4 = hh * 2 + h2
                            for bi, (t0, t1) in enumerate(BLKS):
                                n = t1 - t0
                                ga, gb = [(1, 2), (3, 4)][bi % 2]
                                for kc in range(8):
                                    mm(ps_g[ga][:, 0:n], wring[slot][:, kc, h2 * 256:h2 * 256 + 128], hTall[:, kc, t0:t1], kc == 0, kc == 7,
                                       [("wring", slot)] + hT_keys[t0 // 128:t1 // 128], ["g%d" % ga])
                                if bi == 0:
                                    act(dkT[:, h4, t0:t1], ps_g[ga][:, 0:n], AF.Identity, ["g%d" % ga, "fmb"], [("dkT", h4, bi)], bias=fmb[:, l, 4 + 2 * h4:5 + 2 * h4])
                                    continue
                                for kc in range(8):
                                    mm(ps_g[gb][:, 0:n], wring[slot][:, kc, h2 * 256 + 128:h2 * 256 + 256], hTall[:, kc, t0:t1], kc == 0, kc == 7,
                                       [("wring", slot)] + hT_keys[t0 // 128:t1 // 128], ["g%d" % gb])
                                p0 = t0 - 256
                                dma(ropeC[:, 0:n], rope_d[0, :, p0:p0 + n], [], [("rope4", "c")])
                                dma(ropeS[:, 0:n], rope_d[1, :, p0:p0 + n], [], [("rope4", "s")])
                                stt("dve", rstA[:, 0:n], ps_g[ga][:, 0:n], fmb[:, l, 4 + 2 * h4:5 + 2 * h4], ropeC[:, 0:n], ALU.add, ALU.mult,
                                    ["g%d" % ga, "fmb", ("rope4", "c")], [("rope4", "a")])
                                stt("dve", rstB[:, 0:n], ps_g[gb][:, 0:n], fmb[:, l, 5 + 2 * h4:6 + 2 * h4], ropeS[:, 0:n], ALU.add, ALU.mult,
                                    ["g%d" % gb, "fmb", ("rope4", "s")], [("rope4", "b")])
                                tt("pool", dkT[:, h4, t0:t1], rstA[:, 0:n], rstB[:, 0:n], ALU.add, [("rope4", "a"), ("rope4", "b")], [("dkT", h4, bi)])
                    slot = next_slot()
                    wload(l, C_DAV, 512, slot)
                    for ti in range(NT):
                        gix = next_g(3, 5)
                        pg = ps_g[gix]
                        for kc in range(8):
                            mm(pg[:, :], hTall[:, kc, ti * 128:(ti + 1) * 128], wring[slot][:, kc, :], kc == 0, kc == 7,
                               [("wring", slot), hT_keys[ti]], ["g%d" % gix])
                        tt("dve", dvaug[:, ti, :, 0:128], v3(pg[:, :], 4), v3(biasb[:, B_DAV:B_DAV + 512], 4), ALU.add,
                           ["g%d" % gix, ("biasb", B_DAV)], [("dvaug", ti)])
                    s_a = next_slot()
                    wload(l, C_SG, 512, s_a)
                    s_b = next_slot()
                    wload(l, C_SG + 512, 256, s_b)
                    for tp_ in range(NT // 2):
                        for j in range(2):
                            ti = 2 * tp_ + j
                            ga = 1 if j == 0 else 3
                            for kc in range(8):
                                mm(ps_g[ga][:, :], hTall[:, kc, ti * 128:(ti + 1) * 128], wring[s_a][:, kc, :], kc == 0, kc == 7,
                                   [("wring", s_a), hT_keys[ti]], ["g%d" % ga])
                            for kc in range(8):
                                mm(ps_g[2][:, j * 256:(j + 1) * 256], hTall[:, kc, ti * 128:(ti + 1) * 128], wring[s_b][:, kc, 0:256], kc == 0, kc == 7,
                                   [("wring", s_b), hT_keys[ti]], ["g2"])
                            tt("dve", sgp2[:, j, 0:512], ps_g[ga][:, :], biasb[:, B_SG:B_SG + 512], ALU.add, ["g%d" % ga, ("biasb", B_SG)], [("sgt", "p", j)])
                        tt("dve", sgp2[:, :, 512:768], v3(ps_g[2][:, :], 2), bc(biasb[:, B_SG + 512:B_SG + 768].unsqueeze(1), [128, 2, 256]), ALU.add,
                           ["g2", ("biasb", B_SG)], [("sgt", "pz")])
                        kp = [("sgt", "p", 0), ("sgt", "p", 1)]
                        act(sgu2, sgp2[:, :, 0:256], AF.Gelu, kp, [("sgt", "u")])
                        for j in range(2):
                            act(sgp2[:, j, 256:512], sgp2[:, j, 256:512], AF.Gelu, [("sgt", "p", j)], [("sgt", "gv", j)], accum=small[:, 240 + j:241 + j])
                            act(sgjunk2[:, j, :], sgp2[:, j, 256:512], AF.Square, [("sgt", "gv", j)], [("sgt", "junk", j)], accum=small[:, 244 + j:245 + j])
                        act(sgzs2, sgp2[:, :, 512:768], AF.Tanh, [("sgt", "pz")], [("sgt", "zs")], scale=0.5)
                        kgv = [("sgt", "gv", 0), ("sgt", "gv", 1)]
                        kjk = [("sgt", "junk", 0), ("sgt", "junk", 1)]
                        mean, rstd = stats(small[:, 240:242], small[:, 244:246], 2, 256.0, (kgv, kjk))
                        for j in range(2):
                            ts("dve", sgjunk2[:, j, :], sgp2[:, j, 256:512], mean[:, j:j + 1], rstd[:, j:j + 1], ALU.subtract, ALU.mult,
                               [("sgt", "gv", j), "st_mean", "st_rstd"], [("sgt", "junk", j)])
                        tt("dve", sgjunk2, sgjunk2, bc(sggt.unsqueeze(1), [128, 2, 256]), ALU.mult, kjk + ["sggt"], kjk)
                        tt("dve", sgvn2, sgjunk2, bc(sgbt.unsqueeze(1), [128, 2, 256]), ALU.add, kjk + ["sgbt"], [("sgt", "vn")])
                        for j in range(2):
                            for g in range(4):
                                mm(ps_g[4][:, j * 256 + g * 64:j * 256 + (g + 1) * 64], sgW[:, l, g, :], sgvn2[:, j, g * 64:(g + 1) * 64], True, True,
                                   ["sgW", ("sgt", "vn")], ["g4"])
                        for j in range(2):
                            tt("dve", v3(sgy2[:, j, :], 4), v3(ps_g[4][:, j * 256:(j + 1) * 256], 4), bc(sgbs[:, l, :].unsqueeze(2), [128, 4, 64]), ALU.add,
                               ["g4", "sgbs"], [("sgt", "y", j)])
                        ky = [("sgt", "y", 0), ("sgt", "y", 1)]
                        tt("dve", sgy2, sgy2, sgu2, ALU.mult, ky + [("sgt", "u")], ky)
                        stt("dve", sgzs2, sgzs2, 1.0, sgp2[:, :, 512:768], ALU.add, ALU.mult, [("sgt", "zs"), ("sgt", "pz")], [("sgt", "zs")])
                        stt("dve", ysg[:, 2 * tp_:2 * tp_ + 2, :], sgy2, 0.5, sgzs2, ALU.mult, ALU.mult, ky + [("sgt", "zs")], [("ysg", 2 * tp_), ("ysg", 2 * tp_ + 1)])
                    if dbg:
                        tap("dkT", dkT[:, :, :], [("dkT", h_, b_) for h_ in range(4) for b_ in range(5)], BF16)
                        tap("ysg", ysg[:, :, :], [("ysg", i) for i in range(NT)], BF16)

                    chk('ph4')
                    fence("X")
                    fence("A")
                    last = (l == NL - 1)
                    for bi, (t0, t1) in enumerate(BLKS):
                        n = t1 - t0
                        ntile = n // 128
                        isctx = bi == 0
                        if last and isctx:
                            continue
                        xb = xblk[0]
                        if l == 0:
                            srcx = ctx_d[b, :, :] if isctx else x_d[b, t0 - 256:t1 - 256, :]
                            rk = []
                        else:
                            srcx = xs_d[b, t0:t1, :]
                            rk = [("xs", b, bi, ti_) for ti_ in range((t1 - t0) // 128)]
                        srcx3 = srcx.rearrange("(n p) d -> p n d", p=128)
                        for ti in range(ntile):
                            dma(xb[:, ti, :], srcx3[:, ti, :], rk, [("xblk", 0, ti)])
                        s1p, shp = (modp[:, 3, :], modp[:, 2, :]) if isctx else (modp[:, 1, :], modp[:, 0, :])
                        ln_block(xb, ntile, lambda ti: ("xblk", 0, ti), xn5, ("p5", "xn"), lambda ti: hTblk[:, :, ti * 128:(ti + 1) * 128], lambda ti: ("p5", "hT", ti), s1p, shp, modk)
                        hk = [("p5", "hT", ti) for ti in range(ntile)]
                        if not isctx:
                            p0 = t0 - 256
                            dma(p5ropeC[:, 0:n], rope_d[0, :, p0:p0 + n], [], [("p5x", "c")])
                            dma(p5ropeS[:, 0:n], rope_d[1, :, p0:p0 + n], [], [("p5x", "s")])
                        for hh in range(2):
                            slot = next_slot()
                            wload(l, C_DAQ + hh * 512, 512, slot)
                            for h2 in range(2):
                                h4 = hh * 2 + h2
                                ga, gb = 0, 1
                                for kc in range(8):
                                    mm(ps_g[ga][:, 0:n], wring[slot][:, kc, h2 * 256:h2 * 256 + 128], hTblk[:, kc, 0:n], kc == 0, kc == 7, [("wring", slot)] + hk, ["g%d" % ga])
                                if isctx:
                                    act(qTblk[:, h4, 0:n], ps_g[ga][:, 0:n], AF.Identity, ["g%d" % ga, "fmb"], [("p5", "qT", h4)], bias=fmb[:, l, 12 + 2 * h4:13 + 2 * h4])
                                    continue
                                for kc in range(8):
                                    mm(ps_g[gb][:, 0:n], wring[slot][:, kc, h2 * 256 + 128:h2 * 256 + 256], hTblk[:, kc, 0:n], kc == 0, kc == 7, [("wring", slot)] + hk, ["g%d" % gb])
                                stt("dve", p5stA[:, 0:n], ps_g[ga][:, 0:n], fmb[:, l, 12 + 2 * h4:13 + 2 * h4], p5ropeC[:, 0:n], ALU.add, ALU.mult,
                                    ["g%d" % ga, "fmb", ("p5x", "c")], [("p5x", "a")])
                                stt("dve", p5stB[:, 0:n], ps_g[gb][:, 0:n], fmb[:, l, 13 + 2 * h4:14 + 2 * h4], p5ropeS[:, 0:n], ALU.add, ALU.mult,
                                    ["g%d" % gb, "fmb", ("p5x", "s")], [("p5x", "b")])
                                tt("pool", qTblk[:, h4, 0:n], p5stA[:, 0:n], p5stB[:, 0:n], ALU.add, [("p5x", "a"), ("p5x", "b")], [("p5", "qT", h4)])
                        for gi_, (c0, boff) in enumerate([(C_MLOZ, B_MLOZ), (C_DAZ, B_DAZ)]):
                            slot = next_slot()
                            wload(l, c0, 512, slot)
                            for ti in range(ntile):
                                gix = 2 + ti % 2
                                pg = ps_g[gix]
                                for kc in range(8):
                                    mm(pg[:, :], hTblk[:, kc, ti * 128:(ti + 1) * 128], wring[slot][:, kc, :], kc == 0, kc == 7, [("wring", slot), hk[ti]], ["g%d" % gix])
                                tt("dve", p5tmp, pg[:, :], biasb[:, boff:boff + 512], ALU.add, ["g%d" % gix, ("biasb", boff)], [("p5x", "tmp")])
                                act(p5t2, p5tmp, AF.Tanh, [("p5x", "tmp")], [("p5x", "t2")], scale=0.5)
                                gdst = gates[:, ti, gi_ * 512:(gi_ + 1) * 512]
                                if gi_ == 0:
                                    ts("dve", gdst[:, 0:256], p5t2[:, 0:256], 0.5, 0.5, ALU.mult, ALU.add, [("p5x", "t2")], [("p5", "g", ti, 0)])
                                    stt("dve", gdst[:, 256:512], p5t2[:, 256:512], 1.0, p5tmp[:, 256:512], ALU.add, ALU.mult, [("p5x", "t2"), ("p5x", "tmp")], [("p5", "g", ti, 1)])
                                else:
                                    stt("dve", gdst, p5t2, 1.0, p5tmp, ALU.add, ALU.mult, [("p5x", "t2"), ("p5x", "tmp")], [("p5", "g", ti, 2)])
                        tg0 = t0 // 128
                        mlv = v3(p5x_ml, 4)[:, 0:ntile, :]
                        sqv = v3(p5x_sq, 4)[:, 0:ntile, :]
                        kml = [("p5x", "c"), ("p5x", "s")]
                        ksq = [("p5x", "tmp"), ("p5x", "t2")]
                        k4 = ntile * 4
                        tt("dve", mlv, gates[:, 0:ntile, 0:256], Hml[:, tg0:tg0 + ntile, :], ALU.mult,
                           [("p5", "g", ti, 0) for ti in range(ntile)] + [("Hml", tg0 + ti) for ti in range(ntile)], kml)
                        red(small[:, 64:64 + k4], mlv.rearrange("p a (h d) -> p (a h) d", h=4), kml, ["mls1"])
                        tt("dve", sqv, mlv, mlv, ALU.mult, kml, ksq)
                        red(small[:, 80:80 + k4], sqv.rearrange("p a (h d) -> p (a h) d", h=4), ksq, ["mls2"])
                        mean, rstd = stats(small[:, 64:64 + k4], small[:, 80:80 + k4], k4, 64.0, "ml")
                        ml3 = mlv.rearrange("p a (h d) -> p (a h) d", h=4)
                        tt("dve", ml3, ml3, bc(mean.unsqueeze(2), [128, k4, 64]), ALU.subtract, kml + ["st_mean"], kml)
                        tt("dve", ml3, ml3, bc(rstd.unsqueeze(2), [128, k4, 64]), ALU.mult, kml + ["st_rstd"], kml)
                        tt("dve", mlv, mlv, bc(mlgc.unsqueeze(1), [128, ntile, 256]), ALU.mult, kml + ["mlgc"], kml)
                        gml = gates[:, 0:ntile, 0:256]
                        tt("dve", gml, mlv, gates[:, 0:ntile, 256:512], ALU.mult, kml + [("p5", "g", ti, 1) for ti in range(ntile)] + [("p5", "g", ti, 0) for ti in range(ntile)], [("p5", "mixml")])
                        ktiles = list(range(0, 2)) if isctx else list(range(NT))
                        STB = [(ps_g[4], "g4"), (ps_tp[:, 0:512], "tpa"), (ps_tp[:, 512:1024], "tpb")]
                        nk = len(ktiles)
                        LOOK = 2

                        def combine(h4):
                            nt_ = ntile
                            ok_ = [("p5", "oev", qi, c) for qi in range(nt_) for c in range(2)]
                            rr = v3(small[:, 192:200], 4)[:, 0:nt_, :]
                            attv = v3(p5stA, 4)[:, 0:nt_, :]
                            tmpv = v3(p5stB, 4)[:, 0:nt_, :]
                            ssv = small[:, 200:200 + nt_]
                            S.op("dve", lambda h, rr=rr, nt_=nt_: h.reciprocal(out=rr, in_=oev[:, 0:nt_, :, 128]), ok_, ["p5rr"])
                            ts("dve", rr[:, :, 1], rr[:, :, 1], neglam[:, l:l + 1], None, ALU.mult, None, ["p5rr", "neglam"], ["p5rr"])
                            tt("dve", attv, oev[:, 0:nt_, 0, 0:128], bc(rr[:, :, 0:1], [128, nt_, 128]), ALU.mult, ok_ + ["p5rr"], [("p5x", "a")])
                            tt("dve", tmpv, oev[:, 0:nt_, 1, 0:128], bc(rr[:, :, 1:2], [128, nt_, 128]), ALU.mult, ok_ + ["p5rr"], [("p5x", "b")])
                            tt("pool", attv, attv, tmpv, ALU.add, [("p5x", "a"), ("p5x", "b")], [("p5x", "a")])
                            tt("dve", tmpv, attv, attv, ALU.mult, [("p5x", "a")], [("p5x", "b")])
                            red(ssv, tmpv, [("p5x", "b")], ["p5ss"])
                            ts("dve", ssv, ssv, 1.0 / 128.0, EPS, ALU.mult, ALU.add, ["p5ss"], ["p5ss"])
                            tt("pool", ssv, ssv, neghalf[:, 0:nt_], ALU.pow, ["p5ss", "cst"], ["p5ss"])
                            tt("dve", attv, attv, bc(ssv.unsqueeze(2), [128, nt_, 128]), ALU.mult, [("p5x", "a"), "p5ss"], [("p5x", "a")])
                            tt("dve", attv, attv, bc(dagc.unsqueeze(1), [128, nt_, 128]), ALU.mult, [("p5x", "a"), "dagc"], [("p5x", "a")])
                            gsl = gates[:, 0:nt_, 512 + h4 * 128:512 + (h4 + 1) * 128]
                            tt("dve", gsl, attv, gsl, ALU.mult, [("p5x", "a")] + [("p5", "g", qi, 2) for qi in range(nt_)], [("p5", "damix", h4)])
                        tg0 = t0 // 128

                        items = [(h4, c, kti) for h4 in range(4) for c in range(2) for kti in range(nk)]
                        for g in range(len(items) + LOOK):
                            if g < len(items):
                                h4, c, kti = items[g]
                                kt = ktiles[kti]
                                pr = slice(c * 64, c * 64 + 64)
                                pS, gk = STB[g % 3]
                                mm(pS[:, 0:n], dkT[pr, h4, kt * 128:(kt + 1) * 128], qTblk[pr, h4, 0:n], True, True,
                                   [("dkT", h4, 0 if kt < 2 else 1 + (kt - 2) // 4), ("p5", "qT", h4)], [gk])
                                act(ptb[g % 4][:, 0:n], pS[:, 0:n], AF.Exp, [gk], [("p5", "pt", g % 4)], scale=0.125)
                            j = g - LOOK
                            if j >= 0:
                                h4, c, kti = items[j]
                                kt = ktiles[kti]
                                for qi in range(ntile):
                                    mm(ps_g[qi][:, 0:129], ptb[j % 4][:, qi * 128:(qi + 1) * 128], dvaug[:, kt, h4, :], kti == 0, kti == nk - 1,
                                       [("p5", "pt", j % 4), ("dvaug", kt), ("dvaug", "ones")], ["g%d" % qi])
                                if kti == nk - 1:
                                    for qi in range(ntile):
                                        cp("act" if qi % 2 else "dve", oev[:, qi, c, :], ps_g[qi][:, 0:129], ["g%d" % qi], [("p5", "oev", qi, c)])
                                    if c == 1:
                                        combine(h4)
                        for ti in range(ntile):
                            tg = tg0 + ti
                            for kc in range(8):
                                if kc < 2:
                                    src_, rk_ = gates[:, ti, kc * 128:(kc + 1) * 128], [("p5", "mixml")]
                                elif kc < 6:
                                    src_, rk_ = gates[:, ti, 512 + (kc - 2) * 128:512 + (kc - 1) * 128], [("p5", "damix", kc - 2)]
                                else:
                                    src_, rk_ = ysg[:, tg, (kc - 6) * 128:(kc - 5) * 128], [("ysg", tg)]
                                tr(ps_tb[:, kc * 128:(kc + 1) * 128], src_, identB, rk_ + ["cstb"], ["tb"])
                            cp("act", mixT[:, :, ti * 128:(ti + 1) * 128], v3(ps_tb[:, :], 8), ["tb"], [("p5", "mixT", ti)])
                        if dbg:
                            tap("mix%d" % bi, mixT[:, :, 0:n], [("p5", "mixT", ti) for ti in range(ntile)], BF16)
                        gateb = gate_c if isctx else gate_l
                        gkey = "gate_c" if isctx else "gate_l"
                        for hf in range(2):
                            slot = next_slot()
                            wload(l, hf * 512, 512, slot, src=wobf_d)
                            for ti in range(ntile):
                                gix = ti % 2
                                pg = ps_g[gix]
                                for kc in range(8):
                                    mm(pg[:, :], mixT[:, kc, ti * 128:(ti + 1) * 128], wring[slot][:, kc, :], kc == 0, kc == 7, [("wring", slot), ("p5", "mixT", ti)], ["g%d" % gix])
                                tt("dve", p5tmp, pg[:, :], gateb[:, hf * 512:(hf + 1) * 512], ALU.mult, ["g%d" % gix, gkey], [("p5x", "tmp")])
                                stt("dve", xb[:, ti, hf * 512:(hf + 1) * 512], xb[:, ti, hf * 512:(hf + 1) * 512], ALPHA, p5tmp, ALU.mult, ALU.add,
                                    [("xblk", 0, ti), ("p5x", "tmp")], [("xblk", 0, ti)])
                        s1 = small[:, 224:224 + ntile]
                        s2 = small[:, 232:232 + ntile]
                        k1 = [("lnst", 1, ti) for ti in range(ntile)]
                        k2 = [("lnst", 2, ti) for ti in range(ntile)]
                        for ti in range(ntile):
                            act(xn5, xb[:, ti, :], AF.Square, [("xblk", 0, ti)], [("p5", "xn"), k2[ti]], accum=s2[:, ti:ti + 1])
                            act(xn5, xb[:, ti, :], AF.Identity, [("xblk", 0, ti)], [("p5", "xn"), k1[ti]], accum=s1[:, ti:ti + 1])
                        mean, rstd = stats(s1, s2, ntile, 1024.0, (k1, k2))
                        if last:
                            dst3 = y_d[b, t0 - 256:t1 - 256, :].rearrange("(n p) d -> p n d", p=128)
                        else:
                            dst3 = xs_d[b, t0:t1, :].rearrange("(n p) d -> p n d", p=128)
                        for ti in range(ntile):
                            xk_ = ("xblk", 0, ti)
                            ts("dve", xb[:, ti, :], xb[:, ti, :], mean[:, ti:ti + 1], rstd[:, ti:ti + 1], ALU.subtract, ALU.mult, [xk_, "st_mean", "st_rstd"], [xk_])
                            tt("dve", xb[:, ti, :], xb[:, ti, :], lng, ALU.mult, [xk_, "lng"], [xk_])
                            tt("pool" if ti % 2 else "dve", xb[:, ti, :], xb[:, ti, :], lnb, ALU.add, [xk_, "lnb"], [xk_])
                            dma(dst3[:, ti, :], xb[:, ti, :], [xk_], [("y", b, bi, ti) if last else ("xs", b, bi, ti)])

        except _Stop:
            pass
        S.emit(nc, st)
    return nc, S, dbg_out


def make_consts():
    i = np.arange(128)
    ident = np.eye(128, dtype=np.float32)
    triF = (i[:, None] <= i[None, :]).astype(np.float32)
    triB = (i[:, None] >= i[None, :]).astype(np.float32)
    ones = np.ones((128, 128), np.float32)
    partner = np.where((i % 32) < 16, i + 16, i - 16)
    rperm = np.zeros((128, 128), np.float32)
    rperm[partner, i] = 1.0
    cst = np.concatenate([ident, triF, triB, ones, rperm], 1)
    t = np.arange(LAT)
    row = (t // 64).astype(np.float32)
    col = (t % 64).astype(np.float32)
    half = 32
    inv = (10000.0 ** (-(np.arange(0, half, 2, dtype=np.float32)) / half)).astype(np.float32)
    ang_r = row[:, None] * inv
    ang_c = col[:, None] * inv
    ang = np.concatenate([ang_r, ang_r, ang_c, ang_c], -1).astype(np.float32)
    cos = np.cos(ang).astype(np.float32)
    sin = np.sin(ang).astype(np.float32)
    d = np.arange(64)
    sign = np.where((d % 32) < 16, -1.0, 1.0).astype(np.float32)
    sinS = sin * sign[None, :]
    cosT = np.concatenate([cos.T, cos.T], 0)
    sinT = np.concatenate([sinS.T, sinS.T], 0)
    rope = np.ascontiguousarray(np.stack([cosT, sinT], 0)).astype(np.float32)
    return np.ascontiguousarray(cst), rope


_CACHE = {}


def kernel(**inputs):
    NCORES = 8
    NB = 32 // NCORES
    key = (NB, DEPTH)
    if key not in _CACHE:
        _CACHE[key] = build(NB, DEPTH)
    nc = _CACHE[key][0]
    cst, rope = make_consts()
    f = lambda a: np.ascontiguousarray(np.asarray(a, dtype=np.float32))
    shared = {k: f(inputs[k]) for k in ["w_mod", "b_mod", "w_in", "b_in", "ml_conv_w", "ml_conv_b", "ml_norm_g",
                                        "da_lam_q1", "da_lam_k1", "da_lam_q2", "da_lam_k2", "da_norm_g", "sg_norm_g",
                                        "sg_norm_b", "sg_w_s", "sg_b_s", "w_out", "ln_g", "ln_b"]}
    shared["c_ctx"] = f(inputs["c_ctx"]).reshape(1, D)
    shared["cst"] = cst
    shared["rope"] = rope
    x = f(inputs["x"])
    ctx = f(inputs["ctx"])
    c = f(inputs["c"])
    in_maps = []
    for i in range(NCORES):
        m = dict(shared)
        m["x"] = x[i * NB:(i + 1) * NB]
        m["ctx"] = ctx[i * NB:(i + 1) * NB]
        m["c"] = c[i * NB:(i + 1) * NB]
        in_maps.append(m)
    res = run_bass_kernel_spmd(nc, in_maps, core_ids=list(range(NCORES)))
    return np.concatenate([r["y"] for r in res.results], axis=0).astype(np.float32)
```

```python
import math
import numpy as np
from contextlib import ExitStack
import concourse.bass as bass
import concourse.mybir as mybir
from concourse.bass_utils import run_bass_kernel_spmd

F32 = mybir.dt.float32
BF16 = mybir.dt.bfloat16
AF = mybir.ActivationFunctionType
ALU = mybir.AluOpType
AX = mybir.AxisListType

D = 1024
CTXL = 256
LAT = 2048
T = CTXL + LAT
NT = T // 128
DEPTH = 4
EPS = 1e-5
ALPHA = (2 * DEPTH) ** 0.25
NWC = 5136
BLKS = [(0, 256), (256, 768), (768, 1280), (1280, 1792), (1792, 2304)]
NDMA = 24
PSUM_KEYS = frozenset(["g0", "g1", "g2", "g3", "g4", "tpa", "tpb", "tb"])

C_MLQK, C_MLV, C_DAK, C_DAV, C_SG, C_DAQ, C_MLOZ, C_DAZ = 0, 512, 784, 1808, 2320, 3088, 4112, 4624
B_MLV, B_DAV, B_SG, B_MLOZ, B_DAZ, NBIAS = 0, 272, 784, 1552, 2064, 2576


class Sched:
    def __init__(self):
        self.ins = []
        self.lastw = {}
        self.readers = {}
        self.region = {}
        self.rlast = {}
        self.rdma = {}

    def _add(self, eng, fn, reads, writes, is_dma, extra=()):
        reads = list(reads)
        writes = list(writes)
        for k in reads:
            if k in PSUM_KEYS:
                writes.append(("rdser", k))
        deps = set(extra)
        touched = set()
        for k in reads + writes:
            nm = k[0] if isinstance(k, tuple) else k
            r = self.region.get(nm)
            if r is not None:
                touched.add(r)
        for r in touched:
            w = self.lastw.get(("R", r))
            if w is not None:
                deps.add(w)
        for k in reads:
            w = self.lastw.get(k)
            if w is not None:
                deps.add(w)
        for k in writes:
            w = self.lastw.get(k)
            if w is not None:
                deps.add(w)
            rd = self.readers.get(k)
            if rd:
                deps.update(rd[0].values())
                deps.update(rd[1])
        idx = len(self.ins)
        self.ins.append([eng, fn, deps, is_dma])
        for k in writes:
            self.lastw[k] = idx
            self.readers[k] = [{}, []]
        ws = set(writes)
        for k in reads:
            if k not in ws:
                rd = self.readers.setdefault(k, [{}, []])
                if is_dma:
                    rd[1].append(idx)
                else:
                    rd[0][eng] = idx
        for r in touched:
            if is_dma:
                self.rdma.setdefault(r, []).append(idx)
            else:
                self.rlast.setdefault(r, {})[eng] = idx
        return idx

    def fence(self, region, eng, fn):
        deps = set(self.rlast.get(region, {}).values()) | set(self.rdma.get(region, []))
        idx = self._add(eng, fn, (), (), False, extra=deps)
        self.lastw[("R", region)] = idx
        self.rlast[region] = {}
        self.rdma[region] = []
        return idx

    def op(self, eng, fn, reads=(), writes=()):
        return self._add(eng, fn, reads, writes, False)

    def dma(self, eng, fn, reads=(), writes=()):
        return self._add(eng, fn, reads, writes, True)

    def emit(self, nc, stack):
        ins = self.ins
        n = len(ins)
        engs = ["pe", "act", "dve", "pool", "sp"]
        dma_list = [i for i in range(n) if ins[i][3]]
        dma_slot = {}
        for j, i in enumerate(dma_list):
            dma_slot[i] = j
            if j >= NDMA:
                ins[i][2].add(dma_list[j - NDMA])
        needed = [False] * n
        for i in range(n):
            e = ins[i][0]
            nd = set()
            for d in ins[i][2]:
                if ins[d][0] == e and e == "pe" and not ins[d][3]:
                    continue
                nd.add(d)
                needed[d] = True
            ins[i][2] = nd
        esem = {e: stack.enter_context(nc.semaphore("s_" + e)) for e in engs}
        dsem = [stack.enter_context(nc.semaphore("d_%d" % j)) for j in range(NDMA)]
        cnt = {e: 0 for e in engs}
        tok = [None] * n
        for i in range(n):
            e, fn, deps, is_dma = ins[i]
            if is_dma:
                j = dma_slot[i]
                tok[i] = (("d", j % NDMA), 16 * (j // NDMA + 1))
            elif needed[i]:
                cnt[e] += 1
                tok[i] = (("e", e), cnt[e])
        per = {e: [] for e in engs}
        for i in range(n):
            per[ins[i][0]].append(i)
        self.counts = {e: len(per[e]) for e in engs}

        def semof(key):
            return esem[key[1]] if key[0] == "e" else dsem[key[1]]

        def run(e, h):
            seen = {}
            for i in per[e]:
                _, fn, deps, is_dma = ins[i]
                want = {}
                for d in deps:
                    k, v = tok[d]
                    if v > want.get(k, 0):
                        want[k] = v
                for k, v in want.items():
                    if seen.get(k, 0) < v:
                        h.wait_ge(semof(k), v)
                        seen[k] = v
                r = fn(h)
                if tok[i] is not None:
                    k, v = tok[i]
                    r.then_inc(semof(k), 16 if is_dma else 1)
            for i in per[e]:
                if ins[i][3]:
                    k, v = tok[i]
                    if seen.get(k, 0) < v:
                        h.wait_ge(semof(k), v)
                        seen[k] = v

        with nc.Block() as block:
            @block.tensor
            def _(h):
                run("pe", h)

            @block.scalar
            def _(h):
                run("act", h)

            @block.vector
            def _(h):
                run("dve", h)

            @block.gpsimd
            def _(h):
                run("pool", h)

            @block.sync
            def _(h):
                run("sp", h)


class Arena:
    def __init__(self, ap):
        self.ap = ap
        self.off = 0
        self.size = ap.shape[1]

    def f32(self, n):
        a = self.ap[:, self.off:self.off + n]
        self.off += (n + 7) // 8 * 8
        assert self.off <= self.size, (self.off, self.size)
        return a

    def bf16(self, n):
        w = (n + 1) // 2
        a = self.ap[:, self.off:self.off + w].bitcast(BF16)
        self.off += (w + 7) // 8 * 8
        assert self.off <= self.size, (self.off, self.size)
        return a[:, 0:n]


def v3(ap, a):
    return ap.rearrange("p (a b) -> p a b", a=a)


def v4(ap, a, b):
    return ap.rearrange("p (a b c) -> p a b c", a=a, b=b)


def bc(ap, shape):
    return ap.to_broadcast(shape)


class _Stop(Exception):
    pass


def build(NB, NL, dbg=None, stop=None):
    nc = bass.Bass("TRN2", target_bir_lowering=False)
    S = Sched()

    def dram(name, shape, dtype=F32, kind="ExternalInput"):
        return nc.dram_tensor(name, shape, dtype, kind=kind).ap()

    x_d = dram("x", [NB, LAT, D])
    ctx_d = dram("ctx", [NB, CTXL, D])
    c_d = dram("c", [NB, D])
    cctx_d = dram("c_ctx", [1, D])
    wmod_d = dram("w_mod", [DEPTH, D, 3 * D])
    bmod_d = dram("b_mod", [DEPTH, 3 * D])
    win_d = dram("w_in", [DEPTH, D, 4112])
    bin_d = dram("b_in", [DEPTH, 4112])
    convw_d = dram("ml_conv_w", [DEPTH, 3, 512])
    convb_d = dram("ml_conv_b", [DEPTH, 512])
    mlg_d = dram("ml_norm_g", [DEPTH, 256])
    lq1_d = dram("da_lam_q1", [DEPTH, 64])
    lk1_d = dram("da_lam_k1", [DEPTH, 64])
    lq2_d = dram("da_lam_q2", [DEPTH, 64])
    lk2_d = dram("da_lam_k2", [DEPTH, 64])
    dag_d = dram("da_norm_g", [DEPTH, 128])
    sgg_d = dram("sg_norm_g", [DEPTH, 256])
    sgb_d = dram("sg_norm_b", [DEPTH, 256])
    sgw_d = dram("sg_w_s", [DEPTH, 4, 128, 128])
    sgbs_d = dram("sg_b_s", [DEPTH, 4, 128])
    wout_d = dram("w_out", [DEPTH, D, D])
    lng_d = dram("ln_g", [DEPTH, D])
    lnb_d = dram("ln_b", [DEPTH, D])
    cst_d = dram("cst", [128, 640])
    rope_d = dram("rope", [2, 128, LAT])
    y_d = dram("y", [NB, LAT, D], kind="ExternalOutput")
    wbf_d = dram("wbf", [DEPTH, D, NWC], BF16, kind="Internal")
    wobf_d = dram("wobf", [DEPTH, D, D], BF16, kind="Internal")
    mod_d = dram("modscr", [DEPTH, 8, 3 * D], F32, kind="Internal")
    xs_d = dram("xs", [NB, T, D], F32, kind="Internal")
    dbg_out = {}

    with ExitStack() as st:
        arena_t = st.enter_context(nc.sbuf_tensor("arena", [128, 51200], F32))
        AR = Arena(arena_t[:, :])
        ps_tp = st.enter_context(nc.psum_tensor("ps_tp", [128, 1024], F32))
        ps_tb = st.enter_context(nc.psum_tensor("ps_tb", [128, 1024], BF16))
        ps_g = [st.enter_context(nc.psum_tensor("ps_g%d" % i, [128, 512], F32)) for i in range(5)]

        cstF = AR.f32(640)
        identF, triF, triB, onesF, rperm = (cstF[:, i * 128:(i + 1) * 128] for i in range(5))
        identB = AR.bf16(128)
        maskF = AR.bf16(128)
        maskB = AR.bf16(128)
        neghalf = AR.f32(64)
        dummy = AR.f32(8)
        sgW = v4(AR.bf16(DEPTH * 4 * 128), DEPTH, 4)
        sgbs = v3(AR.f32(DEPTH * 4), DEPTH)
        fmb = v3(AR.f32(DEPTH * 20), DEPTH)
        convw = v4(AR.f32(DEPTH * 12), DEPTH, 3)
        convb = v3(AR.f32(DEPTH * 4), DEPTH)
        hconvb = v3(AR.f32(DEPTH * 4), DEPTH)
        lamv = AR.f32(DEPTH)
        neglam = AR.f32(DEPTH)
        biasb = AR.f32(NBIAS)
        lng = AR.f32(D)
        lnb = AR.f32(D)
        mlgc = AR.f32(256)
        dagc = AR.f32(128)
        sggt = AR.f32(256)
        sgbt = AR.f32(256)
        gate_l = AR.f32(D)
        gate_c = AR.f32(D)
        modp = v3(AR.f32(32), 4)
        WR_OFF = AR.off
        wring = [v3(AR.bf16(8 * 512), 8) for _ in range(2)]
        wq32 = v3(arena_t[:, WR_OFF:WR_OFF + 4096], 8)
        wg32 = v3(AR.f32(128), 8)
        Hml = v3(AR.f32(NT * 256), NT)
        small = AR.f32(256)
        XOFF = AR.off
        XSZ = 8192
        AR.off += XSZ
        BOFF = AR.off
        BSZ = 12900
        AR.off += BSZ
        AOFF = AR.off
        ASZ = 51200 - AOFF
        assert ASZ >= 9216, ASZ

        def sub(off, size):
            return Arena(arena_t[:, off:off + size])

        for nm in ["X", "B", "A"]:
            pass

        aX = sub(XOFF, XSZ)
        xblk = [v3(aX.f32(4096), 4), v3(aX.f32(4096), 4)]
        aX = sub(XOFF, XSZ)
        pst = aX.f32(T)
        cacc = aX.f32(T)
        aX = sub(XOFF, XSZ)
        m_pt = [v3(aX.f32(512), 4) for _ in range(2)]
        m_vsf = [aX.f32(288) for _ in range(2)]
        m_vs = [v3(m_vsf[i], 4)[:, :, 0:65] for i in range(2)]
        m_kt = [aX.f32(256) for _ in range(2)]
        m_hs = [v3(aX.f32(288), 4)[:, :, 0:65] for _ in range(2)]
        m_cst_flat = aX.f32(576)
        m_cst = v4(m_cst_flat, 2, 4)[:, :, :, 0:65]
        m_tmp = v3(aX.f32(288), 4)[:, :, 0:65]
        m_hn = v3(aX.f32(256), 4)
        aX = sub(XOFF, XSZ)
        ropeC = aX.f32(512)
        ropeS = aX.f32(512)
        rstA = aX.f32(512)
        rstB = aX.f32(512)
        sgp = aX.f32(768)
        sgp2 = v3(aX.f32(2 * 768), 2)
        sgu2 = v3(aX.f32(512), 2)
        sgjunk2 = v3(aX.f32(512), 2)
        sgvn2 = v3(aX.bf16(512), 2)
        sgy2 = v3(aX.f32(512), 2)
        sgzs2 = v3(aX.f32(512), 2)
        aB = sub(BOFF, BSZ)
        qkT = v3(aB.f32(4 * T), 4)
        vaug = v4(aB.bf16(NT * 4 * 72), NT, 4)[:, :, :, 0:65]
        Gt = v3(aB.f32(NT * 16), NT)
        LFp = v3(aB.f32(144), 2)
        Ej = v3(aB.f32(144), 2)
        Ws = v3(aB.f32(144), 2)
        Eend = v3(aB.f32(144), 2)
        gtmp = v3(aB.f32(144), 2)
        aB = sub(BOFF, BSZ)
        dkT = v3(aB.bf16(4 * T), 4)
        dvaug = v4(aB.bf16(NT * 4 * 136), NT, 4)[:, :, :, 0:129]
        ysg = v3(aB.bf16(NT * 256), NT)
        aA = sub(AOFF, ASZ)
        hTall = v3(aA.bf16(8 * T), 8)
        xn_a = aA.f32(0) if False else None
        aA = sub(AOFF, ASZ)
        hTblk = v3(aA.bf16(8 * 512), 8)
        qTblk = v3(aA.bf16(4 * 512), 4)
        gates = v3(aA.bf16(4 * 1024), 4)
        mixT = v3(aA.bf16(8 * 512), 8)
        ptb = [aA.bf16(512) for _ in range(4)]
        oev = v4(aA.f32(4 * 2 * 132), 4, 2)[:, :, :, 0:129]
        xn5 = aA.f32(1024)
        xn5b = aA.f32(1024)
        aX5 = sub(XOFF + 4096, 4096)
        p5ropeC = aX5.f32(512)
        p5ropeS = aX5.f32(512)
        p5stA = aX5.f32(512)
        p5stB = aX5.f32(512)
        p5tmp = aX5.f32(512)
        p5t2 = aX5.f32(512)
        p5ml = aX5.f32(256)
        p5x_ml = arena_t[:, XOFF + 4096:XOFF + 4096 + 1024]
        p5x_sq = arena_t[:, XOFF + 4096 + 2048:XOFF + 4096 + 3072]
        p5sq = aX5.f32(256)

        for nm in ["xblk", "pst", "cacc", "mtmp", "sgt", "rope4", "p5x"]:
            S.region[nm] = "X"
        for nm in ["qkT", "qkpre", "vaug", "ktok", "Gt", "gder", "dkT", "dvaug", "ysg"]:
            S.region[nm] = "B"
        for nm in ["hTall", "p5"]:
            S.region[nm] = "A"

        def fence(region):
            S.fence(region, "pool", lambda h: h.memset(dummy[:, 0:1], 0.0))

        def mm(out, lhsT, rhs, start, stop, reads, writes):
            S.op("pe", lambda h: h.matmul(out, lhsT=lhsT, rhs=rhs, start=start, stop=stop), reads, writes)

        def tr(out, in_, ident, reads, writes):
            S.op("pe", lambda h: h.transpose(out, in_, ident), reads, writes)

        def act(out, in_, func, reads, writes, bias=None, scale=None, accum=None):
            kw = {}
            if accum is not None:
                kw["accum_out"] = accum
            if bias is not None:
                kw["bias"] = bias
            if scale is not None:
                kw["scale"] = scale
            S.op("act", lambda h: h.activation(out=out, in_=in_, func=func, **kw), reads, writes)

        def tt(eng, out, in0, in1, op, reads, writes):
            S.op(eng, lambda h: h.tensor_tensor(out=out, in0=in0, in1=in1, op=op), reads, writes)

        def ts(eng, out, in0, s1, s2, op0, op1, reads, writes):
            if s2 is None:
                S.op(eng, lambda h: h.tensor_scalar(out=out, in0=in0, scalar1=s1, scalar2=None, op0=op0), reads, writes)
            else:
                S.op(eng, lambda h: h.tensor_scalar(out=out, in0=in0, scalar1=s1, scalar2=s2, op0=op0, op1=op1), reads, writes)

        def stt(eng, out, in0, scalar, in1, op0, op1, reads, writes):
            S.op(eng, lambda h: h.scalar_tensor_tensor(out=out, in0=in0, scalar=scalar, in1=in1, op0=op0, op1=op1), reads, writes)

        def cp(eng, out, in_, reads, writes):
            if eng == "act":
                S.op(eng, lambda h: h.activation(out=out, in_=in_, func=AF.Copy), reads, writes)
            else:
                S.op(eng, lambda h: h.tensor_copy(out=out, in_=in_), reads, writes)

        def red(out, in_, reads, writes):
            S.op("dve", lambda h: h.tensor_reduce(out=out, in_=in_, axis=AX.X, op=ALU.add), reads, writes)

        def dma(out, in_, reads, writes, slow=False):
            if slow:
                S.dma("sp", lambda h: h.dma_start(out=out, in_=in_, allow_slow_non_contiguous=True), reads, writes)
            else:
                S.dma("sp", lambda h: h.dma_start(out=out, in_=in_), reads, writes)

        def tap(name, ap, key, dtype=F32):
            if dbg is None or name not in dbg:
                return
            shp = list(ap.shape)
            d = nc.dram_tensor("dbg_" + name, shp, dtype, kind="ExternalOutput").ap()
            dbg_out[name] = shp
            dma(d, ap, list(key) if isinstance(key, list) else [key], ["dbg_" + name])

        def stats(s1, s2, k, n, tag):
            rk1 = tag[0] if isinstance(tag, tuple) else [tag + "s1"]
            rk2 = tag[1] if isinstance(tag, tuple) else [tag + "s2"]
            mean = small[:, 0:k]
            msq = small[:, 16:16 + k]
            var = small[:, 32:32 + k]
            rstd = small[:, 48:48 + k]
            ts("dve", mean, s1, 1.0 / n, None, ALU.mult, None, rk1, ["st_mean"])
            tt("dve", msq, mean, mean, ALU.mult, ["st_mean"], ["st_msq"])
            stt("dve", var, s2, 1.0 / n, msq, ALU.mult, ALU.subtract, rk2 + ["st_msq"], ["st_var"])
            ts("dve", var, var, EPS, None, ALU.add, None, ["st_var"], ["st_var"])
            tt("pool", rstd, var, neghalf[:, 0:k], ALU.pow, ["st_var", "cst"], ["st_rstd"])
            return mean, rstd

        def chk(name):
            if stop == name:
                raise _Stop()

        try:
            dma(cstF, cst_d[:, :], [], ["cst"])
            cp("dve", identB, identF, ["cst"], ["cstb"])
            cp("dve", maskF, triF, ["cst"], ["cstb"])
            cp("dve", maskB, triB, ["cst"], ["cstb"])
            S.op("pool", lambda h: h.memset(neghalf, -0.5), [], ["cst"])

            S.op("pool", lambda h: h.memset(fmb[:, :, :], 0.0), [], ["fmb"])
            for l in range(DEPTH):
                for j in range(4):
                    dma(fmb[:, l, j:j + 1], bin_d[l, j * 128:(j + 1) * 128].rearrange("(p o) -> p o", o=1), [], ["fmb"])
                for h4 in range(4):
                    dma(fmb[:, l, 4 + 2 * h4:5 + 2 * h4], bin_d[l, 1808 + h4 * 128:1808 + (h4 + 1) * 128].rearrange("(p o) -> p o", o=1), [], ["fmb"])
                    dma(fmb[:, l, 12 + 2 * h4:13 + 2 * h4], bin_d[l, 1296 + h4 * 128:1296 + (h4 + 1) * 128].rearrange("(p o) -> p o", o=1), [], ["fmb"])
                for j in range(3):
                    dma(convw[:, l, j, :], convw_d[l, j, :].rearrange("(c p) -> p c", p=128), [], ["convp"], slow=True)
                dma(convb[:, l, :], convb_d[l, :].rearrange("(c p) -> p c", p=128), [], ["convp"], slow=True)
                dma(sgbs[:, l, :], sgbs_d[l, :, :].rearrange("g p -> p g"), [], ["sgbs"], slow=True)
            ts("dve", hconvb[:, :, :], convb[:, :, :], 0.5, None, ALU.mult, None, ["convp"], ["hconvb"])
            for l in range(DEPTH):
                pso = ps_g[0][:, 0:16]
                mm(pso, rperm, fmb[:, l, 4:20], True, True, ["cst", "fmb"], ["g0"])
                src = v3(pso, 8)[:, :, 0:1]
                dst = v3(fmb[:, l, 4:20], 8)[:, :, 1:2]
                cp("dve", dst, src, ["g0"], ["fmb"])
            for l in range(DEPTH):
                for g in range(4):
                    stg = small[:, 64:192]
                    dma(stg, sgw_d[l, g, :, :], [], ["sgstg"])
                    tr(ps_g[1][:, 0:128], stg, identF, ["sgstg", "cst"], ["g1"])
                    cp("dve", sgW[:, l, g, :], ps_g[1][:, 0:128], ["g1"], ["sgW"])
            lt = rstA
            fence("X")
            for i, (a_d, b_d) in enumerate([(lq1_d, lk1_d), (lq2_d, lk2_d)]):
                dma(lt[:, 0:256], a_d.rearrange("l k -> (l k)").partition_broadcast(128), [], [("rope4", "a")])
                dma(lt[:, 256:512], b_d.rearrange("l k -> (l k)").partition_broadcast(128), [], [("rope4", "b")])
                tt("dve", lt[:, 0:256], lt[:, 0:256], lt[:, 256:512], ALU.mult, [("rope4", "a"), ("rope4", "b")], [("rope4", "a")])
                red(small[:, 200 + 4 * i:204 + 4 * i], v3(lt[:, 0:256], 4), [("rope4", "a")], ["lam%d" % i])
                act(small[:, 200 + 4 * i:204 + 4 * i], small[:, 200 + 4 * i:204 + 4 * i], AF.Exp, ["lam%d" % i], ["lam%d" % i])
            tt("dve", lamv, small[:, 200:204], small[:, 204:208], ALU.subtract, ["lam0", "lam1"], ["lamv"])
            for l in range(DEPTH):
                lam_init = 0.8 - 0.6 * math.exp(-0.3 * l)
                ts("dve", lamv[:, l:l + 1], lamv[:, l:l + 1], lam_init, None, ALU.add, None, ["lamv"], ["lamv"])
            ts("dve", neglam, lamv, -1.0, None, ALU.mult, None, ["lamv"], ["neglam"])

            chk('consts')
            groups = [
                (C_MLQK, [(0, 0, 512, False)]),
                (C_MLV, [(0, 512, 256, False), (256, 1280, 16, False)]),
                (C_DAK, [(0, 1808, 128, False), (128, 1808, 128, True), (256, 1936, 128, False), (384, 1936, 128, True)]),
                (C_DAK + 512, [(0, 2064, 128, False), (128, 2064, 128, True), (256, 2192, 128, False), (384, 2192, 128, True)]),
                (C_DAV, [(0, 2320, 512, False)]),
                (C_SG, [(0, 3344, 512, False)]),
                (C_SG + 512, [(0, 3856, 256, False)]),
                (C_DAQ, [(0, 1296, 128, False), (128, 1296, 128, True), (256, 1424, 128, False), (384, 1424, 128, True)]),
                (C_DAQ + 512, [(0, 1552, 128, False), (128, 1552, 128, True), (256, 1680, 128, False), (384, 1680, 128, True)]),
                (C_MLOZ, [(0, 768, 512, False)]),
                (C_DAZ, [(0, 2832, 512, False)]),
            ]
            fence("X")
            fence("A")
            stg32 = [v3(sub(XOFF, XSZ).f32(4096), 8), v3(sub(XOFF + 4096, 4096).f32(4096), 8)]
            aA = sub(AOFF, ASZ)
            stg16 = [v3(aA.bf16(4096), 8), v3(aA.bf16(4096), 8)]
            S.region["stg32"] = "X"
            S.region["stg16"] = "A"
            gi = 0
            prev_store = [[], []]
            ceng = ["dve", "pool", "act"]

            def castcp(i, out, in_, reads, writes):
                e = ceng[i % 3]
                if e == "act":
                    act(out, in_, AF.Copy, reads, writes)
                else:
                    cp(e, out, in_, reads, writes)

            for l in range(DEPTH):
                for (dst0, pieces) in groups:
                    sl = gi % 2
                    wtot = max(p[0] + p[2] for p in pieces)
                    for (doff, src0, w, sw) in pieces:
                        dma(stg32[sl][:, :, doff:doff + w], win_d[l, :, src0:src0 + w].rearrange("(kc p) w -> p kc w", p=128), prev_store[sl], [("stg32", sl, doff)])
                    for pi, (doff, src0, w, sw) in enumerate(pieces):
                        if not sw:
                            castcp(gi + pi, stg16[sl][:, :, doff:doff + w], stg32[sl][:, :, doff:doff + w], [("stg32", sl, doff)], [("stg16", sl, doff)])
                        else:
                            i5 = stg32[sl][:, :, doff:doff + w].rearrange("p k (b t s) -> p k b t s", b=4, t=2)
                            o5 = stg16[sl][:, :, doff:doff + w].rearrange("p k (b t s) -> p k b t s", b=4, t=2)
                            for kc in range(8):
                                cp("pool" if kc % 2 else "dve", o5[:, kc, :, 0, :], i5[:, kc, :, 1, :], [("stg32", sl, doff)], [("stg16", sl, doff, kc, 0)])
                                cp("dve" if kc % 2 else "pool", o5[:, kc, :, 1, :], i5[:, kc, :, 0, :], [("stg32", sl, doff)], [("stg16", sl, doff, kc, 1)])
                    rk = []
                    for (doff, src0, w, sw) in pieces:
                        if sw:
                            rk += [("stg16", sl, doff, kc, t) for kc in range(8) for t in range(2)]
                        else:
                            rk.append(("stg16", sl, doff))
                    dma(wbf_d[l, :, dst0:dst0 + wtot].rearrange("(kc p) w -> p kc w", p=128), stg16[sl][:, :, 0:wtot], rk, [("wbf", l, dst0)])
                    prev_store[sl] = [("wbf", l, dst0)]
                    gi += 1
                for hf in range(2):
                    sl = gi % 2
                    dma(stg32[sl][:, :, :], wout_d[l, :, hf * 512:(hf + 1) * 512].rearrange("(kc p) w -> p kc w", p=128), prev_store[sl], [("stg32", sl, 0)])
                    castcp(gi, stg16[sl][:, :, :], stg32[sl][:, :, :], [("stg32", sl, 0)], [("stg16", sl, 0)])
                    dma(wobf_d[l, :, hf * 512:(hf + 1) * 512].rearrange("(kc p) w -> p kc w", p=128), stg16[sl][:, :, :], [("stg16", sl, 0)], [("wobf", l, hf)])
                    prev_store[sl] = [("wobf", l, hf)]
                    gi += 1

            chk('conv')
            csT = v3(small[:, 64:64 + 64], 8)
            for r in range(NB + 1):
                src = c_d[r, :] if r < NB else cctx_d[0, :]
                dma(csT[:, :, r:r + 1], src.rearrange("(kc p o) -> p kc o", p=128, o=1), ["sgW"], [("csT", r)], slow=True)
            NR = NB + 1
            cs_r = [("csT", r) for r in range(NR)]
            tnh = v3(small[:, 128:192], 8)
            act(tnh[:, :, 0:NR], csT[:, :, 0:NR], AF.Tanh, cs_r, ["cs_t"], scale=0.5)
            ts("dve", tnh[:, :, 0:NR], tnh[:, :, 0:NR], 0.5, 0.5, ALU.mult, ALU.add, ["cs_t"], ["cs_t"])
            tt("dve", csT[:, :, 0:NR], csT[:, :, 0:NR], tnh[:, :, 0:NR], ALU.mult, cs_r + ["cs_t"], ["csS"])
            mrow = sub(AOFF, ASZ).f32(1024)
            S.region["mrow"] = "A"
            wi = 0
            for l in range(NL):
                for cb in range(6):
                    sl = wi % 2
                    dma(stg32[sl][:, :, :], wmod_d[l, :, cb * 512:(cb + 1) * 512].rearrange("(kc p) w -> p kc w", p=128), prev_store[sl], [("stg32", sl, 0)])
                    pg = ps_g[2 + (wi % 2)]
                    for kc in range(8):
                        mm(pg[0:NR, :], csT[:, kc, 0:NR], stg32[sl][:, kc, :], kc == 0, kc == 7, ["csS", ("stg32", sl, 0)], ["g%d" % (2 + wi % 2)])
                    bm = mrow[0:NR, 512:1024]
                    dma(bm, bmod_d[l, cb * 512:(cb + 1) * 512].partition_broadcast(NR), [], [("mrow", "b")])
                    tt("dve", mrow[0:NR, 0:512], pg[0:NR, :], bm, ALU.add, ["g%d" % (2 + wi % 2), ("mrow", "b")], [("mrow", "o")])
                    dma(mod_d[l, 0:NR, cb * 512:(cb + 1) * 512], mrow[0:NR, 0:512], [("mrow", "o")], [("mod", l)])
                    wi += 1

            chk('mod')
            def wload(l, col0, ncols, slot, src=None):
                srcd = wbf_d if src is None else src
                key = ("wbf", l, col0) if src is None else ("wobf", l, col0 // 512)
                rk = [("wbf", l, g[0]) for g in groups] if src is None else [key]
                dma(wring[slot][:, :, 0:ncols], srcd[l, :, col0:col0 + ncols].rearrange("(kc p) w -> p kc w", p=128), rk, [("wring", slot)])

            ring_ctr = [0]

            def next_slot():
                s_ = ring_ctr[0] % 2
                ring_ctr[0] += 1
                return s_

            g_ctr = [0]

            def next_g(lo=0, hi=5):
                i = lo + g_ctr[0] % (hi - lo)
                g_ctr[0] += 1
                return i

            def ln_block(xb, ntile, xkey, xnbuf, xnkey, hT_of, hkey_of, s1p, shp, modkey, hTf=None, hfkey=None, post=None):
                s1 = small[:, 224:224 + ntile]
                s2 = small[:, 232:232 + ntile]
                k1 = [("lnst", 1, ti) for ti in range(ntile)]
                k2 = [("lnst", 2, ti) for ti in range(ntile)]
                xk = xkey if callable(xkey) else (lambda ti: xkey)
                xnbufs = xnbuf if isinstance(xnbuf, list) else [xnbuf]
                xnkeys = xnkey if isinstance(xnkey, list) else [xnkey]
                for ti in range(ntile):
                    xnbuf, xnkey = xnbufs[ti % len(xnbufs)], xnkeys[ti % len(xnbufs)]
                    act(xnbuf, xb[:, ti, :], AF.Square, [xk(ti)], [xnkey, k2[ti]], accum=s2[:, ti:ti + 1])
                    act(xnbuf, xb[:, ti, :], AF.Identity, [xk(ti)], [xnkey, k1[ti]], accum=s1[:, ti:ti + 1])
                mean, rstd = stats(s1, s2, ntile, 1024.0, (k1, k2))
                for ti in range(ntile):
                    xnbuf, xnkey = xnbufs[ti % len(xnbufs)], xnkeys[ti % len(xnbufs)]
                    ts("dve", xnbuf, xb[:, ti, :], mean[:, ti:ti + 1], rstd[:, ti:ti + 1], ALU.subtract, ALU.mult, [xk(ti), "st_mean", "st_rstd"], [xnkey])
                    for kc in range(8):
                        tr(ps_tp[:, kc * 128:(kc + 1) * 128], xnbuf[:, kc * 128:(kc + 1) * 128], identF, [xnkey, "cst"], ["tpa", "tpb"])
                    tmpm = v3(xnbuf, 8)
                    tt("dve", tmpm, v3(ps_tp[:, :], 8), bc(s1p.unsqueeze(2), [128, 8, 128]), ALU.mult, ["tpa", "tpb"] + modkey, [xnkey])
                    if hTf is None:
                        tt("pool", hT_of(ti), tmpm, bc(shp.unsqueeze(2), [128, 8, 128]), ALU.add, [xnkey] + modkey, [hkey_of(ti)])
                    else:
                        tt("pool", hTf(ti), tmpm, bc(shp.unsqueeze(2), [128, 8, 128]), ALU.add, [xnkey] + modkey, [hfkey(ti)])
                        cp("act", hT_of(ti), hTf(ti), [hfkey(ti)], [hkey_of(ti)])
                    if post is not None:
                        post(ti)

            for b in range(NB):
                for l in range(NL):
                    lam_init = 0.8 - 0.6 * math.exp(-0.3 * l)
                    pk = ("par", b, l)
                    for (off, s0, w) in [(B_MLV, 512, 256), (B_MLV + 256, 1280, 16), (B_DAV, 2320, 512), (B_SG, 3344, 768), (B_MLOZ, 768, 512), (B_DAZ, 2832, 512)]:
                        dma(biasb[:, off:off + w], bin_d[l, s0:s0 + w].partition_broadcast(128), [], [("biasb", off)])
                    dma(lng, lng_d[l, :].partition_broadcast(128), [], ["lng"])
                    dma(lnb, lnb_d[l, :].partition_broadcast(128), [], ["lnb"])
                    dma(mlgc, mlg_d[l, :].partition_broadcast(128), [], ["mlgc"])
                    dma(dagc, dag_d[l, :].partition_broadcast(128), [], ["dagc"])
                    dma(sggt, sgg_d[l, :].partition_broadcast(128), [], ["sggt"])
                    dma(sgbt, sgb_d[l, :].partition_broadcast(128), [], ["sgbt"])
                    ts("dve", mlgc, mlgc, 0.5, None, ALU.mult, None, ["mlgc"], ["mlgc"])
                    ts("dve", dagc, dagc, 0.5 * (1.0 - lam_init), None, ALU.mult, None, ["dagc"], ["dagc"])
                    dma(gate_l, mod_d[l, b, 2048:3072].partition_broadcast(128), [("mod", l)], ["gate_l"])
                    dma(gate_c, mod_d[l, NB, 2048:3072].partition_broadcast(128), [("mod", l)], ["gate_c"])
                    for i, (row, c0) in enumerate([(b, 0), (b, 1024), (NB, 0), (NB, 1024)]):
                        dma(modp[:, i, :], mod_d[l, row, c0:c0 + 1024].rearrange("(kc p) -> p kc", p=128), [("mod", l)], [("modp", i)], slow=True)
                    for i in (1, 3):
                        ts("dve", modp[:, i, :], modp[:, i, :], 1.0, None, ALU.add, None, [("modp", i)], [("modp", i)])
                    modk = [("modp", i) for i in range(4)]

                    fence("X")
                    fence("A")
                    fence("B")
                    dma(wq32, win_d[l, :, 0:512].rearrange("(kc p) w -> p kc w", p=128), [], [("wring", 0), ("wring", 1)])
                    dma(wg32, win_d[l, :, 1280:1296].rearrange("(kc p) w -> p kc w", p=128), [], ["wg32"])
                    hTf2 = xblk[1][:, 1:3, :].rearrange("p a (k t) -> p k (a t)", k=8) if False else None
                    hTfbuf = v3(arena_t[:, XOFF + 4096 + 1024:XOFF + 4096 + 3072], 8)
                    for bi, (t0, t1) in enumerate(BLKS):
                        ntile = (t1 - t0) // 128
                        xb = xblk[0]
                        if l == 0:
                            srcx = ctx_d[b, :, :] if bi == 0 else x_d[b, t0 - 256:t1 - 256, :]
                            rk = []
                        else:
                            srcx = xs_d[b, t0:t1, :]
                            rk = [("xs", b, bi, ti_) for ti_ in range((t1 - t0) // 128)]
                        srcx3 = srcx.rearrange("(n p) d -> p n d", p=128)
                        for ti_ in range(ntile):
                            dma(xb[:, ti_, :], srcx3[:, ti_, :], rk, [("xblk", 0, ti_)])
                        s1p, shp = (modp[:, 3, :], modp[:, 2, :]) if bi == 0 else (modp[:, 1, :], modp[:, 0, :])

                        def post1(ti, t0=t0):
                            tk = t0 + ti * 128
                            tg = tk // 128
                            hTf_t = hTfbuf[:, :, (ti % 2) * 128:(ti % 2) * 128 + 128]
                            gg = 2 + tg % 2
                            for kc in range(8):
                                mm(ps_g[gg][:, 0:16], hTf_t[:, kc, :], wg32[:, kc, :], kc == 0, kc == 7, ["wg32", ("xblk", 2, ti % 2)], ["g%d" % gg])
                            tt("dve", Gt[:, tg, :], ps_g[gg][:, 0:16], biasb[:, B_MLV + 256:B_MLV + 272], ALU.add,
                               ["g%d" % gg, ("biasb", B_MLV + 256)], [("Gt", tg)])
                            if ti % 2 == 0:
                                return
                            tk0 = tk - 128
                            for c in range(4):
                                pq = ps_g[c // 2][:, (c % 2) * 256:(c % 2) * 256 + 256]
                                for kc in range(8):
                                    mm(pq, wq32[:, kc, c * 128:(c + 1) * 128], hTfbuf[:, kc, :], kc == 0, kc == 7,
                                       [("wring", 0), ("wring", 1), ("xblk", 2, 0), ("xblk", 2, 1)], ["g%d" % (c // 2)])
                            for hb in range(2):
                                tt("dve", qkT[:, 2 * hb:2 * hb + 2, tk0:tk0 + 256], v3(ps_g[hb][:, :], 2), bc(fmb[:, l, 2 * hb:2 * hb + 2].unsqueeze(2), [128, 2, 256]), ALU.add,
                                   ["g%d" % hb, "fmb"], [("qkpre", tg - 1, hb), ("qkpre", tg, hb)])

                        ln_block(xb, ntile, lambda ti: ("xblk", 0, ti), [xblk[1][:, 0, :], xblk[1][:, 3, :]], [("xblk", 1), ("xblk", 3)],
                                 lambda ti, t0=t0: hTall[:, :, t0 + ti * 128:t0 + (ti + 1) * 128], lambda ti, t0=t0: ("hTall", t0 // 128 + ti),
                                 s1p, shp, modk, hTf=lambda ti: hTfbuf[:, :, (ti % 2) * 128:(ti % 2) * 128 + 128], hfkey=lambda ti: ("xblk", 2, ti % 2), post=post1)
                    if dbg:
                        tap("hT", hTall[:, :, :], [("hTall", i) for i in range(NT)], BF16)

                    chk('ph1')
                    hT_keys = [("hTall", i) for i in range(NT)]
                    slot = next_slot()
                    wload(l, C_MLV, 272, slot)
                    S.op("pool", lambda h: h.memset(vaug[:, :, :, 64:65], 1.0), [], [("vaug", "ones")])
                    for ti in range(NT):
                        gix = next_g()
                        pg = ps_g[gix]
                        for kc in range(8):
                            mm(pg[:, 0:272], hTall[:, kc, ti * 128:(ti + 1) * 128], wring[slot][:, kc, 0:272], kc == 0, kc == 7,
                               [("wring", slot), hT_keys[ti]], ["g%d" % gix])
                        tt("dve", vaug[:, ti, :, 0:64], v3(pg[:, 0:256], 4), v3(biasb[:, B_MLV:B_MLV + 256], 4), ALU.add,
                           ["g%d" % gix, ("biasb", B_MLV)], [("vaug", ti)])
                    Gk = [("Gt", ti) for ti in range(NT)]
                    for d_ in range(2):
                        fsl = Gt[:, :, 4 + 8 * d_:8 + 8 * d_]
                        act(v3(gtmp[:, d_, :], NT), fsl, AF.Exp, Gk, [("gder", "t", d_)], scale=-1.0)
                        act(LFp[:, d_, :], gtmp[:, d_, :], AF.Ln, [("gder", "t", d_)], [("gder", "L", d_)], bias=1.0)
                    chk('g1')
                    pc = ps_g[0]
                    mm(pc[:, 0:72], triF, LFp[:, 0, :], True, True, ["cst", ("gder", "L", 0)], ["g0"])
                    mm(pc[:, 72:144], triB, LFp[:, 1, :], True, True, ["cst", ("gder", "L", 1)], ["g0"])
                    mm(pc[:, 144:288], onesF, LFp[:, :, :].rearrange("p a b -> p (a b)"), True, True, ["cst", ("gder", "L", 0), ("gder", "L", 1)], ["g0"])
                    chk('g2')
                    act(Ej[:, :, :].rearrange("p a b -> p (a b)"), pc[:, 0:144], AF.Exp, ["g0"], [("gder", "Ej")], scale=-1.0)
                    act(Eend[:, :, :].rearrange("p a b -> p (a b)"), pc[:, 144:288], AF.Exp, ["g0"], [("gder", "Eend")], scale=-1.0)
                    if dbg and 'cum' in dbg:
                        cp('dve', sgp[:, 0:288], pc[:, 0:288], ['g0'], [('mtmp', 'dbgc')])
                        tap('cum', sgp[:, 0:288], [('mtmp', 'dbgc')])
                        tap('LFp', LFp[:, :, :], [('gder', 'L', 0), ('gder', 'L', 1)])
                    chk('g3')
                    for d_ in range(2):
                        tt("dve", v3(gtmp[:, d_, :], NT), v3(pc[:, 72 * d_:72 * d_ + 72], NT), Gt[:, :, 8 * d_:8 * d_ + 4], ALU.add,
                           ["g0"] + Gk, [("gder", "t2", d_)])
                        act(Ws[:, d_, :], gtmp[:, d_, :], AF.Exp, [("gder", "t2", d_)], [("gder", "Ws", d_)])
                    pre_all = [("qkpre", ti, hb) for ti in range(NT) for hb in range(2)]
                    for c in range(4):
                        pst = qkT[:, c, :]
                        w0, w1, w2 = (convw[:, l, j, c:c + 1] for j in range(3))
                        ts("dve", cacc, pst, w1, convb[:, l, c:c + 1], ALU.mult, ALU.add, pre_all + ["convp"], ["cacc"])
                        for (a, e) in [(0, CTXL), (CTXL, T)]:
                            stt("dve", cacc[:, a + 1:e], pst[:, a:e - 1], w0, cacc[:, a + 1:e], ALU.mult, ALU.add, pre_all + ["convp", "cacc"], ["cacc"])
                            stt("dve", cacc[:, a:e - 1], pst[:, a + 1:e], w2, cacc[:, a:e - 1], ALU.mult, ALU.add, pre_all + ["convp", "cacc"], ["cacc"])
                        act(pst, cacc, AF.Tanh, ["cacc"] + pre_all, [("qkT", c)], scale=0.5)
                        sc_ = 0.5 if c < 2 else 0.0625
                        ts("pool", pst, pst, sc_, sc_, ALU.mult, ALU.add, [("qkT", c)], [("qkT", c)])
                        tt("dve", pst, pst, cacc, ALU.mult, [("qkT", c), "cacc"], [("qkT", c)])
                    if dbg:
                        tap("qkT", qkT[:, :, :], [("qkT", i) for i in range(4)], F32)
                        tap("vaug", vaug[:, :, :, :], [("vaug", i) for i in range(NT)] + [("vaug", "ones")], BF16)
                        tap("Gt", Gt[:, :, :], [("Gt", i) for i in range(NT)])

                    chk('ph2')
                    fence("X")
                    chk('ph3a')
                    orders = [list(range(NT)), [1, 0] + list(range(NT - 1, 1, -1))]
                    written = set()
                    S.op("pool", lambda h: h.memset(m_cst_flat, 0.0), [], [("mtmp", "cst", 0), ("mtmp", "cst", 1)])
                    for d_ in range(2):
                        S.op("pool", lambda h, d_=d_: h.memset(m_vsf[d_], 0.0), [], [("mtmp", "vs", d_)])
                    for step in range(NT):
                        for d_ in range(2):
                            ti = orders[d_][step]
                            first = step == 0
                            tks = slice(ti * 128, (ti + 1) * 128)
                            for h4 in range(4):
                                pr = slice((h4 % 2) * 64, (h4 % 2) * 64 + 64)
                                gS = 1 + (h4 % 2)
                                mm(ps_g[gS][:, (h4 // 2) * 128:(h4 // 2 + 1) * 128], qkT[pr, 2 + h4 // 2, tks], qkT[pr, h4 // 2, tks], True, True,
                                   [("qkT", 2 + h4 // 2), ("qkT", h4 // 2)], ["g%d" % gS])
                            if step == 0 and d_ == 0: chk('sa')
                            mk = triF if d_ == 0 else triB
                            for j in range(2):
                                tr(ps_tp[:, j * 128:(j + 1) * 128], qkT[:, 2 + j, tks], identF, [("qkT", 2 + j), "cst"], ["tpa"])
                            cp("act", m_kt[d_], ps_tp[:, 0:256], ["tpa"], [("mtmp", "kt", d_)])
                            for par in range(2):
                                tt("dve", m_pt[d_][:, par * 2:par * 2 + 2, :], v3(ps_g[1 + par][:, 0:256], 2), bc(mk.unsqueeze(1), [128, 2, 128]), ALU.mult,
                                   ["g%d" % (1 + par), "cst"], [("mtmp", "pt", d_, par)])
                            wsl = v3(Ws[:, d_, :], NT)[:, ti, :]
                            tt("pool", m_vs[d_], vaug[:, ti, :, :], bc(wsl.unsqueeze(2), [128, 4, 65]), ALU.mult,
                               [("vaug", ti), ("vaug", "ones"), ("gder", "Ws", d_)], [("mtmp", "vs", d_)])
                            if step == 0 and d_ == 0: chk('sb')
                            gH = 3
                            pH = ps_g[gH]
                            pD = ps_g[4]
                            for h4 in range(4):
                                pr = slice((h4 % 2) * 64, (h4 % 2) * 64 + 64)
                                mm(pH[:, h4 * 72:h4 * 72 + 65], m_pt[d_][:, (h4 % 2) * 2 + h4 // 2, :], m_vs[d_][:, h4, :], True, first,
                                   [("mtmp", "pt", d_, h4 % 2), ("mtmp", "vs", d_)], ["g3"])
                                if not first:
                                    mm(pH[:, h4 * 72:h4 * 72 + 65], qkT[pr, h4 // 2, tks], m_cst[pr, d_, h4, :], False, True,
                                       [("qkT", h4 // 2), ("mtmp", "cst", d_)], ["g3"])
                            for pp in range(2):
                                mm(pD[:, pp * 144:pp * 144 + 144], m_kt[d_][:, pp * 128:pp * 128 + 128], m_vsf[d_][:, pp * 144:pp * 144 + 144], True, True,
                                   [("mtmp", "kt", d_), ("mtmp", "vs", d_)], ["g4"])
                            if step == 0 and d_ == 0: chk('sc')
                            ejs = v3(Ej[:, d_, :], NT)[:, ti, :]
                            tt("dve", m_hs[d_], v3(pH[:, 0:288], 4)[:, :, 0:65], bc(ejs.unsqueeze(2), [128, 4, 65]), ALU.mult, ["g3", ("gder", "Ej")], [("mtmp", "hs", d_)])
                            if step == 0 and d_ == 0: chk('sd')
                            den = m_hs[d_][:, :, 64]
                            d2 = small[:, 208:212]
                            tt("dve", d2, den, den, ALU.mult, [("mtmp", "hs", d_)], ["md2"])
                            ts("dve", d2, d2, 1.0, None, ALU.max, None, ["md2"], ["md2"])
                            tt("pool", d2, d2, neghalf[:, 0:4], ALU.pow, ["md2", "cst"], ["md2"])
                            if ti not in written:
                                tt("dve", v3(Hml[:, ti, :], 4), m_hs[d_][:, :, 0:64], bc(d2.unsqueeze(2), [128, 4, 64]), ALU.mult,
                                   [("mtmp", "hs", d_), "md2"], [("Hml", ti)])
                                written.add(ti)
                            else:
                                tt("dve", m_hn, m_hs[d_][:, :, 0:64], bc(d2.unsqueeze(2), [128, 4, 64]), ALU.mult,
                                   [("mtmp", "hs", d_), "md2"], [("mtmp", "hn")])
                                tt("pool", v3(Hml[:, ti, :], 4), v3(Hml[:, ti, :], 4), m_hn, ALU.add, [("mtmp", "hn"), ("Hml", ti)], [("Hml", ti)])
                            if step == 0 and d_ == 0: chk('se')
                            tt("dve", m_tmp, v3(pD[:, 0:288], 4)[:, :, 0:65], m_cst[:, d_, :, :], ALU.add, ["g4", ("mtmp", "cst", d_)], [("mtmp", "tmp")])
                            ees = v3(Eend[:, d_, :], NT)[:, ti, :]
                            tt("dve", m_cst[:, d_, :, :], m_tmp, bc(ees.unsqueeze(2), [128, 4, 65]), ALU.mult, [("mtmp", "tmp"), ("gder", "Eend")], [("mtmp", "cst", d_)])
                    chk('sf') if False else None
                    if dbg:
                        tap("Hml", Hml[:, :, :], [("Hml", i) for i in range(NT)])

                    chk('ph3')
                    fence("X")
                    fence("B")
                    S.op("pool", lambda h: h.memset(dvaug[:, :, :, 128:129], 1.0), [], [("dvaug", "ones")])
                    for hh in range(2):
                        slot = next_slot()
                        wload(l, C_DAK + hh * 512, 512, slot)
                        for h2 in range(2):
                            h4 = hh * 2 + h2
                            for bi, (t0, t1) in enumerate(BLKS):
                                n = t1 - t0
                                ga, gb = [(1, 2), (3, 4)][bi % 2]
                                for kc in range(8):
                                    mm(ps_g[ga][:, 0:n], wring[slot][:, kc, h2 * 256:h2 * 256 + 128], hTall[:, kc, t0:t1], kc == 0, kc == 7,
                                       [("wring", slot)] + hT_keys[t0 // 128:t1 // 128], ["g%d" % ga])
                                if bi == 0:
                                    act(dkT[:, h4, t0:t1], ps_g[ga][:, 0:n], AF.Identity, ["g%d" % ga, "fmb"], [("dkT", h4, bi)], bias=fmb[:, l, 4 + 2 * h4:5 + 2 * h4])
                                    continue
                                for kc in range(8):
                                    mm(ps_g[gb][:, 0:n], wring[slot][:, kc, h2 * 256 + 128:h2 * 256 + 256], hTall[:, kc, t0:t1], kc == 0, kc == 7,
                                       [("wring", slot)] + hT_keys[t0 // 128:t1 // 128], ["g%d" % gb])
                                p0 = t0 - 256
                                dma(ropeC[:, 0:n], rope_d[0, :, p0:p0 + n], [], [("rope4", "c")])
                                dma(ropeS[:, 0:n], rope_d[1, :, p0:p0 + n], [], [("rope4", "s")])
                                stt("dve", rstA[:, 0:n], ps_g[ga][:, 0:n], fmb[:, l, 4 + 2 * h4:5 + 2 * h4], ropeC[:, 0:n], ALU.add, ALU.mult,
                                    ["g%d" % ga, "fmb", ("rope4", "c")], [("rope4", "a")])
                                stt("dve", rstB[:, 0:n], ps_g[gb][:, 0:n], fmb[:, l, 5 + 2 * h4:6 + 2 * h4], ropeS[:, 0:n], ALU.add, ALU.mult,
                                    ["g%d" % gb, "fmb", ("rope4", "s")], [("rope4", "b")])
                                tt("pool", dkT[:, h4, t0:t1], rstA[:, 0:n], rstB[:, 0:n], ALU.add, [("rope4", "a"), ("rope4", "b")], [("dkT", h4, bi)])
                    slot = next_slot()
                    wload(l, C_DAV, 512, slot)
                    for ti in range(NT):
                        gix = next_g(3, 5)
                        pg = ps_g[gix]
                        for kc in range(8):
                            mm(pg[:, :], hTall[:, kc, ti * 128:(ti + 1) * 128], wring[slot][:, kc, :], kc == 0, kc == 7,
                               [("wring", slot), hT_keys[ti]], ["g%d" % gix])
                        tt("dve", dvaug[:, ti, :, 0:128], v3(pg[:, :], 4), v3(biasb[:, B_DAV:B_DAV + 512], 4), ALU.add,
                           ["g%d" % gix, ("biasb", B_DAV)], [("dvaug", ti)])
                    s_a = next_slot()
                    wload(l, C_SG, 512, s_a)
                    s_b = next_slot()
                    wload(l, C_SG + 512, 256, s_b)
                    for tp_ in range(NT // 2):
                        for j in range(2):
                            ti = 2 * tp_ + j
                            ga = 1 if j == 0 else 3
                            for kc in range(8):
                                mm(ps_g[ga][:, :], hTall[:, kc, ti * 128:(ti + 1) * 128], wring[s_a][:, kc, :], kc == 0, kc == 7,
                                   [("wring", s_a), hT_keys[ti]], ["g%d" % ga])
                            for kc in range(8):
                                mm(ps_g[2][:, j * 256:(j + 1) * 256], hTall[:, kc, ti * 128:(ti + 1) * 128], wring[s_b][:, kc, 0:256], kc == 0, kc == 7,
                                   [("wring", s_b), hT_keys[ti]], ["g2"])
                            tt("dve", sgp2[:, j, 0:512], ps_g[ga][:, :], biasb[:, B_SG:B_SG + 512], ALU.add, ["g%d" % ga, ("biasb", B_SG)], [("sgt", "p", j)])
                        tt("dve", sgp2[:, :, 512:768], v3(ps_g[2][:, :], 2), bc(biasb[:, B_SG + 512:B_SG + 768].unsqueeze(1), [128, 2, 256]), ALU.add,
                           ["g2", ("biasb", B_SG)], [("sgt", "pz")])
                        kp = [("sgt", "p", 0), ("sgt", "p", 1)]
                        act(sgu2, sgp2[:, :, 0:256], AF.Gelu, kp, [("sgt", "u")])
                        for j in range(2):
                            act(sgp2[:, j, 256:512], sgp2[:, j, 256:512], AF.Gelu, [("sgt", "p", j)], [("sgt", "gv", j)], accum=small[:, 240 + j:241 + j])
                            act(sgjunk2[:, j, :], sgp2[:, j, 256:512], AF.Square, [("sgt", "gv", j)], [("sgt", "junk", j)], accum=small[:, 244 + j:245 + j])
                        act(sgzs2, sgp2[:, :, 512:768], AF.Tanh, [("sgt", "pz")], [("sgt", "zs")], scale=0.5)
                        kgv = [("sgt", "gv", 0), ("sgt", "gv", 1)]
                        kjk = [("sgt", "junk", 0), ("sgt", "junk", 1)]
                        mean, rstd = stats(small[:, 240:242], small[:, 244:246], 2, 256.0, (kgv, kjk))
                        for j in range(2):
                            ts("dve", sgjunk2[:, j, :], sgp2[:, j, 256:512], mean[:, j:j + 1], rstd[:, j:j + 1], ALU.subtract, ALU.mult,
                               [("sgt", "gv", j), "st_mean", "st_rstd"], [("sgt", "junk", j)])
                        tt("dve", sgjunk2, sgjunk2, bc(sggt.unsqueeze(1), [128, 2, 256]), ALU.mult, kjk + ["sggt"], kjk)
                        tt("dve", sgvn2, sgjunk2, bc(sgbt.unsqueeze(1), [128, 2, 256]), ALU.add, kjk + ["sgbt"], [("sgt", "vn")])
                        for j in range(2):
                            for g in range(4):
                                mm(ps_g[4][:, j * 256 + g * 64:j * 256 + (g + 1) * 64], sgW[:, l, g, :], sgvn2[:, j, g * 64:(g + 1) * 64], True, True,
                                   ["sgW", ("sgt", "vn")], ["g4"])
                        for j in range(2):
                            tt("dve", v3(sgy2[:, j, :], 4), v3(ps_g[4][:, j * 256:(j + 1) * 256], 4), bc(sgbs[:, l, :].unsqueeze(2), [128, 4, 64]), ALU.add,
                               ["g4", "sgbs"], [("sgt", "y", j)])
                        ky = [("sgt", "y", 0), ("sgt", "y", 1)]
                        tt("dve", sgy2, sgy2, sgu2, ALU.mult, ky + [("sgt", "u")], ky)
                        stt("dve", sgzs2, sgzs2, 1.0, sgp2[:, :, 512:768], ALU.add, ALU.mult, [("sgt", "zs"), ("sgt", "pz")], [("sgt", "zs")])
                        stt("dve", ysg[:, 2 * tp_:2 * tp_ + 2, :], sgy2, 0.5, sgzs2, ALU.mult, ALU.mult, ky + [("sgt", "zs")], [("ysg", 2 * tp_), ("ysg", 2 * tp_ + 1)])
                    if dbg:
                        tap("dkT", dkT[:, :, :], [("dkT", h_, b_) for h_ in range(4) for b_ in range(5)], BF16)
                        tap("ysg", ysg[:, :, :], [("ysg", i) for i in range(NT)], BF16)

                    chk('ph4')
                    fence("X")
                    fence("A")
                    last = (l == NL - 1)
                    for bi, (t0, t1) in enumerate(BLKS):
                        n = t1 - t0
                        ntile = n // 128
                        isctx = bi == 0
                        if last and isctx:
                            continue
                        xb = xblk[0]
                        if l == 0:
                            srcx = ctx_d[b, :, :] if isctx else x_d[b, t0 - 256:t1 - 256, :]
                            rk = []
                        else:
                            srcx = xs_d[b, t0:t1, :]
                            rk = [("xs", b, bi, ti_) for ti_ in range((t1 - t0) // 128)]
                        srcx3 = srcx.rearrange("(n p) d -> p n d", p=128)
                        for ti in range(ntile):
                            dma(xb[:, ti, :], srcx3[:, ti, :], rk, [("xblk", 0, ti)])
                        s1p, shp = (modp[:, 3, :], modp[:, 2, :]) if isctx else (modp[:, 1, :], modp[:, 0, :])
                        ln_block(xb, ntile, lambda ti: ("xblk", 0, ti), [xn5, xn5b], [("p5", "xn"), ("p5", "xnb")], lambda ti: hTblk[:, :, ti * 128:(ti + 1) * 128], lambda ti: ("p5", "hT", ti), s1p, shp, modk)
                        hk = [("p5", "hT", ti) for ti in range(ntile)]
                        if not isctx:
                            p0 = t0 - 256
                            dma(p5ropeC[:, 0:n], rope_d[0, :, p0:p0 + n], [], [("p5x", "c")])
                            dma(p5ropeS[:, 0:n], rope_d[1, :, p0:p0 + n], [], [("p5x", "s")])
                        for hh in range(2):
                            slot = next_slot()
                            wload(l, C_DAQ + hh * 512, 512, slot)
                            for h2 in range(2):
                                h4 = hh * 2 + h2
                                ga, gb = 0, 1
                                for kc in range(8):
                                    mm(ps_g[ga][:, 0:n], wring[slot][:, kc, h2 * 256:h2 * 256 + 128], hTblk[:, kc, 0:n], kc == 0, kc == 7, [("wring", slot)] + hk, ["g%d" % ga])
                                if isctx:
                                    act(qTblk[:, h4, 0:n], ps_g[ga][:, 0:n], AF.Identity, ["g%d" % ga, "fmb"], [("p5", "qT", h4)], bias=fmb[:, l, 12 + 2 * h4:13 + 2 * h4])
                                    continue
                                for kc in range(8):
                                    mm(ps_g[gb][:, 0:n], wring[slot][:, kc, h2 * 256 + 128:h2 * 256 + 256], hTblk[:, kc, 0:n], kc == 0, kc == 7, [("wring", slot)] + hk, ["g%d" % gb])
                                stt("dve", p5stA[:, 0:n], ps_g[ga][:, 0:n], fmb[:, l, 12 + 2 * h4:13 + 2 * h4], p5ropeC[:, 0:n], ALU.add, ALU.mult,
                                    ["g%d" % ga, "fmb", ("p5x", "c")], [("p5x", "a")])
                                stt("dve", p5stB[:, 0:n], ps_g[gb][:, 0:n], fmb[:, l, 13 + 2 * h4:14 + 2 * h4], p5ropeS[:, 0:n], ALU.add, ALU.mult,
                                    ["g%d" % gb, "fmb", ("p5x", "s")], [("p5x", "b")])
                                tt("pool", qTblk[:, h4, 0:n], p5stA[:, 0:n], p5stB[:, 0:n], ALU.add, [("p5x", "a"), ("p5x", "b")], [("p5", "qT", h4)])
                        for gi_, (c0, boff) in enumerate([(C_MLOZ, B_MLOZ), (C_DAZ, B_DAZ)]):
                            slot = next_slot()
                            wload(l, c0, 512, slot)
                            for ti in range(ntile):
                                gix = 2 + ti % 2
                                pg = ps_g[gix]
                                for kc in range(8):
                                    mm(pg[:, :], hTblk[:, kc, ti * 128:(ti + 1) * 128], wring[slot][:, kc, :], kc == 0, kc == 7, [("wring", slot), hk[ti]], ["g%d" % gix])
                                tt("dve", p5tmp, pg[:, :], biasb[:, boff:boff + 512], ALU.add, ["g%d" % gix, ("biasb", boff)], [("p5x", "tmp")])
                                act(p5t2, p5tmp, AF.Tanh, [("p5x", "tmp")], [("p5x", "t2")], scale=0.5)
                                gdst = gates[:, ti, gi_ * 512:(gi_ + 1) * 512]
                                if gi_ == 0:
                                    ts("dve", gdst[:, 0:256], p5t2[:, 0:256], 0.5, 0.5, ALU.mult, ALU.add, [("p5x", "t2")], [("p5", "g", ti, 0)])
                                    stt("dve", gdst[:, 256:512], p5t2[:, 256:512], 1.0, p5tmp[:, 256:512], ALU.add, ALU.mult, [("p5x", "t2"), ("p5x", "tmp")], [("p5", "g", ti, 1)])
                                else:
                                    stt("dve", gdst, p5t2, 1.0, p5tmp, ALU.add, ALU.mult, [("p5x", "t2"), ("p5x", "tmp")], [("p5", "g", ti, 2)])
                        tg0 = t0 // 128
                        mlv = v3(p5x_ml, 4)[:, 0:ntile, :]
                        sqv = v3(p5x_sq, 4)[:, 0:ntile, :]
                        kml = [("p5x", "c"), ("p5x", "s")]
                        ksq = [("p5x", "tmp"), ("p5x", "t2")]
                        k4 = ntile * 4
                        tt("dve", mlv, gates[:, 0:ntile, 0:256], Hml[:, tg0:tg0 + ntile, :], ALU.mult,
                           [("p5", "g", ti, 0) for ti in range(ntile)] + [("Hml", tg0 + ti) for ti in range(ntile)], kml)
                        red(small[:, 64:64 + k4], mlv.rearrange("p a (h d) -> p (a h) d", h=4), kml, ["mls1"])
                        tt("dve", sqv, mlv, mlv, ALU.mult, kml, ksq)
                        red(small[:, 80:80 + k4], sqv.rearrange("p a (h d) -> p (a h) d", h=4), ksq, ["mls2"])
                        mean, rstd = stats(small[:, 64:64 + k4], small[:, 80:80 + k4], k4, 64.0, "ml")
                        ml3 = mlv.rearrange("p a (h d) -> p (a h) d", h=4)
                        tt("dve", ml3, ml3, bc(mean.unsqueeze(2), [128, k4, 64]), ALU.subtract, kml + ["st_mean"], kml)
                        tt("dve", ml3, ml3, bc(rstd.unsqueeze(2), [128, k4, 64]), ALU.mult, kml + ["st_rstd"], kml)
                        tt("dve", mlv, mlv, bc(mlgc.unsqueeze(1), [128, ntile, 256]), ALU.mult, kml + ["mlgc"], kml)
                        gml = gates[:, 0:ntile, 0:256]
                        tt("dve", gml, mlv, gates[:, 0:ntile, 256:512], ALU.mult, kml + [("p5", "g", ti, 1) for ti in range(ntile)] + [("p5", "g", ti, 0) for ti in range(ntile)], [("p5", "mixml")])
                        ktiles = list(range(0, 2)) if isctx else list(range(NT))
                        STB = [(ps_g[4], "g4"), (ps_tp[:, 0:512], "tpa"), (ps_tp[:, 512:1024], "tpb")]
                        nk = len(ktiles)
                        LOOK = 2

                        def combine(h4):
                            nt_ = ntile
                            ok_ = [("p5", "oev", qi, c) for qi in range(nt_) for c in range(2)]
                            rr = v3(small[:, 192:200], 4)[:, 0:nt_, :]
                            attv = v3(p5stA, 4)[:, 0:nt_, :]
                            tmpv = v3(p5stB, 4)[:, 0:nt_, :]
                            ssv = small[:, 200:200 + nt_]
                            S.op("dve", lambda h, rr=rr, nt_=nt_: h.reciprocal(out=rr, in_=oev[:, 0:nt_, :, 128]), ok_, ["p5rr"])
                            ts("dve", rr[:, :, 1], rr[:, :, 1], neglam[:, l:l + 1], None, ALU.mult, None, ["p5rr", "neglam"], ["p5rr"])
                            tt("dve", attv, oev[:, 0:nt_, 0, 0:128], bc(rr[:, :, 0:1], [128, nt_, 128]), ALU.mult, ok_ + ["p5rr"], [("p5x", "a")])
                            tt("dve", tmpv, oev[:, 0:nt_, 1, 0:128], bc(rr[:, :, 1:2], [128, nt_, 128]), ALU.mult, ok_ + ["p5rr"], [("p5x", "b")])
                            tt("pool", attv, attv, tmpv, ALU.add, [("p5x", "a"), ("p5x", "b")], [("p5x", "a")])
                            tt("dve", tmpv, attv, attv, ALU.mult, [("p5x", "a")], [("p5x", "b")])
                            red(ssv, tmpv, [("p5x", "b")], ["p5ss"])
                            ts("dve", ssv, ssv, 1.0 / 128.0, EPS, ALU.mult, ALU.add, ["p5ss"], ["p5ss"])
                            tt("pool", ssv, ssv, neghalf[:, 0:nt_], ALU.pow, ["p5ss", "cst"], ["p5ss"])
                            tt("dve", attv, attv, bc(ssv.unsqueeze(2), [128, nt_, 128]), ALU.mult, [("p5x", "a"), "p5ss"], [("p5x", "a")])
                            tt("dve", attv, attv, bc(dagc.unsqueeze(1), [128, nt_, 128]), ALU.mult, [("p5x", "a"), "dagc"], [("p5x", "a")])
                            gsl = gates[:, 0:nt_, 512 + h4 * 128:512 + (h4 + 1) * 128]
                            tt("dve", gsl, attv, gsl, ALU.mult, [("p5x", "a")] + [("p5", "g", qi, 2) for qi in range(nt_)], [("p5", "damix", h4)])
                        tg0 = t0 // 128

                        items = [(h4, c, kti) for h4 in range(4) for c in range(2) for kti in range(nk)]
                        for g in range(len(items) + LOOK):
                            if g < len(items):
                                h4, c, kti = items[g]
                                kt = ktiles[kti]
                                pr = slice(c * 64, c * 64 + 64)
                                pS, gk = STB[g % 3]
                                mm(pS[:, 0:n], dkT[pr, h4, kt * 128:(kt + 1) * 128], qTblk[pr, h4, 0:n], True, True,
                                   [("dkT", h4, 0 if kt < 2 else 1 + (kt - 2) // 4), ("p5", "qT", h4)], [gk])
                                act(ptb[g % 4][:, 0:n], pS[:, 0:n], AF.Exp, [gk], [("p5", "pt", g % 4)], scale=0.125)
                            j = g - LOOK
                            if j >= 0:
                                h4, c, kti = items[j]
                                kt = ktiles[kti]
                                for qi in range(ntile):
                                    mm(ps_g[qi][:, 0:129], ptb[j % 4][:, qi * 128:(qi + 1) * 128], dvaug[:, kt, h4, :], kti == 0, kti == nk - 1,
                                       [("p5", "pt", j % 4), ("dvaug", kt), ("dvaug", "ones")], ["g%d" % qi])
                                if kti == nk - 1:
                                    for qi in range(ntile):
                                        cp("act" if qi % 2 else "dve", oev[:, qi, c, :], ps_g[qi][:, 0:129], ["g%d" % qi], [("p5", "oev", qi, c)])
                                    if c == 1:
                                        combine(h4)
                        for ti in range(ntile):
                            tg = tg0 + ti
                            for kc in range(8):
                                if kc < 2:
                                    src_, rk_ = gates[:, ti, kc * 128:(kc + 1) * 128], [("p5", "mixml")]
                                elif kc < 6:
                                    src_, rk_ = gates[:, ti, 512 + (kc - 2) * 128:512 + (kc - 1) * 128], [("p5", "damix", kc - 2)]
                                else:
                                    src_, rk_ = ysg[:, tg, (kc - 6) * 128:(kc - 5) * 128], [("ysg", tg)]
                                tr(ps_tb[:, kc * 128:(kc + 1) * 128], src_, identB, rk_ + ["cstb"], ["tb"])
                            cp("act", mixT[:, :, ti * 128:(ti + 1) * 128], v3(ps_tb[:, :], 8), ["tb"], [("p5", "mixT", ti)])
                        if dbg:
                            tap("mix%d" % bi, mixT[:, :, 0:n], [("p5", "mixT", ti) for ti in range(ntile)], BF16)
                        gateb = gate_c if isctx else gate_l
                        gkey = "gate_c" if isctx else "gate_l"
                        for hf in range(2):
                            slot = next_slot()
                            wload(l, hf * 512, 512, slot, src=wobf_d)
                            for ti in range(ntile):
                                gix = ti % 2
                                pg = ps_g[gix]
                                for kc in range(8):
                                    mm(pg[:, :], mixT[:, kc, ti * 128:(ti + 1) * 128], wring[slot][:, kc, :], kc == 0, kc == 7, [("wring", slot), ("p5", "mixT", ti)], ["g%d" % gix])
                                tt("dve", p5tmp, pg[:, :], gateb[:, hf * 512:(hf + 1) * 512], ALU.mult, ["g%d" % gix, gkey], [("p5x", "tmp")])
                                stt("dve", xb[:, ti, hf * 512:(hf + 1) * 512], xb[:, ti, hf * 512:(hf + 1) * 512], ALPHA, p5tmp, ALU.mult, ALU.add,
                                    [("xblk", 0, ti), ("p5x", "tmp")], [("xblk", 0, ti)])
                        s1 = small[:, 224:224 + ntile]
                        s2 = small[:, 232:232 + ntile]
                        k1 = [("lnst", 1, ti) for ti in range(ntile)]
                        k2 = [("lnst", 2, ti) for ti in range(ntile)]
                        for ti in range(ntile):
                            act(xn5, xb[:, ti, :], AF.Square, [("xblk", 0, ti)], [("p5", "xn"), k2[ti]], accum=s2[:, ti:ti + 1])
                            act(xn5, xb[:, ti, :], AF.Identity, [("xblk", 0, ti)], [("p5", "xn"), k1[ti]], accum=s1[:, ti:ti + 1])
                        mean, rstd = stats(s1, s2, ntile, 1024.0, (k1, k2))
                        if last:
                            dst3 = y_d[b, t0 - 256:t1 - 256, :].rearrange("(n p) d -> p n d", p=128)
                        else:
                            dst3 = xs_d[b, t0:t1, :].rearrange("(n p) d -> p n d", p=128)
                        for ti in range(ntile):
                            xk_ = ("xblk", 0, ti)
                            ts("dve", xb[:, ti, :], xb[:, ti, :], mean[:, ti:ti + 1], rstd[:, ti:ti + 1], ALU.subtract, ALU.mult, [xk_, "st_mean", "st_rstd"], [xk_])
                            tt("dve", xb[:, ti, :], xb[:, ti, :], lng, ALU.mult, [xk_, "lng"], [xk_])
                            tt("pool" if ti % 2 else "dve", xb[:, ti, :], xb[:, ti, :], lnb, ALU.add, [xk_, "lnb"], [xk_])
                            dma(dst3[:, ti, :], xb[:, ti, :], [xk_], [("y", b, bi, ti) if last else ("xs", b, bi, ti)])

        except _Stop:
            pass
        S.emit(nc, st)
    return nc, S, dbg_out


def make_consts():
    i = np.arange(128)
    ident = np.eye(128, dtype=np.float32)
    triF = (i[:, None] <= i[None, :]).astype(np.float32)
    triB = (i[:, None] >= i[None, :]).astype(np.float32)
    ones = np.ones((128, 128), np.float32)
    partner = np.where((i % 32) < 16, i + 16, i - 16)
    rperm = np.zeros((128, 128), np.float32)
    rperm[partner, i] = 1.0
    cst = np.concatenate([ident, triF, triB, ones, rperm], 1)
    t = np.arange(LAT)
    row = (t // 64).astype(np.float32)
    col = (t % 64).astype(np.float32)
    half = 32
    inv = (10000.0 ** (-(np.arange(0, half, 2, dtype=np.float32)) / half)).astype(np.float32)
    ang_r = row[:, None] * inv
    ang_c = col[:, None] * inv
    ang = np.concatenate([ang_r, ang_r, ang_c, ang_c], -1).astype(np.float32)
    cos = np.cos(ang).astype(np.float32)
    sin = np.sin(ang).astype(np.float32)
    d = np.arange(64)
    sign = np.where((d % 32) < 16, -1.0, 1.0).astype(np.float32)
    sinS = sin * sign[None, :]
    cosT = np.concatenate([cos.T, cos.T], 0)
    sinT = np.concatenate([sinS.T, sinS.T], 0)
    rope = np.ascontiguousarray(np.stack([cosT, sinT], 0)).astype(np.float32)
    return np.ascontiguousarray(cst), rope


_CACHE = {}


def kernel(**inputs):
    NCORES = 8
    NB = 32 // NCORES
    key = (NB, DEPTH)
    if key not in _CACHE:
        _CACHE[key] = build(NB, DEPTH)
    nc = _CACHE[key][0]
    cst, rope = make_consts()
    f = lambda a: np.ascontiguousarray(np.asarray(a, dtype=np.float32))
    shared = {k: f(inputs[k]) for k in ["w_mod", "b_mod", "w_in", "b_in", "ml_conv_w", "ml_conv_b", "ml_norm_g",
                                        "da_lam_q1", "da_lam_k1", "da_lam_q2", "da_lam_k2", "da_norm_g", "sg_norm_g",
                                        "sg_norm_b", "sg_w_s", "sg_b_s", "w_out", "ln_g", "ln_b"]}
    shared["c_ctx"] = f(inputs["c_ctx"]).reshape(1, D)
    shared["cst"] = cst
    shared["rope"] = rope
    x = f(inputs["x"])
    ctx = f(inputs["ctx"])
    c = f(inputs["c"])
    in_maps = []
    for i in range(NCORES):
        m = dict(shared)
        m["x"] = x[i * NB:(i + 1) * NB]
        m["ctx"] = ctx[i * NB:(i + 1) * NB]
        m["c"] = c[i * NB:(i + 1) * NB]
        in_maps.append(m)
    res = run_bass_kernel_spmd(nc, in_maps, core_ids=list(range(NCORES)))
    return np.concatenate([r["y"] for r in res.results], axis=0).astype(np.float32)
```

```python
import math
import numpy as np
from contextlib import ExitStack
import concourse.bass as bass
import concourse.mybir as mybir
from concourse.bass_utils import run_bass_kernel_spmd

F32 = mybir.dt.float32
BF16 = mybir.dt.bfloat16
AF = mybir.ActivationFunctionType
ALU = mybir.AluOpType
AX = mybir.AxisListType

D = 1024
CTXL = 256
LAT = 2048
T = CTXL + LAT
NT = T // 128
DEPTH = 4
EPS = 1e-5
ALPHA = (2 * DEPTH) ** 0.25
NWC = 5136
BLKS = [(0, 256), (256, 768), (768, 1280), (1280, 1792), (1792, 2304)]
NDMA = 24
PSUM_KEYS = frozenset(["g0", "g1", "g2", "g3", "g4", "tpa", "tpb", "tb"])

C_MLQK, C_MLV, C_DAK, C_DAV, C_SG, C_DAQ, C_MLOZ, C_DAZ = 0, 512, 784, 1808, 2320, 3088, 4112, 4624
B_MLV, B_DAV, B_SG, B_MLOZ, B_DAZ, NBIAS = 0, 272, 784, 1552, 2064, 2576


class Sched:
    def __init__(self):
        self.ins = []
        self.lastw = {}
        self.readers = {}
        self.region = {}
        self.rlast = {}
        self.rdma = {}

    def _add(self, eng, fn, reads, writes, is_dma, extra=()):
        reads = list(reads)
        writes = list(writes)
        for k in reads:
            if k in PSUM_KEYS:
                writes.append(("rdser", k))
        deps = set(extra)
        touched = set()
        for k in reads + writes:
            nm = k[0] if isinstance(k, tuple) else k
            r = self.region.get(nm)
            if r is not None:
                touched.add(r)
        for r in touched:
            w = self.lastw.get(("R", r))
            if w is not None:
                deps.add(w)
        for k in reads:
            w = self.lastw.get(k)
            if w is not None:
                deps.add(w)
        for k in writes:
            w = self.lastw.get(k)
            if w is not None:
                deps.add(w)
            rd = self.readers.get(k)
            if rd:
                deps.update(rd[0].values())
                deps.update(rd[1])
        idx = len(self.ins)
        self.ins.append([eng, fn, deps, is_dma])
        for k in writes:
            self.lastw[k] = idx
            self.readers[k] = [{}, []]
        ws = set(writes)
        for k in reads:
            if k not in ws:
                rd = self.readers.setdefault(k, [{}, []])
                if is_dma:
                    rd[1].append(idx)
                else:
                    rd[0][eng] = idx
        for r in touched:
            if is_dma:
                self.rdma.setdefault(r, []).append(idx)
            else:
                self.rlast.setdefault(r, {})[eng] = idx
        return idx

    def fence(self, region, eng, fn):
        deps = set(self.rlast.get(region, {}).values()) | set(self.rdma.get(region, []))
        idx = self._add(eng, fn, (), ("fence_dummy",), False, extra=deps)
        self.lastw[("R", region)] = idx
        self.rlast[region] = {}
        self.rdma[region] = []
        return idx

    def op(self, eng, fn, reads=(), writes=()):
        return self._add(eng, fn, reads, writes, False)

    def dma(self, eng, fn, reads=(), writes=()):
        return self._add(eng, fn, reads, writes, True)

    def emit(self, nc, stack):
        ins = self.ins
        n = len(ins)
        engs = ["pe", "act", "dve", "pool", "sp"]
        dma_list = [i for i in range(n) if ins[i][3]]
        dma_slot = {}
        for j, i in enumerate(dma_list):
            dma_slot[i] = j
            if j >= NDMA:
                ins[i][2].add(dma_list[j - NDMA])
        needed = [False] * n
        for i in range(n):
            e = ins[i][0]
            nd = set()
            for d in ins[i][2]:
                if ins[d][0] == e and e == "pe" and not ins[d][3]:
                    continue
                nd.add(d)
                needed[d] = True
            ins[i][2] = nd
        esem = {e: stack.enter_context(nc.semaphore("s_" + e)) for e in engs}
        dsem = [stack.enter_context(nc.semaphore("d_%d" % j)) for j in range(NDMA)]
        cnt = {e: 0 for e in engs}
        tok = [None] * n
        for i in range(n):
            e, fn, deps, is_dma = ins[i]
            if is_dma:
                j = dma_slot[i]
                tok[i] = (("d", j % NDMA), 16 * (j // NDMA + 1))
            elif needed[i]:
                cnt[e] += 1
                tok[i] = (("e", e), cnt[e])
        per = {e: [] for e in engs}
        for i in range(n):
            per[ins[i][0]].append(i)
        self.counts = {e: len(per[e]) for e in engs}

        def semof(key):
            return esem[key[1]] if key[0] == "e" else dsem[key[1]]

        def run(e, h):
            seen = {}
            for i in per[e]:
                _, fn, deps, is_dma = ins[i]
                want = {}
                for d in deps:
                    k, v = tok[d]
                    if v > want.get(k, 0):
                        want[k] = v
                for k, v in want.items():
                    if seen.get(k, 0) < v:
                        h.wait_ge(semof(k), v)
                        seen[k] = v
                r = fn(h)
                if tok[i] is not None:
                    k, v = tok[i]
                    r.then_inc(semof(k), 16 if is_dma else 1)
            for i in per[e]:
                if ins[i][3]:
                    k, v = tok[i]
                    if seen.get(k, 0) < v:
                        h.wait_ge(semof(k), v)
                        seen[k] = v

        with nc.Block() as block:
            @block.tensor
            def _(h):
                run("pe", h)

            @block.scalar
            def _(h):
                run("act", h)

            @block.vector
            def _(h):
                run("dve", h)

            @block.gpsimd
            def _(h):
                run("pool", h)

            @block.sync
            def _(h):
                run("sp", h)


class Arena:
    def __init__(self, ap):
        self.ap = ap
        self.off = 0
        self.size = ap.shape[1]

    def f32(self, n):
        a = self.ap[:, self.off:self.off + n]
        self.off += (n + 7) // 8 * 8
        assert self.off <= self.size, (self.off, self.size)
        return a

    def bf16(self, n):
        w = (n + 1) // 2
        a = self.ap[:, self.off:self.off + w].bitcast(BF16)
        self.off += (w + 7) // 8 * 8
        assert self.off <= self.size, (self.off, self.size)
        return a[:, 0:n]


def v3(ap, a):
    return ap.rearrange("p (a b) -> p a b", a=a)


def v4(ap, a, b):
    return ap.rearrange("p (a b c) -> p a b c", a=a, b=b)


def bc(ap, shape):
    return ap.to_broadcast(shape)


class _Stop(Exception):
    pass


def build(NB, NL, dbg=None, stop=None):
    nc = bass.Bass("TRN2", target_bir_lowering=False)
    S = Sched()

    def dram(name, shape, dtype=F32, kind="ExternalInput"):
        return nc.dram_tensor(name, shape, dtype, kind=kind).ap()

    x_d = dram("x", [NB, LAT, D])
    ctx_d = dram("ctx", [NB, CTXL, D])
    c_d = dram("c", [NB, D])
    cctx_d = dram("c_ctx", [1, D])
    wmod_d = dram("w_mod", [DEPTH, D, 3 * D])
    bmod_d = dram("b_mod", [DEPTH, 3 * D])
    win_d = dram("w_in", [DEPTH, D, 4112])
    bin_d = dram("b_in", [DEPTH, 4112])
    convw_d = dram("ml_conv_w", [DEPTH, 3, 512])
    convb_d = dram("ml_conv_b", [DEPTH, 512])
    mlg_d = dram("ml_norm_g", [DEPTH, 256])
    lq1_d = dram("da_lam_q1", [DEPTH, 64])
    lk1_d = dram("da_lam_k1", [DEPTH, 64])
    lq2_d = dram("da_lam_q2", [DEPTH, 64])
    lk2_d = dram("da_lam_k2", [DEPTH, 64])
    dag_d = dram("da_norm_g", [DEPTH, 128])
    sgg_d = dram("sg_norm_g", [DEPTH, 256])
    sgb_d = dram("sg_norm_b", [DEPTH, 256])
    sgw_d = dram("sg_w_s", [DEPTH, 4, 128, 128])
    sgbs_d = dram("sg_b_s", [DEPTH, 4, 128])
    wout_d = dram("w_out", [DEPTH, D, D])
    lng_d = dram("ln_g", [DEPTH, D])
    lnb_d = dram("ln_b", [DEPTH, D])
    cst_d = dram("cst", [128, 640])
    rope_d = dram("rope", [2, 128, LAT])
    y_d = dram("y", [NB, LAT, D], kind="ExternalOutput")
    wbf_d = dram("wbf", [DEPTH, D, NWC], BF16, kind="Internal")
    wobf_d = dram("wobf", [DEPTH, D, D], BF16, kind="Internal")
    mod_d = dram("modscr", [DEPTH, 8, 3 * D], F32, kind="Internal")
    xs_d = dram("xs", [NB, T, D], F32, kind="Internal")
    dbg_out = {}

    with ExitStack() as st:
        arena_t = st.enter_context(nc.sbuf_tensor("arena", [128, 51200], F32))
        AR = Arena(arena_t[:, :])
        ps_tp = st.enter_context(nc.psum_tensor("ps_tp", [128, 1024], F32))
        ps_tb = st.enter_context(nc.psum_tensor("ps_tb", [128, 1024], BF16))
        ps_g = [st.enter_context(nc.psum_tensor("ps_g%d" % i, [128, 512], F32)) for i in range(5)]

        cstF = AR.f32(640)
        identF, triF, triB, onesF, rperm = (cstF[:, i * 128:(i + 1) * 128] for i in range(5))
        identB = AR.bf16(128)
        maskF = AR.bf16(128)
        maskB = AR.bf16(128)
        neghalf = AR.f32(64)
        dummy = AR.f32(8)
        sgW = v4(AR.bf16(DEPTH * 4 * 128), DEPTH, 4)
        sgbs = v3(AR.f32(DEPTH * 4), DEPTH)
        fmb = v3(AR.f32(DEPTH * 20), DEPTH)
        convw = v4(AR.f32(DEPTH * 12), DEPTH, 3)
        convb = v3(AR.f32(DEPTH * 4), DEPTH)
        hconvb = v3(AR.f32(DEPTH * 4), DEPTH)
        lamv = AR.f32(DEPTH)
        neglam = AR.f32(DEPTH)
        biasb = AR.f32(NBIAS)
        lng = AR.f32(D)
        lnb = AR.f32(D)
        mlgc = AR.f32(256)
        dagc = AR.f32(128)
        sggt = AR.f32(256)
        sgbt = AR.f32(256)
        gate_l = AR.f32(D)
        gate_c = AR.f32(D)
        modp = v3(AR.f32(32), 4)
        WR_OFF = AR.off
        wring = [v3(AR.bf16(8 * 512), 8) for _ in range(2)]
        wq32 = v3(arena_t[:, WR_OFF:WR_OFF + 4096], 8)
        wg32 = v3(AR.f32(128), 8)
        Hml = v3(AR.f32(NT * 256), NT)
        small = AR.f32(256)
        XOFF = AR.off
        XSZ = 8192
        AR.off += XSZ
        BOFF = AR.off
        BSZ = 12900
        AR.off += BSZ
        AOFF = AR.off
        ASZ = 51200 - AOFF
        assert ASZ >= 9216, ASZ

        def sub(off, size):
            return Arena(arena_t[:, off:off + size])

        for nm in ["X", "B", "A"]:
            pass

        aX = sub(XOFF, XSZ)
        xblk = [v3(aX.f32(4096), 4), v3(aX.f32(4096), 4)]
        aX = sub(XOFF, XSZ)
        pst = aX.f32(T)
        cacc = aX.f32(T)
        aX = sub(XOFF, XSZ)
        m_pt = [v3(aX.f32(512), 4) for _ in range(2)]
        m_vsf = [aX.f32(288) for _ in range(2)]
        m_vs = [v3(m_vsf[i], 4)[:, :, 0:65] for i in range(2)]
        m_kt = [aX.f32(256) for _ in range(2)]
        m_hs = [v3(aX.f32(288), 4)[:, :, 0:65] for _ in range(2)]
        m_cst_flat = aX.f32(576)
        m_cst = v4(m_cst_flat, 2, 4)[:, :, :, 0:65]
        m_tmp = v3(aX.f32(288), 4)[:, :, 0:65]
        m_hn = v3(aX.f32(256), 4)
        aX = sub(XOFF, XSZ)
        ropeC = aX.f32(512)
        ropeS = aX.f32(512)
        rstA = aX.f32(512)
        rstB = aX.f32(512)
        sgp = aX.f32(768)
        sgp2 = v3(aX.f32(2 * 768), 2)
        sgu2 = v3(aX.f32(512), 2)
        sgjunk2 = v3(aX.f32(512), 2)
        sgvn2 = v3(aX.bf16(512), 2)
        sgy2 = v3(aX.f32(512), 2)
        sgzs2 = v3(aX.f32(512), 2)
        aB = sub(BOFF, BSZ)
        qkT = v3(aB.f32(4 * T), 4)
        vaug = v4(aB.bf16(NT * 4 * 72), NT, 4)[:, :, :, 0:65]
        Gt = v3(aB.f32(NT * 16), NT)
        LFp = v3(aB.f32(144), 2)
        Ej = v3(aB.f32(144), 2)
        Ws = v3(aB.f32(144), 2)
        Eend = v3(aB.f32(144), 2)
        gtmp = v3(aB.f32(144), 2)
        aB = sub(BOFF, BSZ)
        dkT = v3(aB.bf16(4 * T), 4)
        dvaug = v4(aB.bf16(NT * 4 * 136), NT, 4)[:, :, :, 0:129]
        ysg = v3(aB.bf16(NT * 256), NT)
        aA = sub(AOFF, ASZ)
        hTall = v3(aA.bf16(8 * T), 8)
        xn_a = aA.f32(0) if False else None
        aA = sub(AOFF, ASZ)
        hTblk = v3(aA.bf16(8 * 512), 8)
        qTblk = v3(aA.bf16(4 * 512), 4)
        gates = v3(aA.bf16(4 * 1024), 4)
        mixT = v3(aA.bf16(8 * 512), 8)
        ptb = [aA.bf16(512) for _ in range(4)]
        oev = v4(aA.f32(4 * 2 * 132), 4, 2)[:, :, :, 0:129]
        xn5 = aA.f32(1024)
        xn5b = aA.f32(1024)
        aX5 = sub(XOFF + 4096, 4096)
        p5ropeC = aX5.f32(512)
        p5ropeS = aX5.f32(512)
        p5stA = aX5.f32(512)
        p5stB = aX5.f32(512)
        p5tmp = aX5.f32(512)
        p5t2 = aX5.f32(512)
        p5ml = aX5.f32(256)
        p5x_ml = arena_t[:, XOFF + 4096:XOFF + 4096 + 1024]
        p5x_sq = arena_t[:, XOFF + 4096 + 2048:XOFF + 4096 + 3072]
        p5sq = aX5.f32(256)

        for nm in ["xblk", "pst", "cacc", "mtmp", "sgt", "rope4", "p5x"]:
            S.region[nm] = "X"
        for nm in ["qkT", "qkpre", "vaug", "ktok", "Gt", "gder", "dkT", "dvaug", "ysg"]:
            S.region[nm] = "B"
        for nm in ["hTall", "p5"]:
            S.region[nm] = "A"

        def fence(region):
            S.fence(region, "pool", lambda h: h.memset(dummy[:, 0:1], 0.0))

        def mm(out, lhsT, rhs, start, stop, reads, writes):
            S.op("pe", lambda h: h.matmul(out, lhsT=lhsT, rhs=rhs, start=start, stop=stop), reads, writes)

        def tr(out, in_, ident, reads, writes):
            S.op("pe", lambda h: h.transpose(out, in_, ident), reads, writes)

        def act(out, in_, func, reads, writes, bias=None, scale=None, accum=None):
            kw = {}
            if accum is not None:
                kw["accum_out"] = accum
            if bias is not None:
                kw["bias"] = bias
            if scale is not None:
                kw["scale"] = scale
            S.op("act", lambda h: h.activation(out=out, in_=in_, func=func, **kw), reads, writes)

        def tt(eng, out, in0, in1, op, reads, writes):
            S.op(eng, lambda h: h.tensor_tensor(out=out, in0=in0, in1=in1, op=op), reads, writes)

        def ts(eng, out, in0, s1, s2, op0, op1, reads, writes):
            if s2 is None:
                S.op(eng, lambda h: h.tensor_scalar(out=out, in0=in0, scalar1=s1, scalar2=None, op0=op0), reads, writes)
            else:
                S.op(eng, lambda h: h.tensor_scalar(out=out, in0=in0, scalar1=s1, scalar2=s2, op0=op0, op1=op1), reads, writes)

        def stt(eng, out, in0, scalar, in1, op0, op1, reads, writes):
            S.op(eng, lambda h: h.scalar_tensor_tensor(out=out, in0=in0, scalar=scalar, in1=in1, op0=op0, op1=op1), reads, writes)

        def cp(eng, out, in_, reads, writes):
            if eng == "act":
                S.op(eng, lambda h: h.activation(out=out, in_=in_, func=AF.Copy), reads, writes)
            else:
                S.op(eng, lambda h: h.tensor_copy(out=out, in_=in_), reads, writes)

        def red(out, in_, reads, writes):
            S.op("dve", lambda h: h.tensor_reduce(out=out, in_=in_, axis=AX.X, op=ALU.add), reads, writes)

        def dma(out, in_, reads, writes, slow=False):
            if slow:
                S.dma("sp", lambda h: h.dma_start(out=out, in_=in_, allow_slow_non_contiguous=True), reads, writes)
            else:
                S.dma("sp", lambda h: h.dma_start(out=out, in_=in_), reads, writes)

        def tap(name, ap, key, dtype=F32):
            if dbg is None or name not in dbg:
                return
            shp = list(ap.shape)
            d = nc.dram_tensor("dbg_" + name, shp, dtype, kind="ExternalOutput").ap()
            dbg_out[name] = shp
            dma(d, ap, list(key) if isinstance(key, list) else [key], ["dbg_" + name])

        def stats(s1, s2, k, n, tag):
            rk1 = tag[0] if isinstance(tag, tuple) else [tag + "s1"]
            rk2 = tag[1] if isinstance(tag, tuple) else [tag + "s2"]
            mean = small[:, 0:k]
            msq = small[:, 16:16 + k]
            var = small[:, 32:32 + k]
            rstd = small[:, 48:48 + k]
            ts("dve", mean, s1, 1.0 / n, None, ALU.mult, None, rk1, ["st_mean"])
            tt("dve", msq, mean, mean, ALU.mult, ["st_mean"], ["st_msq"])
            stt("dve", var, s2, 1.0 / n, msq, ALU.mult, ALU.subtract, rk2 + ["st_msq"], ["st_var"])
            ts("dve", var, var, EPS, None, ALU.add, None, ["st_var"], ["st_var"])
            tt("pool", rstd, var, neghalf[:, 0:k], ALU.pow, ["st_var", "cst"], ["st_rstd"])
            return mean, rstd

        def chk(name):
            if stop == name:
                raise _Stop()

        try:
            dma(cstF, cst_d[:, :], [], ["cst"])
            cp("dve", identB, identF, ["cst"], ["cstb"])
            cp("dve", maskF, triF, ["cst"], ["cstb"])
            cp("dve", maskB, triB, ["cst"], ["cstb"])
            S.op("pool", lambda h: h.memset(neghalf, -0.5), [], ["cst"])

            S.op("pool", lambda h: h.memset(fmb[:, :, :], 0.0), [], ["fmb"])
            for l in range(DEPTH):
                for j in range(4):
                    dma(fmb[:, l, j:j + 1], bin_d[l, j * 128:(j + 1) * 128].rearrange("(p o) -> p o", o=1), [], ["fmb"])
                for h4 in range(4):
                    dma(fmb[:, l, 4 + 2 * h4:5 + 2 * h4], bin_d[l, 1808 + h4 * 128:1808 + (h4 + 1) * 128].rearrange("(p o) -> p o", o=1), [], ["fmb"])
                    dma(fmb[:, l, 12 + 2 * h4:13 + 2 * h4], bin_d[l, 1296 + h4 * 128:1296 + (h4 + 1) * 128].rearrange("(p o) -> p o", o=1), [], ["fmb"])
                for j in range(3):
                    dma(convw[:, l, j, :], convw_d[l, j, :].rearrange("(c p) -> p c", p=128), [], ["convp"], slow=True)
                dma(convb[:, l, :], convb_d[l, :].rearrange("(c p) -> p c", p=128), [], ["convp"], slow=True)
                dma(sgbs[:, l, :], sgbs_d[l, :, :].rearrange("g p -> p g"), [], ["sgbs"], slow=True)
            ts("dve", hconvb[:, :, :], convb[:, :, :], 0.5, None, ALU.mult, None, ["convp"], ["hconvb"])
            for l in range(DEPTH):
                pso = ps_g[0][:, 0:16]
                mm(pso, rperm, fmb[:, l, 4:20], True, True, ["cst", "fmb"], ["g0"])
                src = v3(pso, 8)[:, :, 0:1]
                dst = v3(fmb[:, l, 4:20], 8)[:, :, 1:2]
                cp("dve", dst, src, ["g0"], ["fmb"])
            for l in range(DEPTH):
                for g in range(4):
                    stg = small[:, 64:192]
                    dma(stg, sgw_d[l, g, :, :], [], ["sgstg"])
                    tr(ps_g[1][:, 0:128], stg, identF, ["sgstg", "cst"], ["g1"])
                    cp("dve", sgW[:, l, g, :], ps_g[1][:, 0:128], ["g1"], ["sgW"])
            lt = rstA
            fence("X")
            for i, (a_d, b_d) in enumerate([(lq1_d, lk1_d), (lq2_d, lk2_d)]):
                dma(lt[:, 0:256], a_d.rearrange("l k -> (l k)").partition_broadcast(128), [], [("rope4", "a")])
                dma(lt[:, 256:512], b_d.rearrange("l k -> (l k)").partition_broadcast(128), [], [("rope4", "b")])
                tt("dve", lt[:, 0:256], lt[:, 0:256], lt[:, 256:512], ALU.mult, [("rope4", "a"), ("rope4", "b")], [("rope4", "a")])
                red(small[:, 200 + 4 * i:204 + 4 * i], v3(lt[:, 0:256], 4), [("rope4", "a")], ["lam%d" % i])
                act(small[:, 200 + 4 * i:204 + 4 * i], small[:, 200 + 4 * i:204 + 4 * i], AF.Exp, ["lam%d" % i], ["lam%d" % i])
            tt("dve", lamv, small[:, 200:204], small[:, 204:208], ALU.subtract, ["lam0", "lam1"], ["lamv"])
            for l in range(DEPTH):
                lam_init = 0.8 - 0.6 * math.exp(-0.3 * l)
                ts("dve", lamv[:, l:l + 1], lamv[:, l:l + 1], lam_init, None, ALU.add, None, ["lamv"], ["lamv"])
            ts("dve", neglam, lamv, -1.0, None, ALU.mult, None, ["lamv"], ["neglam"])

            chk('consts')
            groups = [
                (C_MLQK, [(0, 0, 512, False)]),
                (C_MLV, [(0, 512, 256, False), (256, 1280, 16, False)]),
                (C_DAK, [(0, 1808, 128, False), (128, 1808, 128, True), (256, 1936, 128, False), (384, 1936, 128, True)]),
                (C_DAK + 512, [(0, 2064, 128, False), (128, 2064, 128, True), (256, 2192, 128, False), (384, 2192, 128, True)]),
                (C_DAV, [(0, 2320, 512, False)]),
                (C_SG, [(0, 3344, 512, False)]),
                (C_SG + 512, [(0, 3856, 256, False)]),
                (C_DAQ, [(0, 1296, 128, False), (128, 1296, 128, True), (256, 1424, 128, False), (384, 1424, 128, True)]),
                (C_DAQ + 512, [(0, 1552, 128, False), (128, 1552, 128, True), (256, 1680, 128, False), (384, 1680, 128, True)]),
                (C_MLOZ, [(0, 768, 512, False)]),
                (C_DAZ, [(0, 2832, 512, False)]),
            ]
            fence("X")
            fence("A")
            stg32 = [v3(sub(XOFF, XSZ).f32(4096), 8), v3(sub(XOFF + 4096, 4096).f32(4096), 8)]
            aA = sub(AOFF, ASZ)
            stg16 = [v3(aA.bf16(4096), 8), v3(aA.bf16(4096), 8)]
            S.region["stg32"] = "X"
            S.region["stg16"] = "A"
            gi = 0
            prev_store = [[], []]
            ceng = ["dve", "pool", "act"]

            def castcp(i, out, in_, reads, writes):
                e = ceng[i % 3]
                if e == "act":
                    act(out, in_, AF.Copy, reads, writes)
                else:
                    cp(e, out, in_, reads, writes)

            for l in range(DEPTH):
                for (dst0, pieces) in groups:
                    sl = gi % 2
                    wtot = max(p[0] + p[2] for p in pieces)
                    for (doff, src0, w, sw) in pieces:
                        dma(stg32[sl][:, :, doff:doff + w], win_d[l, :, src0:src0 + w].rearrange("(kc p) w -> p kc w", p=128), prev_store[sl], [("stg32", sl, doff)])
                    for pi, (doff, src0, w, sw) in enumerate(pieces):
                        if not sw:
                            castcp(gi + pi, stg16[sl][:, :, doff:doff + w], stg32[sl][:, :, doff:doff + w], [("stg32", sl, doff)], [("stg16", sl, doff)])
                        else:
                            i5 = stg32[sl][:, :, doff:doff + w].rearrange("p k (b t s) -> p k b t s", b=4, t=2)
                            o5 = stg16[sl][:, :, doff:doff + w].rearrange("p k (b t s) -> p k b t s", b=4, t=2)
                            for kc in range(8):
                                cp("pool" if kc % 2 else "dve", o5[:, kc, :, 0, :], i5[:, kc, :, 1, :], [("stg32", sl, doff)], [("stg16", sl, doff, kc, 0)])
                                cp("dve" if kc % 2 else "pool", o5[:, kc, :, 1, :], i5[:, kc, :, 0, :], [("stg32", sl, doff)], [("stg16", sl, doff, kc, 1)])
                    rk = []
                    for (doff, src0, w, sw) in pieces:
                        if sw:
                            rk += [("stg16", sl, doff, kc, t) for kc in range(8) for t in range(2)]
                        else:
                            rk.append(("stg16", sl, doff))
                    dma(wbf_d[l, :, dst0:dst0 + wtot].rearrange("(kc p) w -> p kc w", p=128), stg16[sl][:, :, 0:wtot], rk, [("wbf", l, dst0)])
                    prev_store[sl] = [("wbf", l, dst0)]
                    gi += 1
                for hf in range(2):
                    sl = gi % 2
                    dma(stg32[sl][:, :, :], wout_d[l, :, hf * 512:(hf + 1) * 512].rearrange("(kc p) w -> p kc w", p=128), prev_store[sl], [("stg32", sl, 0)])
                    castcp(gi, stg16[sl][:, :, :], stg32[sl][:, :, :], [("stg32", sl, 0)], [("stg16", sl, 0)])
                    dma(wobf_d[l, :, hf * 512:(hf + 1) * 512].rearrange("(kc p) w -> p kc w", p=128), stg16[sl][:, :, :], [("stg16", sl, 0)], [("wobf", l, hf)])
                    prev_store[sl] = [("wobf", l, hf)]
                    gi += 1

            chk('conv')
            csT = v3(small[:, 64:64 + 64], 8)
            for r in range(NB + 1):
                src = c_d[r, :] if r < NB else cctx_d[0, :]
                dma(csT[:, :, r:r + 1], src.rearrange("(kc p o) -> p kc o", p=128, o=1), ["sgW"], [("csT", r)], slow=True)
            NR = NB + 1
            cs_r = [("csT", r) for r in range(NR)]
            tnh = v3(small[:, 128:192], 8)
            act(tnh[:, :, 0:NR], csT[:, :, 0:NR], AF.Tanh, cs_r, ["cs_t"], scale=0.5)
            ts("dve", tnh[:, :, 0:NR], tnh[:, :, 0:NR], 0.5, 0.5, ALU.mult, ALU.add, ["cs_t"], ["cs_t"])
            tt("dve", csT[:, :, 0:NR], csT[:, :, 0:NR], tnh[:, :, 0:NR], ALU.mult, cs_r + ["cs_t"], ["csS"])
            mrow = sub(AOFF, ASZ).f32(1024)
            S.region["mrow"] = "A"
            wi = 0
            for l in range(NL):
                for cb in range(6):
                    sl = wi % 2
                    dma(stg32[sl][:, :, :], wmod_d[l, :, cb * 512:(cb + 1) * 512].rearrange("(kc p) w -> p kc w", p=128), prev_store[sl], [("stg32", sl, 0)])
                    pg = ps_g[2 + (wi % 2)]
                    for kc in range(8):
                        mm(pg[0:NR, :], csT[:, kc, 0:NR], stg32[sl][:, kc, :], kc == 0, kc == 7, ["csS", ("stg32", sl, 0)], ["g%d" % (2 + wi % 2)])
                    bm = mrow[0:NR, 512:1024]
                    dma(bm, bmod_d[l, cb * 512:(cb + 1) * 512].partition_broadcast(NR), [], [("mrow", "b")])
                    tt("dve", mrow[0:NR, 0:512], pg[0:NR, :], bm, ALU.add, ["g%d" % (2 + wi % 2), ("mrow", "b")], [("mrow", "o")])
                    dma(mod_d[l, 0:NR, cb * 512:(cb + 1) * 512], mrow[0:NR, 0:512], [("mrow", "o")], [("mod", l)])
                    wi += 1

            chk('mod')
            def wload(l, col0, ncols, slot, src=None):
                srcd = wbf_d if src is None else src
                key = ("wbf", l, col0) if src is None else ("wobf", l, col0 // 512)
                rk = [("wbf", l, g[0]) for g in groups] if src is None else [key]
                dma(wring[slot][:, :, 0:ncols], srcd[l, :, col0:col0 + ncols].rearrange("(kc p) w -> p kc w", p=128), rk, [("wring", slot)])

            ring_ctr = [0]

            def next_slot():
                s_ = ring_ctr[0] % 2
                ring_ctr[0] += 1
                return s_

            g_ctr = [0]

            def next_g(lo=0, hi=5):
                i = lo + g_ctr[0] % (hi - lo)
                g_ctr[0] += 1
                return i

            def ln_block(xb, ntile, xkey, xnbuf, xnkey, hT_of, hkey_of, s1p, shp, modkey, hTf=None, hfkey=None, post=None):
                s1 = small[:, 224:224 + ntile]
                s2 = small[:, 232:232 + ntile]
                k1 = [("lnst", 1, ti) for ti in range(ntile)]
                k2 = [("lnst", 2, ti) for ti in range(ntile)]
                xk = xkey if callable(xkey) else (lambda ti: xkey)
                xnbufs = xnbuf if isinstance(xnbuf, list) else [xnbuf]
                xnkeys = xnkey if isinstance(xnkey, list) else [xnkey]
                for ti in range(ntile):
                    xnbuf, xnkey = xnbufs[ti % len(xnbufs)], xnkeys[ti % len(xnbufs)]
                    act(xnbuf, xb[:, ti, :], AF.Square, [xk(ti)], [xnkey, k2[ti]], accum=s2[:, ti:ti + 1])
                    act(xnbuf, xb[:, ti, :], AF.Identity, [xk(ti)], [xnkey, k1[ti]], accum=s1[:, ti:ti + 1])
                mean, rstd = stats(s1, s2, ntile, 1024.0, (k1, k2))
                for ti in range(ntile):
                    xnbuf, xnkey = xnbufs[ti % len(xnbufs)], xnkeys[ti % len(xnbufs)]
                    ts("dve", xnbuf, xb[:, ti, :], mean[:, ti:ti + 1], rstd[:, ti:ti + 1], ALU.subtract, ALU.mult, [xk(ti), "st_mean", "st_rstd"], [xnkey])
                    for kc in range(8):
                        tr(ps_tp[:, kc * 128:(kc + 1) * 128], xnbuf[:, kc * 128:(kc + 1) * 128], identF, [xnkey, "cst"], ["tpa", "tpb"])
                    tmpm = v3(xnbuf, 8)
                    tt("dve", tmpm, v3(ps_tp[:, :], 8), bc(s1p.unsqueeze(2), [128, 8, 128]), ALU.mult, ["tpa", "tpb"] + modkey, [xnkey])
                    if hTf is None:
                        tt("pool", hT_of(ti), tmpm, bc(shp.unsqueeze(2), [128, 8, 128]), ALU.add, [xnkey] + modkey, [hkey_of(ti)])
                    else:
                        tt("pool", hTf(ti), tmpm, bc(shp.unsqueeze(2), [128, 8, 128]), ALU.add, [xnkey] + modkey, [hfkey(ti)])
                        cp("act", hT_of(ti), hTf(ti), [hfkey(ti)], [hkey_of(ti)])
                    if post is not None:
                        post(ti)

            for b in range(NB):
                for l in range(NL):
                    lam_init = 0.8 - 0.6 * math.exp(-0.3 * l)
                    pk = ("par", b, l)
                    for (off, s0, w) in [(B_MLV, 512, 256), (B_MLV + 256, 1280, 16), (B_DAV, 2320, 512), (B_SG, 3344, 768), (B_MLOZ, 768, 512), (B_DAZ, 2832, 512)]:
                        dma(biasb[:, off:off + w], bin_d[l, s0:s0 + w].partition_broadcast(128), [], [("biasb", off)])
                    dma(lng, lng_d[l, :].partition_broadcast(128), [], ["lng"])
                    dma(lnb, lnb_d[l, :].partition_broadcast(128), [], ["lnb"])
                    dma(mlgc, mlg_d[l, :].partition_broadcast(128), [], ["mlgc"])
                    dma(dagc, dag_d[l, :].partition_broadcast(128), [], ["dagc"])
                    dma(sggt, sgg_d[l, :].partition_broadcast(128), [], ["sggt"])
                    dma(sgbt, sgb_d[l, :].partition_broadcast(128), [], ["sgbt"])
                    ts("dve", mlgc, mlgc, 0.5, None, ALU.mult, None, ["mlgc"], ["mlgc"])
                    ts("dve", dagc, dagc, 0.5 * (1.0 - lam_init), None, ALU.mult, None, ["dagc"], ["dagc"])
                    dma(gate_l, mod_d[l, b, 2048:3072].partition_broadcast(128), [("mod", l)], ["gate_l"])
                    dma(gate_c, mod_d[l, NB, 2048:3072].partition_broadcast(128), [("mod", l)], ["gate_c"])
                    for i, (row, c0) in enumerate([(b, 0), (b, 1024), (NB, 0), (NB, 1024)]):
                        dma(modp[:, i, :], mod_d[l, row, c0:c0 + 1024].rearrange("(kc p) -> p kc", p=128), [("mod", l)], [("modp", i)], slow=True)
                    for i in (1, 3):
                        ts("dve", modp[:, i, :], modp[:, i, :], 1.0, None, ALU.add, None, [("modp", i)], [("modp", i)])
                    modk = [("modp", i) for i in range(4)]

                    fence("X")
                    fence("A")
                    fence("B")
                    dma(wq32, win_d[l, :, 0:512].rearrange("(kc p) w -> p kc w", p=128), [], [("wring", 0), ("wring", 1)])
                    dma(wg32, win_d[l, :, 1280:1296].rearrange("(kc p) w -> p kc w", p=128), [], ["wg32"])
                    hTf2 = xblk[1][:, 1:3, :].rearrange("p a (k t) -> p k (a t)", k=8) if False else None
                    hTfbuf = v3(arena_t[:, XOFF + 4096 + 1024:XOFF + 4096 + 3072], 8)
                    for bi, (t0, t1) in enumerate(BLKS):
                        ntile = (t1 - t0) // 128
                        xb = xblk[0]
                        if l == 0:
                            srcx = ctx_d[b, :, :] if bi == 0 else x_d[b, t0 - 256:t1 - 256, :]
                            rk = []
                        else:
                            srcx = xs_d[b, t0:t1, :]
                            rk = [("xs", b, bi, ti_) for ti_ in range((t1 - t0) // 128)]
                        srcx3 = srcx.rearrange("(n p) d -> p n d", p=128)
                        for ti_ in range(ntile):
                            dma(xb[:, ti_, :], srcx3[:, ti_, :], rk, [("xblk", 0, ti_)])
                        s1p, shp = (modp[:, 3, :], modp[:, 2, :]) if bi == 0 else (modp[:, 1, :], modp[:, 0, :])

                        def post1(ti, t0=t0):
                            tk = t0 + ti * 128
                            tg = tk // 128
                            hTf_t = hTfbuf[:, :, (ti % 2) * 128:(ti % 2) * 128 + 128]
                            gg = 2 + tg % 2
                            for kc in range(8):
                                mm(ps_g[gg][:, 0:16], hTf_t[:, kc, :], wg32[:, kc, :], kc == 0, kc == 7, ["wg32", ("xblk", 2, ti % 2)], ["g%d" % gg])
                            tt("dve", Gt[:, tg, :], ps_g[gg][:, 0:16], biasb[:, B_MLV + 256:B_MLV + 272], ALU.add,
                               ["g%d" % gg, ("biasb", B_MLV + 256)], [("Gt", tg)])
                            if ti % 2 == 0:
                                return
                            tk0 = tk - 128
                            for c in range(4):
                                pq = ps_g[c // 2][:, (c % 2) * 256:(c % 2) * 256 + 256]
                                for kc in range(8):
                                    mm(pq, wq32[:, kc, c * 128:(c + 1) * 128], hTfbuf[:, kc, :], kc == 0, kc == 7,
                                       [("wring", 0), ("wring", 1), ("xblk", 2, 0), ("xblk", 2, 1)], ["g%d" % (c // 2)])
                            for hb in range(2):
                                tt("dve", qkT[:, 2 * hb:2 * hb + 2, tk0:tk0 + 256], v3(ps_g[hb][:, :], 2), bc(fmb[:, l, 2 * hb:2 * hb + 2].unsqueeze(2), [128, 2, 256]), ALU.add,
                                   ["g%d" % hb, "fmb"], [("qkpre", tg - 1, hb), ("qkpre", tg, hb)])

                        ln_block(xb, ntile, lambda ti: ("xblk", 0, ti), [xblk[1][:, 0, :], xblk[1][:, 3, :]], [("xblk", 1), ("xblk", 3)],
                                 lambda ti, t0=t0: hTall[:, :, t0 + ti * 128:t0 + (ti + 1) * 128], lambda ti, t0=t0: ("hTall", t0 // 128 + ti),
                                 s1p, shp, modk, hTf=lambda ti: hTfbuf[:, :, (ti % 2) * 128:(ti % 2) * 128 + 128], hfkey=lambda ti: ("xblk", 2, ti % 2), post=post1)
                    if dbg:
                        tap("hT", hTall[:, :, :], [("hTall", i) for i in range(NT)], BF16)

                    chk('ph1')
                    hT_keys = [("hTall", i) for i in range(NT)]
                    slot = next_slot()
                    wload(l, C_MLV, 272, slot)
                    S.op("pool", lambda h: h.memset(vaug[:, :, :, 64:65], 1.0), [], [("vaug", "ones")])
                    for ti in range(NT):
                        gix = next_g()
                        pg = ps_g[gix]
                        for kc in range(8):
                            mm(pg[:, 0:272], hTall[:, kc, ti * 128:(ti + 1) * 128], wring[slot][:, kc, 0:272], kc == 0, kc == 7,
                               [("wring", slot), hT_keys[ti]], ["g%d" % gix])
                        tt("dve", vaug[:, ti, :, 0:64], v3(pg[:, 0:256], 4), v3(biasb[:, B_MLV:B_MLV + 256], 4), ALU.add,
                           ["g%d" % gix, ("biasb", B_MLV)], [("vaug", ti)])
                    Gk = [("Gt", ti) for ti in range(NT)]
                    for d_ in range(2):
                        fsl = Gt[:, :, 4 + 8 * d_:8 + 8 * d_]
                        act(v3(gtmp[:, d_, :], NT), fsl, AF.Exp, Gk, [("gder", "t", d_)], scale=-1.0)
                        act(LFp[:, d_, :], gtmp[:, d_, :], AF.Ln, [("gder", "t", d_)], [("gder", "L", d_)], bias=1.0)
                    chk('g1')
                    pc = ps_g[0]
                    mm(pc[:, 0:72], triF, LFp[:, 0, :], True, True, ["cst", ("gder", "L", 0)], ["g0"])
                    mm(pc[:, 72:144], triB, LFp[:, 1, :], True, True, ["cst", ("gder", "L", 1)], ["g0"])
                    mm(pc[:, 144:288], onesF, LFp[:, :, :].rearrange("p a b -> p (a b)"), True, True, ["cst", ("gder", "L", 0), ("gder", "L", 1)], ["g0"])
                    chk('g2')
                    act(Ej[:, :, :].rearrange("p a b -> p (a b)"), pc[:, 0:144], AF.Exp, ["g0"], [("gder", "Ej")], scale=-1.0)
                    act(Eend[:, :, :].rearrange("p a b -> p (a b)"), pc[:, 144:288], AF.Exp, ["g0"], [("gder", "Eend")], scale=-1.0)
                    if dbg and 'cum' in dbg:
                        cp('dve', sgp[:, 0:288], pc[:, 0:288], ['g0'], [('mtmp', 'dbgc')])
                        tap('cum', sgp[:, 0:288], [('mtmp', 'dbgc')])
                        tap('LFp', LFp[:, :, :], [('gder', 'L', 0), ('gder', 'L', 1)])
                    chk('g3')
                    for d_ in range(2):
                        tt("dve", v3(gtmp[:, d_, :], NT), v3(pc[:, 72 * d_:72 * d_ + 72], NT), Gt[:, :, 8 * d_:8 * d_ + 4], ALU.add,
                           ["g0"] + Gk, [("gder", "t2", d_)])
                        act(Ws[:, d_, :], gtmp[:, d_, :], AF.Exp, [("gder", "t2", d_)], [("gder", "Ws", d_)])
                    pre_all = [("qkpre", ti, hb) for ti in range(NT) for hb in range(2)]
                    for c in range(4):
                        pst = qkT[:, c, :]
                        w0, w1, w2 = (convw[:, l, j, c:c + 1] for j in range(3))
                        ts("dve", cacc, pst, w1, convb[:, l, c:c + 1], ALU.mult, ALU.add, pre_all + ["convp"], ["cacc"])
                        for (a, e) in [(0, CTXL), (CTXL, T)]:
                            stt("dve", cacc[:, a + 1:e], pst[:, a:e - 1], w0, cacc[:, a + 1:e], ALU.mult, ALU.add, pre_all + ["convp", "cacc"], ["cacc"])
                            stt("dve", cacc[:, a:e - 1], pst[:, a + 1:e], w2, cacc[:, a:e - 1], ALU.mult, ALU.add, pre_all + ["convp", "cacc"], ["cacc"])
                        act(pst, cacc, AF.Tanh, ["cacc"] + pre_all, [("qkT", c)], scale=0.5)
                        sc_ = 0.5 if c < 2 else 0.0625
                        ts("pool", pst, pst, sc_, sc_, ALU.mult, ALU.add, [("qkT", c)], [("qkT", c)])
                        tt("dve", pst, pst, cacc, ALU.mult, [("qkT", c), "cacc"], [("qkT", c)])
                    if dbg:
                        tap("qkT", qkT[:, :, :], [("qkT", i) for i in range(4)], F32)
                        tap("vaug", vaug[:, :, :, :], [("vaug", i) for i in range(NT)] + [("vaug", "ones")], BF16)
                        tap("Gt", Gt[:, :, :], [("Gt", i) for i in range(NT)])

                    chk('ph2')
                    fence("X")
                    chk('ph3a')
                    orders = [list(range(NT)), [1, 0] + list(range(NT - 1, 1, -1))]
                    written = set()
                    S.op("pool", lambda h: h.memset(m_cst_flat, 0.0), [], [("mtmp", "cst", 0), ("mtmp", "cst", 1)])
                    for d_ in range(2):
                        S.op("pool", lambda h, d_=d_: h.memset(m_vsf[d_], 0.0), [], [("mtmp", "vs", d_)])
                    for step in range(NT):
                        for d_ in range(2):
                            ti = orders[d_][step]
                            first = step == 0
                            tks = slice(ti * 128, (ti + 1) * 128)
                            for h4 in range(4):
                                pr = slice((h4 % 2) * 64, (h4 % 2) * 64 + 64)
                                gS = 1 + (h4 % 2)
                                mm(ps_g[gS][:, (h4 // 2) * 128:(h4 // 2 + 1) * 128], qkT[pr, 2 + h4 // 2, tks], qkT[pr, h4 // 2, tks], True, True,
                                   [("qkT", 2 + h4 // 2), ("qkT", h4 // 2)], ["g%d" % gS])
                            if step == 0 and d_ == 0: chk('sa')
                            mk = triF if d_ == 0 else triB
                            for j in range(2):
                                tr(ps_tp[:, j * 128:(j + 1) * 128], qkT[:, 2 + j, tks], identF, [("qkT", 2 + j), "cst"], ["tpa"])
                            cp("act", m_kt[d_], ps_tp[:, 0:256], ["tpa"], [("mtmp", "kt", d_)])
                            for par in range(2):
                                tt("dve", m_pt[d_][:, par * 2:par * 2 + 2, :], v3(ps_g[1 + par][:, 0:256], 2), bc(mk.unsqueeze(1), [128, 2, 128]), ALU.mult,
                                   ["g%d" % (1 + par), "cst"], [("mtmp", "pt", d_, par)])
                            wsl = v3(Ws[:, d_, :], NT)[:, ti, :]
                            tt("pool", m_vs[d_], vaug[:, ti, :, :], bc(wsl.unsqueeze(2), [128, 4, 65]), ALU.mult,
                               [("vaug", ti), ("vaug", "ones"), ("gder", "Ws", d_)], [("mtmp", "vs", d_)])
                            if step == 0 and d_ == 0: chk('sb')
                            gH = 3
                            pH = ps_g[gH]
                            pD = ps_g[4]
                            for h4 in range(4):
                                pr = slice((h4 % 2) * 64, (h4 % 2) * 64 + 64)
                                mm(pH[:, h4 * 72:h4 * 72 + 65], m_pt[d_][:, (h4 % 2) * 2 + h4 // 2, :], m_vs[d_][:, h4, :], True, first,
                                   [("mtmp", "pt", d_, h4 % 2), ("mtmp", "vs", d_)], ["g3"])
                                if not first:
                                    mm(pH[:, h4 * 72:h4 * 72 + 65], qkT[pr, h4 // 2, tks], m_cst[pr, d_, h4, :], False, True,
                                       [("qkT", h4 // 2), ("mtmp", "cst", d_)], ["g3"])
                            for pp in range(2):
                                mm(pD[:, pp * 144:pp * 144 + 144], m_kt[d_][:, pp * 128:pp * 128 + 128], m_vsf[d_][:, pp * 144:pp * 144 + 144], True, True,
                                   [("mtmp", "kt", d_), ("mtmp", "vs", d_)], ["g4"])
                            if step == 0 and d_ == 0: chk('sc')
                            ejs = v3(Ej[:, d_, :], NT)[:, ti, :]
                            tt("dve", m_hs[d_], v3(pH[:, 0:288], 4)[:, :, 0:65], bc(ejs.unsqueeze(2), [128, 4, 65]), ALU.mult, ["g3", ("gder", "Ej")], [("mtmp", "hs", d_)])
                            if step == 0 and d_ == 0: chk('sd')
                            den = m_hs[d_][:, :, 64]
                            d2 = small[:, 208:212]
                            tt("dve", d2, den, den, ALU.mult, [("mtmp", "hs", d_)], ["md2"])
                            ts("dve", d2, d2, 1.0, None, ALU.max, None, ["md2"], ["md2"])
                            tt("pool", d2, d2, neghalf[:, 0:4], ALU.pow, ["md2", "cst"], ["md2"])
                            if ti not in written:
                                tt("dve", v3(Hml[:, ti, :], 4), m_hs[d_][:, :, 0:64], bc(d2.unsqueeze(2), [128, 4, 64]), ALU.mult,
                                   [("mtmp", "hs", d_), "md2"], [("Hml", ti)])
                                written.add(ti)
                            else:
                                tt("dve", m_hn, m_hs[d_][:, :, 0:64], bc(d2.unsqueeze(2), [128, 4, 64]), ALU.mult,
                                   [("mtmp", "hs", d_), "md2"], [("mtmp", "hn")])
                                tt("pool", v3(Hml[:, ti, :], 4), v3(Hml[:, ti, :], 4), m_hn, ALU.add, [("mtmp", "hn"), ("Hml", ti)], [("Hml", ti)])
                            if step == 0 and d_ == 0: chk('se')
                            tt("dve", m_tmp, v3(pD[:, 0:288], 4)[:, :, 0:65], m_cst[:, d_, :, :], ALU.add, ["g4", ("mtmp", "cst", d_)], [("mtmp", "tmp")])
                            ees = v3(Eend[:, d_, :], NT)[:, ti, :]
                            tt("dve", m_cst[:, d_, :, :], m_tmp, bc(ees.unsqueeze(2), [128, 4, 65]), ALU.mult, [("mtmp", "tmp"), ("gder", "Eend")], [("mtmp", "cst", d_)])
                    chk('sf') if False else None
                    if dbg:
                        tap("Hml", Hml[:, :, :], [("Hml", i) for i in range(NT)])

                    chk('ph3')
                    fence("X")
                    fence("B")
                    S.op("pool", lambda h: h.memset(dvaug[:, :, :, 128:129], 1.0), [], [("dvaug", "ones")])
                    for hh in range(2):
                        slot = next_slot()
                        wload(l, C_DAK + hh * 512, 512, slot)
                        for h2 in range(2):
                            h4 = hh * 2 + h2
                            for bi, (t0, t1) in enumerate(BLKS):
                                n = t1 - t0
                                ga, gb = [(1, 2), (3, 4)][bi % 2]
                                for kc in range(8):
                                    mm(ps_g[ga][:, 0:n], wring[slot][:, kc, h2 * 256:h2 * 256 + 128], hTall[:, kc, t0:t1], kc == 0, kc == 7,
                                       [("wring", slot)] + hT_keys[t0 // 128:t1 // 128], ["g%d" % ga])
                                if bi == 0:
                                    act(dkT[:, h4, t0:t1], ps_g[ga][:, 0:n], AF.Identity, ["g%d" % ga, "fmb"], [("dkT", h4, bi)], bias=fmb[:, l, 4 + 2 * h4:5 + 2 * h4])
                                    continue
                                for kc in range(8):
                                    mm(ps_g[gb][:, 0:n], wring[slot][:, kc, h2 * 256 + 128:h2 * 256 + 256], hTall[:, kc, t0:t1], kc == 0, kc == 7,
                                       [("wring", slot)] + hT_keys[t0 // 128:t1 // 128], ["g%d" % gb])
                                p0 = t0 - 256
                                dma(ropeC[:, 0:n], rope_d[0, :, p0:p0 + n], [], [("rope4", "c")])
                                dma(ropeS[:, 0:n], rope_d[1, :, p0:p0 + n], [], [("rope4", "s")])
                                stt("dve", rstA[:, 0:n], ps_g[ga][:, 0:n], fmb[:, l, 4 + 2 * h4:5 + 2 * h4], ropeC[:, 0:n], ALU.add, ALU.mult,
                                    ["g%d" % ga, "fmb", ("rope4", "c")], [("rope4", "a")])
                                stt("dve", rstB[:, 0:n], ps_g[gb][:, 0:n], fmb[:, l, 5 + 2 * h4:6 + 2 * h4], ropeS[:, 0:n], ALU.add, ALU.mult,
                                    ["g%d" % gb, "fmb", ("rope4", "s")], [("rope4", "b")])
                                tt("pool", dkT[:, h4, t0:t1], rstA[:, 0:n], rstB[:, 0:n], ALU.add, [("rope4", "a"), ("rope4", "b")], [("dkT", h4, bi)])
                    slot = next_slot()
                    wload(l, C_DAV, 512, slot)
                    for ti in range(NT):
                        gix = next_g(3, 5)
                        pg = ps_g[gix]
                        for kc in range(8):
                            mm(pg[:, :], hTall[:, kc, ti * 128:(ti + 1) * 128], wring[slot][:, kc, :], kc == 0, kc == 7,
                               [("wring", slot), hT_keys[ti]], ["g%d" % gix])
                        tt("dve", dvaug[:, ti, :, 0:128], v3(pg[:, :], 4), v3(biasb[:, B_DAV:B_DAV + 512], 4), ALU.add,
                           ["g%d" % gix, ("biasb", B_DAV)], [("dvaug", ti)])
                    s_a = next_slot()
                    wload(l, C_SG, 512, s_a)
                    s_b = next_slot()
                    wload(l, C_SG + 512, 256, s_b)
                    for tp_ in range(NT // 2):
                        for j in range(2):
                            ti = 2 * tp_ + j
                            ga = 1 if j == 0 else 3
                            for kc in range(8):
                                mm(ps_g[ga][:, :], hTall[:, kc, ti * 128:(ti + 1) * 128], wring[s_a][:, kc, :], kc == 0, kc == 7,
                                   [("wring", s_a), hT_keys[ti]], ["g%d" % ga])
                            for kc in range(8):
                                mm(ps_g[2][:, j * 256:(j + 1) * 256], hTall[:, kc, ti * 128:(ti + 1) * 128], wring[s_b][:, kc, 0:256], kc == 0, kc == 7,
                                   [("wring", s_b), hT_keys[ti]], ["g2"])
                            tt("dve", sgp2[:, j, 0:512], ps_g[ga][:, :], biasb[:, B_SG:B_SG + 512], ALU.add, ["g%d" % ga, ("biasb", B_SG)], [("sgt", "p", j)])
                        tt("dve", sgp2[:, :, 512:768], v3(ps_g[2][:, :], 2), bc(biasb[:, B_SG + 512:B_SG + 768].unsqueeze(1), [128, 2, 256]), ALU.add,
                           ["g2", ("biasb", B_SG)], [("sgt", "pz")])
                        kp = [("sgt", "p", 0), ("sgt", "p", 1)]
                        act(sgu2, sgp2[:, :, 0:256], AF.Gelu, kp, [("sgt", "u")])
                        for j in range(2):
                            act(sgp2[:, j, 256:512], sgp2[:, j, 256:512], AF.Gelu, [("sgt", "p", j)], [("sgt", "gv", j)], accum=small[:, 240 + j:241 + j])
                            act(sgjunk2[:, j, :], sgp2[:, j, 256:512], AF.Square, [("sgt", "gv", j)], [("sgt", "junk", j)], accum=small[:, 244 + j:245 + j])
                        act(sgzs2, sgp2[:, :, 512:768], AF.Tanh, [("sgt", "pz")], [("sgt", "zs")], scale=0.5)
                        kgv = [("sgt", "gv", 0), ("sgt", "gv", 1)]
                        kjk = [("sgt", "junk", 0), ("sgt", "junk", 1)]
                        mean, rstd = stats(small[:, 240:242], small[:, 244:246], 2, 256.0, (kgv, kjk))
                        for j in range(2):
                            ts("dve", sgjunk2[:, j, :], sgp2[:, j, 256:512], mean[:, j:j + 1], rstd[:, j:j + 1], ALU.subtract, ALU.mult,
                               [("sgt", "gv", j), "st_mean", "st_rstd"], [("sgt", "junk", j)])
                        tt("dve", sgjunk2, sgjunk2, bc(sggt.unsqueeze(1), [128, 2, 256]), ALU.mult, kjk + ["sggt"], kjk)
                        tt("dve", sgvn2, sgjunk2, bc(sgbt.unsqueeze(1), [128, 2, 256]), ALU.add, kjk + ["sgbt"], [("sgt", "vn")])
                        for j in range(2):
                            for g in range(4):
                                mm(ps_g[4][:, j * 256 + g * 64:j * 256 + (g + 1) * 64], sgW[:, l, g, :], sgvn2[:, j, g * 64:(g + 1) * 64], True, True,
                                   ["sgW", ("sgt", "vn")], ["g4"])
                        for j in range(2):
                            tt("dve", v3(sgy2[:, j, :], 4), v3(ps_g[4][:, j * 256:(j + 1) * 256], 4), bc(sgbs[:, l, :].unsqueeze(2), [128, 4, 64]), ALU.add,
                               ["g4", "sgbs"], [("sgt", "y", j)])
                        ky = [("sgt", "y", 0), ("sgt", "y", 1)]
                        tt("dve", sgy2, sgy2, sgu2, ALU.mult, ky + [("sgt", "u")], ky)
                        stt("dve", sgzs2, sgzs2, 1.0, sgp2[:, :, 512:768], ALU.add, ALU.mult, [("sgt", "zs"), ("sgt", "pz")], [("sgt", "zs")])
                        stt("dve", ysg[:, 2 * tp_:2 * tp_ + 2, :], sgy2, 0.5, sgzs2, ALU.mult, ALU.mult, ky + [("sgt", "zs")], [("ysg", 2 * tp_), ("ysg", 2 * tp_ + 1)])
                    if dbg:
                        tap("dkT", dkT[:, :, :], [("dkT", h_, b_) for h_ in range(4) for b_ in range(5)], BF16)
                        tap("ysg", ysg[:, :, :], [("ysg", i) for i in range(NT)], BF16)

                    chk('ph4')
                    fence("X")
                    fence("A")
                    last = (l == NL - 1)
                    for bi, (t0, t1) in enumerate(BLKS):
                        n = t1 - t0
                        ntile = n // 128
                        isctx = bi == 0
                        if last and isctx:
                            continue
                        xb = xblk[0]
                        if l == 0:
                            srcx = ctx_d[b, :, :] if isctx else x_d[b, t0 - 256:t1 - 256, :]
                            rk = []
                        else:
                            srcx = xs_d[b, t0:t1, :]
                            rk = [("xs", b, bi, ti_) for ti_ in range((t1 - t0) // 128)]
                        srcx3 = srcx.rearrange("(n p) d -> p n d", p=128)
                        for ti in range(ntile):
                            dma(xb[:, ti, :], srcx3[:, ti, :], rk, [("xblk", 0, ti)])
                        s1p, shp = (modp[:, 3, :], modp[:, 2, :]) if isctx else (modp[:, 1, :], modp[:, 0, :])
                        ln_block(xb, ntile, lambda ti: ("xblk", 0, ti), [xn5, xn5b], [("p5", "xn"), ("p5", "xnb")], lambda ti: hTblk[:, :, ti * 128:(ti + 1) * 128], lambda ti: ("p5", "hT", ti), s1p, shp, modk)
                        hk = [("p5", "hT", ti) for ti in range(ntile)]
                        if not isctx:
                            p0 = t0 - 256
                            dma(p5ropeC[:, 0:n], rope_d[0, :, p0:p0 + n], [], [("p5x", "c")])
                            dma(p5ropeS[:, 0:n], rope_d[1, :, p0:p0 + n], [], [("p5x", "s")])
                        for hh in range(2):
                            slot = next_slot()
                            wload(l, C_DAQ + hh * 512, 512, slot)
                            for h2 in range(2):
                                h4 = hh * 2 + h2
                                ga, gb = 0, 1
                                for kc in range(8):
                                    mm(ps_g[ga][:, 0:n], wring[slot][:, kc, h2 * 256:h2 * 256 + 128], hTblk[:, kc, 0:n], kc == 0, kc == 7, [("wring", slot)] + hk, ["g%d" % ga])
                                if isctx:
                                    act(qTblk[:, h4, 0:n], ps_g[ga][:, 0:n], AF.Identity, ["g%d" % ga, "fmb"], [("p5", "qT", h4)], bias=fmb[:, l, 12 + 2 * h4:13 + 2 * h4])
                                    continue
                                for kc in range(8):
                                    mm(ps_g[gb][:, 0:n], wring[slot][:, kc, h2 * 256 + 128:h2 * 256 + 256], hTblk[:, kc, 0:n], kc == 0, kc == 7, [("wring", slot)] + hk, ["g%d" % gb])
                                stt("dve", p5stA[:, 0:n], ps_g[ga][:, 0:n], fmb[:, l, 12 + 2 * h4:13 + 2 * h4], p5ropeC[:, 0:n], ALU.add, ALU.mult,
                                    ["g%d" % ga, "fmb", ("p5x", "c")], [("p5x", "a")])
                                stt("dve", p5stB[:, 0:n], ps_g[gb][:, 0:n], fmb[:, l, 13 + 2 * h4:14 + 2 * h4], p5ropeS[:, 0:n], ALU.add, ALU.mult,
                                    ["g%d" % gb, "fmb", ("p5x", "s")], [("p5x", "b")])
                                tt("pool", qTblk[:, h4, 0:n], p5stA[:, 0:n], p5stB[:, 0:n], ALU.add, [("p5x", "a"), ("p5x", "b")], [("p5", "qT", h4)])
                        for gi_, (c0, boff) in enumerate([(C_MLOZ, B_MLOZ), (C_DAZ, B_DAZ)]):
                            slot = next_slot()
                            wload(l, c0, 512, slot)
                            for ti in range(ntile):
                                gix = 2 + ti % 2
                                pg = ps_g[gix]
                                for kc in range(8):
                                    mm(pg[:, :], hTblk[:, kc, ti * 128:(ti + 1) * 128], wring[slot][:, kc, :], kc == 0, kc == 7, [("wring", slot), hk[ti]], ["g%d" % gix])
                                tt("dve", p5tmp, pg[:, :], biasb[:, boff:boff + 512], ALU.add, ["g%d" % gix, ("biasb", boff)], [("p5x", "tmp")])
                                act(p5t2, p5tmp, AF.Tanh, [("p5x", "tmp")], [("p5x", "t2")], scale=0.5)
                                gdst = gates[:, ti, gi_ * 512:(gi_ + 1) * 512]
                                if gi_ == 0:
                                    ts("dve", gdst[:, 0:256], p5t2[:, 0:256], 0.5, 0.5, ALU.mult, ALU.add, [("p5x", "t2")], [("p5", "g", ti, 0)])
                                    stt("dve", gdst[:, 256:512], p5t2[:, 256:512], 1.0, p5tmp[:, 256:512], ALU.add, ALU.mult, [("p5x", "t2"), ("p5x", "tmp")], [("p5", "g", ti, 1)])
                                else:
                                    stt("dve", gdst, p5t2, 1.0, p5tmp, ALU.add, ALU.mult, [("p5x", "t2"), ("p5x", "tmp")], [("p5", "g", ti, 2)])
                        tg0 = t0 // 128
                        mlv = v3(p5x_ml, 4)[:, 0:ntile, :]
                        sqv = v3(p5x_sq, 4)[:, 0:ntile, :]
                        kml = [("p5x", "c"), ("p5x", "s")]
                        ksq = [("p5x", "tmp"), ("p5x", "t2")]
                        k4 = ntile * 4
                        tt("dve", mlv, gates[:, 0:ntile, 0:256], Hml[:, tg0:tg0 + ntile, :], ALU.mult,
                           [("p5", "g", ti, 0) for ti in range(ntile)] + [("Hml", tg0 + ti) for ti in range(ntile)], kml)
                        red(small[:, 64:64 + k4], mlv.rearrange("p a (h d) -> p (a h) d", h=4), kml, ["mls1"])
                        tt("dve", sqv, mlv, mlv, ALU.mult, kml, ksq)
                        red(small[:, 80:80 + k4], sqv.rearrange("p a (h d) -> p (a h) d", h=4), ksq, ["mls2"])
                        mean, rstd = stats(small[:, 64:64 + k4], small[:, 80:80 + k4], k4, 64.0, "ml")
                        ml3 = mlv.rearrange("p a (h d) -> p (a h) d", h=4)
                        tt("dve", ml3, ml3, bc(mean.unsqueeze(2), [128, k4, 64]), ALU.subtract, kml + ["st_mean"], kml)
                        tt("dve", ml3, ml3, bc(rstd.unsqueeze(2), [128, k4, 64]), ALU.mult, kml + ["st_rstd"], kml)
                        tt("dve", mlv, mlv, bc(mlgc.unsqueeze(1), [128, ntile, 256]), ALU.mult, kml + ["mlgc"], kml)
                        gml = gates[:, 0:ntile, 0:256]
                        tt("dve", gml, mlv, gates[:, 0:ntile, 256:512], ALU.mult, kml + [("p5", "g", ti, 1) for ti in range(ntile)] + [("p5", "g", ti, 0) for ti in range(ntile)], [("p5", "mixml")])
                        ktiles = list(range(0, 2)) if isctx else list(range(NT))
                        STB = [(ps_g[4], "g4"), (ps_tp[:, 0:512], "tpa"), (ps_tp[:, 512:1024], "tpb")]
                        nk = len(ktiles)
                        LOOK = 2

                        def combine(h4):
                            nt_ = ntile
                            ok_ = [("p5", "oev", qi, c) for qi in range(nt_) for c in range(2)]
                            rr = v3(small[:, 192:200], 4)[:, 0:nt_, :]
                            attv = v3(p5stA, 4)[:, 0:nt_, :]
                            tmpv = v3(p5stB, 4)[:, 0:nt_, :]
                            ssv = small[:, 200:200 + nt_]
                            S.op("dve", lambda h, rr=rr, nt_=nt_: h.reciprocal(out=rr, in_=oev[:, 0:nt_, :, 128]), ok_, ["p5rr"])
                            ts("dve", rr[:, :, 1], rr[:, :, 1], neglam[:, l:l + 1], None, ALU.mult, None, ["p5rr", "neglam"], ["p5rr"])
                            tt("dve", attv, oev[:, 0:nt_, 0, 0:128], bc(rr[:, :, 0:1], [128, nt_, 128]), ALU.mult, ok_ + ["p5rr"], [("p5x", "a")])
                            tt("dve", tmpv, oev[:, 0:nt_, 1, 0:128], bc(rr[:, :, 1:2], [128, nt_, 128]), ALU.mult, ok_ + ["p5rr"], [("p5x", "b")])
                            tt("pool", attv, attv, tmpv, ALU.add, [("p5x", "a"), ("p5x", "b")], [("p5x", "a")])
                            tt("dve", tmpv, attv, attv, ALU.mult, [("p5x", "a")], [("p5x", "b")])
                            red(ssv, tmpv, [("p5x", "b")], ["p5ss"])
                            ts("dve", ssv, ssv, 1.0 / 128.0, EPS, ALU.mult, ALU.add, ["p5ss"], ["p5ss"])
                            tt("pool", ssv, ssv, neghalf[:, 0:nt_], ALU.pow, ["p5ss", "cst"], ["p5ss"])
                            tt("dve", attv, attv, bc(ssv.unsqueeze(2), [128, nt_, 128]), ALU.mult, [("p5x", "a"), "p5ss"], [("p5x", "a")])
                            tt("dve", attv, attv, bc(dagc.unsqueeze(1), [128, nt_, 128]), ALU.mult, [("p5x", "a"), "dagc"], [("p5x", "a")])
                            gsl = gates[:, 0:nt_, 512 + h4 * 128:512 + (h4 + 1) * 128]
                            tt("dve", gsl, attv, gsl, ALU.mult, [("p5x", "a")] + [("p5", "g", qi, 2) for qi in range(nt_)], [("p5", "damix", h4)])
                        tg0 = t0 // 128

                        items = [(h4, c, kti) for h4 in range(4) for c in range(2) for kti in range(nk)]
                        for g in range(len(items) + LOOK):
                            if g < len(items):
                                h4, c, kti = items[g]
                                kt = ktiles[kti]
                                pr = slice(c * 64, c * 64 + 64)
                                pS, gk = STB[g % 3]
                                mm(pS[:, 0:n], dkT[pr, h4, kt * 128:(kt + 1) * 128], qTblk[pr, h4, 0:n], True, True,
                                   [("dkT", h4, 0 if kt < 2 else 1 + (kt - 2) // 4), ("p5", "qT", h4)], [gk])
                                act(ptb[g % 4][:, 0:n], pS[:, 0:n], AF.Exp, [gk], [("p5", "pt", g % 4)], scale=0.125)
                            j = g - LOOK
                            if j >= 0:
                                h4, c, kti = items[j]
                                kt = ktiles[kti]
                                for qi in range(ntile):
                                    mm(ps_g[qi][:, 0:129], ptb[j % 4][:, qi * 128:(qi + 1) * 128], dvaug[:, kt, h4, :], kti == 0, kti == nk - 1,
                                       [("p5", "pt", j % 4), ("dvaug", kt), ("dvaug", "ones")], ["g%d" % qi])
                                if kti == nk - 1:
                                    for qi in range(ntile):
                                        cp("act" if qi % 2 else "dve", oev[:, qi, c, :], ps_g[qi][:, 0:129], ["g%d" % qi], [("p5", "oev", qi, c)])
                                    if c == 1:
                                        combine(h4)
                        for ti in range(ntile):
                            tg = tg0 + ti
                            for kc in range(8):
                                if kc < 2:
                                    src_, rk_ = gates[:, ti, kc * 128:(kc + 1) * 128], [("p5", "mixml")]
                                elif kc < 6:
                                    src_, rk_ = gates[:, ti, 512 + (kc - 2) * 128:512 + (kc - 1) * 128], [("p5", "damix", kc - 2)]
                                else:
                                    src_, rk_ = ysg[:, tg, (kc - 6) * 128:(kc - 5) * 128], [("ysg", tg)]
                                tr(ps_tb[:, kc * 128:(kc + 1) * 128], src_, identB, rk_ + ["cstb"], ["tb"])
                            cp("act", mixT[:, :, ti * 128:(ti + 1) * 128], v3(ps_tb[:, :], 8), ["tb"], [("p5", "mixT", ti)])
                        if dbg:
                            tap("mix%d" % bi, mixT[:, :, 0:n], [("p5", "mixT", ti) for ti in range(ntile)], BF16)
                        gateb = gate_c if isctx else gate_l
                        gkey = "gate_c" if isctx else "gate_l"
                        for hf in range(2):
                            slot = next_slot()
                            wload(l, hf * 512, 512, slot, src=wobf_d)
                            for ti in range(ntile):
                                gix = ti % 2
                                pg = ps_g[gix]
                                for kc in range(8):
                                    mm(pg[:, :], mixT[:, kc, ti * 128:(ti + 1) * 128], wring[slot][:, kc, :], kc == 0, kc == 7, [("wring", slot), ("p5", "mixT", ti)], ["g%d" % gix])
                                tt("dve", p5tmp, pg[:, :], gateb[:, hf * 512:(hf + 1) * 512], ALU.mult, ["g%d" % gix, gkey], [("p5x", "tmp")])
                                stt("dve", xb[:, ti, hf * 512:(hf + 1) * 512], xb[:, ti, hf * 512:(hf + 1) * 512], ALPHA, p5tmp, ALU.mult, ALU.add,
                                    [("xblk", 0, ti), ("p5x", "tmp")], [("xblk", 0, ti)])
                        s1 = small[:, 224:224 + ntile]
                        s2 = small[:, 232:232 + ntile]
                        k1 = [("lnst", 1, ti) for ti in range(ntile)]
                        k2 = [("lnst", 2, ti) for ti in range(ntile)]
                        for ti in range(ntile):
                            act(xn5, xb[:, ti, :], AF.Square, [("xblk", 0, ti)], [("p5", "xn"), k2[ti]], accum=s2[:, ti:ti + 1])
                            act(xn5, xb[:, ti, :], AF.Identity, [("xblk", 0, ti)], [("p5", "xn"), k1[ti]], accum=s1[:, ti:ti + 1])
                        mean, rstd = stats(s1, s2, ntile, 1024.0, (k1, k2))
                        if last:
                            dst3 = y_d[b, t0 - 256:t1 - 256, :].rearrange("(n p) d -> p n d", p=128)
                        else:
                            dst3 = xs_d[b, t0:t1, :].rearrange("(n p) d -> p n d", p=128)
                        for ti in range(ntile):
                            xk_ = ("xblk", 0, ti)
                            ts("dve", xb[:, ti, :], xb[:, ti, :], mean[:, ti:ti + 1], rstd[:, ti:ti + 1], ALU.subtract, ALU.mult, [xk_, "st_mean", "st_rstd"], [xk_])
                            tt("dve", xb[:, ti, :], xb[:, ti, :], lng, ALU.mult, [xk_, "lng"], [xk_])
                            tt("pool" if ti % 2 else "dve", xb[:, ti, :], xb[:, ti, :], lnb, ALU.add, [xk_, "lnb"], [xk_])
                            dma(dst3[:, ti, :], xb[:, ti, :], [xk_], [("y", b, bi, ti) if last else ("xs", b, bi, ti)])

        except _Stop:
            pass
        S.emit(nc, st)
    return nc, S, dbg_out


def make_consts():
    i = np.arange(128)
    ident = np.eye(128, dtype=np.float32)
    triF = (i[:, None] <= i[None, :]).astype(np.float32)
    triB = (i[:, None] >= i[None, :]).astype(np.float32)
    ones = np.ones((128, 128), np.float32)
    partner = np.where((i % 32) < 16, i + 16, i - 16)
    rperm = np.zeros((128, 128), np.float32)
    rperm[partner, i] = 1.0
    cst = np.concatenate([ident, triF, triB, ones, rperm], 1)
    t = np.arange(LAT)
    row = (t // 64).astype(np.float32)
    col = (t % 64).astype(np.float32)
    half = 32
    inv = (10000.0 ** (-(np.arange(0, half, 2, dtype=np.float32)) / half)).astype(np.float32)
    ang_r = row[:, None] * inv
    ang_c = col[:, None] * inv
    ang = np.concatenate([ang_r, ang_r, ang_c, ang_c], -1).astype(np.float32)
    cos = np.cos(ang).astype(np.float32)
    sin = np.sin(ang).astype(np.float32)
    d = np.arange(64)
    sign = np.where((d % 32) < 16, -1.0, 1.0).astype(np.float32)
    sinS = sin * sign[None, :]
    cosT = np.concatenate([cos.T, cos.T], 0)
    sinT = np.concatenate([sinS.T, sinS.T], 0)
    rope = np.ascontiguousarray(np.stack([cosT, sinT], 0)).astype(np.float32)
    return np.ascontiguousarray(cst), rope


_CACHE = {}


def kernel(**inputs):
    NCORES = 8
    NB = 32 // NCORES
    key = (NB, DEPTH)
    if key not in _CACHE:
        _CACHE[key] = build(NB, DEPTH)
    nc = _CACHE[key][0]
    cst, rope = make_consts()
    f = lambda a: np.ascontiguousarray(np.asarray(a, dtype=np.float32))
    shared = {k: f(inputs[k]) for k in ["w_mod", "b_mod", "w_in", "b_in", "ml_conv_w", "ml_conv_b", "ml_norm_g",
                                        "da_lam_q1", "da_lam_k1", "da_lam_q2", "da_lam_k2", "da_norm_g", "sg_norm_g",
                                        "sg_norm_b", "sg_w_s", "sg_b_s", "w_out", "ln_g", "ln_b"]}
    shared["c_ctx"] = f(inputs["c_ctx"]).reshape(1, D)
    shared["cst"] = cst
    shared["rope"] = rope
    x = f(inputs["x"])
    ctx = f(inputs["ctx"])
    c = f(inputs["c"])
    in_maps = []
    for i in range(NCORES):
        m = dict(shared)
        m["x"] = x[i * NB:(i + 1) * NB]
        m["ctx"] = ctx[i * NB:(i + 1) * NB]
        m["c"] = c[i * NB:(i + 1) * NB]
        in_maps.append(m)
    res = run_bass_kernel_spmd(nc, in_maps, core_ids=list(range(NCORES)))
    return np.concatenate([r["y"] for r in res.results], axis=0).astype(np.float32)
```

```python
import math
import numpy as np
from contextlib import ExitStack
import concourse.bass as bass
import concourse.mybir as mybir
from concourse.bass_utils import run_bass_kernel_spmd

F32 = mybir.dt.float32
BF16 = mybir.dt.bfloat16
AF = mybir.ActivationFunctionType
ALU = mybir.AluOpType
AX = mybir.AxisListType

D = 1024
CTXL = 256
LAT = 2048
T = CTXL + LAT
NT = T // 128
DEPTH = 4
EPS = 1e-5
ALPHA = (2 * DEPTH) ** 0.25
NWC = 5136
BLKS = [(0, 256), (256, 768), (768, 1280), (1280, 1792), (1792, 2304)]
NDMA = 48
PSUM_KEYS = frozenset(["g0", "g1", "g2", "g3", "g4", "tpa", "tpb", "tb"])

C_MLQK, C_MLV, C_DAK, C_DAV, C_SG, C_DAQ, C_MLOZ, C_DAZ = 0, 512, 784, 1808, 2320, 3088, 4112, 4624
B_MLV, B_DAV, B_SG, B_MLOZ, B_DAZ, NBIAS = 0, 272, 784, 1552, 2064, 2576


class Sched:
    def __init__(self):
        self.ins = []
        self.lastw = {}
        self.readers = {}
        self.region = {}
        self.rlast = {}
        self.rdma = {}

    def _add(self, eng, fn, reads, writes, is_dma, extra=()):
        reads = list(reads)
        writes = list(writes)
        for k in reads:
            if k in PSUM_KEYS:
                writes.append(("rdser", k))
        deps = set(extra)
        touched = set()
        for k in reads + writes:
            nm = k[0] if isinstance(k, tuple) else k
            r = self.region.get(nm)
            if r is not None:
                touched.add(r)
        for r in touched:
            w = self.lastw.get(("R", r))
            if w is not None:
                deps.add(w)
        for k in reads:
            w = self.lastw.get(k)
            if w is not None:
                deps.add(w)
        for k in writes:
            w = self.lastw.get(k)
            if w is not None:
                deps.add(w)
            rd = self.readers.get(k)
            if rd:
                deps.update(rd[0].values())
                deps.update(rd[1])
        idx = len(self.ins)
        self.ins.append([eng, fn, deps, is_dma])
        for k in writes:
            self.lastw[k] = idx
            self.readers[k] = [{}, []]
        ws = set(writes)
        for k in reads:
            if k not in ws:
                rd = self.readers.setdefault(k, [{}, []])
                if is_dma:
                    rd[1].append(idx)
                else:
                    rd[0][eng] = idx
        for r in touched:
            if is_dma:
                self.rdma.setdefault(r, []).append(idx)
            else:
                self.rlast.setdefault(r, {})[eng] = idx
        return idx

    def fence(self, region, eng, fn):
        deps = set(self.rlast.get(region, {}).values()) | set(self.rdma.get(region, []))
        idx = self._add(eng, fn, (), ("fence_dummy",), False, extra=deps)
        self.lastw[("R", region)] = idx
        self.rlast[region] = {}
        self.rdma[region] = []
        return idx

    def op(self, eng, fn, reads=(), writes=()):
        return self._add(eng, fn, reads, writes, False)

    def dma(self, eng, fn, reads=(), writes=()):
        return self._add(eng, fn, reads, writes, True)

    def emit(self, nc, stack):
        ins = self.ins
        n = len(ins)
        engs = ["pe", "act", "dve", "pool", "sp"]
        dma_list = [i for i in range(n) if ins[i][3]]
        dma_slot = {}
        for j, i in enumerate(dma_list):
            dma_slot[i] = j
            if j >= NDMA:
                ins[i][2].add(dma_list[j - NDMA])
        needed = [False] * n
        for i in range(n):
            e = ins[i][0]
            nd = set()
            for d in ins[i][2]:
                if ins[d][0] == e and e == "pe" and not ins[d][3]:
                    continue
                nd.add(d)
                needed[d] = True
            ins[i][2] = nd
        esem = {e: stack.enter_context(nc.semaphore("s_" + e)) for e in engs}
        dsem = [stack.enter_context(nc.semaphore("d_%d" % j)) for j in range(NDMA)]
        cnt = {e: 0 for e in engs}
        tok = [None] * n
        for i in range(n):
            e, fn, deps, is_dma = ins[i]
            if is_dma:
                j = dma_slot[i]
                tok[i] = (("d", j % NDMA), 16 * (j // NDMA + 1))
            elif needed[i]:
                cnt[e] += 1
                tok[i] = (("e", e), cnt[e])
        per = {e: [] for e in engs}
        for i in range(n):
            per[ins[i][0]].append(i)
        self.counts = {e: len(per[e]) for e in engs}

        def semof(key):
            return esem[key[1]] if key[0] == "e" else dsem[key[1]]

        def run(e, h):
            seen = {}
            for i in per[e]:
                _, fn, deps, is_dma = ins[i]
                want = {}
                for d in deps:
                    k, v = tok[d]
                    if v > want.get(k, 0):
                        want[k] = v
                for k, v in want.items():
                    if seen.get(k, 0) < v:
                        h.wait_ge(semof(k), v)
                        seen[k] = v
                r = fn(h)
                if tok[i] is not None:
                    k, v = tok[i]
                    r.then_inc(semof(k), 16 if is_dma else 1)
            for i in per[e]:
                if ins[i][3]:
                    k, v = tok[i]
                    if seen.get(k, 0) < v:
                        h.wait_ge(semof(k), v)
                        seen[k] = v

        with nc.Block() as block:
            @block.tensor
            def _(h):
                run("pe", h)

            @block.scalar
            def _(h):
                run("act", h)

            @block.vector
            def _(h):
                run("dve", h)

            @block.gpsimd
            def _(h):
                run("pool", h)

            @block.sync
            def _(h):
                run("sp", h)


class Arena:
    def __init__(self, ap):
        self.ap = ap
        self.off = 0
        self.size = ap.shape[1]

    def f32(self, n):
        a = self.ap[:, self.off:self.off + n]
        self.off += (n + 7) // 8 * 8
        assert self.off <= self.size, (self.off, self.size)
        return a

    def bf16(self, n):
        w = (n + 1) // 2
        a = self.ap[:, self.off:self.off + w].bitcast(BF16)
        self.off += (w + 7) // 8 * 8
        assert self.off <= self.size, (self.off, self.size)
        return a[:, 0:n]


def v3(ap, a):
    return ap.rearrange("p (a b) -> p a b", a=a)


def v4(ap, a, b):
    return ap.rearrange("p (a b c) -> p a b c", a=a, b=b)


def bc(ap, shape):
    return ap.to_broadcast(shape)


class _Stop(Exception):
    pass


def build(NB, NL, dbg=None, stop=None):
    nc = bass.Bass("TRN2", target_bir_lowering=False)
    S = Sched()

    def dram(name, shape, dtype=F32, kind="ExternalInput"):
        return nc.dram_tensor(name, shape, dtype, kind=kind).ap()

    x_d = dram("x", [NB, LAT, D])
    ctx_d = dram("ctx", [NB, CTXL, D])
    c_d = dram("c", [NB, D])
    cctx_d = dram("c_ctx", [1, D])
    wmod_d = dram("w_mod", [DEPTH, D, 3 * D])
    bmod_d = dram("b_mod", [DEPTH, 3 * D])
    win_d = dram("w_in", [DEPTH, D, 4112])
    bin_d = dram("b_in", [DEPTH, 4112])
    convw_d = dram("ml_conv_w", [DEPTH, 3, 512])
    convb_d = dram("ml_conv_b", [DEPTH, 512])
    mlg_d = dram("ml_norm_g", [DEPTH, 256])
    lq1_d = dram("da_lam_q1", [DEPTH, 64])
    lk1_d = dram("da_lam_k1", [DEPTH, 64])
    lq2_d = dram("da_lam_q2", [DEPTH, 64])
    lk2_d = dram("da_lam_k2", [DEPTH, 64])
    dag_d = dram("da_norm_g", [DEPTH, 128])
    sgg_d = dram("sg_norm_g", [DEPTH, 256])
    sgb_d = dram("sg_norm_b", [DEPTH, 256])
    sgw_d = dram("sg_w_s", [DEPTH, 4, 128, 128])
    sgbs_d = dram("sg_b_s", [DEPTH, 4, 128])
    wout_d = dram("w_out", [DEPTH, D, D])
    lng_d = dram("ln_g", [DEPTH, D])
    lnb_d = dram("ln_b", [DEPTH, D])
    cst_d = dram("cst", [128, 640])
    rope_d = dram("rope", [2, 128, LAT])
    y_d = dram("y", [NB, LAT, D], kind="ExternalOutput")
    wbf_d = dram("wbf", [DEPTH, D, NWC], BF16, kind="Internal")
    wobf_d = dram("wobf", [DEPTH, D, D], BF16, kind="Internal")
    mod_d = dram("modscr", [DEPTH, 8, 3 * D], F32, kind="Internal")
    xs_d = dram("xs", [NB, T, D], F32, kind="Internal")
    dbg_out = {}

    with ExitStack() as st:
        arena_t = st.enter_context(nc.sbuf_tensor("arena", [128, 51200], F32))
        AR = Arena(arena_t[:, :])
        ps_tp = st.enter_context(nc.psum_tensor("ps_tp", [128, 1024], F32))
        ps_tb = st.enter_context(nc.psum_tensor("ps_tb", [128, 1024], BF16))
        ps_g = [st.enter_context(nc.psum_tensor("ps_g%d" % i, [128, 512], F32)) for i in range(5)]

        cstF = AR.f32(640)
        identF, triF, triB, onesF, rperm = (cstF[:, i * 128:(i + 1) * 128] for i in range(5))
        identB = AR.bf16(128)
        maskF = AR.bf16(128)
        maskB = AR.bf16(128)
        neghalf = AR.f32(64)
        dummy = AR.f32(8)
        sgW = v4(AR.bf16(DEPTH * 4 * 128), DEPTH, 4)
        sgbs = v3(AR.f32(DEPTH * 4), DEPTH)
        fmb = v3(AR.f32(DEPTH * 20), DEPTH)
        convw = v4(AR.f32(DEPTH * 12), DEPTH, 3)
        convb = v3(AR.f32(DEPTH * 4), DEPTH)
        hconvb = v3(AR.f32(DEPTH * 4), DEPTH)
        lamv = AR.f32(DEPTH)
        neglam = AR.f32(DEPTH)
        biasb = AR.f32(NBIAS)
        lng = AR.f32(D)
        lnb = AR.f32(D)
        mlgc = AR.f32(256)
        dagc = AR.f32(128)
        sggt = AR.f32(256)
        sgbt = AR.f32(256)
        gate_l = AR.f32(D)
        gate_c = AR.f32(D)
        modp = v3(AR.f32(32), 4)
        WR_OFF = AR.off
        wring = [v3(AR.bf16(8 * 512), 8) for _ in range(2)]
        wq32 = v3(arena_t[:, WR_OFF:WR_OFF + 4096], 8)
        wg32 = v3(AR.f32(128), 8)
        Hml = v3(AR.f32(NT * 256), NT)
        small = AR.f32(256)
        XOFF = AR.off
        XSZ = 8192
        AR.off += XSZ
        BOFF = AR.off
        BSZ = 12900
        AR.off += BSZ
        AOFF = AR.off
        ASZ = 51200 - AOFF
        assert ASZ >= 9216, ASZ

        def sub(off, size):
            return Arena(arena_t[:, off:off + size])

        for nm in ["X", "B", "A"]:
            pass

        aX = sub(XOFF, XSZ)
        xblk = [v3(aX.f32(4096), 4), v3(aX.f32(4096), 4)]
        aX = sub(XOFF, XSZ)
        pst = aX.f32(T)
        cacc = aX.f32(T)
        aX = sub(XOFF, XSZ)
        m_pt = [v3(aX.f32(512), 4) for _ in range(2)]
        m_vsf = [aX.f32(288) for _ in range(2)]
        m_vs = [v3(m_vsf[i], 4)[:, :, 0:65] for i in range(2)]
        m_kt = [aX.f32(256) for _ in range(2)]
        m_hs = [v3(aX.f32(288), 4)[:, :, 0:65] for _ in range(2)]
        m_cst_flat = aX.f32(576)
        m_cst = v4(m_cst_flat, 2, 4)[:, :, :, 0:65]
        m_tmp = v3(aX.f32(288), 4)[:, :, 0:65]
        m_hn = v3(aX.f32(256), 4)
        aX = sub(XOFF, XSZ)
        ropeC = aX.f32(512)
        ropeS = aX.f32(512)
        rstA = aX.f32(512)
        rstB = aX.f32(512)
        sgp = aX.f32(768)
        sgp2 = v3(aX.f32(2 * 768), 2)
        sgu2 = v3(aX.f32(512), 2)
        sgjunk2 = v3(aX.f32(512), 2)
        sgvn2 = v3(aX.bf16(512), 2)
        sgy2 = v3(aX.f32(512), 2)
        sgzs2 = v3(aX.f32(512), 2)
        aB = sub(BOFF, BSZ)
        qkT = v3(aB.f32(4 * T), 4)
        vaug = v4(aB.bf16(NT * 4 * 72), NT, 4)[:, :, :, 0:65]
        Gt = v3(aB.f32(NT * 16), NT)
        LFp = v3(aB.f32(144), 2)
        Ej = v3(aB.f32(144), 2)
        Ws = v3(aB.f32(144), 2)
        Eend = v3(aB.f32(144), 2)
        gtmp = v3(aB.f32(144), 2)
        aB = sub(BOFF, BSZ)
        dkT = v3(aB.bf16(4 * T), 4)
        dvaug = v4(aB.bf16(NT * 4 * 136), NT, 4)[:, :, :, 0:129]
        ysg = v3(aB.bf16(NT * 256), NT)
        aA = sub(AOFF, ASZ)
        hTall = v3(aA.bf16(8 * T), 8)
        xn_a = aA.f32(0) if False else None
        aA = sub(AOFF, ASZ)
        hTblk = v3(aA.bf16(8 * 512), 8)
        qTblk = v3(aA.bf16(4 * 512), 4)
        gates = v3(aA.bf16(4 * 1024), 4)
        mixT = v3(aA.bf16(8 * 512), 8)
        ptb = [aA.bf16(512) for _ in range(4)]
        oev = v4(aA.f32(4 * 2 * 132), 4, 2)[:, :, :, 0:129]
        xn5 = aA.f32(1024)
        xn5b = aA.f32(1024)
        aX5 = sub(XOFF + 4096, 4096)
        p5ropeC = aX5.f32(512)
        p5ropeS = aX5.f32(512)
        p5stA = aX5.f32(512)
        p5stB = aX5.f32(512)
        p5tmp = aX5.f32(512)
        p5t2 = aX5.f32(512)
        p5ml = aX5.f32(256)
        p5x_ml = arena_t[:, XOFF + 4096:XOFF + 4096 + 1024]
        p5x_sq = arena_t[:, XOFF + 4096 + 2048:XOFF + 4096 + 3072]
        p5sq = aX5.f32(256)

        for nm in ["xblk", "pst", "cacc", "mtmp", "sgt", "rope4", "p5x"]:
            S.region[nm] = "X"
        for nm in ["qkT", "qkpre", "vaug", "ktok", "Gt", "gder", "dkT", "dvaug", "ysg"]:
            S.region[nm] = "B"
        for nm in ["hTall", "p5"]:
            S.region[nm] = "A"

        def fence(region):
            S.fence(region, "pool", lambda h: h.memset(dummy[:, 0:1], 0.0))

        def mm(out, lhsT, rhs, start, stop, reads, writes):
            S.op("pe", lambda h: h.matmul(out, lhsT=lhsT, rhs=rhs, start=start, stop=stop), reads, writes)

        def tr(out, in_, ident, reads, writes):
            S.op("pe", lambda h: h.transpose(out, in_, ident), reads, writes)

        def act(out, in_, func, reads, writes, bias=None, scale=None, accum=None):
            kw = {}
            if accum is not None:
                kw["accum_out"] = accum
            if bias is not None:
                kw["bias"] = bias
            if scale is not None:
                kw["scale"] = scale
            S.op("act", lambda h: h.activation(out=out, in_=in_, func=func, **kw), reads, writes)

        def tt(eng, out, in0, in1, op, reads, writes):
            S.op(eng, lambda h: h.tensor_tensor(out=out, in0=in0, in1=in1, op=op), reads, writes)

        def ts(eng, out, in0, s1, s2, op0, op1, reads, writes):
            if s2 is None:
                S.op(eng, lambda h: h.tensor_scalar(out=out, in0=in0, scalar1=s1, scalar2=None, op0=op0), reads, writes)
            else:
                S.op(eng, lambda h: h.tensor_scalar(out=out, in0=in0, scalar1=s1, scalar2=s2, op0=op0, op1=op1), reads, writes)

        def stt(eng, out, in0, scalar, in1, op0, op1, reads, writes):
            S.op(eng, lambda h: h.scalar_tensor_tensor(out=out, in0=in0, scalar=scalar, in1=in1, op0=op0, op1=op1), reads, writes)

        def cp(eng, out, in_, reads, writes):
            if eng == "act":
                S.op(eng, lambda h: h.activation(out=out, in_=in_, func=AF.Copy), reads, writes)
            else:
                S.op(eng, lambda h: h.tensor_copy(out=out, in_=in_), reads, writes)

        def red(out, in_, reads, writes):
            S.op("dve", lambda h: h.tensor_reduce(out=out, in_=in_, axis=AX.X, op=ALU.add), reads, writes)

        def dma(out, in_, reads, writes, slow=False):
            if slow:
                S.dma("sp", lambda h: h.dma_start(out=out, in_=in_, allow_slow_non_contiguous=True), reads, writes)
            else:
                S.dma("sp", lambda h: h.dma_start(out=out, in_=in_), reads, writes)

        def tap(name, ap, key, dtype=F32):
            if dbg is None or name not in dbg:
                return
            shp = list(ap.shape)
            d = nc.dram_tensor("dbg_" + name, shp, dtype, kind="ExternalOutput").ap()
            dbg_out[name] = shp
            dma(d, ap, list(key) if isinstance(key, list) else [key], ["dbg_" + name])

        def stats(s1, s2, k, n, tag):
            rk1 = tag[0] if isinstance(tag, tuple) else [tag + "s1"]
            rk2 = tag[1] if isinstance(tag, tuple) else [tag + "s2"]
            mean = small[:, 0:k]
            msq = small[:, 16:16 + k]
            var = small[:, 32:32 + k]
            rstd = small[:, 48:48 + k]
            ts("dve", mean, s1, 1.0 / n, None, ALU.mult, None, rk1, ["st_mean"])
            tt("dve", msq, mean, mean, ALU.mult, ["st_mean"], ["st_msq"])
            stt("dve", var, s2, 1.0 / n, msq, ALU.mult, ALU.subtract, rk2 + ["st_msq"], ["st_var"])
            ts("dve", var, var, EPS, None, ALU.add, None, ["st_var"], ["st_var"])
            tt("pool", rstd, var, neghalf[:, 0:k], ALU.pow, ["st_var", "cst"], ["st_rstd"])
            return mean, rstd

        def chk(name):
            if stop == name:
                raise _Stop()

        try:
            dma(cstF, cst_d[:, :], [], ["cst"])
            cp("dve", identB, identF, ["cst"], ["cstb"])
            cp("dve", maskF, triF, ["cst"], ["cstb"])
            cp("dve", maskB, triB, ["cst"], ["cstb"])
            S.op("pool", lambda h: h.memset(neghalf, -0.5), [], ["cst"])

            S.op("pool", lambda h: h.memset(fmb[:, :, :], 0.0), [], ["fmb"])
            for l in range(DEPTH):
                for j in range(4):
                    dma(fmb[:, l, j:j + 1], bin_d[l, j * 128:(j + 1) * 128].rearrange("(p o) -> p o", o=1), [], ["fmb"])
                for h4 in range(4):
                    dma(fmb[:, l, 4 + 2 * h4:5 + 2 * h4], bin_d[l, 1808 + h4 * 128:1808 + (h4 + 1) * 128].rearrange("(p o) -> p o", o=1), [], ["fmb"])
                    dma(fmb[:, l, 12 + 2 * h4:13 + 2 * h4], bin_d[l, 1296 + h4 * 128:1296 + (h4 + 1) * 128].rearrange("(p o) -> p o", o=1), [], ["fmb"])
                for j in range(3):
                    dma(convw[:, l, j, :], convw_d[l, j, :].rearrange("(c p) -> p c", p=128), [], ["convp"], slow=True)
                dma(convb[:, l, :], convb_d[l, :].rearrange("(c p) -> p c", p=128), [], ["convp"], slow=True)
                dma(sgbs[:, l, :], sgbs_d[l, :, :].rearrange("g p -> p g"), [], ["sgbs"], slow=True)
            ts("dve", hconvb[:, :, :], convb[:, :, :], 0.5, None, ALU.mult, None, ["convp"], ["hconvb"])
            for l in range(DEPTH):
                pso = ps_g[0][:, 0:16]
                mm(pso, rperm, fmb[:, l, 4:20], True, True, ["cst", "fmb"], ["g0"])
                src = v3(pso, 8)[:, :, 0:1]
                dst = v3(fmb[:, l, 4:20], 8)[:, :, 1:2]
                cp("dve", dst, src, ["g0"], ["fmb"])
            for l in range(DEPTH):
                for g in range(4):
                    stg = small[:, 64:192]
                    dma(stg, sgw_d[l, g, :, :], [], ["sgstg"])
                    tr(ps_g[1][:, 0:128], stg, identF, ["sgstg", "cst"], ["g1"])
                    cp("dve", sgW[:, l, g, :], ps_g[1][:, 0:128], ["g1"], ["sgW"])
            lt = rstA
            fence("X")
            for i, (a_d, b_d) in enumerate([(lq1_d, lk1_d), (lq2_d, lk2_d)]):
                dma(lt[:, 0:256], a_d.rearrange("l k -> (l k)").partition_broadcast(128), [], [("rope4", "a")])
                dma(lt[:, 256:512], b_d.rearrange("l k -> (l k)").partition_broadcast(128), [], [("rope4", "b")])
                tt("dve", lt[:, 0:256], lt[:, 0:256], lt[:, 256:512], ALU.mult, [("rope4", "a"), ("rope4", "b")], [("rope4", "a")])
                red(small[:, 200 + 4 * i:204 + 4 * i], v3(lt[:, 0:256], 4), [("rope4", "a")], ["lam%d" % i])
                act(small[:, 200 + 4 * i:204 + 4 * i], small[:, 200 + 4 * i:204 + 4 * i], AF.Exp, ["lam%d" % i], ["lam%d" % i])
            tt("dve", lamv, small[:, 200:204], small[:, 204:208], ALU.subtract, ["lam0", "lam1"], ["lamv"])
            for l in range(DEPTH):
                lam_init = 0.8 - 0.6 * math.exp(-0.3 * l)
                ts("dve", lamv[:, l:l + 1], lamv[:, l:l + 1], lam_init, None, ALU.add, None, ["lamv"], ["lamv"])
            ts("dve", neglam, lamv, -1.0, None, ALU.mult, None, ["lamv"], ["neglam"])

            chk('consts')
            groups = [
                (C_MLQK, [(0, 0, 512, False)]),
                (C_MLV, [(0, 512, 256, False), (256, 1280, 16, False)]),
                (C_DAK, [(0, 1808, 128, False), (128, 1808, 128, True), (256, 1936, 128, False), (384, 1936, 128, True)]),
                (C_DAK + 512, [(0, 2064, 128, False), (128, 2064, 128, True), (256, 2192, 128, False), (384, 2192, 128, True)]),
                (C_DAV, [(0, 2320, 512, False)]),
                (C_SG, [(0, 3344, 512, False)]),
                (C_SG + 512, [(0, 3856, 256, False)]),
                (C_DAQ, [(0, 1296, 128, False), (128, 1296, 128, True), (256, 1424, 128, False), (384, 1424, 128, True)]),
                (C_DAQ + 512, [(0, 1552, 128, False), (128, 1552, 128, True), (256, 1680, 128, False), (384, 1680, 128, True)]),
                (C_MLOZ, [(0, 768, 512, False)]),
                (C_DAZ, [(0, 2832, 512, False)]),
            ]
            fence("X")
            fence("A")
            stg32 = [v3(sub(XOFF, XSZ).f32(4096), 8), v3(sub(XOFF + 4096, 4096).f32(4096), 8)]
            aA = sub(AOFF, ASZ)
            stg16 = [v3(aA.bf16(4096), 8), v3(aA.bf16(4096), 8)]
            S.region["stg32"] = "X"
            S.region["stg16"] = "A"
            gi = 0
            prev_store = [[], []]
            ceng = ["dve", "pool", "act"]

            def castcp(i, out, in_, reads, writes):
                e = ceng[i % 3]
                if e == "act":
                    act(out, in_, AF.Copy, reads, writes)
                else:
                    cp(e, out, in_, reads, writes)

            for l in range(DEPTH):
                for (dst0, pieces) in groups:
                    sl = gi % 2
                    wtot = max(p[0] + p[2] for p in pieces)
                    for (doff, src0, w, sw) in pieces:
                        dma(stg32[sl][:, :, doff:doff + w], win_d[l, :, src0:src0 + w].rearrange("(kc p) w -> p kc w", p=128), prev_store[sl], [("stg32", sl, doff)])
                    for pi, (doff, src0, w, sw) in enumerate(pieces):
                        if not sw:
                            castcp(gi + pi, stg16[sl][:, :, doff:doff + w], stg32[sl][:, :, doff:doff + w], [("stg32", sl, doff)], [("stg16", sl, doff)])
                        else:
                            i5 = stg32[sl][:, :, doff:doff + w].rearrange("p k (b t s) -> p k b t s", b=4, t=2)
                            o5 = stg16[sl][:, :, doff:doff + w].rearrange("p k (b t s) -> p k b t s", b=4, t=2)
                            for kc in range(8):
                                cp("pool" if kc % 2 else "dve", o5[:, kc, :, 0, :], i5[:, kc, :, 1, :], [("stg32", sl, doff)], [("stg16", sl, doff, kc, 0)])
                                cp("dve" if kc % 2 else "pool", o5[:, kc, :, 1, :], i5[:, kc, :, 0, :], [("stg32", sl, doff)], [("stg16", sl, doff, kc, 1)])
                    rk = []
                    for (doff, src0, w, sw) in pieces:
                        if sw:
                            rk += [("stg16", sl, doff, kc, t) for kc in range(8) for t in range(2)]
                        else:
                            rk.append(("stg16", sl, doff))
                    dma(wbf_d[l, :, dst0:dst0 + wtot].rearrange("(kc p) w -> p kc w", p=128), stg16[sl][:, :, 0:wtot], rk, [("wbf", l, dst0)])
                    prev_store[sl] = [("wbf", l, dst0)]
                    gi += 1
                for hf in range(2):
                    sl = gi % 2
                    dma(stg32[sl][:, :, :], wout_d[l, :, hf * 512:(hf + 1) * 512].rearrange("(kc p) w -> p kc w", p=128), prev_store[sl], [("stg32", sl, 0)])
                    castcp(gi, stg16[sl][:, :, :], stg32[sl][:, :, :], [("stg32", sl, 0)], [("stg16", sl, 0)])
                    dma(wobf_d[l, :, hf * 512:(hf + 1) * 512].rearrange("(kc p) w -> p kc w", p=128), stg16[sl][:, :, :], [("stg16", sl, 0)], [("wobf", l, hf)])
                    prev_store[sl] = [("wobf", l, hf)]
                    gi += 1

            chk('conv')
            csT = v3(small[:, 64:64 + 64], 8)
            for r in range(NB + 1):
                src = c_d[r, :] if r < NB else cctx_d[0, :]
                dma(csT[:, :, r:r + 1], src.rearrange("(kc p o) -> p kc o", p=128, o=1), ["sgW"], [("csT", r)], slow=True)
            NR = NB + 1
            cs_r = [("csT", r) for r in range(NR)]
            tnh = v3(small[:, 128:192], 8)
            act(tnh[:, :, 0:NR], csT[:, :, 0:NR], AF.Tanh, cs_r, ["cs_t"], scale=0.5)
            ts("dve", tnh[:, :, 0:NR], tnh[:, :, 0:NR], 0.5, 0.5, ALU.mult, ALU.add, ["cs_t"], ["cs_t"])
            tt("dve", csT[:, :, 0:NR], csT[:, :, 0:NR], tnh[:, :, 0:NR], ALU.mult, cs_r + ["cs_t"], ["csS"])
            mrow = sub(AOFF, ASZ).f32(1024)
            S.region["mrow"] = "A"
            wi = 0
            for l in range(NL):
                for cb in range(6):
                    sl = wi % 2
                    dma(stg32[sl][:, :, :], wmod_d[l, :, cb * 512:(cb + 1) * 512].rearrange("(kc p) w -> p kc w", p=128), prev_store[sl], [("stg32", sl, 0)])
                    pg = ps_g[2 + (wi % 2)]
                    for kc in range(8):
                        mm(pg[0:NR, :], csT[:, kc, 0:NR], stg32[sl][:, kc, :], kc == 0, kc == 7, ["csS", ("stg32", sl, 0)], ["g%d" % (2 + wi % 2)])
                    bm = mrow[0:NR, 512:1024]
                    dma(bm, bmod_d[l, cb * 512:(cb + 1) * 512].partition_broadcast(NR), [], [("mrow", "b")])
                    tt("dve", mrow[0:NR, 0:512], pg[0:NR, :], bm, ALU.add, ["g%d" % (2 + wi % 2), ("mrow", "b")], [("mrow", "o")])
                    dma(mod_d[l, 0:NR, cb * 512:(cb + 1) * 512], mrow[0:NR, 0:512], [("mrow", "o")], [("mod", l)])
                    wi += 1

            chk('mod')
            def wload(l, col0, ncols, slot, src=None):
                srcd = wbf_d if src is None else src
                key = ("wbf", l, col0) if src is None else ("wobf", l, col0 // 512)
                rk = [("wbf", l, g[0]) for g in groups] if src is None else [key]
                dma(wring[slot][:, :, 0:ncols], srcd[l, :, col0:col0 + ncols].rearrange("(kc p) w -> p kc w", p=128), rk, [("wring", slot)])

            ring_ctr = [0]

            def next_slot():
                s_ = ring_ctr[0] % 2
                ring_ctr[0] += 1
                return s_

            g_ctr = [0]

            def next_g(lo=0, hi=5):
                i = lo + g_ctr[0] % (hi - lo)
                g_ctr[0] += 1
                return i

            def ln_block(xb, ntile, xkey, xnbuf, xnkey, hT_of, hkey_of, s1p, shp, modkey, hTf=None, hfkey=None, post=None):
                s1 = small[:, 224:224 + ntile]
                s2 = small[:, 232:232 + ntile]
                k1 = [("lnst", 1, ti) for ti in range(ntile)]
                k2 = [("lnst", 2, ti) for ti in range(ntile)]
                xk = xkey if callable(xkey) else (lambda ti: xkey)
                xnbufs = xnbuf if isinstance(xnbuf, list) else [xnbuf]
                xnkeys = xnkey if isinstance(xnkey, list) else [xnkey]
                for ti in range(ntile):
                    xnbuf, xnkey = xnbufs[ti % len(xnbufs)], xnkeys[ti % len(xnbufs)]
                    act(xnbuf, xb[:, ti, :], AF.Square, [xk(ti)], [xnkey, k2[ti]], accum=s2[:, ti:ti + 1])
                    act(xnbuf, xb[:, ti, :], AF.Identity, [xk(ti)], [xnkey, k1[ti]], accum=s1[:, ti:ti + 1])
                mean, rstd = stats(s1, s2, ntile, 1024.0, (k1, k2))
                for ti in range(ntile):
                    xnbuf, xnkey = xnbufs[ti % len(xnbufs)], xnkeys[ti % len(xnbufs)]
                    ts("dve", xnbuf, xb[:, ti, :], mean[:, ti:ti + 1], rstd[:, ti:ti + 1], ALU.subtract, ALU.mult, [xk(ti), "st_mean", "st_rstd"], [xnkey])
                    for kc in range(8):
                        tr(ps_tp[:, kc * 128:(kc + 1) * 128], xnbuf[:, kc * 128:(kc + 1) * 128], identF, [xnkey, "cst"], ["tpa", "tpb"])
                    tmpm = v3(xnbuf, 8)
                    tt("dve", tmpm, v3(ps_tp[:, :], 8), bc(s1p.unsqueeze(2), [128, 8, 128]), ALU.mult, ["tpa", "tpb"] + modkey, [xnkey])
                    if hTf is None:
                        tt("pool", hT_of(ti), tmpm, bc(shp.unsqueeze(2), [128, 8, 128]), ALU.add, [xnkey] + modkey, [hkey_of(ti)])
                    else:
                        tt("pool", hTf(ti), tmpm, bc(shp.unsqueeze(2), [128, 8, 128]), ALU.add, [xnkey] + modkey, [hfkey(ti)])
                        cp("act", hT_of(ti), hTf(ti), [hfkey(ti)], [hkey_of(ti)])
                    if post is not None:
                        post(ti)

            for b in range(NB):
                for l in range(NL):
                    lam_init = 0.8 - 0.6 * math.exp(-0.3 * l)
                    pk = ("par", b, l)
                    for (off, s0, w) in [(B_MLV, 512, 256), (B_MLV + 256, 1280, 16), (B_DAV, 2320, 512), (B_SG, 3344, 768), (B_MLOZ, 768, 512), (B_DAZ, 2832, 512)]:
                        dma(biasb[:, off:off + w], bin_d[l, s0:s0 + w].partition_broadcast(128), [], [("biasb", off)])
                    dma(lng, lng_d[l, :].partition_broadcast(128), [], ["lng"])
                    dma(lnb, lnb_d[l, :].partition_broadcast(128), [], ["lnb"])
                    dma(mlgc, mlg_d[l, :].partition_broadcast(128), [], ["mlgc"])
                    dma(dagc, dag_d[l, :].partition_broadcast(128), [], ["dagc"])
                    dma(sggt, sgg_d[l, :].partition_broadcast(128), [], ["sggt"])
                    dma(sgbt, sgb_d[l, :].partition_broadcast(128), [], ["sgbt"])
                    ts("dve", mlgc, mlgc, 0.5, None, ALU.mult, None, ["mlgc"], ["mlgc"])
                    ts("dve", dagc, dagc, 0.5 * (1.0 - lam_init), None, ALU.mult, None, ["dagc"], ["dagc"])
                    dma(gate_l, mod_d[l, b, 2048:3072].partition_broadcast(128), [("mod", l)], ["gate_l"])
                    dma(gate_c, mod_d[l, NB, 2048:3072].partition_broadcast(128), [("mod", l)], ["gate_c"])
                    for i, (row, c0) in enumerate([(b, 0), (b, 1024), (NB, 0), (NB, 1024)]):
                        dma(modp[:, i, :], mod_d[l, row, c0:c0 + 1024].rearrange("(kc p) -> p kc", p=128), [("mod", l)], [("modp", i)], slow=True)
                    for i in (1, 3):
                        ts("dve", modp[:, i, :], modp[:, i, :], 1.0, None, ALU.add, None, [("modp", i)], [("modp", i)])
                    modk = [("modp", i) for i in range(4)]

                    fence("X")
                    fence("A")
                    fence("B")
                    dma(wq32, win_d[l, :, 0:512].rearrange("(kc p) w -> p kc w", p=128), [], [("wring", 0), ("wring", 1)])
                    dma(wg32, win_d[l, :, 1280:1296].rearrange("(kc p) w -> p kc w", p=128), [], ["wg32"])
                    hTf2 = xblk[1][:, 1:3, :].rearrange("p a (k t) -> p k (a t)", k=8) if False else None
                    hTfbuf = v3(arena_t[:, XOFF + 4096 + 1024:XOFF + 4096 + 3072], 8)
                    for bi, (t0, t1) in enumerate(BLKS):
                        ntile = (t1 - t0) // 128
                        xb = xblk[0]
                        if l == 0:
                            srcx = ctx_d[b, :, :] if bi == 0 else x_d[b, t0 - 256:t1 - 256, :]
                            rk = []
                        else:
                            srcx = xs_d[b, t0:t1, :]
                            rk = [("xs", b, bi, ti_) for ti_ in range((t1 - t0) // 128)]
                        srcx3 = srcx.rearrange("(n p) d -> p n d", p=128)
                        for ti_ in range(ntile):
                            dma(xb[:, ti_, :], srcx3[:, ti_, :], rk, [("xblk", 0, ti_)])
                        s1p, shp = (modp[:, 3, :], modp[:, 2, :]) if bi == 0 else (modp[:, 1, :], modp[:, 0, :])

                        def post1(ti, t0=t0):
                            tk = t0 + ti * 128
                            tg = tk // 128
                            hTf_t = hTfbuf[:, :, (ti % 2) * 128:(ti % 2) * 128 + 128]
                            gg = 2 + tg % 2
                            for kc in range(8):
                                mm(ps_g[gg][:, 0:16], hTf_t[:, kc, :], wg32[:, kc, :], kc == 0, kc == 7, ["wg32", ("xblk", 2, ti % 2)], ["g%d" % gg])
                            tt("dve", Gt[:, tg, :], ps_g[gg][:, 0:16], biasb[:, B_MLV + 256:B_MLV + 272], ALU.add,
                               ["g%d" % gg, ("biasb", B_MLV + 256)], [("Gt", tg)])
                            if ti % 2 == 0:
                                return
                            tk0 = tk - 128
                            for c in range(4):
                                pq = ps_g[c // 2][:, (c % 2) * 256:(c % 2) * 256 + 256]
                                for kc in range(8):
                                    mm(pq, wq32[:, kc, c * 128:(c + 1) * 128], hTfbuf[:, kc, :], kc == 0, kc == 7,
                                       [("wring", 0), ("wring", 1), ("xblk", 2, 0), ("xblk", 2, 1)], ["g%d" % (c // 2)])
                            for hb in range(2):
                                tt("dve", qkT[:, 2 * hb:2 * hb + 2, tk0:tk0 + 256], v3(ps_g[hb][:, :], 2), bc(fmb[:, l, 2 * hb:2 * hb + 2].unsqueeze(2), [128, 2, 256]), ALU.add,
                                   ["g%d" % hb, "fmb"], [("qkpre", tg - 1, hb), ("qkpre", tg, hb)])

                        ln_block(xb, ntile, lambda ti: ("xblk", 0, ti), [xblk[1][:, 0, :], xblk[1][:, 3, :]], [("xblk", 1), ("xblk", 3)],
                                 lambda ti, t0=t0: hTall[:, :, t0 + ti * 128:t0 + (ti + 1) * 128], lambda ti, t0=t0: ("hTall", t0 // 128 + ti),
                                 s1p, shp, modk, hTf=lambda ti: hTfbuf[:, :, (ti % 2) * 128:(ti % 2) * 128 + 128], hfkey=lambda ti: ("xblk", 2, ti % 2), post=post1)
                    if dbg:
                        tap("hT", hTall[:, :, :], [("hTall", i) for i in range(NT)], BF16)

                    chk('ph1')
                    hT_keys = [("hTall", i) for i in range(NT)]
                    slot = next_slot()
                    wload(l, C_MLV, 272, slot)
                    S.op("pool", lambda h: h.memset(vaug[:, :, :, 64:65], 1.0), [], [("vaug", "ones")])
                    for ti in range(NT):
                        gix = next_g()
                        pg = ps_g[gix]
                        for kc in range(8):
                            mm(pg[:, 0:272], hTall[:, kc, ti * 128:(ti + 1) * 128], wring[slot][:, kc, 0:272], kc == 0, kc == 7,
                               [("wring", slot), hT_keys[ti]], ["g%d" % gix])
                        tt("dve", vaug[:, ti, :, 0:64], v3(pg[:, 0:256], 4), v3(biasb[:, B_MLV:B_MLV + 256], 4), ALU.add,
                           ["g%d" % gix, ("biasb", B_MLV)], [("vaug", ti)])
                    Gk = [("Gt", ti) for ti in range(NT)]
                    for d_ in range(2):
                        fsl = Gt[:, :, 4 + 8 * d_:8 + 8 * d_]
                        act(v3(gtmp[:, d_, :], NT), fsl, AF.Exp, Gk, [("gder", "t", d_)], scale=-1.0)
                        act(LFp[:, d_, :], gtmp[:, d_, :], AF.Ln, [("gder", "t", d_)], [("gder", "L", d_)], bias=1.0)
                    chk('g1')
                    pc = ps_g[0]
                    mm(pc[:, 0:72], triF, LFp[:, 0, :], True, True, ["cst", ("gder", "L", 0)], ["g0"])
                    mm(pc[:, 72:144], triB, LFp[:, 1, :], True, True, ["cst", ("gder", "L", 1)], ["g0"])
                    mm(pc[:, 144:288], onesF, LFp[:, :, :].rearrange("p a b -> p (a b)"), True, True, ["cst", ("gder", "L", 0), ("gder", "L", 1)], ["g0"])
                    chk('g2')
                    act(Ej[:, :, :].rearrange("p a b -> p (a b)"), pc[:, 0:144], AF.Exp, ["g0"], [("gder", "Ej")], scale=-1.0)
                    act(Eend[:, :, :].rearrange("p a b -> p (a b)"), pc[:, 144:288], AF.Exp, ["g0"], [("gder", "Eend")], scale=-1.0)
                    if dbg and 'cum' in dbg:
                        cp('dve', sgp[:, 0:288], pc[:, 0:288], ['g0'], [('mtmp', 'dbgc')])
                        tap('cum', sgp[:, 0:288], [('mtmp', 'dbgc')])
                        tap('LFp', LFp[:, :, :], [('gder', 'L', 0), ('gder', 'L', 1)])
                    chk('g3')
                    for d_ in range(2):
                        tt("dve", v3(gtmp[:, d_, :], NT), v3(pc[:, 72 * d_:72 * d_ + 72], NT), Gt[:, :, 8 * d_:8 * d_ + 4], ALU.add,
                           ["g0"] + Gk, [("gder", "t2", d_)])
                        act(Ws[:, d_, :], gtmp[:, d_, :], AF.Exp, [("gder", "t2", d_)], [("gder", "Ws", d_)])
                    pre_all = [("qkpre", ti, hb) for ti in range(NT) for hb in range(2)]
                    for c in range(4):
                        pst = qkT[:, c, :]
                        w0, w1, w2 = (convw[:, l, j, c:c + 1] for j in range(3))
                        ts("dve", cacc, pst, w1, convb[:, l, c:c + 1], ALU.mult, ALU.add, pre_all + ["convp"], ["cacc"])
                        for (a, e) in [(0, CTXL), (CTXL, T)]:
                            stt("dve", cacc[:, a + 1:e], pst[:, a:e - 1], w0, cacc[:, a + 1:e], ALU.mult, ALU.add, pre_all + ["convp", "cacc"], ["cacc"])
                            stt("dve", cacc[:, a:e - 1], pst[:, a + 1:e], w2, cacc[:, a:e - 1], ALU.mult, ALU.add, pre_all + ["convp", "cacc"], ["cacc"])
                        act(pst, cacc, AF.Tanh, ["cacc"] + pre_all, [("qkT", c)], scale=0.5)
                        sc_ = 0.5 if c < 2 else 0.0625
                        ts("pool", pst, pst, sc_, sc_, ALU.mult, ALU.add, [("qkT", c)], [("qkT", c)])
                        tt("dve", pst, pst, cacc, ALU.mult, [("qkT", c), "cacc"], [("qkT", c)])
                    if dbg:
                        tap("qkT", qkT[:, :, :], [("qkT", i) for i in range(4)], F32)
                        tap("vaug", vaug[:, :, :, :], [("vaug", i) for i in range(NT)] + [("vaug", "ones")], BF16)
                        tap("Gt", Gt[:, :, :], [("Gt", i) for i in range(NT)])

                    chk('ph2')
                    fence("X")
                    chk('ph3a')
                    orders = [list(range(NT)), [1, 0] + list(range(NT - 1, 1, -1))]
                    written = set()
                    S.op("pool", lambda h: h.memset(m_cst_flat, 0.0), [], [("mtmp", "cst", 0), ("mtmp", "cst", 1)])
                    for d_ in range(2):
                        S.op("pool", lambda h, d_=d_: h.memset(m_vsf[d_], 0.0), [], [("mtmp", "vs", d_)])
                    for step in range(NT):
                        for d_ in range(2):
                            ti = orders[d_][step]
                            first = step == 0
                            tks = slice(ti * 128, (ti + 1) * 128)
                            for h4 in range(4):
                                pr = slice((h4 % 2) * 64, (h4 % 2) * 64 + 64)
                                gS = 1 + (h4 % 2)
                                mm(ps_g[gS][:, (h4 // 2) * 128:(h4 // 2 + 1) * 128], qkT[pr, 2 + h4 // 2, tks], qkT[pr, h4 // 2, tks], True, True,
                                   [("qkT", 2 + h4 // 2), ("qkT", h4 // 2)], ["g%d" % gS])
                            if step == 0 and d_ == 0: chk('sa')
                            mk = triF if d_ == 0 else triB
                            for j in range(2):
                                tr(ps_tp[:, j * 128:(j + 1) * 128], qkT[:, 2 + j, tks], identF, [("qkT", 2 + j), "cst"], ["tpa"])
                            cp("act", m_kt[d_], ps_tp[:, 0:256], ["tpa"], [("mtmp", "kt", d_)])
                            for par in range(2):
                                tt("dve", m_pt[d_][:, par * 2:par * 2 + 2, :], v3(ps_g[1 + par][:, 0:256], 2), bc(mk.unsqueeze(1), [128, 2, 128]), ALU.mult,
                                   ["g%d" % (1 + par), "cst"], [("mtmp", "pt", d_, par)])
                            wsl = v3(Ws[:, d_, :], NT)[:, ti, :]
                            tt("pool", m_vs[d_], vaug[:, ti, :, :], bc(wsl.unsqueeze(2), [128, 4, 65]), ALU.mult,
                               [("vaug", ti), ("vaug", "ones"), ("gder", "Ws", d_)], [("mtmp", "vs", d_)])
                            if step == 0 and d_ == 0: chk('sb')
                            gH = 3
                            pH = ps_g[gH]
                            pD = ps_g[4]
                            for h4 in range(4):
                                pr = slice((h4 % 2) * 64, (h4 % 2) * 64 + 64)
                                mm(pH[:, h4 * 72:h4 * 72 + 65], m_pt[d_][:, (h4 % 2) * 2 + h4 // 2, :], m_vs[d_][:, h4, :], True, first,
                                   [("mtmp", "pt", d_, h4 % 2), ("mtmp", "vs", d_)], ["g3"])
                                if not first:
                                    mm(pH[:, h4 * 72:h4 * 72 + 65], qkT[pr, h4 // 2, tks], m_cst[pr, d_, h4, :], False, True,
                                       [("qkT", h4 // 2), ("mtmp", "cst", d_)], ["g3"])
                            for pp in range(2):
                                mm(pD[:, pp * 144:pp * 144 + 144], m_kt[d_][:, pp * 128:pp * 128 + 128], m_vsf[d_][:, pp * 144:pp * 144 + 144], True, True,
                                   [("mtmp", "kt", d_), ("mtmp", "vs", d_)], ["g4"])
                            if step == 0 and d_ == 0: chk('sc')
                            ejs = v3(Ej[:, d_, :], NT)[:, ti, :]
                            tt("dve", m_hs[d_], v3(pH[:, 0:288], 4)[:, :, 0:65], bc(ejs.unsqueeze(2), [128, 4, 65]), ALU.mult, ["g3", ("gder", "Ej")], [("mtmp", "hs", d_)])
                            if step == 0 and d_ == 0: chk('sd')
                            den = m_hs[d_][:, :, 64]
                            d2 = small[:, 208:212]
                            tt("dve", d2, den, den, ALU.mult, [("mtmp", "hs", d_)], ["md2"])
                            ts("dve", d2, d2, 1.0, None, ALU.max, None, ["md2"], ["md2"])
                            tt("pool", d2, d2, neghalf[:, 0:4], ALU.pow, ["md2", "cst"], ["md2"])
                            if ti not in written:
                                tt("dve", v3(Hml[:, ti, :], 4), m_hs[d_][:, :, 0:64], bc(d2.unsqueeze(2), [128, 4, 64]), ALU.mult,
                                   [("mtmp", "hs", d_), "md2"], [("Hml", ti)])
                                written.add(ti)
                            else:
                                tt("dve", m_hn, m_hs[d_][:, :, 0:64], bc(d2.unsqueeze(2), [128, 4, 64]), ALU.mult,
                                   [("mtmp", "hs", d_), "md2"], [("mtmp", "hn")])
                                tt("pool", v3(Hml[:, ti, :], 4), v3(Hml[:, ti, :], 4), m_hn, ALU.add, [("mtmp", "hn"), ("Hml", ti)], [("Hml", ti)])
                            if step == 0 and d_ == 0: chk('se')
                            tt("dve", m_tmp, v3(pD[:, 0:288], 4)[:, :, 0:65], m_cst[:, d_, :, :], ALU.add, ["g4", ("mtmp", "cst", d_)], [("mtmp", "tmp")])
                            ees = v3(Eend[:, d_, :], NT)[:, ti, :]
                            tt("dve", m_cst[:, d_, :, :], m_tmp, bc(ees.unsqueeze(2), [128, 4, 65]), ALU.mult, [("mtmp", "tmp"), ("gder", "Eend")], [("mtmp", "cst", d_)])
                    chk('sf') if False else None
                    if dbg:
                        tap("Hml", Hml[:, :, :], [("Hml", i) for i in range(NT)])

                    chk('ph3')
                    fence("X")
                    fence("B")
                    S.op("pool", lambda h: h.memset(dvaug[:, :, :, 128:129], 1.0), [], [("dvaug", "ones")])
                    for hh in range(2):
                        slot = next_slot()
                        wload(l, C_DAK + hh * 512, 512, slot)
                        for h2 in range(2):
                            h4 = hh * 2 + h2
                            for bi, (t0, t1) in enumerate(BLKS):
                                n = t1 - t0
                                ga, gb = [(1, 2), (3, 4)][bi % 2]
                                for kc in range(8):
                                    mm(ps_g[ga][:, 0:n], wring[slot][:, kc, h2 * 256:h2 * 256 + 128], hTall[:, kc, t0:t1], kc == 0, kc == 7,
                                       [("wring", slot)] + hT_keys[t0 // 128:t1 // 128], ["g%d" % ga])
                                if bi == 0:
                                    act(dkT[:, h4, t0:t1], ps_g[ga][:, 0:n], AF.Identity, ["g%d" % ga, "fmb"], [("dkT", h4, bi)], bias=fmb[:, l, 4 + 2 * h4:5 + 2 * h4])
                                    continue
                                for kc in range(8):
                                    mm(ps_g[gb][:, 0:n], wring[slot][:, kc, h2 * 256 + 128:h2 * 256 + 256], hTall[:, kc, t0:t1], kc == 0, kc == 7,
                                       [("wring", slot)] + hT_keys[t0 // 128:t1 // 128], ["g%d" % gb])
                                p0 = t0 - 256
                                dma(ropeC[:, 0:n], rope_d[0, :, p0:p0 + n], [], [("rope4", "c")])
                                dma(ropeS[:, 0:n], rope_d[1, :, p0:p0 + n], [], [("rope4", "s")])
                                stt("dve", rstA[:, 0:n], ps_g[ga][:, 0:n], fmb[:, l, 4 + 2 * h4:5 + 2 * h4], ropeC[:, 0:n], ALU.add, ALU.mult,
                                    ["g%d" % ga, "fmb", ("rope4", "c")], [("rope4", "a")])
                                stt("dve", rstB[:, 0:n], ps_g[gb][:, 0:n], fmb[:, l, 5 + 2 * h4:6 + 2 * h4], ropeS[:, 0:n], ALU.add, ALU.mult,
                                    ["g%d" % gb, "fmb", ("rope4", "s")], [("rope4", "b")])
                                tt("pool", dkT[:, h4, t0:t1], rstA[:, 0:n], rstB[:, 0:n], ALU.add, [("rope4", "a"), ("rope4", "b")], [("dkT", h4, bi)])
                    slot = next_slot()
                    wload(l, C_DAV, 512, slot)
                    for ti in range(NT):
                        gix = next_g(3, 5)
                        pg = ps_g[gix]
                        for kc in range(8):
                            mm(pg[:, :], hTall[:, kc, ti * 128:(ti + 1) * 128], wring[slot][:, kc, :], kc == 0, kc == 7,
                               [("wring", slot), hT_keys[ti]], ["g%d" % gix])
                        tt("dve", dvaug[:, ti, :, 0:128], v3(pg[:, :], 4), v3(biasb[:, B_DAV:B_DAV + 512], 4), ALU.add,
                           ["g%d" % gix, ("biasb", B_DAV)], [("dvaug", ti)])
                    s_a = next_slot()
                    wload(l, C_SG, 512, s_a)
                    s_b = next_slot()
                    wload(l, C_SG + 512, 256, s_b)
                    for tp_ in range(NT // 2):
                        for j in range(2):
                            ti = 2 * tp_ + j
                            ga = 1 if j == 0 else 3
                            for kc in range(8):
                                mm(ps_g[ga][:, :], hTall[:, kc, ti * 128:(ti + 1) * 128], wring[s_a][:, kc, :], kc == 0, kc == 7,
                                   [("wring", s_a), hT_keys[ti]], ["g%d" % ga])
                            for kc in range(8):
                                mm(ps_g[2][:, j * 256:(j + 1) * 256], hTall[:, kc, ti * 128:(ti + 1) * 128], wring[s_b][:, kc, 0:256], kc == 0, kc == 7,
                                   [("wring", s_b), hT_keys[ti]], ["g2"])
                            tt("dve", sgp2[:, j, 0:512], ps_g[ga][:, :], biasb[:, B_SG:B_SG + 512], ALU.add, ["g%d" % ga, ("biasb", B_SG)], [("sgt", "p", j)])
                        tt("dve", sgp2[:, :, 512:768], v3(ps_g[2][:, :], 2), bc(biasb[:, B_SG + 512:B_SG + 768].unsqueeze(1), [128, 2, 256]), ALU.add,
                           ["g2", ("biasb", B_SG)], [("sgt", "pz")])
                        kp = [("sgt", "p", 0), ("sgt", "p", 1)]
                        act(sgu2, sgp2[:, :, 0:256], AF.Gelu, kp, [("sgt", "u")])
                        for j in range(2):
                            act(sgp2[:, j, 256:512], sgp2[:, j, 256:512], AF.Gelu, [("sgt", "p", j)], [("sgt", "gv", j)], accum=small[:, 240 + j:241 + j])
                            act(sgjunk2[:, j, :], sgp2[:, j, 256:512], AF.Square, [("sgt", "gv", j)], [("sgt", "junk", j)], accum=small[:, 244 + j:245 + j])
                        act(sgzs2, sgp2[:, :, 512:768], AF.Tanh, [("sgt", "pz")], [("sgt", "zs")], scale=0.5)
                        kgv = [("sgt", "gv", 0), ("sgt", "gv", 1)]
                        kjk = [("sgt", "junk", 0), ("sgt", "junk", 1)]
                        mean, rstd = stats(small[:, 240:242], small[:, 244:246], 2, 256.0, (kgv, kjk))
                        for j in range(2):
                            ts("dve", sgjunk2[:, j, :], sgp2[:, j, 256:512], mean[:, j:j + 1], rstd[:, j:j + 1], ALU.subtract, ALU.mult,
                               [("sgt", "gv", j), "st_mean", "st_rstd"], [("sgt", "junk", j)])
                        tt("dve", sgjunk2, sgjunk2, bc(sggt.unsqueeze(1), [128, 2, 256]), ALU.mult, kjk + ["sggt"], kjk)
                        tt("dve", sgvn2, sgjunk2, bc(sgbt.unsqueeze(1), [128, 2, 256]), ALU.add, kjk + ["sgbt"], [("sgt", "vn")])
                        for j in range(2):
                            for g in range(4):
                                mm(ps_g[4][:, j * 256 + g * 64:j * 256 + (g + 1) * 64], sgW[:, l, g, :], sgvn2[:, j, g * 64:(g + 1) * 64], True, True,
                                   ["sgW", ("sgt", "vn")], ["g4"])
                        for j in range(2):
                            tt("dve", v3(sgy2[:, j, :], 4), v3(ps_g[4][:, j * 256:(j + 1) * 256], 4), bc(sgbs[:, l, :].unsqueeze(2), [128, 4, 64]), ALU.add,
                               ["g4", "sgbs"], [("sgt", "y", j)])
                        ky = [("sgt", "y", 0), ("sgt", "y", 1)]
                        tt("dve", sgy2, sgy2, sgu2, ALU.mult, ky + [("sgt", "u")], ky)
                        stt("dve", sgzs2, sgzs2, 1.0, sgp2[:, :, 512:768], ALU.add, ALU.mult, [("sgt", "zs"), ("sgt", "pz")], [("sgt", "zs")])
                        stt("dve", ysg[:, 2 * tp_:2 * tp_ + 2, :], sgy2, 0.5, sgzs2, ALU.mult, ALU.mult, ky + [("sgt", "zs")], [("ysg", 2 * tp_), ("ysg", 2 * tp_ + 1)])
                    if dbg:
                        tap("dkT", dkT[:, :, :], [("dkT", h_, b_) for h_ in range(4) for b_ in range(5)], BF16)
                        tap("ysg", ysg[:, :, :], [("ysg", i) for i in range(NT)], BF16)

                    chk('ph4')
                    fence("X")
                    fence("A")
                    last = (l == NL - 1)
                    for bi, (t0, t1) in enumerate(BLKS):
                        n = t1 - t0
                        ntile = n // 128
                        isctx = bi == 0
                        if last and isctx:
                            continue
                        xb = xblk[0]
                        if l == 0:
                            srcx = ctx_d[b, :, :] if isctx else x_d[b, t0 - 256:t1 - 256, :]
                            rk = []
                        else:
                            srcx = xs_d[b, t0:t1, :]
                            rk = [("xs", b, bi, ti_) for ti_ in range((t1 - t0) // 128)]
                        srcx3 = srcx.rearrange("(n p) d -> p n d", p=128)
                        for ti in range(ntile):
                            dma(xb[:, ti, :], srcx3[:, ti, :], rk, [("xblk", 0, ti)])
                        s1p, shp = (modp[:, 3, :], modp[:, 2, :]) if isctx else (modp[:, 1, :], modp[:, 0, :])
                        ln_block(xb, ntile, lambda ti: ("xblk", 0, ti), [xn5, xn5b], [("p5", "xn"), ("p5", "xnb")], lambda ti: hTblk[:, :, ti * 128:(ti + 1) * 128], lambda ti: ("p5", "hT", ti), s1p, shp, modk)
                        hk = [("p5", "hT", ti) for ti in range(ntile)]
                        if not isctx:
                            p0 = t0 - 256
                            dma(p5ropeC[:, 0:n], rope_d[0, :, p0:p0 + n], [], [("p5x", "c")])
                            dma(p5ropeS[:, 0:n], rope_d[1, :, p0:p0 + n], [], [("p5x", "s")])
                        for hh in range(2):
                            slot = next_slot()
                            wload(l, C_DAQ + hh * 512, 512, slot)
                            for h2 in range(2):
                                h4 = hh * 2 + h2
                                ga, gb = 0, 1
                                for kc in range(8):
                                    mm(ps_g[ga][:, 0:n], wring[slot][:, kc, h2 * 256:h2 * 256 + 128], hTblk[:, kc, 0:n], kc == 0, kc == 7, [("wring", slot)] + hk, ["g%d" % ga])
                                if isctx:
                                    act(qTblk[:, h4, 0:n], ps_g[ga][:, 0:n], AF.Identity, ["g%d" % ga, "fmb"], [("p5", "qT", h4)], bias=fmb[:, l, 12 + 2 * h4:13 + 2 * h4])
                                    continue
                                for kc in range(8):
                                    mm(ps_g[gb][:, 0:n], wring[slot][:, kc, h2 * 256 + 128:h2 * 256 + 256], hTblk[:, kc, 0:n], kc == 0, kc == 7, [("wring", slot)] + hk, ["g%d" % gb])
                                stt("dve", p5stA[:, 0:n], ps_g[ga][:, 0:n], fmb[:, l, 12 + 2 * h4:13 + 2 * h4], p5ropeC[:, 0:n], ALU.add, ALU.mult,
                                    ["g%d" % ga, "fmb", ("p5x", "c")], [("p5x", "a")])
                                stt("dve", p5stB[:, 0:n], ps_g[gb][:, 0:n], fmb[:, l, 13 + 2 * h4:14 + 2 * h4], p5ropeS[:, 0:n], ALU.add, ALU.mult,
                                    ["g%d" % gb, "fmb", ("p5x", "s")], [("p5x", "b")])
                                tt("pool", qTblk[:, h4, 0:n], p5stA[:, 0:n], p5stB[:, 0:n], ALU.add, [("p5x", "a"), ("p5x", "b")], [("p5", "qT", h4)])
                        for gi_, (c0, boff) in enumerate([(C_MLOZ, B_MLOZ), (C_DAZ, B_DAZ)]):
                            slot = next_slot()
                            wload(l, c0, 512, slot)
                            for ti in range(ntile):
                                gix = 2 + ti % 2
                                pg = ps_g[gix]
                                for kc in range(8):
                                    mm(pg[:, :], hTblk[:, kc, ti * 128:(ti + 1) * 128], wring[slot][:, kc, :], kc == 0, kc == 7, [("wring", slot), hk[ti]], ["g%d" % gix])
                                tt("dve", p5tmp, pg[:, :], biasb[:, boff:boff + 512], ALU.add, ["g%d" % gix, ("biasb", boff)], [("p5x", "tmp")])
                                act(p5t2, p5tmp, AF.Tanh, [("p5x", "tmp")], [("p5x", "t2")], scale=0.5)
                                gdst = gates[:, ti, gi_ * 512:(gi_ + 1) * 512]
                                if gi_ == 0:
                                    ts("dve", gdst[:, 0:256], p5t2[:, 0:256], 0.5, 0.5, ALU.mult, ALU.add, [("p5x", "t2")], [("p5", "g", ti, 0)])
                                    stt("dve", gdst[:, 256:512], p5t2[:, 256:512], 1.0, p5tmp[:, 256:512], ALU.add, ALU.mult, [("p5x", "t2"), ("p5x", "tmp")], [("p5", "g", ti, 1)])
                                else:
                                    stt("dve", gdst, p5t2, 1.0, p5tmp, ALU.add, ALU.mult, [("p5x", "t2"), ("p5x", "tmp")], [("p5", "g", ti, 2)])
                        tg0 = t0 // 128
                        mlv = v3(p5x_ml, 4)[:, 0:ntile, :]
                        sqv = v3(p5x_sq, 4)[:, 0:ntile, :]
                        kml = [("p5x", "c"), ("p5x", "s")]
                        ksq = [("p5x", "tmp"), ("p5x", "t2")]
                        k4 = ntile * 4
                        tt("dve", mlv, gates[:, 0:ntile, 0:256], Hml[:, tg0:tg0 + ntile, :], ALU.mult,
                           [("p5", "g", ti, 0) for ti in range(ntile)] + [("Hml", tg0 + ti) for ti in range(ntile)], kml)
                        red(small[:, 64:64 + k4], mlv.rearrange("p a (h d) -> p (a h) d", h=4), kml, ["mls1"])
                        tt("dve", sqv, mlv, mlv, ALU.mult, kml, ksq)
                        red(small[:, 80:80 + k4], sqv.rearrange("p a (h d) -> p (a h) d", h=4), ksq, ["mls2"])
                        mean, rstd = stats(small[:, 64:64 + k4], small[:, 80:80 + k4], k4, 64.0, "ml")
                        ml3 = mlv.rearrange("p a (h d) -> p (a h) d", h=4)
                        tt("dve", ml3, ml3, bc(mean.unsqueeze(2), [128, k4, 64]), ALU.subtract, kml + ["st_mean"], kml)
                        tt("dve", ml3, ml3, bc(rstd.unsqueeze(2), [128, k4, 64]), ALU.mult, kml + ["st_rstd"], kml)
                        tt("dve", mlv, mlv, bc(mlgc.unsqueeze(1), [128, ntile, 256]), ALU.mult, kml + ["mlgc"], kml)
                        gml = gates[:, 0:ntile, 0:256]
                        tt("dve", gml, mlv, gates[:, 0:ntile, 256:512], ALU.mult, kml + [("p5", "g", ti, 1) for ti in range(ntile)] + [("p5", "g", ti, 0) for ti in range(ntile)], [("p5", "mixml")])
                        ktiles = list(range(0, 2)) if isctx else list(range(NT))
                        STB = [(ps_g[4], "g4"), (ps_tp[:, 0:512], "tpa"), (ps_tp[:, 512:1024], "tpb")]
                        nk = len(ktiles)
                        LOOK = 2

                        def combine(h4):
                            nt_ = ntile
                            ok_ = [("p5", "oev", qi, c) for qi in range(nt_) for c in range(2)]
                            rr = v3(small[:, 192:200], 4)[:, 0:nt_, :]
                            attv = v3(p5stA, 4)[:, 0:nt_, :]
                            tmpv = v3(p5stB, 4)[:, 0:nt_, :]
                            ssv = small[:, 200:200 + nt_]
                            S.op("dve", lambda h, rr=rr, nt_=nt_: h.reciprocal(out=rr, in_=oev[:, 0:nt_, :, 128]), ok_, ["p5rr"])
                            ts("dve", rr[:, :, 1], rr[:, :, 1], neglam[:, l:l + 1], None, ALU.mult, None, ["p5rr", "neglam"], ["p5rr"])
                            tt("dve", attv, oev[:, 0:nt_, 0, 0:128], bc(rr[:, :, 0:1], [128, nt_, 128]), ALU.mult, ok_ + ["p5rr"], [("p5x", "a")])
                            tt("dve", tmpv, oev[:, 0:nt_, 1, 0:128], bc(rr[:, :, 1:2], [128, nt_, 128]), ALU.mult, ok_ + ["p5rr"], [("p5x", "b")])
                            tt("pool", attv, attv, tmpv, ALU.add, [("p5x", "a"), ("p5x", "b")], [("p5x", "a")])
                            tt("dve", tmpv, attv, attv, ALU.mult, [("p5x", "a")], [("p5x", "b")])
                            red(ssv, tmpv, [("p5x", "b")], ["p5ss"])
                            ts("dve", ssv, ssv, 1.0 / 128.0, EPS, ALU.mult, ALU.add, ["p5ss"], ["p5ss"])
                            tt("pool", ssv, ssv, neghalf[:, 0:nt_], ALU.pow, ["p5ss", "cst"], ["p5ss"])
                            tt("dve", attv, attv, bc(ssv.unsqueeze(2), [128, nt_, 128]), ALU.mult, [("p5x", "a"), "p5ss"], [("p5x", "a")])
                            tt("dve", attv, attv, bc(dagc.unsqueeze(1), [128, nt_, 128]), ALU.mult, [("p5x", "a"), "dagc"], [("p5x", "a")])
                            gsl = gates[:, 0:nt_, 512 + h4 * 128:512 + (h4 + 1) * 128]
                            tt("dve", gsl, attv, gsl, ALU.mult, [("p5x", "a")] + [("p5", "g", qi, 2) for qi in range(nt_)], [("p5", "damix", h4)])
                        tg0 = t0 // 128

                        items = [(h4, c, kti) for h4 in range(4) for c in range(2) for kti in range(nk)]
                        for g in range(len(items) + LOOK):
                            if g < len(items):
                                h4, c, kti = items[g]
                                kt = ktiles[kti]
                                pr = slice(c * 64, c * 64 + 64)
                                pS, gk = STB[g % 3]
                                mm(pS[:, 0:n], dkT[pr, h4, kt * 128:(kt + 1) * 128], qTblk[pr, h4, 0:n], True, True,
                                   [("dkT", h4, 0 if kt < 2 else 1 + (kt - 2) // 4), ("p5", "qT", h4)], [gk])
                                act(ptb[g % 4][:, 0:n], pS[:, 0:n], AF.Exp, [gk], [("p5", "pt", g % 4)], scale=0.125)
                            j = g - LOOK
                            if j >= 0:
                                h4, c, kti = items[j]
                                kt = ktiles[kti]
                                for qi in range(ntile):
                                    mm(ps_g[qi][:, 0:129], ptb[j % 4][:, qi * 128:(qi + 1) * 128], dvaug[:, kt, h4, :], kti == 0, kti == nk - 1,
                                       [("p5", "pt", j % 4), ("dvaug", kt), ("dvaug", "ones")], ["g%d" % qi])
                                if kti == nk - 1:
                                    for qi in range(ntile):
                                        cp("act" if qi % 2 else "dve", oev[:, qi, c, :], ps_g[qi][:, 0:129], ["g%d" % qi], [("p5", "oev", qi, c)])
                                    if c == 1:
                                        combine(h4)
                        for ti in range(ntile):
                            tg = tg0 + ti
                            for kc in range(8):
                                if kc < 2:
                                    src_, rk_ = gates[:, ti, kc * 128:(kc + 1) * 128], [("p5", "mixml")]
                                elif kc < 6:
                                    src_, rk_ = gates[:, ti, 512 + (kc - 2) * 128:512 + (kc - 1) * 128], [("p5", "damix", kc - 2)]
                                else:
                                    src_, rk_ = ysg[:, tg, (kc - 6) * 128:(kc - 5) * 128], [("ysg", tg)]
                                tr(ps_tb[:, kc * 128:(kc + 1) * 128], src_, identB, rk_ + ["cstb"], ["tb"])
                            cp("act", mixT[:, :, ti * 128:(ti + 1) * 128], v3(ps_tb[:, :], 8), ["tb"], [("p5", "mixT", ti)])
                        if dbg:
                            tap("mix%d" % bi, mixT[:, :, 0:n], [("p5", "mixT", ti) for ti in range(ntile)], BF16)
                        gateb = gate_c if isctx else gate_l
                        gkey = "gate_c" if isctx else "gate_l"
                        for hf in range(2):
                            slot = next_slot()
                            wload(l, hf * 512, 512, slot, src=wobf_d)
                            for ti in range(ntile):
                                gix = ti % 2
                                pg = ps_g[gix]
                                for kc in range(8):
                                    mm(pg[:, :], mixT[:, kc, ti * 128:(ti + 1) * 128], wring[slot][:, kc, :], kc == 0, kc == 7, [("wring", slot), ("p5", "mixT", ti)], ["g%d" % gix])
                                tt("dve", p5tmp, pg[:, :], gateb[:, hf * 512:(hf + 1) * 512], ALU.mult, ["g%d" % gix, gkey], [("p5x", "tmp")])
                                stt("dve", xb[:, ti, hf * 512:(hf + 1) * 512], xb[:, ti, hf * 512:(hf + 1) * 512], ALPHA, p5tmp, ALU.mult, ALU.add,
                                    [("xblk", 0, ti), ("p5x", "tmp")], [("xblk", 0, ti)])
                        s1 = small[:, 224:224 + ntile]
                        s2 = small[:, 232:232 + ntile]
                        k1 = [("lnst", 1, ti) for ti in range(ntile)]
                        k2 = [("lnst", 2, ti) for ti in range(ntile)]
                        for ti in range(ntile):
                            act(xn5, xb[:, ti, :], AF.Square, [("xblk", 0, ti)], [("p5", "xn"), k2[ti]], accum=s2[:, ti:ti + 1])
                            act(xn5, xb[:, ti, :], AF.Identity, [("xblk", 0, ti)], [("p5", "xn"), k1[ti]], accum=s1[:, ti:ti + 1])
                        mean, rstd = stats(s1, s2, ntile, 1024.0, (k1, k2))
                        if last:
                            dst3 = y_d[b, t0 - 256:t1 - 256, :].rearrange("(n p) d -> p n d", p=128)
                        else:
                            dst3 = xs_d[b, t0:t1, :].rearrange("(n p) d -> p n d", p=128)
                        for ti in range(ntile):
                            xk_ = ("xblk", 0, ti)
                            ts("dve", xb[:, ti, :], xb[:, ti, :], mean[:, ti:ti + 1], rstd[:, ti:ti + 1], ALU.subtract, ALU.mult, [xk_, "st_mean", "st_rstd"], [xk_])
                            tt("dve", xb[:, ti, :], xb[:, ti, :], lng, ALU.mult, [xk_, "lng"], [xk_])
                            tt("pool" if ti % 2 else "dve", xb[:, ti, :], xb[:, ti, :], lnb, ALU.add, [xk_, "lnb"], [xk_])
                            dma(dst3[:, ti, :], xb[:, ti, :], [xk_], [("y", b, bi, ti) if last else ("xs", b, bi, ti)])

        except _Stop:
            pass
        S.emit(nc, st)
    return nc, S, dbg_out


def make_consts():
    i = np.arange(128)
    ident = np.eye(128, dtype=np.float32)
    triF = (i[:, None] <= i[None, :]).astype(np.float32)
    triB = (i[:, None] >= i[None, :]).astype(np.float32)
    ones = np.ones((128, 128), np.float32)
    partner = np.where((i % 32) < 16, i + 16, i - 16)
    rperm = np.zeros((128, 128), np.float32)
    rperm[partner, i] = 1.0
    cst = np.concatenate([ident, triF, triB, ones, rperm], 1)
    t = np.arange(LAT)
    row = (t // 64).astype(np.float32)
    col = (t % 64).astype(np.float32)
    half = 32
    inv = (10000.0 ** (-(np.arange(0, half, 2, dtype=np.float32)) / half)).astype(np.float32)
    ang_r = row[:, None] * inv
    ang_c = col[:, None] * inv
    ang = np.concatenate([ang_r, ang_r, ang_c, ang_c], -1).astype(np.float32)
    cos = np.cos(ang).astype(np.float32)
    sin = np.sin(ang).astype(np.float32)
    d = np.arange(64)
    sign = np.where((d % 32) < 16, -1.0, 1.0).astype(np.float32)
    sinS = sin * sign[None, :]
    cosT = np.concatenate([cos.T, cos.T], 0)
    sinT = np.concatenate([sinS.T, sinS.T], 0)
    rope = np.ascontiguousarray(np.stack([cosT, sinT], 0)).astype(np.float32)
    return np.ascontiguousarray(cst), rope


_CACHE = {}


def kernel(**inputs):
    NCORES = 8
    NB = 32 // NCORES
    key = (NB, DEPTH)
    if key not in _CACHE:
        _CACHE[key] = build(NB, DEPTH)
    nc = _CACHE[key][0]
    cst, rope = make_consts()
    f = lambda a: np.ascontiguousarray(np.asarray(a, dtype=np.float32))
    shared = {k: f(inputs[k]) for k in ["w_mod", "b_mod", "w_in", "b_in", "ml_conv_w", "ml_conv_b", "ml_norm_g",
                                        "da_lam_q1", "da_lam_k1", "da_lam_q2", "da_lam_k2", "da_norm_g", "sg_norm_g",
                                        "sg_norm_b", "sg_w_s", "sg_b_s", "w_out", "ln_g", "ln_b"]}
    shared["c_ctx"] = f(inputs["c_ctx"]).reshape(1, D)
    shared["cst"] = cst
    shared["rope"] = rope
    x = f(inputs["x"])
    ctx = f(inputs["ctx"])
    c = f(inputs["c"])
    in_maps = []
    for i in range(NCORES):
        m = dict(shared)
        m["x"] = x[i * NB:(i + 1) * NB]
        m["ctx"] = ctx[i * NB:(i + 1) * NB]
        m["c"] = c[i * NB:(i + 1) * NB]
        in_maps.append(m)
    res = run_bass_kernel_spmd(nc, in_maps, core_ids=list(range(NCORES)))
    return np.concatenate([r["y"] for r in res.results], axis=0).astype(np.float32)
```
